# Optimizing a Trainium2 kernel written in Bass

```python
import jax
import jax.numpy as jnp
from jax import lax
import numpy as np

D_MODEL = 2048
BATCH = 1
SEQ = 16384
DEPTH = 1

N_HEADS = 16
N_KV_GROUPS = 2
HEADS_PER_GROUP = N_HEADS // N_KV_GROUPS
HEAD_DIM = 128
CMP_BLOCK = 32
CMP_STRIDE = 16
CMP_HIDDEN = 2 * HEAD_DIM
SEL_BLOCK = 64
SEL_TOP_N = 16
WINDOW = 512
Q_BLOCK = SEL_BLOCK
FORCE_BONUS = 1e6
CONV_CH = D_MODEL
CONV_WIDTH = 31
FFN_HIDDEN = 256 * ((8 * D_MODEL // 3 + 255) // 256)
FFN_CONV_WIDTH = 3
ADA_SCALE = 0.5
EPS = 1e-6
NEG_INF = -1e30

Q_WIDTH = N_HEADS * HEAD_DIM
KV_WIDTH = N_KV_GROUPS * HEAD_DIM
IN_PROJ_WIDTHS = (Q_WIDTH,) + (KV_WIDTH,) * 6 + (3 * N_HEADS, 2 * CONV_CH, 2 * D_MODEL)
IN_PROJ_WIDTH = sum(IN_PROJ_WIDTHS)

kernel_name = 'nsa_conformer_convffn_hybrid_block'


def rmsnorm(x, g):
    xf = x.astype(jnp.float32)
    y = xf * lax.rsqrt(jnp.mean(xf * xf, axis=-1, keepdims=True) + EPS)
    return (y * g.astype(jnp.float32)).astype(x.dtype)


def layernorm(x, g, b):
    xf = x.astype(jnp.float32)
    mu = jnp.mean(xf, axis=-1, keepdims=True)
    var = jnp.mean(jnp.square(xf - mu), axis=-1, keepdims=True)
    y = (xf - mu) * lax.rsqrt(var + EPS) * g.astype(jnp.float32) + b.astype(jnp.float32)
    return y.astype(x.dtype)


def masked_softmax(s, mask):
    s = jnp.where(mask, s.astype(jnp.float32), NEG_INF)
    m = jnp.max(s, axis=-1, keepdims=True)
    e = jnp.where(mask, jnp.exp(s - m), 0.0)
    return e / jnp.maximum(jnp.sum(e, axis=-1, keepdims=True), 1e-30)


def causal_dwconv(x, w, b):
    k = w.shape[0]
    y = lax.conv_general_dilated(x, w[:, None, :], window_strides=(1,), padding=[(k - 1, 0)],
                                 dimension_numbers=('NWC', 'WIO', 'NWC'),
                                 feature_group_count=x.shape[-1])
    return y + b


def compress(k_raw, pe, w1, w2):
    B, T, G, dk = k_raw.shape
    r = CMP_BLOCK // CMP_STRIDE
    n_chunks = T // CMP_STRIDE
    n_cmp = n_chunks - r + 1
    chunks = k_raw.reshape(B, n_chunks, CMP_STRIDE, G, dk)
    blocks = jnp.concatenate([chunks[:, j:j + n_cmp] for j in range(r)], axis=2)
    blocks = blocks + pe[None, None, :, None, :]
    flat = blocks.transpose(0, 3, 1, 2, 4).reshape(B, G, n_cmp, CMP_BLOCK * dk)
    return jax.nn.silu(flat @ w1) @ w2


def nsa_attention(q, k_cmp, v_cmp, k_slc, v_slc, k_win, v_win, gate_logits,
                  cmp_pe, w_kc1, w_kc2, w_vc1, w_vc2):
    B, T = q.shape[0], q.shape[1]
    G, Hg, dk = N_KV_GROUPS, HEADS_PER_GROUP, HEAD_DIM
    f32 = jnp.float32
    scale = dk ** -0.5
    slopes = jnp.exp2(-8.0 * jnp.arange(1, N_HEADS + 1, dtype=f32) / N_HEADS).reshape(G, Hg)
    slopes = slopes[None, :, :, None, None]

    kc = compress(k_cmp, cmp_pe, w_kc1, w_kc2)
    vc = compress(v_cmp, cmp_pe, w_vc1, w_vc2)
    n_cmp = kc.shape[2]
    cmp_start = jnp.arange(n_cmp, dtype=jnp.int32) * CMP_STRIDE
    pos_cmp = cmp_start + CMP_BLOCK - 1

    n_sel = T // SEL_BLOCK
    top_n = min(SEL_TOP_N, n_sel)
    sel_start = jnp.arange(n_sel, dtype=jnp.int32) * SEL_BLOCK
    overlap = ((cmp_start[:, None] < sel_start[None, :] + SEL_BLOCK)
               & (cmp_start[:, None] + CMP_BLOCK > sel_start[None, :])).astype(f32)
    ks_blk = k_slc.reshape(B, n_sel, SEL_BLOCK, G, dk).transpose(0, 3, 1, 2, 4)
    vs_blk = v_slc.reshape(B, n_sel, SEL_BLOCK, G, dk).transpose(0, 3, 1, 2, 4)
    kw_pad = jnp.pad(k_win, ((0, 0), (WINDOW, 0), (0, 0), (0, 0)))
    vw_pad = jnp.pad(v_win, ((0, 0), (WINDOW, 0), (0, 0), (0, 0)))

    n_qb = T // Q_BLOCK
    q_blocks = q.reshape(B, n_qb, Q_BLOCK, G, Hg, dk).transpose(1, 0, 3, 4, 2, 5)
    g_blocks = jax.nn.sigmoid(gate_logits.astype(f32)).reshape(
        B, n_qb, Q_BLOCK, 3, G, Hg).transpose(1, 0, 3, 4, 5, 2)
    gather = jax.vmap(jax.vmap(lambda blk, idx: blk[idx]))
    sel_off = jnp.arange(SEL_BLOCK, dtype=jnp.int32)
    win_off = jnp.arange(WINDOW + Q_BLOCK, dtype=jnp.int32)
    blk_ids = jnp.arange(n_sel, dtype=jnp.int32)

    def block_step(args):
        qb, qblk, gblk = args
        t0 = qb * Q_BLOCK
        t = t0 + jnp.arange(Q_BLOCK, dtype=jnp.int32)
        tf = t.astype(f32)

        s = jnp.einsum('bghqd,bgnd->bghqn', qblk, kc).astype(f32) * scale
        s = s - slopes * (tf[:, None] - pos_cmp.astype(f32)[None, :])
        p_cmp = masked_softmax(s, pos_cmp[None, :] <= t[:, None])
        o_cmp = jnp.einsum('bghqn,bgnd->bghqd', p_cmp.astype(vc.dtype), vc).astype(f32)

        imp = jnp.einsum('bghqn,ns->bgqs', p_cmp, overlap)
        cur = t // SEL_BLOCK
        valid = blk_ids[None, :] <= cur[:, None]
        forced = ((blk_ids[None, :] == 0) | (blk_ids[None, :] == cur[:, None])
                  | (blk_ids[None, :] == cur[:, None] - 1))
        score = jnp.where(valid, imp + FORCE_BONUS * forced.astype(f32), -1.0)
        _, idx = lax.top_k(score, top_n)
        k_sel = gather(ks_blk, idx).reshape(B, G, Q_BLOCK, top_n * SEL_BLOCK, dk)
        v_sel = gather(vs_blk, idx).reshape(B, G, Q_BLOCK, top_n * SEL_BLOCK, dk)
        pos_sel = (idx[..., None] * SEL_BLOCK + sel_off).reshape(B, G, Q_BLOCK, top_n * SEL_BLOCK)
        pos_sel = pos_sel[:, :, None]
        s = jnp.einsum('bghqd,bgqkd->bghqk', qblk, k_sel).astype(f32) * scale
        s = s - slopes * (tf[:, None] - pos_sel.astype(f32))
        p = masked_softmax(s, pos_sel <= t[:, None])
        o_slc = jnp.einsum('bghqk,bgqkd->bghqd', p.astype(v_sel.dtype), v_sel).astype(f32)

        k_w = lax.dynamic_slice_in_dim(kw_pad, t0, WINDOW + Q_BLOCK, axis=1)
        v_w = lax.dynamic_slice_in_dim(vw_pad, t0, WINDOW + Q_BLOCK, axis=1)
        pos_w = t0 - WINDOW + win_off
        dist = t[:, None] - pos_w[None, :]
        mask_w = (dist >= 0) & (dist < WINDOW) & (pos_w[None, :] >= 0)
        s = jnp.einsum('bghqd,bkgd->bghqk', qblk, k_w).astype(f32) * scale
        s = s - slopes * dist.astype(f32)
        p = masked_softmax(s, mask_w)
        o_win = jnp.einsum('bghqk,bkgd->bghqd', p.astype(v_w.dtype), v_w).astype(f32)

        out = (gblk[:, 0, ..., None] * o_cmp + gblk[:, 1, ..., None] * o_slc
               + gblk[:, 2, ..., None] * o_win)
        return out.astype(q.dtype)

    o = lax.map(block_step, (jnp.arange(n_qb, dtype=jnp.int32), q_blocks, g_blocks))
    return o.transpose(1, 0, 4, 2, 3, 5).reshape(B, T, N_HEADS * dk)


def conformer_conv(glu_in, w_dw, b_dw, ln_g, ln_b, w_pw, b_pw):
    a, g = jnp.split(glu_in, 2, axis=-1)
    u = a * jax.nn.sigmoid(g)
    u = causal_dwconv(u, w_dw, b_dw)
    u = jax.nn.silu(layernorm(u, ln_g, ln_b))
    return u @ w_pw + b_pw


def conv_ffn(h, w_up, w_dw, b_dw, w_down):
    u = causal_dwconv(h @ w_up, w_dw, b_dw)
    a, g = jnp.split(u, 2, axis=-1)
    return (jax.nn.silu(g) * a) @ w_down


def setup_inputs(seed: int = 0) -> dict:
    key = jax.random.key(seed)
    ks = jax.random.split(key, 25)
    L, D = DEPTH, D_MODEL

    def nrm(k, shape, s):
        return jax.random.normal(k, shape, jnp.float32) * s

    return {
        'x': nrm(ks[0], (BATCH, SEQ, D), 1.0),
        'c': nrm(ks[1], (BATCH, D), 1.0),
        'w_ada': nrm(ks[2], (L, D, 6 * D), ADA_SCALE * D ** -0.5),
        'b_ada': nrm(ks[3], (L, 6 * D), 0.01),
        'norm1_g': 1.0 + nrm(ks[4], (L, D), 0.02),
        'w_in': nrm(ks[5], (L, D, IN_PROJ_WIDTH), D ** -0.5),
        'cmp_pe': nrm(ks[6], (L, CMP_BLOCK, HEAD_DIM), 0.1),
        'w_kc1': nrm(ks[7], (L, CMP_BLOCK * HEAD_DIM, CMP_HIDDEN), (CMP_BLOCK * HEAD_DIM) ** -0.5),
        'w_kc2': nrm(ks[8], (L, CMP_HIDDEN, HEAD_DIM), CMP_HIDDEN ** -0.5),
        'w_vc1': nrm(ks[9], (L, CMP_BLOCK * HEAD_DIM, CMP_HIDDEN), (CMP_BLOCK * HEAD_DIM) ** -0.5),
        'w_vc2': nrm(ks[10], (L, CMP_HIDDEN, HEAD_DIM), CMP_HIDDEN ** -0.5),
        'w_o_nsa': nrm(ks[11], (L, Q_WIDTH, D), Q_WIDTH ** -0.5),
        'conv_dw_w': nrm(ks[12], (L, CONV_WIDTH, CONV_CH), CONV_WIDTH ** -0.5),
        'conv_dw_b': nrm(ks[13], (L, CONV_CH), 0.01),
        'conv_ln_g': 1.0 + nrm(ks[14], (L, CONV_CH), 0.02),
        'conv_ln_b': nrm(ks[15], (L, CONV_CH), 0.01),
        'conv_pw_w': nrm(ks[16], (L, CONV_CH, D), CONV_CH ** -0.5),
        'conv_pw_b': nrm(ks[17], (L, D), 0.01),
        'w_out': nrm(ks[18], (L, D, D), D ** -0.5),
        'norm2_g': 1.0 + nrm(ks[19], (L, D), 0.02),
        'ffn_w_up': nrm(ks[20], (L, D, 2 * FFN_HIDDEN), D ** -0.5),
        'ffn_dw_w': nrm(ks[21], (L, FFN_CONV_WIDTH, 2 * FFN_HIDDEN), FFN_CONV_WIDTH ** -0.5),
        'ffn_dw_b': nrm(ks[22], (L, 2 * FFN_HIDDEN), 0.01),
        'ffn_w_down': nrm(ks[23], (L, FFN_HIDDEN, D), FFN_HIDDEN ** -0.5),
        'final_g': 1.0 + nrm(ks[24], (D,), 0.02),
    }


def reference(x, c, w_ada, b_ada, norm1_g, w_in, cmp_pe, w_kc1, w_kc2, w_vc1, w_vc2,
              w_o_nsa, conv_dw_w, conv_dw_b, conv_ln_g, conv_ln_b, conv_pw_w, conv_pw_b,
              w_out, norm2_g, ffn_w_up, ffn_dw_w, ffn_dw_b, ffn_w_down, final_g):
    B, T = x.shape[0], x.shape[1]
    split_points = np.cumsum(IN_PROJ_WIDTHS)[:-1].tolist()
    c_act = jax.nn.silu(c)
    for l in range(DEPTH):
        ada = (c_act @ w_ada[l] + b_ada[l])[:, None, :]
        sh1, sc1, g1, sh2, sc2, g2 = jnp.split(ada, 6, axis=-1)

        h = rmsnorm(x, norm1_g[l]) * (1.0 + sc1) + sh1
        proj = h @ w_in[l]
        q, kc, vc, ksl, vsl, kw, vw, g_nsa, glu_in, g_merge = jnp.split(proj, split_points, axis=-1)
        q = q.reshape(B, T, N_HEADS, HEAD_DIM)
        kv = [a.reshape(B, T, N_KV_GROUPS, HEAD_DIM) for a in (kc, vc, ksl, vsl, kw, vw)]
        o_nsa = nsa_attention(q, kv[0], kv[1], kv[2], kv[3], kv[4], kv[5],
                              g_nsa.reshape(B, T, 3, N_HEADS),
                              cmp_pe[l], w_kc1[l], w_kc2[l], w_vc1[l], w_vc2[l])
        y_a = o_nsa @ w_o_nsa[l]
        y_b = conformer_conv(glu_in, conv_dw_w[l], conv_dw_b[l], conv_ln_g[l], conv_ln_b[l],
                             conv_pw_w[l], conv_pw_b[l])
        ga, gb = jnp.split(g_merge, 2, axis=-1)
        merged = jax.nn.sigmoid(ga) * y_a + jax.nn.sigmoid(gb) * y_b
        x = x + g1 * (merged @ w_out[l])

        h = rmsnorm(x, norm2_g[l]) * (1.0 + sc2) + sh2
        x = x + g2 * conv_ffn(h, ffn_w_up[l], ffn_dw_w[l], ffn_dw_b[l], ffn_w_down[l])
    return rmsnorm(x, final_g)
```

```python
from contextlib import ExitStack
import numpy as np
import ml_dtypes
import concourse.bass as bass
import concourse.mybir as mybir
from concourse.bass_utils import run_bass_kernel_spmd

F32 = mybir.dt.float32
BF16 = mybir.dt.bfloat16
I32 = mybir.dt.int32
AF = mybir.ActivationFunctionType
ALU = mybir.AluOpType
NPBF = ml_dtypes.bfloat16

D = 2048
T = 16384
NCORE = 8
TOWN = T // NCORE
NH, NG, HG, DK = 16, 2, 8, 128
EPS = 1e-6
FF = 5632
WIN = 512
OWN0 = 16384
TV = OWN0 + TOWN
HALO = 64
Q0 = OWN0 - HALO
NQ = TOWN + HALO
W0 = OWN0 - 640
NA = TV - W0
B0 = OWN0 - 128
NB = TV - B0
NCV = TV // 16
C_Q, C_KC, C_VC, C_KS, C_VS, C_KW, C_VW, C_GN, C_GLU, C_GM = 0, 2048, 2304, 2560, 2816, 3072, 3328, 3584, 3632, 7728
INW = 11824
EPOCH = 12000
SLOPES = [2.0 ** (-8.0 * (h + 1) / 16) for h in range(NH)]
NBLKW = 264
NBW = 272
DOFF, NDS = 272, 280
ROFF, NDC = 296, 280
QTILES = [(Q0, 64)] + [(OWN0 + 512 * i, 512) for i in range(4)]
SKIP = 200.0


def exp_width(h, nq):
    w = 64
    while w * 2 <= min(512, nq) and SLOPES[h] * (w * 2) <= 64.0:
        w *= 2
    return min(w, nq)


def n_slc_chunks(h, nq, maxc):
    return int(min(maxc, np.floor((SKIP / SLOPES[h] + nq) / 128) + 1))


def n_cmp_chunks(h, nq):
    return int(min(9, np.floor((SKIP / SLOPES[h] + nq + 15) / 2048) + 1))


def chunk_valid(kind, nq, j):
    i = np.arange(128)[:, None]
    q = np.arange(nq)[None, :]
    if kind == "cmp":
        kp = 16 * i + 31 + nq - 2048 * (j + 1)
        v = kp <= q
    else:
        kp = i + nq - 128 * (j + 1)
        dist = q - kp
        v = dist >= 0
        if kind == "win":
            v = v & (dist < WIN)
    return None if v.all() else v


def mask_index():
    idx = {}
    for nq in (64, 512):
        nwin = 8 if nq == 512 else 5
        for kind, nj in (("slc", 4), ("win", nwin), ("cmp", 1)):
            for j in range(nj):
                v = chunk_valid(kind, nq, j)
                if v is None:
                    continue
                idx[(kind, nq, j)] = v
    uniq, out = [], {}
    for key, v in idx.items():
        for n, u in enumerate(uniq):
            if u.shape == v.shape and (u == v).all():
                out[key] = n
                break
        else:
            uniq.append(v)
            out[key] = len(uniq) - 1
    return out, uniq


MASK_IDX, MASK_LIST = mask_index()
NMASK = len(MASK_LIST)


class Res:
    __slots__ = ("name", "last_w", "readers")

    def __init__(self, name=""):
        self.name = name
        self.last_w = None
        self.readers = []


class Op:
    __slots__ = ("eng", "fn", "deps", "seq", "sig", "is_dma", "chan", "needs_sig", "pos")


class Chan:
    def __init__(self, prog, name):
        self.sem = prog.new_sem("ch_" + name)
        self.count = 0
        self.last_op = None
        self.group = []


class Prog:
    ENGS = ("pe", "act", "dve", "pool", "sp")

    def __init__(self, nc, stack):
        self.nc = nc
        self.stack = stack
        self.ops = {e: [] for e in self.ENGS}
        self.seq = 0
        self.nsem = 0
        self.eng_sems = {e: [] for e in self.ENGS}
        self.chans = []
        self.uid = 0

    def new_sem(self, name):
        self.nsem += 1
        return self.stack.enter_context(self.nc.semaphore(name))

    def chan(self, name):
        c = Chan(self, name)
        self.chans.append(c)
        return c

    def _mk(self, eng, fn, reads, writes):
        op = Op()
        op.eng, op.fn, op.seq = eng, fn, self.seq
        self.seq += 1
        op.is_dma, op.chan, op.needs_sig, op.sig = False, None, False, None
        deps = []
        for r in reads:
            if r.last_w is not None:
                deps.append(r.last_w)
        for w in writes:
            if w.last_w is not None:
                deps.append(w.last_w)
            deps.extend(w.readers)
        for r in reads:
            if not getattr(op, "is_dma", False) and eng in ("pe", "act", "dve"):
                r.readers = [x for x in r.readers if x.eng != eng or x.is_dma]
            r.readers.append(op)
        for w in writes:
            w.last_w = op
            w.readers = []
        seen, dd = set(), []
        for d in deps:
            if id(d) in seen or d is op:
                continue
            seen.add(id(d))
            if eng == "pe" and d.eng == "pe" and not d.is_dma:
                continue
            dd.append(d)
        op.deps = dd
        self.ops[eng].append(op)
        return op

    def op(self, eng, fn, reads=(), writes=()):
        return self._mk(eng, fn, list(reads), list(writes))

    def dma(self, queue, chan, fn, reads=(), writes=(), cont=False):
        op = self._mk(queue, fn, list(reads), list(writes))
        op.is_dma, op.chan = True, chan
        if not cont:
            if chan.last_op is not None and chan.last_op not in op.deps:
                op.deps.append(chan.last_op)
            chan.group = []
        op.deps = [d for d in op.deps if d not in chan.group]
        chan.count += 16
        chan.group.append(op)
        for o in chan.group:
            o.sig = (chan.sem, chan.count)
        op.needs_sig = True
        chan.last_op = op
        return op

    def barrier(self):
        lasts = []
        for e in self.ENGS:
            for op in reversed(self.ops[e]):
                if not op.is_dma:
                    lasts.append(op)
                    break
        for c in self.chans:
            if c.last_op is not None:
                lasts.append(c.last_op)
        nc = self.nc
        eo = {"pe": nc.tensor, "act": nc.scalar, "dve": nc.vector, "pool": nc.gpsimd, "sp": nc.sync}
        for e in self.ENGS:
            op = self._mk(e, (lambda e=e: eo[e].nop()), [], [])
            op.deps = [d for d in lasts if not (d.eng == e and not d.is_dma)]

    def emit(self):
        nc = self.nc
        for e in self.ENGS:
            for op in self.ops[e]:
                for d in op.deps:
                    if not d.is_dma:
                        d.needs_sig = True
        for e in self.ENGS:
            cnt, sem = 0, None
            for op in self.ops[e]:
                if op.is_dma or not op.needs_sig:
                    continue
                if sem is None or cnt >= EPOCH:
                    sem = self.new_sem(f"e_{e}_{len(self.eng_sems[e])}")
                    self.eng_sems[e].append(sem)
                    cnt = 0
                cnt += 1
                op.sig = (sem, cnt)
        with nc.Block() as block:
            for e in self.ENGS:
                ops = self.ops[e]
                if not ops:
                    continue
                deco = {"pe": block.tensor, "act": block.scalar, "dve": block.vector,
                        "pool": block.gpsimd, "sp": block.sync}[e]

                def body(engobj, ops=ops):
                    known = {}
                    for op in ops:
                        for d in op.deps:
                            sem, val = d.sig
                            if known.get(id(sem), 0) >= val:
                                continue
                            engobj.wait_ge(sem, val)
                            known[id(sem)] = val
                        ins = op.fn()
                        if op.needs_sig:
                            ins.then_inc(op.sig[0], 16 if op.is_dma else 1)
                    last = {}
                    for op in ops:
                        if op.is_dma:
                            last[id(op.chan)] = op
                    for op in last.values():
                        sem, val = op.sig
                        if known.get(id(sem), 0) < val:
                            engobj.wait_ge(sem, val)
                            known[id(sem)] = val
                deco(body)


class Pool:
    def __init__(self, P, st, name, n, shape, dt, psum=False):
        self.t, self.r = [], []
        for i in range(n):
            if psum:
                self.t.append(st.enter_context(P.nc.psum_tensor(f"{name}{i}", list(shape), dt)))
            else:
                self.t.append(st.enter_context(P.nc.sbuf_tensor(f"{name}{i}", list(shape), dt)))
            self.r.append(Res(f"{name}{i}"))
        self.i = 0
        self.n = n

    def next(self):
        k = self.i % self.n
        self.i += 1
        return self.t[k], self.r[k]


class K:
    def __init__(self, dbg=None):
        self.dbg = dbg
        nc = self.nc = bass.Bass("TRN2", target_bir_lowering=False)
        self.ins = {}
        self.outs = {}

    def din(self, name, shape, dt=F32):
        t = self.nc.dram_tensor(name, list(shape), dt, kind="ExternalInput").ap()
        self.ins[name] = t
        return t

    def dscr(self, name, shape, dt):
        kind = "ExternalOutput" if (self.dbg and name in self.dbg) else "Internal"
        t = self.nc.dram_tensor(name, list(shape), dt, kind=kind).ap()
        return t

    def build(self, upto=99):
        nc = self.nc
        xv = self.din("xv", [TV, D])
        c_in = self.din("c", [1, D])
        w_ada = self.din("w_ada", [D, 6 * D])
        b_ada = self.din("b_ada", [6 * D])
        norm1_g = self.din("norm1_g", [D])
        w_in = self.din("w_in", [D, INW])
        vtok = self.din("vtok", [128, TV // 128])
        ident_in = self.din("ident", [128, 128], BF16)
        self.i_vcmp = self.din("vcmp", [128, NCV // 128])
        self.i_selb = self.din("selb", [5, 4, 128, NBW])
        self.i_selv = self.din("selv", [5, 4, 128, NBW])
        self.i_hflag = self.din("hflag", [128, 1])
        self.i_bias_s = self.din("bias_s", [128, NH, NDS])
        self.i_bias_c = self.din("bias_c", [128, NH, NDC])
        self.i_masks = self.din("masks", [128, NMASK, 512], BF16)
        self.i_esel = self.din("esel", [128, 64, 128], BF16)
        self.i_ovm = self.din("ovm", [128, 9, NBW], BF16)
        self.i_ones = self.din("ones_bf", [128, 128], BF16)
        self.i_cmp_pe = self.din("cmp_pe", [32, 128])
        self.i_w1 = [self.din("w_kc1", [4096, 256]), self.din("w_vc1", [4096, 256])]
        self.i_w2 = [self.din("w_kc2", [256, 128]), self.din("w_vc2", [256, 128])]
        self.i_wo = self.din("w_o_nsa", [D, D])
        self.i_dww = self.din("conv_dw_w", [31, D])
        self.i_dwb = self.din("conv_dw_b", [D])
        self.i_lng = self.din("conv_ln_g", [D])
        self.i_lnb = self.din("conv_ln_b", [D])
        self.i_wpw = self.din("conv_pw_w", [D, D])
        self.i_pwb = self.din("conv_pw_b", [D])
        self.i_wout = self.din("w_out", [D, D])
        self.i_n2g = self.din("norm2_g", [D])
        self.i_wup = self.din("ffn_w_up", [D, 2 * FF])
        self.i_fdw = self.din("ffn_dw_w", [3, 2 * FF])
        self.i_fdb = self.din("ffn_dw_b", [2 * FF])
        self.i_wdn = self.din("ffn_w_down", [FF, D])
        self.i_fg = self.din("final_g", [D])
        out = self.nc.dram_tensor("out", [TOWN, D], F32, kind="ExternalOutput").ap()
        self.out = out
        ada_d = self.dscr("ada_d", [6 * D], F32)
        kcT_raw = self.dscr("kcT_raw", [NG, 128, TV + 16], BF16)
        vcT_raw = self.dscr("vcT_raw", [NG, 128, TV + 16], BF16)
        kslT = self.dscr("kslT", [NG, 128, TV], BF16)
        vsl = self.dscr("vsl", [TV, NG, 130], BF16)
        QT = self.dscr("QT", [128, NH, NB], BF16)
        kwT = self.dscr("kwT", [NG, 128, NA], BF16)
        vw = self.dscr("vw", [NA, NG, 130], BF16)
        gates = self.dscr("gates", [NB, 48], F32)
        gluT = self.dscr("gluT", [128, 16, NB], BF16)
        mgT = self.dscr("mgT", [128, 32, NB], BF16)
        self.kcT = self.dscr("kcT", [NG, 128, NCV], BF16)
        self.vca = self.dscr("vca", [NCV, NG, 130], BF16)
        self.xmid = self.dscr("xmid", [NQ, D], F32)
        self.accd = self.dscr("accd", [NQ, D], F32)
        self.dumpS = self.dscr("dumpS", [128, 512], F32)
        self.impd = self.dscr("impd", [5, 4, 128, NG, NBW], F32)
        self.wb_d = {n: self.dscr(n, sh, BF16) for n, sh in (("wo_b", [D, D]), ("wpw_b", [D, D]), ("wout_b", [D, D]),
                                                           ("wup_b", [D, 2 * FF]), ("wdn_b", [FF, D]))}
        self.xv, self.kcT_raw, self.vcT_raw, self.kslT, self.vsl = xv, kcT_raw, vcT_raw, kslT, vsl
        self.QT, self.kwT, self.vw, self.gates, self.gluT, self.mgT, self.ada_d = QT, kwT, vw, gates, gluT, mgT, ada_d

        with ExitStack() as st0:
            P = self.P = Prog(nc, st0)
            self.st0 = st0
            sb = lambda name, shape, dt: st0.enter_context(nc.sbuf_tensor(name, list(shape), dt))
            ident = sb("ident_sb", [128, 128], BF16)
            r_const = Res("const")
            ch_c = P.chan("const")
            P.dma("sp", ch_c, lambda: nc.sync.dma_start(out=ident[:], in_=ident_in[:, :]), writes=[r_const])
            vtok_sb = sb("vtok_sb", [128, TV // 128], F32)
            P.dma("sp", ch_c, lambda: nc.sync.dma_start(out=vtok_sb[:], in_=vtok[:, :]), writes=[r_const], cont=True)
            ada = sb("ada_sb", [128, 96], F32)
            r_ada = Res("ada")
            s1 = sb("s1", [128, 16], F32)
            r_s1 = Res("s1")
            self.ident, self.r_const, self.vtok_sb, self.ada, self.r_ada, self.sb0 = ident, r_const, vtok_sb, ada, r_ada, sb
            self.ones = sb("ones_sb", [128, 128], BF16)
            P.dma("sp", ch_c, lambda: nc.sync.dma_start(out=self.ones[:], in_=self.i_ones[:, :]), writes=[r_const], cont=True)
            self.hfl = sb("hfl_sb", [128, 1], F32)
            P.dma("sp", ch_c, lambda: nc.sync.dma_start(out=self.hfl[:], in_=self.i_hflag[:, :]), writes=[r_const], cont=True)
            self.vcmp_sb = sb("vcmp_sb", [128, NCV // 128], F32)
            P.dma("sp", ch_c, lambda: nc.sync.dma_start(out=self.vcmp_sb[:], in_=self.i_vcmp[:, :]), writes=[r_const], cont=True)

            with ExitStack() as st:
                cT = st.enter_context(nc.sbuf_tensor("cT", [128, 16], F32))
                cact = st.enter_context(nc.sbuf_tensor("cact", [128, 16], F32))
                bT = st.enter_context(nc.sbuf_tensor("bT", [128, 96], F32))
                g1T = st.enter_context(nc.sbuf_tensor("g1T", [128, 16], F32))
                r_cT, r_cact, r_bT, r_g1T = Res(), Res(), Res(), Res()
                ch0 = P.chan("p0")
                P.dma("sp", ch0, lambda: nc.sync.dma_start(out=cT[:], in_=c_in[0, :].rearrange("(j p) -> p j", p=128),
                                                         allow_slow_non_contiguous=True), writes=[r_cT])
                P.dma("sp", ch0, lambda: nc.sync.dma_start(out=bT[:], in_=b_ada.rearrange("(f p) -> p f", p=128),
                                                         allow_slow_non_contiguous=True), writes=[r_bT], cont=True)
                P.dma("sp", ch0, lambda: nc.sync.dma_start(out=g1T[:], in_=norm1_g.rearrange("(j p) -> p j", p=128),
                                                         allow_slow_non_contiguous=True), writes=[r_g1T], cont=True)
                P.op("act", lambda: nc.scalar.activation(out=cact[:], in_=cT[:], func=AF.Silu), reads=[r_cT], writes=[r_cact])
                wpool = Pool(P, st, "wada", 2, [128, 16, 512], F32)
                chw = [P.chan("wada0"), P.chan("wada1")]
                aps = st.enter_context(nc.psum_tensor("ada_ps", [128, 96], F32))
                r_aps = Res()
                for blk in range(24):
                    wt, wr = wpool.next()
                    P.dma("sp", chw[blk % 2],
                          lambda wt=wt, blk=blk: nc.sync.dma_start(
                              out=wt[:], in_=w_ada[:, blk * 512:(blk + 1) * 512].rearrange("(k p) c -> p k c", p=128)),
                          writes=[wr])
                    for fl in range(4):
                        f = blk * 4 + fl
                        for kc in range(16):
                            P.op("pe", lambda wt=wt, fl=fl, kc=kc, f=f: nc.tensor.matmul(
                                aps[:, f:f + 1], lhsT=wt[:, kc, fl * 128:(fl + 1) * 128], rhs=cact[:, kc:kc + 1],
                                start=(kc == 0), stop=(kc == 15)), reads=[wr, r_cact], writes=[r_aps])
                P.op("dve", lambda: nc.vector.tensor_tensor(out=ada[:], in0=aps[:], in1=bT[:], op=ALU.add),
                     reads=[r_aps, r_bT], writes=[r_ada])
                P.op("dve", lambda: nc.vector.scalar_tensor_tensor(out=s1[:], in0=ada[:, 16:32], scalar=1.0, in1=g1T[:],
                                                                   op0=ALU.add, op1=ALU.mult),
                     reads=[r_ada, r_g1T], writes=[r_s1])
                r_adad = Res()
                ch_ad = P.chan("adad")
                P.dma("act", ch_ad, lambda: nc.scalar.dma_start(out=ada_d.rearrange("(f p) -> p f", p=128), in_=ada[:],
                                                              allow_slow_non_contiguous=True),
                      reads=[r_ada], writes=[r_adad])
                P.barrier()
            if upto <= 0:
                P.emit()
                return nc

            with ExitStack() as st:
                self.cast_setup(st)
                wkv32 = Pool(P, st, "wkv32_", 2, [128, 16, 128], F32)
                wkv = st.enter_context(nc.sbuf_tensor("wkv", [128, 16, 1024], BF16))
                r_wkv = Res()
                chw = [P.chan("wkv0"), P.chan("wkv1")]
                for q in range(8):
                    wt, wr = wkv32.next()
                    P.dma("sp", chw[q % 2], lambda wt=wt, q=q: nc.sync.dma_start(
                        out=wt[:], in_=w_in[:, C_KC + q * 128:C_KC + (q + 1) * 128].rearrange("(k p) c -> p k c", p=128)),
                        writes=[wr])
                    P.op("pool", lambda wt=wt, q=q: nc.gpsimd.tensor_copy(out=wkv[:, :, q * 128:(q + 1) * 128], in_=wt[:]),
                         reads=[wr], writes=[r_wkv])
                xpool = Pool(P, st, "xt", 2, [128, 4, D], F32)
                chx = [P.chan("x0"), P.chan("x1")]
                junk = st.enter_context(nc.sbuf_tensor("junk", [128, D], BF16))
                r_junk = Res()
                sspool = Pool(P, st, "ss", 2, [128, 4], F32)
                rspool = Pool(P, st, "rs", 2, [128, 4], F32)
                xnpool = Pool(P, st, "xn", 1, [128, 4, D], BF16)
                hTpool = Pool(P, st, "hT", 2, [128, 16, 512], BF16)
                tpp = Pool(P, st, "tp", 4, [128, 512], BF16, psum=True)
                mmp = Pool(P, st, "mm", 3, [128, 512], F32, psum=True)
                stg = Pool(P, st, "stg", 1, [128, 6, 512], BF16)
                stv = Pool(P, st, "stv", 2, [128, 4, NG, 130], BF16)
                chs = [P.chan("st0"), P.chan("st1")]
                chv = [P.chan("sv0"), P.chan("sv1")]
                r_scr = Res("scr1a")
                for i in range(2):
                    P.op("dve", lambda i=i: nc.vector.memset(stv.t[i][:], 0.0), writes=[stv.r[i]])
                nblk = TV // 512
                if self.dbg and "nblk" in self.dbg:
                    nblk = self.dbg["nblk"]
                for tb in range(nblk):
                    t0 = tb * 512
                    if not (self.dbg and "nocast" in self.dbg):
                        self.cast_step(5)
                    hT, r_hT = self.norm_T(P, st, xv, t0, 4, xpool, chx[tb % 2], junk, r_junk, sspool, rspool, xnpool,
                                           hTpool, tpp, ident, r_const, s1, r_s1, ada, r_ada, 0)
                    sg, r_sg = stg.next()
                    for ci in range(6):
                        ps, r_ps = mmp.next()
                        for kc in range(16):
                            P.op("pe", lambda ps=ps, ci=ci, kc=kc, hT=hT: nc.tensor.matmul(
                                ps[:], lhsT=wkv[:, kc, ci * 128:(ci + 1) * 128], rhs=hT[:, kc, :],
                                start=(kc == 0), stop=(kc == 15)), reads=[r_wkv, r_hT], writes=[r_ps])
                        if ci % 2 == 0:
                            P.op("act", lambda ps=ps, sg=sg, ci=ci: nc.scalar.copy(out=sg[:, ci, :], in_=ps[:]),
                                 reads=[r_ps], writes=[r_sg])
                        else:
                            P.op("dve", lambda ps=ps, sg=sg, ci=ci: nc.vector.tensor_copy(out=sg[:, ci, :], in_=ps[:]),
                                 reads=[r_ps], writes=[r_sg])
                    dsts = [kcT_raw, vcT_raw, kslT]
                    for k3 in range(3):
                        P.dma("act", chs[tb % 2], lambda sg=sg, k3=k3, t0=t0: nc.scalar.dma_start(
                            out=dsts[k3][:, :, t0:t0 + 512].rearrange("g p t -> p g t"), in_=sg[:, 2 * k3:2 * k3 + 2, :]),
                            reads=[r_sg], writes=[r_scr], cont=(k3 > 0))
                    sv, r_sv = stv.next()
                    for s in range(4):
                        ps, r_ps = mmp.next()
                        for kc in range(16):
                            P.op("pe", lambda ps=ps, s=s, kc=kc, hT=hT: nc.tensor.matmul(
                                ps[:, 0:256], lhsT=hT[:, kc, s * 128:(s + 1) * 128], rhs=wkv[:, kc, 768:1024],
                                start=(kc == 0), stop=(kc == 15)), reads=[r_wkv, r_hT], writes=[r_ps])
                        tile = tb * 4 + s
                        P.op("dve", lambda ps=ps, sv=sv, s=s, tile=tile: nc.vector.tensor_scalar(
                            out=sv[:, s, :, 0:128], in0=ps[:, 0:256].rearrange("p (g d) -> p g d", g=NG),
                            scalar1=vtok_sb[:, tile:tile + 1], scalar2=None, op0=ALU.mult),
                            reads=[r_ps, r_const], writes=[r_sv])
                        P.op("pool", lambda sv=sv, s=s, tile=tile: nc.gpsimd.tensor_copy(
                            out=sv[:, s, :, 128:130], in_=vtok_sb[:, tile:tile + 1].unsqueeze(1).to_broadcast([128, NG, 2])),
                            reads=[r_const], writes=[r_sv])
                    P.dma("act", chv[tb % 2], lambda sv=sv, t0=t0: nc.scalar.dma_start(
                        out=vsl[t0:t0 + 512, :, :].rearrange("(s p) g d -> p s g d", p=128), in_=sv[:]),
                        reads=[r_sv], writes=[r_scr])
                if not (self.dbg and "nocast" in self.dbg):
                    self.cast_step(100000)
                P.barrier()
            if upto <= 1:
                P.emit()
                return nc

            with ExitStack() as st:
                hTo = st.enter_context(nc.sbuf_tensor("hTo", [128, 16, NA], BF16))
                r_hTo = Res()
                with ExitStack() as st2:
                    xpool = Pool(P, st2, "xtb", 2, [128, 4, D], F32)
                    chx = [P.chan("xb0"), P.chan("xb1")]
                    junk = st2.enter_context(nc.sbuf_tensor("junkb", [128, D], BF16))
                    r_junk = Res()
                    sspool = Pool(P, st2, "ssb", 2, [128, 4], F32)
                    rspool = Pool(P, st2, "rsb", 2, [128, 4], F32)
                    xnpool = Pool(P, st2, "xnb", 2, [128, 4, D], BF16)
                    tpp = Pool(P, st2, "tpb", 4, [128, 512], BF16, psum=True)
                    for tb in range(6):
                        t0 = W0 + tb * 512
                        nsub = 4 if tb < 5 else 1

                        class _HP:
                            def next(self_inner):
                                return hTo[:, :, tb * 512:tb * 512 + nsub * 128], r_hTo
                        self.norm_T(P, st2, xv, t0, nsub, xpool, chx[tb % 2], junk, r_junk, sspool, rspool, xnpool,
                                    _HP(), tpp, ident, r_const, s1, r_s1, ada, r_ada, 0)
                    P.barrier()
                w32 = Pool(P, st, "w32_", 2, [128, 16, 512], F32)
                wbf = Pool(P, st, "wbf_", 2, [128, 16, 512], BF16)
                chw = [P.chan("w1b0"), P.chan("w1b1")]
                mmp = Pool(P, st, "mmb", 4, [128, 512], F32, psum=True)
                stA = Pool(P, st, "stA", 2, [128, NA], BF16)
                chst = [P.chan("stA0"), P.chan("stA1"), P.chan("stA2")]
                sgp = Pool(P, st, "sgp", 1, [128, NB], BF16)
                r_scr = Res("scr1b")
                self.wblk = 0

                def load_w(colranges):
                    wt, wr = w32.next()
                    wb, wbr = wbf.next()
                    ch = chw[self.wblk % 2]
                    self.wblk += 1
                    o = 0
                    for i, (c0, n) in enumerate(colranges):
                        P.dma("sp", ch, lambda wt=wt, c0=c0, n=n, o=o: nc.sync.dma_start(
                            out=wt[:, :, o:o + n], in_=w_in[:, c0:c0 + n].rearrange("(k p) c -> p k c", p=128)),
                            writes=[wr], cont=(i > 0))
                        o += n
                    P.op("pool", lambda wt=wt, wb=wb, o=o: nc.gpsimd.tensor_copy(out=wb[:, :, 0:o], in_=wt[:, :, 0:o]),
                         reads=[wr], writes=[wbr])
                    return wb, wbr

                def blocks(lo, hi):
                    b = []
                    t = lo
                    while t < hi:
                        n = min(512, hi - t)
                        b.append((t, n))
                        t += n
                    return b

                def fm_chunk(wb, wbr, woff, lo, hi, evac):
                    for (t, n) in blocks(lo, hi):
                        ps, r_ps = mmp.next()
                        for kc in range(16):
                            P.op("pe", lambda ps=ps, kc=kc, t=t, n=n: nc.tensor.matmul(
                                ps[:, 0:n], lhsT=wb[:, kc, woff:woff + 128], rhs=hTo[:, kc, t:t + n],
                                start=(kc == 0), stop=(kc == 15)), reads=[wbr, r_hTo], writes=[r_ps])
                        evac(ps, r_ps, t, n)

                ecount = [0]

                def copy_evac(dst, r_dst, off, func=None, scale=1.0):
                    def ev(ps, r_ps, t, n):
                        ecount[0] += 1
                        if func is None and scale == 1.0 and ecount[0] % 2 == 0:
                            P.op("dve", lambda: nc.vector.tensor_copy(out=dst[:, t - off:t - off + n], in_=ps[:, 0:n]),
                                 reads=[r_ps], writes=[r_dst])
                        else:
                            P.op("act", lambda: nc.scalar.activation(out=dst[:, t - off:t - off + n], in_=ps[:, 0:n],
                                                                     func=(func or AF.Copy), scale=scale),
                                 reads=[r_ps], writes=[r_dst])
                    return ev

                stn = [0]

                def store(dst_ap, sg, r_sg, n):
                    ch = chst[stn[0] % 3]
                    stn[0] += 1
                    P.dma("act", ch, lambda: nc.scalar.dma_start(out=dst_ap, in_=sg[:, 0:n]), reads=[r_sg], writes=[r_scr])

                OB = B0 - W0
                for qb in range(4):
                    wb, wbr = load_w([(C_Q + qb * 512, 512)])
                    for hl in range(4):
                        h = qb * 4 + hl
                        sg, r_sg = stA.next()
                        fm_chunk(wb, wbr, hl * 128, OB, NA, copy_evac(sg, r_sg, OB, scale=float(DK) ** -0.5))
                        store(QT[:, h, :], sg, r_sg, NB)
                wb, wbr = load_w([(C_KW, 256), (C_VW, 256)])
                for g in range(NG):
                    sg, r_sg = stA.next()
                    fm_chunk(wb, wbr, g * 128, 0, NA, copy_evac(sg, r_sg, 0))
                    store(kwT[g, :, :], sg, r_sg, NA)
                stv = Pool(P, st, "stvb", 2, [128, NG, 130], BF16)
                chv = [P.chan("svb0"), P.chan("svb1")]
                for i in range(2):
                    P.op("dve", lambda i=i: nc.vector.memset(stv.t[i][:], 0.0), writes=[stv.r[i]])
                for s in range(NA // 128):
                    ps, r_ps = mmp.next()
                    for kc in range(16):
                        P.op("pe", lambda ps=ps, s=s, kc=kc, wb=wb: nc.tensor.matmul(
                            ps[:, 0:256], lhsT=hTo[:, kc, s * 128:(s + 1) * 128], rhs=wb[:, kc, 256:512],
                            start=(kc == 0), stop=(kc == 15)), reads=[wbr, r_hTo], writes=[r_ps])
                    sv, r_sv = stv.next()
                    tile = W0 // 128 + s
                    P.op("dve", lambda ps=ps, sv=sv, tile=tile: nc.vector.tensor_scalar(
                        out=sv[:, :, 0:128], in0=ps[:, 0:256].rearrange("p (g d) -> p g d", g=NG),
                        scalar1=vtok_sb[:, tile:tile + 1], scalar2=None, op0=ALU.mult),
                        reads=[r_ps, r_const], writes=[r_sv])
                    P.op("pool", lambda sv=sv, tile=tile: nc.gpsimd.tensor_copy(
                        out=sv[:, :, 128:130], in_=vtok_sb[:, tile:tile + 1].unsqueeze(1).to_broadcast([128, NG, 2])),
                        reads=[r_const], writes=[r_sv])
                    P.dma("act", chv[s % 2], lambda sv=sv, s=s: nc.scalar.dma_start(
                        out=vw[s * 128:(s + 1) * 128, :, :], in_=sv[:]), reads=[r_sv], writes=[r_scr])
                wb, wbr = load_w([(C_GN, 48)])
                gst = Pool(P, st, "gst", 2, [128, 48], F32)
                chg = [P.chan("gs0"), P.chan("gs1")]
                for s in range(NB // 128):
                    ps, r_ps = mmp.next()
                    for kc in range(16):
                        P.op("pe", lambda ps=ps, s=s, kc=kc, wb=wb: nc.tensor.matmul(
                            ps[:, 0:48], lhsT=hTo[:, kc, OB + s * 128:OB + (s + 1) * 128], rhs=wb[:, kc, 0:48],
                            start=(kc == 0), stop=(kc == 15)), reads=[wbr, r_hTo], writes=[r_ps])
                    gs, r_gs = gst.next()
                    P.op("act", lambda ps=ps, gs=gs: nc.scalar.activation(out=gs[:], in_=ps[:, 0:48], func=AF.Sigmoid),
                         reads=[r_ps], writes=[r_gs])
                    P.dma("act", chg[s % 2], lambda gs=gs, s=s: nc.scalar.dma_start(
                        out=gates[s * 128:(s + 1) * 128, :], in_=gs[:]), reads=[r_gs], writes=[r_scr])
                for cb in range(8):
                    wb, wbr = load_w([(C_GLU + cb * 256, 256), (C_GLU + D + cb * 256, 256)])
                    for cl in range(2):
                        ch_ = cb * 2 + cl
                        sgm, r_sgm = sgp.next()
                        fm_chunk(wb, wbr, 256 + cl * 128, OB, NA, copy_evac(sgm, r_sgm, OB, func=AF.Sigmoid))
                        sg, r_sg = stA.next()

                        def ev(ps, r_ps, t, n, sg=sg, r_sg=r_sg, sgm=sgm, r_sgm=r_sgm):
                            P.op("dve", lambda: nc.vector.tensor_tensor(out=sg[:, t - OB:t - OB + n], in0=ps[:, 0:n],
                                                                        in1=sgm[:, t - OB:t - OB + n], op=ALU.mult),
                                 reads=[r_ps, r_sgm], writes=[r_sg])
                        fm_chunk(wb, wbr, cl * 128, OB, NA, ev)
                        P.op("dve", lambda sg=sg: nc.vector.tensor_scalar(out=sg[:, 0:OWN0 - B0], in0=sg[:, 0:OWN0 - B0], scalar1=self.hfl[:, 0:1],
                                                                        scalar2=None, op0=ALU.mult), reads=[r_sg, r_const], writes=[r_sg])
                        store(gluT[:, ch_, :], sg, r_sg, NB)
                for mb in range(8):
                    wb, wbr = load_w([(C_GM + mb * 512, 512)])
                    for cl in range(4):
                        sg, r_sg = stA.next()
                        fm_chunk(wb, wbr, cl * 128, OB, NA, copy_evac(sg, r_sg, OB, func=AF.Sigmoid))
                        store(mgT[:, mb * 4 + cl, :], sg, r_sg, NB)
                P.barrier()
            self.r_scr_all = Res("scr_all")
            if upto >= 3:
                self.phase_compress()
            if upto >= 4:
                self.phase_attn(upto)
            if upto >= 5:
                self.phase_mix()
            if upto >= 6:
                self.phase_ffn()
            P.emit()
        return nc

    def norm_T(self, P, st, src, t0, nsub, xpool, chx, junk, r_junk, sspool, rspool, xnpool, hTpool, tpp,
               ident, r_const, sc, r_sc, ada, r_ada, sh_col, xt_in=None):
        nc = self.nc
        if xt_in is None:
            xt, r_xt = xpool.next()
            P.dma("sp", chx, lambda: nc.sync.dma_start(
                out=xt[:, 0:nsub, :], in_=src[t0:t0 + nsub * 128, :].rearrange("(s p) d -> p s d", p=128)), writes=[r_xt])
        else:
            xt, r_xt = xt_in
        ss, r_ss = sspool.next()
        rs, r_rs = rspool.next()
        for s in range(nsub):
            P.op("act", lambda s=s: nc.scalar.activation(out=junk[:], in_=xt[:, s, :], func=AF.Square,
                                                         accum_out=ss[:, s:s + 1]),
                 reads=[r_xt], writes=[r_junk, r_ss])
        P.op("dve", lambda: nc.vector.tensor_scalar(out=rs[:, 0:nsub], in0=ss[:, 0:nsub], scalar1=1.0 / D, scalar2=EPS,
                                                    op0=ALU.mult, op1=ALU.add), reads=[r_ss], writes=[r_rs])
        P.op("act", lambda: nc.scalar.sqrt(out=rs[:, 0:nsub], in_=rs[:, 0:nsub]), reads=[r_rs], writes=[r_rs])
        P.op("dve", lambda: nc.vector.reciprocal(out=rs[:, 0:nsub], in_=rs[:, 0:nsub]), reads=[r_rs], writes=[r_rs])
        xn, r_xn = xnpool.next()
        for s in range(nsub):
            P.op("dve", lambda s=s: nc.vector.tensor_scalar(out=xn[:, s, :], in0=xt[:, s, :], scalar1=rs[:, s:s + 1],
                                                            scalar2=None, op0=ALU.mult),
                 reads=[r_xt, r_rs], writes=[r_xn])
        hT, r_hT = hTpool.next()
        for j in range(16):
            tp, r_tp = tpp.next()
            for s in range(nsub):
                P.op("pe", lambda tp=tp, s=s, j=j: nc.tensor.transpose(
                    out=tp[:, s * 128:(s + 1) * 128], in_=xn[:, s, j * 128:(j + 1) * 128], identity=ident[:]),
                    reads=[r_xn, r_const], writes=[r_tp])
            if j % 2 == 0:
                P.op("act", lambda tp=tp, j=j: nc.scalar.activation(
                    out=hT[:, j, 0:nsub * 128], in_=tp[:, 0:nsub * 128], func=AF.Identity,
                    scale=sc[:, j:j + 1], bias=ada[:, sh_col + j:sh_col + j + 1]),
                    reads=[r_tp, r_sc, r_ada], writes=[r_hT])
            else:
                P.op("dve", lambda tp=tp, j=j: nc.vector.tensor_scalar(
                    out=hT[:, j, 0:nsub * 128], in0=tp[:, 0:nsub * 128], scalar1=sc[:, j:j + 1],
                    scalar2=ada[:, sh_col + j:sh_col + j + 1], op0=ALU.mult, op1=ALU.add),
                    reads=[r_tp, r_sc, r_ada], writes=[r_hT])
        self.last_rs = (rs, r_rs)
        self.last_xt = (xt, r_xt)
        return hT, r_hT


    def phase_compress(self):
        nc, P = self.nc, self.P
        with ExitStack() as st:
            sbt = lambda name, shape, dt: st.enter_context(nc.sbuf_tensor(name, list(shape), dt))
            raw = sbt("c_raw", [128, TV + 16], BF16)
            R = sbt("c_R", [128, 16, NCV + 1], BF16)
            w1f = sbt("c_w1f", [128, 32, 256], F32)
            w1b = sbt("c_w1b", [128, 32, 256], BF16)
            w2f = sbt("c_w2f", [128, 2, 128], F32)
            w2b = sbt("c_w2b", [128, 2, 128], BF16)
            pef = sbt("c_pef", [128, 32], F32)
            peb = sbt("c_peb", [128, 32], BF16)
            bia = sbt("c_bia", [128, 2], F32)
            hid = sbt("c_hid", [128, 2, NCV], BF16)
            kst = sbt("c_kst", [128, NCV], BF16)
            zpad = sbt("c_zpad", [128, NG, 16], BF16)
            r_raw, r_R, r_w1f, r_w1b, r_w2f, r_w2b, r_pe, r_bia, r_hid, r_kst, r_z = [Res() for _ in range(11)]
            vst = Pool(P, st, "c_vst", 2, [128, NG, 130], BF16)
            mmp = Pool(P, st, "c_mm", 3, [128, 512], F32, psum=True)
            bps = st.enter_context(nc.psum_tensor("c_bps", [128, 2], F32))
            r_bps = Res()
            ch = {n: P.chan("c_" + n) for n in ("raw", "w1", "w2", "pe", "k", "v0", "v1", "z")}
            r_out = self.r_scr_all
            P.op("dve", lambda: nc.vector.memset(zpad[:], 0.0), writes=[r_z])
            for i, rt in enumerate((self.kcT_raw, self.vcT_raw)):
                P.dma("act", ch["z"], lambda rt=rt: nc.scalar.dma_start(
                    out=rt[:, :, TV:TV + 16].rearrange("g p t -> p g t"), in_=zpad[:]), reads=[r_z], writes=[r_out], cont=(i > 0))
            P.dma("sp", ch["pe"], lambda: nc.sync.dma_start(out=pef[:], in_=self.i_cmp_pe.rearrange("l d -> d l"),
                                                           allow_slow_non_contiguous=True), writes=[r_pe])
            P.op("dve", lambda: nc.vector.tensor_copy(out=peb[:], in_=pef[:]), reads=[r_pe], writes=[r_pe])
            for i in range(2):
                P.op("dve", lambda i=i: nc.vector.memset(vst.t[i][:], 0.0), writes=[vst.r[i]])
            for kv in range(2):
                P.dma("sp", ch["w1"], lambda kv=kv: nc.sync.dma_start(
                    out=w1f[:], in_=self.i_w1[kv].rearrange("(l d) c -> d l c", d=128)), writes=[r_w1f])
                P.op("pool", lambda: nc.gpsimd.tensor_copy(out=w1b[:], in_=w1f[:]), reads=[r_w1f], writes=[r_w1b])
                P.dma("sp", ch["w2"], lambda kv=kv: nc.sync.dma_start(
                    out=w2f[:], in_=self.i_w2[kv].rearrange("(c p) d -> p c d", p=128)), writes=[r_w2f])
                P.op("dve", lambda: nc.vector.tensor_copy(out=w2b[:], in_=w2f[:]), reads=[r_w2f], writes=[r_w2b])
                for hc in range(2):
                    for l in range(32):
                        P.op("pe", lambda hc=hc, l=l: nc.tensor.matmul(
                            bps[:, hc:hc + 1], lhsT=w1b[:, l, hc * 128:(hc + 1) * 128], rhs=peb[:, l:l + 1],
                            start=(l == 0), stop=(l == 31)), reads=[r_w1b, r_pe], writes=[r_bps])
                P.op("dve", lambda: nc.vector.tensor_copy(out=bia[:], in_=bps[:]), reads=[r_bps], writes=[r_bia])
                src = (self.kcT_raw, self.vcT_raw)[kv]
                for g in range(NG):
                    P.dma("sp", ch["raw"], lambda g=g, src=src: nc.sync.dma_start(out=raw[:], in_=src[g, :, :]),
                          reads=[r_out], writes=[r_raw])
                    P.op("pool", lambda: nc.gpsimd.tensor_copy(
                        out=R[:], in_=raw[:].rearrange("p (m l) -> p l m", l=16)), reads=[r_raw], writes=[r_R])
                    for hc in range(2):
                        for (n0, nn) in ((0, 512), (512, 512), (1024, 128)):
                            ps, r_ps = mmp.next()
                            for l in range(32):
                                P.op("pe", lambda ps=ps, hc=hc, l=l, n0=n0, nn=nn: nc.tensor.matmul(
                                    ps[:, 0:nn], lhsT=w1b[:, l, hc * 128:(hc + 1) * 128],
                                    rhs=R[:, l % 16, (l // 16) + n0:(l // 16) + n0 + nn],
                                    start=(l == 0), stop=(l == 31)), reads=[r_w1b, r_R], writes=[r_ps])
                            P.op("act", lambda ps=ps, hc=hc, n0=n0, nn=nn: nc.scalar.activation(
                                out=hid[:, hc, n0:n0 + nn], in_=ps[:, 0:nn], func=AF.Silu, bias=bia[:, hc:hc + 1]),
                                reads=[r_ps, r_bia], writes=[r_hid])
                    if kv == 0:
                        for (n0, nn) in ((0, 512), (512, 512), (1024, 128)):
                            ps, r_ps = mmp.next()
                            for hc in range(2):
                                P.op("pe", lambda ps=ps, hc=hc, n0=n0, nn=nn: nc.tensor.matmul(
                                    ps[:, 0:nn], lhsT=w2b[:, hc, :], rhs=hid[:, hc, n0:n0 + nn],
                                    start=(hc == 0), stop=(hc == 1)), reads=[r_w2b, r_hid], writes=[r_ps])
                            P.op("dve", lambda ps=ps, n0=n0, nn=nn: nc.vector.tensor_copy(out=kst[:, n0:n0 + nn], in_=ps[:, 0:nn]),
                                 reads=[r_ps], writes=[r_kst])
                        P.dma("act", ch["k"], lambda g=g: nc.scalar.dma_start(out=self.kcT[g, :, :], in_=kst[:]),
                              reads=[r_kst], writes=[r_out])
                    else:
                        for tl in range(NCV // 128):
                            ps, r_ps = mmp.next()
                            for hc in range(2):
                                P.op("pe", lambda ps=ps, hc=hc, tl=tl: nc.tensor.matmul(
                                    ps[:, 0:128], lhsT=hid[:, hc, tl * 128:(tl + 1) * 128], rhs=w2b[:, hc, :],
                                    start=(hc == 0), stop=(hc == 1)), reads=[r_w2b, r_hid], writes=[r_ps])
                            sv, r_sv = vst.next()
                            P.op("dve", lambda ps=ps, sv=sv, tl=tl, g=g: nc.vector.tensor_scalar(
                                out=sv[:, g, 0:128], in0=ps[:, 0:128], scalar1=self.vcmp_sb[:, tl:tl + 1], scalar2=None,
                                op0=ALU.mult), reads=[r_ps, self.r_const], writes=[r_sv])
                            P.op("pool", lambda sv=sv, tl=tl, g=g: nc.gpsimd.tensor_copy(
                                out=sv[:, g, 128:130], in_=self.vcmp_sb[:, tl:tl + 1].to_broadcast([128, 2])), reads=[self.r_const], writes=[r_sv])
                            P.dma("act", ch["v%d" % (tl % 2)], lambda sv=sv, tl=tl, g=g: nc.scalar.dma_start(
                                out=self.vca[tl * 128:(tl + 1) * 128, g, :], in_=sv[:, g, :]), reads=[r_sv], writes=[r_out])
            P.barrier()


    def phase_attn(self, upto):
        nc, P = self.nc, self.P
        with ExitStack() as st:
            sbt = lambda name, shape, dt: st.enter_context(nc.sbuf_tensor(name, list(shape), dt))
            bias_s = sbt("a_bs", [128, NH, NDS], F32)
            bias_c = sbt("a_bc", [128, NH, NDC], F32)
            masks = sbt("a_mk", [128, NMASK, 512], BF16)
            esel = sbt("a_es", [128, 64, 128], BF16)
            r_tab = Res("tables")
            cht = P.chan("a_tab")
            for i, (dst, src) in enumerate(((bias_s, self.i_bias_s), (bias_c, self.i_bias_c), (masks, self.i_masks), (esel, self.i_esel))):
                P.dma("sp", cht, lambda dst=dst, src=src: nc.sync.dma_start(out=dst[:], in_=src[:, :, :]), writes=[r_tab], cont=(i > 0))
            crhs = [[sbt(f"a_cr{g}_{jj}", [128, 130 + NBW], BF16) for jj in range(9)] for g in range(NG)]
            r_crhs = [[Res() for jj in range(9)] for g in range(NG)]
            ch_cr = [P.chan("a_cr0"), P.chan("a_cr1")]
            ovm = sbt("a_ovm", [128, 9, NBW], BF16)
            P.dma("sp", cht, lambda: nc.sync.dma_start(out=ovm[:], in_=self.i_ovm[:, :, :]), writes=[r_tab], cont=True)
            QTt = sbt("a_qt", [128, NH, 512], BF16)
            gat = sbt("a_gat", [128, 4, 48], F32)
            selb = sbt("a_selb", [128, 4, NBW], F32)
            selv = sbt("a_selv", [128, 4, NBW], F32)
            acc = sbt("a_acc", [128, 4, D], F32)
            imp = sbt("a_imp", [128, 4, NG, NBW], F32)
            mneg = sbt("a_mneg", [128, NG, 3, 512], BF16)
            r_qt, r_gat, r_sel, r_acc, r_imp, r_mneg = [Res() for _ in range(6)]
            ch_q = P.chan("a_q")
            kpool = Pool(P, st, "a_k", 3, [128, 2048], BF16)
            vpool = Pool(P, st, "a_v", 3, [128, 16, 130], BF16)
            chk = [P.chan(f"a_k{i}") for i in range(3)]
            chv = [P.chan(f"a_v{i}") for i in range(3)]
            ptp = Pool(P, st, "a_pt", 4, [128, 512], BF16)
            sps = Pool(P, st, "a_S", 2, [128, 512], F32, psum=True)
            ops_ = Pool(P, st, "a_o", 4, [128, 512], F32, psum=True)
            tps = Pool(P, st, "a_tp", 2, [128, 512], BF16, psum=True)
            small = Pool(P, st, "a_sm", 4, [128, 4], F32)
            sc1 = sbt("a_sc1", [128, 384], F32)
            sc2 = sbt("a_sc2", [128, 384], F32)
            m8 = sbt("a_m8", [128, 16], F32)
            mbf = sbt("a_mbf", [128, 384], BF16)
            r_sc1, r_sc2, r_m8, r_mbf = Res(), Res(), Res(), Res()
            P.op("dve", lambda: nc.vector.memset(mbf[:], 0.0), writes=[r_mbf])
            P.op("dve", lambda: nc.vector.memset(sc1[:], 0.0), writes=[r_sc1])
            ch_dbg = P.chan("a_dbg")
            r_out = self.r_scr_all
            kcount = [0]

            def do_tile(ti, q0, nq):
                self._cr_loaded = [0, 0]
                qend = q0 + nq
                nsub = max(1, nq // 128)
                rows = min(128, nq)
                nend = qend // 16
                P.dma("sp", ch_q, lambda q0=q0, nq=nq: nc.sync.dma_start(out=QTt[:, :, 0:nq], in_=self.QT[:, :, q0 - B0:q0 - B0 + nq]),
                      reads=[r_out], writes=[r_qt])
                P.dma("sp", ch_q, lambda q0=q0, nq=nq, rows=rows, nsub=nsub: nc.sync.dma_start(
                    out=gat[0:rows, 0:nsub, :], in_=self.gates[q0 - B0:q0 - B0 + nq, :].rearrange("(s p) c -> p s c", p=rows)),
                    reads=[r_out], writes=[r_gat], cont=True)
                P.dma("sp", ch_q, lambda ti=ti: nc.sync.dma_start(out=selb[:], in_=self.i_selb[ti].rearrange("s p b -> p s b")),
                      writes=[r_sel], cont=True)
                P.dma("sp", ch_q, lambda ti=ti: nc.sync.dma_start(out=selv[:], in_=self.i_selv[ti].rearrange("s p b -> p s b")),
                      writes=[r_sel], cont=True)

                def run_branch(bi, h):
                    g = h // HG
                    W = exp_width(h, nq)
                    if bi == 0:
                        nch = min(n_cmp_chunks(h, nq), nend // 128)
                    elif bi == 1:
                        nch = min(n_slc_chunks(h, nq, 132), qend // 128)
                    else:
                        nch = min(n_slc_chunks(h, nq, 8 if nq == 512 else 5), 8 if nq == 512 else 5)
                    ncol = 130 + NBW if bi == 0 else 130
                    oacc = [ops_.next() for _ in range(nsub)]
                    kt = vt = None
                    stA = {}

                    def stageB(j, pt, r_pt, rhsV, r_rhsV):
                        for s_ in range(nsub):
                            o, r_o = oacc[s_]
                            P.op("pe", lambda o=o, s_=s_, pt=pt, rhsV=rhsV, j=j: nc.tensor.matmul(
                                o[0:rows, 0:ncol], lhsT=pt[:, s_ * 128:s_ * 128 + rows], rhs=rhsV,
                                start=(j == 0), stop=(j == nch - 1)), reads=[r_pt, r_rhsV], writes=[r_o])

                    for j in range(nch):
                        if bi == 0:
                            n0 = nend - 128 * (j + 1)
                            kt, r_kt = kpool.next()
                            kc_ = kcount[0] % 3
                            kcount[0] += 1
                            P.dma("sp", chk[kc_], lambda kt=kt, n0=n0: nc.sync.dma_start(out=kt[:, 0:128], in_=self.kcT[g, :, n0:n0 + 128]),
                                  reads=[r_out], writes=[r_kt])
                            if h % HG == 0 or j >= self._cr_loaded[g]:
                                P.dma("sp", ch_cr[g], lambda n0=n0, j=j: nc.sync.dma_start(out=crhs[g][j][:, 0:130], in_=self.vca[n0:n0 + 128, g, :]),
                                      reads=[r_out], writes=[r_crhs[g][j]])
                                P.op("dve", lambda j=j: nc.vector.tensor_scalar(out=crhs[g][j][:, 130:130 + NBW], in0=ovm[:, j, :],
                                                                               scalar1=crhs[g][j][:, 128:129], scalar2=None, op0=ALU.mult),
                                     reads=[r_tab, r_crhs[g][j]], writes=[r_crhs[g][j]])
                                self._cr_loaded[g] = max(self._cr_loaded[g], j + 1) if h % HG else j + 1
                            klhs, rhsV, r_rhsV = kt[:, 0:128], crhs[g][j][:, 0:ncol], r_crhs[g][j]
                            mk = MASK_IDX.get(("cmp", nq, j))
                            bcol = lambda r, j=j: bias_c[:, h, (nq - 2048 * (j + 1) - r * W) // 64 + ROFF:(nq - 2048 * (j + 1) - r * W) // 64 + ROFF + 1]
                        else:
                            if j % 16 == 0:
                                nsup = min(16, nch - j)
                                lo = qend - 128 * (j + nsup)
                                kt, r_kt = kpool.next()
                                vt, r_vt = vpool.next()
                                kc_ = kcount[0] % 3
                                kcount[0] += 1
                                if bi == 1:
                                    ksrc, vsrc, off = self.kslT, self.vsl, 0
                                else:
                                    ksrc, vsrc, off = self.kwT, self.vw, W0
                                P.dma("sp", chk[kc_], lambda kt=kt, lo=lo, nsup=nsup, ksrc=ksrc, off=off: nc.sync.dma_start(
                                    out=kt[:, 0:128 * nsup], in_=ksrc[g, :, lo - off:lo - off + 128 * nsup]), reads=[r_out], writes=[r_kt])
                                P.dma("sp", chv[kc_], lambda vt=vt, lo=lo, nsup=nsup, vsrc=vsrc, off=off: nc.sync.dma_start(
                                    out=vt[:, 0:nsup, :], in_=vsrc[lo - off:lo - off + 128 * nsup, g, :].rearrange("(s p) d -> p s d", p=128)),
                                    reads=[r_out], writes=[r_vt])
                                sup_n = nsup
                            sl = sup_n - 1 - (j % 16)
                            klhs, rhsV, r_rhsV = kt[:, sl * 128:(sl + 1) * 128], vt[:, sl, :], r_vt
                            mk = MASK_IDX.get(("slc" if bi == 1 else "win", nq, j))
                            bcol = lambda r, j=j: bias_s[:, h, (nq - 128 * (j + 1) - r * W) // 64 + DOFF:(nq - 128 * (j + 1) - r * W) // 64 + DOFF + 1]
                        S, r_S = sps.next()
                        nmm = 1 + (mk is not None) + (bi == 1)
                        cnt = [0]

                        def mm(lhsT, rhs, reads):
                            first, last = cnt[0] == 0, cnt[0] == nmm - 1
                            cnt[0] += 1
                            P.op("pe", lambda S=S: nc.tensor.matmul(S[:, 0:nq], lhsT=lhsT, rhs=rhs, start=first, stop=last),
                                 reads=reads, writes=[r_S])
                        mm(klhs, QTt[:, h, 0:nq], [r_kt, r_qt])
                        if mk is not None:
                            mm(self.ident[:], masks[:, mk, 0:nq], [self.r_const, r_tab])
                        if bi == 1:
                            b0 = NBLKW - 2 * (j + 1)
                            mm(esel[:, (b0 % 128) // 2, :], mneg[:, g, b0 // 128, 0:nq], [r_tab, r_mneg])
                        if self.dbg and "dumpS" in self.dbg and bi == 0 and h == 0 and j == 0:
                            dS = sbt("dbg_S", [128, 512], F32)
                            r_dS = Res()
                            P.op("dve", lambda: nc.vector.tensor_copy(out=dS[:, 0:nq], in_=S[:, 0:nq]), reads=[r_S], writes=[r_dS])
                            P.dma("act", ch_dbg, lambda: nc.scalar.dma_start(out=self.dumpS[:, 0:nq], in_=dS[:, 0:nq]), reads=[r_dS], writes=[r_out])
                            raise StopIteration
                        pt, r_pt = ptp.next()
                        for r in range(nq // W):
                            P.op("act", lambda r=r, bcol=bcol, S=S, pt=pt: nc.scalar.activation(
                                out=pt[:, r * W:(r + 1) * W], in_=S[:, r * W:(r + 1) * W], func=AF.Exp, bias=bcol(r)),
                                reads=[r_S, r_tab], writes=[r_pt])
                        if j > 0:
                            stageB(j - 1, *stA.pop(j - 1))
                        stA[j] = (pt, r_pt, rhsV, r_rhsV)
                    stageB(nch - 1, *stA.pop(nch - 1))
                    for s_ in range(nsub):
                        o, r_o = oacc[s_]
                        sm, r_sm = small.next()
                        P.op("dve", lambda o=o, sm=sm: nc.vector.tensor_scalar(out=sm[0:rows, 0:1], in0=o[0:rows, 128:129], scalar1=1e-30,
                                                                             scalar2=None, op0=ALU.max), reads=[r_o], writes=[r_sm])
                        P.op("dve", lambda sm=sm: nc.vector.reciprocal(out=sm[0:rows, 1:2], in_=sm[0:rows, 0:1]), reads=[r_sm], writes=[r_sm])
                        P.op("dve", lambda sm=sm, s_=s_: nc.vector.tensor_tensor(
                            out=sm[0:rows, 2:3], in0=sm[0:rows, 1:2], in1=gat[0:rows, s_, bi * 16 + h:bi * 16 + h + 1], op=ALU.mult),
                            reads=[r_sm, r_gat], writes=[r_sm])
                        dst = acc[0:rows, s_, h * 128:(h + 1) * 128]
                        if bi == 0:
                            P.op("dve", lambda o=o, sm=sm, dst=dst: nc.vector.tensor_scalar(
                                out=dst, in0=o[0:rows, 0:128], scalar1=sm[0:rows, 2:3], scalar2=None, op0=ALU.mult),
                                reads=[r_o, r_sm], writes=[r_acc])
                            idst = imp[0:rows, s_, g, :]
                            if h % HG == 0:
                                P.op("dve", lambda o=o, sm=sm, idst=idst: nc.vector.tensor_scalar(
                                    out=idst, in0=o[0:rows, 130:130 + NBW], scalar1=sm[0:rows, 1:2], scalar2=None, op0=ALU.mult),
                                    reads=[r_o, r_sm], writes=[r_imp])
                            else:
                                P.op("dve", lambda o=o, sm=sm, idst=idst: nc.vector.scalar_tensor_tensor(
                                    out=idst, in0=o[0:rows, 130:130 + NBW], scalar=sm[0:rows, 1:2], in1=idst, op0=ALU.mult, op1=ALU.add),
                                    reads=[r_o, r_sm, r_imp], writes=[r_imp])
                        else:
                            P.op("dve", lambda o=o, sm=sm, dst=dst: nc.vector.scalar_tensor_tensor(
                                out=dst, in0=o[0:rows, 0:128], scalar=sm[0:rows, 2:3], in1=dst, op0=ALU.mult, op1=ALU.add),
                                reads=[r_o, r_sm, r_acc], writes=[r_acc])

                try:
                    for h in range(NH):
                        run_branch(0, h)
                except StopIteration:
                    return "stop"
                if self.dbg and "impd" in self.dbg:
                    P.dma("act", ch_dbg, lambda ti=ti: nc.scalar.dma_start(out=self.impd[ti].rearrange("s p g b -> p s g b"), in_=imp[:]),
                          reads=[r_imp], writes=[r_out])
                for g in range(NG):
                    tpl = [tps.next() for _ in range(3)]
                    for s_ in range(nsub):
                        P.op("dve", lambda s_=s_, g=g: nc.vector.tensor_tensor(out=sc1[0:rows, 0:NBW], in0=imp[0:rows, s_, g, :],
                                                                               in1=selb[0:rows, s_, :], op=ALU.add),
                             reads=[r_imp, r_sel], writes=[r_sc1])
                        P.op("dve", lambda s_=s_: nc.vector.tensor_tensor(out=sc1[0:rows, 0:NBW], in0=sc1[0:rows, 0:NBW],
                                                                          in1=selv[0:rows, s_, :], op=ALU.mult),
                             reads=[r_sc1, r_sel], writes=[r_sc1])
                        P.op("dve", lambda: nc.vector.max(out=m8[0:rows, 0:8], in_=sc1[0:rows, :]), reads=[r_sc1], writes=[r_m8])
                        P.op("dve", lambda: nc.vector.match_replace(out=sc2[0:rows, :], in_to_replace=m8[0:rows, 0:8],
                                                                    in_values=sc1[0:rows, :], imm_value=-1e30),
                             reads=[r_sc1, r_m8], writes=[r_sc2])
                        P.op("dve", lambda: nc.vector.max(out=m8[0:rows, 8:16], in_=sc2[0:rows, :]), reads=[r_sc2], writes=[r_m8])
                        P.op("dve", lambda: nc.vector.tensor_scalar(out=mbf[0:rows, 0:NBW], in0=sc1[0:rows, 0:NBW], scalar1=m8[0:rows, 15:16],
                                                                    scalar2=None, op0=ALU.is_ge), reads=[r_sc1, r_m8], writes=[r_mbf])
                        for bg in range(3):
                            tp, r_tp = tpl[bg]
                            P.op("pe", lambda tp=tp, bg=bg, s_=s_: nc.tensor.transpose(
                                out=tp[:, s_ * 128:s_ * 128 + rows], in_=mbf[0:rows, bg * 128:(bg + 1) * 128], identity=self.ident[0:rows, 0:rows]),
                                reads=[r_mbf, self.r_const], writes=[r_tp])
                    for bg in range(3):
                        tp, r_tp = tpl[bg]
                        P.op("act", lambda tp=tp, bg=bg, g=g: nc.scalar.activation(out=mneg[:, g, bg, 0:nq], in_=tp[:, 0:nq], func=AF.Identity,
                                                                                  scale=30000.0, bias=-30000.0),
                             reads=[r_tp], writes=[r_mneg])
                for bi in (1, 2):
                    if self.dbg and "branches" in self.dbg and bi not in self.dbg["branches"]:
                        continue
                    for h in range(NH):
                        run_branch(bi, h)
                if True:
                    P.dma("act", ch_dbg, lambda q0=q0, nq=nq, rows=rows, nsub=nsub: nc.scalar.dma_start(
                        out=self.accd[q0 - Q0:q0 - Q0 + nq, :].rearrange("(s p) d -> p s d", p=rows), in_=acc[0:rows, 0:nsub, :]),
                        reads=[r_acc], writes=[r_out])

            for ti, (q0, nq) in enumerate(QTILES):
                if self.dbg and "tiles" in self.dbg and ti not in self.dbg["tiles"]:
                    continue
                if do_tile(ti, q0, nq) == "stop":
                    return
            P.barrier()


    def cast_jobs(self):
        jobs = []
        for name, src in (("wo_b", self.i_wo), ("wpw_b", self.i_wpw), ("wout_b", self.i_wout), ("wup_b", self.i_wup), ("wdn_b", self.i_wdn)):
            dst = self.wb_d[name]
            R, C = src.shape
            sv = src.rearrange("(p a) c -> p (a c)", p=128)
            dv = dst.rearrange("(p a) c -> p (a c)", p=128)
            F = R * C // 128
            for o in range(0, F, 2048):
                jobs.append((sv, dv, o, min(2048, F - o)))
        return jobs

    def cast_setup(self, st):
        nc, P = self.nc, self.P
        self.cj = self.cast_jobs()
        self.cji = 0
        self.cpend = None
        self.c32 = Pool(P, st, "cst32_", 2, [128, 2048], F32)
        self.cbf = Pool(P, st, "cstbf_", 2, [128, 2048], BF16)
        self.cch_i = [P.chan("cji0"), P.chan("cji1")]
        self.cch_o = [P.chan("cjo0"), P.chan("cjo1")]
        self.r_wb = Res("wb_scratch")

    def cast_step(self, n):
        nc, P = self.nc, self.P
        for _ in range(n):
            if self.cji >= len(self.cj):
                break
            sv, dv, o, w = self.cj[self.cji]
            i = self.cji % 2
            self.cji += 1
            t32, r32 = self.c32.next()
            tbf, rbf = self.cbf.next()
            P.dma("sp", self.cch_i[i], lambda t32=t32, sv=sv, o=o, w=w: nc.sync.dma_start(out=t32[:, 0:w], in_=sv[:, o:o + w]), writes=[r32])
            P.op("pool", lambda t32=t32, tbf=tbf, w=w: nc.gpsimd.tensor_copy(out=tbf[:, 0:w], in_=t32[:, 0:w]), reads=[r32], writes=[rbf])
            if self.cpend is not None:
                self.cpend()
            self.cpend = (lambda i=i, tbf=tbf, dv=dv, o=o, w=w, rbf=rbf: P.dma(
                "sp", self.cch_o[i], lambda: nc.sync.dma_start(out=dv[:, o:o + w], in_=tbf[:, 0:w]), reads=[rbf], writes=[self.r_wb]))
        if self.cji >= len(self.cj) and self.cpend is not None:
            self.cpend()
            self.cpend = None

    def phase_mix(self):
        nc, P = self.nc, self.P
        with ExitStack() as st:
            sbt = lambda name, shape, dt: st.enter_context(nc.sbuf_tensor(name, list(shape), dt))
            g1bc = sbt("m_g1bc", [128, D], F32)
            dww = sbt("m_dww", [128, 16, 31], F32)
            cols = sbt("m_cols", [128, 4, 16], F32)
            r_cst = Res()
            chc = P.chan("m_c")
            P.dma("sp", chc, lambda: nc.sync.dma_start(out=g1bc[:], in_=self.ada_d[2 * D:3 * D].partition_broadcast(128)), writes=[r_cst])
            for c_ in range(16):
                P.dma("sp", chc, lambda c_=c_: nc.sync.dma_start(out=dww[:, c_, :], in_=self.i_dww[:, c_ * 128:(c_ + 1) * 128].rearrange("k p -> p k"),
                                                              allow_slow_non_contiguous=True), writes=[r_cst], cont=True)
            for i, src in enumerate((self.i_dwb, self.i_lng, self.i_lnb, self.i_pwb)):
                P.dma("sp", chc, lambda i=i, src=src: nc.sync.dma_start(out=cols[:, i, :], in_=src.rearrange("(c p) -> p c", p=128),
                                                                      allow_slow_non_contiguous=True), writes=[r_cst], cont=True)
            oT = sbt("m_oT", [128, 16, 512], BF16)
            glu = sbt("m_glu", [128, 16, 544], BF16)
            ybf = sbt("m_ybf", [128, 16, 512], BF16)
            uc = sbt("m_uc", [128, 16, 512], BF16)
            mer = sbt("m_mer", [128, 16, 512], BF16)
            r_oT, r_glu, r_ybf, r_uc, r_mer = [Res() for _ in range(5)]
            accp = Pool(P, st, "m_acc", 2, [128, D], F32)
            accb = Pool(P, st, "m_accb", 2, [128, D], BF16)
            wblk = Pool(P, st, "m_w", 2, [128, 16, 512], BF16)
            dgp = Pool(P, st, "m_dg", 2, [128, 31, 128], BF16)
            ysq = Pool(P, st, "m_ysq", 2, [128, 512], BF16)
            mgp = Pool(P, st, "m_mg", 4, [128, 512], BF16)
            tmpf = Pool(P, st, "m_tmp", 3, [128, 512], F32)
            xp = Pool(P, st, "m_x", 3, [128, 512], F32)
            stat = sbt("m_stat", [128, 3, 512], F32)
            r_stat = Res()
            chl = [P.chan(f"m_l{i}") for i in range(4)]
            chw = [P.chan("m_w0"), P.chan("m_w1")]
            chs = [P.chan(f"m_s{i}") for i in range(3)]
            mm = Pool(P, st, "m_mm", 3, [128, 512], F32, psum=True)
            sps = Pool(P, st, "m_sp", 2, [128, 512], F32, psum=True)
            tps = Pool(P, st, "m_tp", 2, [128, 512], BF16, psum=True)
            r_in, r_out = self.r_scr_all, Res("xmid")
            cnt = {"l": 0, "w": 0, "s": 0}

            def load_wblk(wsrc, cb):
                wt, wr = wblk.next()
                ch = chw[cnt["w"] % 2]
                cnt["w"] += 1
                P.dma("sp", ch, lambda: nc.sync.dma_start(out=wt[:], in_=wsrc[:, cb * 512:(cb + 1) * 512].rearrange("(k p) c -> p k c", p=128)),
                      reads=[self.r_wb], writes=[wr])
                return wt, wr

            def mix_tile(ti, q0, nq):
                nsub, rows = max(1, nq // 128), min(128, nq)
                for s_ in range(nsub):
                    at, r_at = accp.next()
                    ab, r_ab = accb.next()
                    ch = chl[cnt["l"] % 4]
                    cnt["l"] += 1
                    P.dma("sp", ch, lambda at=at, s_=s_: nc.sync.dma_start(out=at[0:rows, :], in_=self.accd[q0 - Q0 + s_ * 128:q0 - Q0 + s_ * 128 + rows, :]),
                          reads=[r_in], writes=[r_at])
                    P.op("act", lambda at=at, ab=ab: nc.scalar.copy(out=ab[0:rows, :], in_=at[0:rows, :]), reads=[r_at], writes=[r_ab])
                    for hq in range(4):
                        tp, r_tp = tps.next()
                        for hl in range(4):
                            h = hq * 4 + hl
                            P.op("pe", lambda tp=tp, hl=hl, h=h, ab=ab: nc.tensor.transpose(
                                out=tp[:, hl * 128:hl * 128 + rows], in_=ab[0:rows, h * 128:(h + 1) * 128], identity=self.ident[0:rows, 0:rows]),
                                reads=[r_ab, self.r_const], writes=[r_tp])
                        P.op("dve", lambda tp=tp, hq=hq, s_=s_: nc.vector.tensor_copy(
                            out=oT[:, hq * 4:hq * 4 + 4, s_ * 128:s_ * 128 + rows],
                            in_=tp[:, :].rearrange("p (h q) -> p h q", h=4)[:, :, 0:rows]), reads=[r_tp], writes=[r_oT])
                for cb in range(4):
                    wt, wr = load_wblk(self.wb_d["wo_b"], cb)
                    for cl in range(4):
                        c = cb * 4 + cl
                        ps, r_ps = mm.next()
                        for kc in range(16):
                            P.op("pe", lambda ps=ps, kc=kc, cl=cl, wt=wt: nc.tensor.matmul(
                                ps[:, 0:nq], lhsT=wt[:, kc, cl * 128:(cl + 1) * 128], rhs=oT[:, kc, 0:nq], start=(kc == 0), stop=(kc == 15)),
                                reads=[wr, r_oT], writes=[r_ps])
                        mg, r_mg = mgp.next()
                        ch = chl[cnt["l"] % 4]
                        cnt["l"] += 1
                        P.dma("sp", ch, lambda mg=mg, c=c: nc.sync.dma_start(out=mg[:, 0:nq], in_=self.mgT[:, c, q0 - B0:q0 - B0 + nq]),
                              reads=[r_in], writes=[r_mg])
                        P.op("dve", lambda ps=ps, mg=mg, c=c: nc.vector.tensor_tensor(out=mer[:, c, 0:nq], in0=ps[:, 0:nq], in1=mg[:, 0:nq], op=ALU.mult),
                             reads=[r_ps, r_mg], writes=[r_mer])
                P.dma("sp", chl[cnt["l"] % 4], lambda: nc.sync.dma_start(out=glu[:, :, 0:nq + 32], in_=self.gluT[:, :, q0 - 32 - B0:q0 - B0 + nq]),
                      reads=[r_in], writes=[r_glu])
                cnt["l"] += 1
                s_sum, r_ssum = sps.next()
                s_sq, r_ssq = sps.next()
                for chn in range(16):
                    dg, r_dg = dgp.next()
                    for k in range(31):
                        if k % 2 == 0:
                            P.op("dve", lambda dg=dg, k=k, chn=chn: nc.vector.tensor_scalar(
                                out=dg[:, k, :], in0=self.ident[:], scalar1=dww[:, chn, k:k + 1], scalar2=None, op0=ALU.mult),
                                reads=[self.r_const, r_cst], writes=[r_dg])
                        else:
                            P.op("act", lambda dg=dg, k=k, chn=chn: nc.scalar.activation(
                                out=dg[:, k, :], in_=self.ident[:], func=AF.Copy, scale=dww[:, chn, k:k + 1]),
                                reads=[self.r_const, r_cst], writes=[r_dg])
                    ps, r_ps = mm.next()
                    for k in range(31):
                        P.op("pe", lambda ps=ps, dg=dg, k=k, chn=chn: nc.tensor.matmul(
                            ps[:, 0:nq], lhsT=dg[:, k, :], rhs=glu[:, chn, k + 2:k + 2 + nq], start=(k == 0), stop=(k == 30)),
                            reads=[r_dg, r_glu], writes=[r_ps])
                    P.op("act", lambda ps=ps, chn=chn: nc.scalar.activation(out=ybf[:, chn, 0:nq], in_=ps[:, 0:nq], func=AF.Identity,
                                                                          bias=cols[:, 0, chn:chn + 1]), reads=[r_ps, r_cst], writes=[r_ybf])
                    yq, r_yq = ysq.next()
                    P.op("act", lambda ps=ps, chn=chn, yq=yq: nc.scalar.activation(out=yq[:, 0:nq], in_=ps[:, 0:nq], func=AF.Square,
                                                                                 bias=cols[:, 0, chn:chn + 1]), reads=[r_ps, r_cst], writes=[r_yq])
                    P.op("pe", lambda chn=chn: nc.tensor.matmul(s_sum[:, 0:nq], lhsT=self.ones[:], rhs=ybf[:, chn, 0:nq],
                                                                start=(chn == 0), stop=(chn == 15)), reads=[r_ybf, self.r_const], writes=[r_ssum])
                    P.op("pe", lambda chn=chn, yq=yq: nc.tensor.matmul(s_sq[:, 0:nq], lhsT=self.ones[:], rhs=yq[:, 0:nq],
                                                                       start=(chn == 0), stop=(chn == 15)), reads=[r_yq, self.r_const], writes=[r_ssq])
                mean, rstd, msq = stat[:, 0, 0:nq], stat[:, 1, 0:nq], stat[:, 2, 0:nq]
                P.op("dve", lambda: nc.vector.tensor_scalar(out=mean, in0=s_sum[:, 0:nq], scalar1=1.0 / D, scalar2=None, op0=ALU.mult),
                     reads=[r_ssum], writes=[r_stat])
                P.op("dve", lambda: nc.vector.tensor_tensor(out=msq, in0=mean, in1=mean, op=ALU.mult), reads=[r_stat], writes=[r_stat])
                P.op("dve", lambda: nc.vector.scalar_tensor_tensor(out=rstd, in0=s_sq[:, 0:nq], scalar=1.0 / D, in1=msq, op0=ALU.mult, op1=ALU.subtract),
                     reads=[r_ssq, r_stat], writes=[r_stat])
                P.op("dve", lambda: nc.vector.tensor_scalar(out=rstd, in0=rstd, scalar1=EPS, scalar2=None, op0=ALU.add), reads=[r_stat], writes=[r_stat])
                P.op("act", lambda: nc.scalar.sqrt(out=rstd, in_=rstd), reads=[r_stat], writes=[r_stat])
                P.op("dve", lambda: nc.vector.reciprocal(out=rstd, in_=rstd), reads=[r_stat], writes=[r_stat])
                for chn in range(16):
                    tf, r_tf = tmpf.next()
                    P.op("dve", lambda tf=tf, chn=chn: nc.vector.tensor_tensor(out=tf[:, 0:nq], in0=ybf[:, chn, 0:nq], in1=mean, op=ALU.subtract),
                         reads=[r_ybf, r_stat], writes=[r_tf])
                    P.op("dve", lambda tf=tf: nc.vector.tensor_tensor(out=tf[:, 0:nq], in0=tf[:, 0:nq], in1=rstd, op=ALU.mult),
                         reads=[r_tf, r_stat], writes=[r_tf])
                    P.op("act", lambda tf=tf, chn=chn: nc.scalar.activation(out=uc[:, chn, 0:nq], in_=tf[:, 0:nq], func=AF.Silu,
                                                                          scale=cols[:, 1, chn:chn + 1], bias=cols[:, 2, chn:chn + 1]),
                         reads=[r_tf, r_cst], writes=[r_uc])
                for cb in range(4):
                    wt, wr = load_wblk(self.wb_d["wpw_b"], cb)
                    for cl in range(4):
                        c = cb * 4 + cl
                        ps, r_ps = mm.next()
                        for kc in range(16):
                            P.op("pe", lambda ps=ps, kc=kc, cl=cl, wt=wt: nc.tensor.matmul(
                                ps[:, 0:nq], lhsT=wt[:, kc, cl * 128:(cl + 1) * 128], rhs=uc[:, kc, 0:nq], start=(kc == 0), stop=(kc == 15)),
                                reads=[wr, r_uc], writes=[r_ps])
                        mg, r_mg = mgp.next()
                        ch = chl[cnt["l"] % 4]
                        cnt["l"] += 1
                        P.dma("sp", ch, lambda mg=mg, c=c: nc.sync.dma_start(out=mg[:, 0:nq], in_=self.mgT[:, 16 + c, q0 - B0:q0 - B0 + nq]),
                              reads=[r_in], writes=[r_mg])
                        tf, r_tf = tmpf.next()
                        P.op("dve", lambda ps=ps, mg=mg, c=c, tf=tf: nc.vector.scalar_tensor_tensor(
                            out=tf[:, 0:nq], in0=ps[:, 0:nq], scalar=cols[:, 3, c:c + 1], in1=mg[:, 0:nq], op0=ALU.add, op1=ALU.mult),
                            reads=[r_ps, r_mg, r_cst], writes=[r_tf])
                        P.op("dve", lambda c=c, tf=tf: nc.vector.tensor_tensor(out=mer[:, c, 0:nq], in0=mer[:, c, 0:nq], in1=tf[:, 0:nq], op=ALU.add),
                             reads=[r_tf, r_mer], writes=[r_mer])
                for cb in range(4):
                    wt, wr = load_wblk(self.wb_d["wout_b"], cb)
                    for s_ in range(nsub):
                        ps, r_ps = mm.next()
                        for kc in range(16):
                            P.op("pe", lambda ps=ps, kc=kc, s_=s_, wt=wt: nc.tensor.matmul(
                                ps[0:rows, :], lhsT=mer[:, kc, s_ * 128:s_ * 128 + rows], rhs=wt[:, kc, :], start=(kc == 0), stop=(kc == 15)),
                                reads=[wr, r_mer], writes=[r_ps])
                        xt, r_xt = xp.next()
                        i3 = cnt["s"] % 3
                        cnt["s"] += 1
                        r0 = q0 + s_ * 128
                        P.dma("sp", chs[i3], lambda xt=xt, r0=r0, cb=cb: nc.sync.dma_start(out=xt[0:rows, :], in_=self.xv[r0:r0 + rows, cb * 512:(cb + 1) * 512]),
                              writes=[r_xt])
                        tf, r_tf = tmpf.next()
                        P.op("dve", lambda ps=ps, tf=tf, cb=cb: nc.vector.tensor_tensor(out=tf[0:rows, :], in0=ps[0:rows, :], in1=g1bc[0:rows, cb * 512:(cb + 1) * 512],
                                                                                     op=ALU.mult), reads=[r_ps, r_cst], writes=[r_tf])
                        P.op("dve", lambda xt=xt, tf=tf: nc.vector.tensor_tensor(out=xt[0:rows, :], in0=xt[0:rows, :], in1=tf[0:rows, :], op=ALU.add),
                             reads=[r_tf, r_xt], writes=[r_xt])
                        P.dma("act", chs[i3], lambda xt=xt, r0=r0, cb=cb: nc.scalar.dma_start(
                            out=self.xmid[r0 - Q0:r0 - Q0 + rows, cb * 512:(cb + 1) * 512], in_=xt[0:rows, :]), reads=[r_xt], writes=[r_out])

            for ti, (q0, nq) in enumerate(QTILES):
                if self.dbg and "tiles" in self.dbg and ti not in self.dbg["tiles"]:
                    continue
                mix_tile(ti, q0, nq)
            P.barrier()


    def phase_ffn(self):
        nc, P = self.nc, self.P
        with ExitStack() as st:
            sbt = lambda name, shape, dt: st.enter_context(nc.sbuf_tensor(name, list(shape), dt))
            g2bc = sbt("f_g2bc", [128, D], F32)
            fgbc = sbt("f_fgbc", [128, D], F32)
            n2g = sbt("f_n2g", [128, 16], F32)
            s2 = sbt("f_s2", [128, 16], F32)
            fdw = sbt("f_fdw", [128, 3, 88], F32)
            fdb = sbt("f_fdb", [128, 88], F32)
            hfl = sbt("f_hfl", [128, 1], F32)
            r_cst, r_s2 = Res(), Res()
            chc = P.chan("f_c")
            P.dma("sp", chc, lambda: nc.sync.dma_start(out=g2bc[:], in_=self.ada_d[5 * D:6 * D].partition_broadcast(128)), writes=[r_cst])
            P.dma("sp", chc, lambda: nc.sync.dma_start(out=fgbc[:], in_=self.i_fg.partition_broadcast(128)), writes=[r_cst], cont=True)
            P.dma("sp", chc, lambda: nc.sync.dma_start(out=n2g[:], in_=self.i_n2g.rearrange("(c p) -> p c", p=128), allow_slow_non_contiguous=True),
                  writes=[r_cst], cont=True)
            for k in range(3):
                P.dma("sp", chc, lambda k=k: nc.sync.dma_start(out=fdw[:, k, :], in_=self.i_fdw[k, :].rearrange("(c p) -> p c", p=128),
                                                            allow_slow_non_contiguous=True), writes=[r_cst], cont=True)
            P.dma("sp", chc, lambda: nc.sync.dma_start(out=fdb[:], in_=self.i_fdb.rearrange("(c p) -> p c", p=128), allow_slow_non_contiguous=True),
                  writes=[r_cst], cont=True)
            P.dma("sp", chc, lambda: nc.sync.dma_start(out=hfl[:], in_=self.i_hflag[:, :]), writes=[r_cst], cont=True)
            P.op("dve", lambda: nc.vector.scalar_tensor_tensor(out=s2[:], in0=self.ada[:, 64:80], scalar=1.0, in1=n2g[:], op0=ALU.add, op1=ALU.mult),
                 reads=[self.r_ada, r_cst], writes=[r_s2])
            xpool = Pool(P, st, "f_x", 1, [128, 4, D], F32)
            chx = P.chan("f_x")
            junk = sbt("f_junk", [128, D], BF16)
            r_junk = Res()
            sspool = Pool(P, st, "f_ss", 2, [128, 4], F32)
            rspool = Pool(P, st, "f_rs", 2, [128, 4], F32)
            xnpool = Pool(P, st, "f_xn", 1, [128, 4, D], BF16)
            h2T = sbt("f_h2T", [128, 16, 514], BF16)
            r_h2T = Res()
            tpp = Pool(P, st, "f_tp", 2, [128, 512], BF16, psum=True)
            up = Pool(P, st, "f_up", 2, [128, 1024], F32, psum=True)
            dn = Pool(P, st, "f_dn", 2, [128, 512], F32, psum=True)
            wup = Pool(P, st, "f_wu", 2, [128, 16, 256], BF16)
            wdn = Pool(P, st, "f_wd", 2, [128, 44, 256], BF16)
            chwu = [P.chan("f_wu0"), P.chan("f_wu1")]
            chwd = [P.chan("f_wd0"), P.chan("f_wd1")]
            z = sbt("f_z", [128, 44, 512], BF16)
            r_z = Res()
            hh, r_hh = z[:, 0:16, 0:128], r_z
            Tp = Pool(P, st, "f_T", 3, [128, 512], F32)
            sgp = Pool(P, st, "f_sg", 1, [128, 512], BF16)
            tmp = Pool(P, st, "f_tmp", 1, [128, 256], F32)
            cho = [P.chan("f_o0"), P.chan("f_o1")]
            r_in = Res()
            wupb, wdnb = self.wb_d["wup_b"], self.wb_d["wdn_b"]
            cnt = {"u": 0, "d": 0, "o": 0}

            class HP:
                def __init__(s_, ap, res):
                    s_.ap, s_.res = ap, res

                def next(s_):
                    return s_.ap, s_.res

            self.norm_T(P, st, self.xmid, 0, 1, xpool, chx, junk, r_junk, sspool, rspool, xnpool, HP(hh, r_hh), tpp,
                        self.ident, self.r_const, s2, r_s2, self.ada, self.r_ada, 48)
            P.op("dve", lambda: nc.vector.tensor_scalar(out=h2T[:, :, 0:2], in0=hh[:, :, 62:64], scalar1=hfl[:, 0:1], scalar2=None, op0=ALU.mult),
                 reads=[r_hh, r_cst], writes=[r_h2T])

            def window(w):
                row0 = HALO + 512 * w
                if w > 0:
                    P.op("pool", lambda: nc.gpsimd.tensor_copy(out=h2T[:, :, 0:2], in_=h2T[:, :, 512:514]), reads=[r_h2T], writes=[r_h2T])
                self.norm_T(P, st, self.xmid, row0, 4, xpool, chx, junk, r_junk, sspool, rspool, xnpool, HP(h2T[:, :, 2:514], r_h2T), tpp,
                            self.ident, self.r_const, s2, r_s2, self.ada, self.r_ada, 48)
                xt, r_xt = self.last_xt
                for pb in range(44):
                    wt, wr = wup.next()
                    ch = chwu[cnt["u"] % 2]
                    cnt["u"] += 1
                    P.dma("sp", ch, lambda wt=wt, pb=pb: nc.sync.dma_start(
                        out=wt[:, :, 0:128], in_=wupb[:, pb * 128:(pb + 1) * 128].rearrange("(k p) c -> p k c", p=128)), reads=[self.r_wb], writes=[wr])
                    P.dma("sp", ch, lambda wt=wt, pb=pb: nc.sync.dma_start(
                        out=wt[:, :, 128:256], in_=wupb[:, FF + pb * 128:FF + (pb + 1) * 128].rearrange("(k p) c -> p k c", p=128)),
                        reads=[self.r_wb], writes=[wr], cont=True)
                    for cl in range(1):
                        c = pb
                        Ts = []
                        for half in range(2):
                            cc = c + 44 * half
                            woff = half * 128
                            ps, r_ps = up.next()
                            for kc in range(16):
                                P.op("pe", lambda ps=ps, kc=kc, woff=woff, wt=wt: nc.tensor.matmul(
                                    ps[:, 512:1024], lhsT=wt[:, kc, woff:woff + 128], rhs=h2T[:, kc, 2:514], start=(kc == 0), stop=(kc == 15)),
                                    reads=[wr, r_h2T], writes=[r_ps])
                            for kc in range(16):
                                P.op("pe", lambda ps=ps, kc=kc, woff=woff, wt=wt: nc.tensor.matmul(
                                    ps[:, 510:512], lhsT=wt[:, kc, woff:woff + 128], rhs=h2T[:, kc, 0:2], start=(kc == 0), stop=(kc == 15)),
                                    reads=[wr, r_h2T], writes=[r_ps])
                            T_, r_T = Tp.next()
                            P.op("act", lambda ps=ps, T_=T_, cc=cc: nc.scalar.activation(out=T_[:], in_=ps[:, 512:1024], func=AF.Identity,
                                                                                        scale=fdw[:, 2, cc:cc + 1], bias=fdb[:, cc:cc + 1]),
                                 reads=[r_ps, r_cst], writes=[r_T])
                            P.op("dve", lambda ps=ps, T_=T_, cc=cc: nc.vector.scalar_tensor_tensor(
                                out=T_[:], in0=ps[:, 511:1023], scalar=fdw[:, 1, cc:cc + 1], in1=T_[:], op0=ALU.mult, op1=ALU.add),
                                reads=[r_ps, r_T, r_cst], writes=[r_T])
                            P.op("dve", lambda ps=ps, T_=T_, cc=cc: nc.vector.scalar_tensor_tensor(
                                out=T_[:], in0=ps[:, 510:1022], scalar=fdw[:, 0, cc:cc + 1], in1=T_[:], op0=ALU.mult, op1=ALU.add),
                                reads=[r_ps, r_T, r_cst], writes=[r_T])
                            Ts.append((T_, r_T))
                        (Ta, r_Ta), (Tg, r_Tg) = Ts
                        sg, r_sg = sgp.next()
                        P.op("act", lambda Tg=Tg, sg=sg: nc.scalar.activation(out=sg[:], in_=Tg[:], func=AF.Silu), reads=[r_Tg], writes=[r_sg])
                        P.op("dve", lambda Ta=Ta, sg=sg, c=c: nc.vector.tensor_tensor(out=z[:, c, :], in0=Ta[:], in1=sg[:], op=ALU.mult),
                             reads=[r_Ta, r_sg], writes=[r_z])
                for cb in range(8):
                    wt, wr = wdn.next()
                    ch = chwd[cnt["d"] % 2]
                    cnt["d"] += 1
                    P.dma("sp", ch, lambda wt=wt, cb=cb: nc.sync.dma_start(
                        out=wt[:], in_=wdnb[:, cb * 256:(cb + 1) * 256].rearrange("(k p) c -> p k c", p=128)), reads=[self.r_wb], writes=[wr])
                    for s_ in range(4):
                        ps, r_ps = dn.next()
                        for kc in range(44):
                            P.op("pe", lambda ps=ps, kc=kc, s_=s_, wt=wt: nc.tensor.matmul(
                                ps[:, 0:256], lhsT=z[:, kc, s_ * 128:(s_ + 1) * 128], rhs=wt[:, kc, :], start=(kc == 0), stop=(kc == 43)),
                                reads=[wr, r_z], writes=[r_ps])
                        tf, r_tf = tmp.next()
                        P.op("dve", lambda ps=ps, tf=tf, cb=cb: nc.vector.tensor_tensor(out=tf[:], in0=ps[:, 0:256], in1=g2bc[:, cb * 256:(cb + 1) * 256], op=ALU.mult),
                             reads=[r_ps, r_cst], writes=[r_tf])
                        P.op("dve", lambda tf=tf, s_=s_, cb=cb, xt=xt: nc.vector.tensor_tensor(
                            out=xt[:, s_, cb * 256:(cb + 1) * 256], in0=xt[:, s_, cb * 256:(cb + 1) * 256], in1=tf[:], op=ALU.add),
                            reads=[r_tf, r_xt], writes=[r_xt])
                ss, r_ss = sspool.next()
                rs, r_rs = rspool.next()
                for s_ in range(4):
                    P.op("act", lambda s_=s_, xt=xt, ss=ss: nc.scalar.activation(out=junk[:], in_=xt[:, s_, :], func=AF.Square, accum_out=ss[:, s_:s_ + 1]),
                         reads=[r_xt], writes=[r_junk, r_ss])
                P.op("dve", lambda ss=ss, rs=rs: nc.vector.tensor_scalar(out=rs[:], in0=ss[:], scalar1=1.0 / D, scalar2=EPS, op0=ALU.mult, op1=ALU.add),
                     reads=[r_ss], writes=[r_rs])
                P.op("act", lambda rs=rs: nc.scalar.sqrt(out=rs[:], in_=rs[:]), reads=[r_rs], writes=[r_rs])
                P.op("dve", lambda rs=rs: nc.vector.reciprocal(out=rs[:], in_=rs[:]), reads=[r_rs], writes=[r_rs])
                for s_ in range(4):
                    P.op("dve", lambda s_=s_, xt=xt, rs=rs: nc.vector.scalar_tensor_tensor(
                        out=xt[:, s_, :], in0=xt[:, s_, :], scalar=rs[:, s_:s_ + 1], in1=fgbc[:], op0=ALU.mult, op1=ALU.mult),
                        reads=[r_xt, r_rs, r_cst], writes=[r_xt])
                    i2 = cnt["o"] % 2
                    cnt["o"] += 1
                    t0 = 512 * w + 128 * s_
                    P.dma("act", cho[i2], lambda s_=s_, xt=xt, t0=t0: nc.scalar.dma_start(out=self.out[t0:t0 + 128, :], in_=xt[:, s_, :]), reads=[r_xt])

            for w in range(4):
                if self.dbg and "wins" in self.dbg and w not in self.dbg["wins"]:
                    continue
                window(w)
            P.barrier()

_STATIC = {}


def static_tables():
    if _STATIC:
        return _STATIC
    sl = np.array(SLOPES, np.float64)
    i = np.arange(128, dtype=np.float64)
    bs = sl[None, :, None] * (i[:, None, None] + 64.0 * (np.arange(NDS)[None, None, :] - DOFF))
    bc = sl[None, :, None] * (16.0 * i[:, None, None] + 31.0 + 64.0 * (np.arange(NDC)[None, None, :] - ROFF))
    masks = np.zeros((128, NMASK, 512), np.float32)
    for n, v in enumerate(MASK_LIST):
        masks[:, n, :v.shape[1]] = np.where(v, 0.0, -30000.0)
    E = np.zeros((128, 64, 128), np.float32)
    for u in range(64):
        for k in range(128):
            E[2 * u + k // 64, u, k] = 1.0
    Ov = np.zeros((128, 9, NBW), np.float32)
    for jj in range(9):
        for ii in range(128):
            for b in range(NBLKW):
                dlt = ii - 128 * (jj + 1) - 4 * b + 1056
                if -1 <= dlt <= 3:
                    Ov[ii, jj, b] = 1.0
    _STATIC.update({"ident": np.eye(128, dtype=np.float32).astype(NPBF),
                    "bias_s": bs.astype(np.float32), "bias_c": bc.astype(np.float32),
                    "masks": masks.astype(NPBF), "esel": E.astype(NPBF), "ovm": Ov.astype(NPBF),
                    "ones_bf": np.ones((128, 128), np.float32).astype(NPBF)})
    return _STATIC


def host_tables(c):
    t_start = c * TOWN
    real = np.arange(TV) - OWN0 + t_start
    vt = (real >= 0).astype(np.float32)
    nv = np.arange(NCV)
    rn = nv - (OWN0 - t_start) // 16
    vc = ((rn >= 0) & (rn <= 1022) & (nv <= 1150)).astype(np.float32)
    selb = np.zeros((5, 4, 128, NBW), np.float32)
    selv = np.zeros((5, 4, 128, NBW), np.float32)
    b = np.arange(NBLKW)
    for ti, (q0, nq) in enumerate(QTILES):
        qend = q0 + nq
        jv = b + qend // 64 - NBLKW
        realb = jv - (OWN0 - t_start) // 64
        for s_ in range(max(1, nq // 128)):
            rows = min(128, nq)
            tq = q0 + 128 * s_ + np.arange(rows)
            cur = tq // 64
            valid = (realb[None, :] >= 0) & (jv[None, :] <= cur[:, None])
            forced = (realb[None, :] == 0) | (jv[None, :] == cur[:, None]) | (jv[None, :] == cur[:, None] - 1)
            selv[ti, s_, :rows, :NBLKW] = valid
            selb[ti, s_, :rows, :NBLKW] = 1.0 + 1e6 * forced
    d = {"vtok": np.ascontiguousarray(vt.reshape(TV // 128, 128).T),
         "vcmp": np.ascontiguousarray(vc.reshape(NCV // 128, 128).T),
         "selb": selb, "selv": selv,
         "hflag": np.full((128, 1), 1.0 if c > 0 else 0.0, np.float32)}
    d.update(static_tables())
    return d


def make_inputs(inputs, c):
    x = np.asarray(inputs["x"], np.float32)[0]
    t_start = c * TOWN
    xv = np.zeros((TV, D), np.float32)
    lo = OWN0 - t_start
    xv[lo:] = x[:t_start + TOWN]
    m = {"xv": xv, "c": np.asarray(inputs["c"], np.float32),
         "w_ada": np.asarray(inputs["w_ada"], np.float32)[0], "b_ada": np.asarray(inputs["b_ada"], np.float32)[0],
         "norm1_g": np.asarray(inputs["norm1_g"], np.float32)[0], "w_in": np.asarray(inputs["w_in"], np.float32)[0]}
    for n in ("cmp_pe", "w_kc1", "w_kc2", "w_vc1", "w_vc2", "w_o_nsa", "conv_dw_w", "conv_dw_b", "conv_ln_g", "conv_ln_b",
              "conv_pw_w", "conv_pw_b", "w_out", "norm2_g", "ffn_w_up", "ffn_dw_w", "ffn_dw_b", "ffn_w_down"):
        m[n] = np.asarray(inputs[n], np.float32)[0]
    m["final_g"] = np.asarray(inputs["final_g"], np.float32)
    m.update(host_tables(c))
    return m


def kernel(**inputs):
    k = K()
    nc = k.build()
    in_maps = [{n: v for n, v in make_inputs(inputs, c).items() if n in k.ins} for c in range(NCORE)]
    res = run_bass_kernel_spmd(nc, in_maps, core_ids=list(range(NCORE)))
    outs = [np.asarray(res.results[c]["out"], np.float32) for c in range(NCORE)]
    return np.concatenate(outs, axis=0)[None]
```

```python
from contextlib import ExitStack
import numpy as np
import ml_dtypes
import concourse.bass as bass
import concourse.mybir as mybir
from concourse.bass_utils import run_bass_kernel_spmd

F32 = mybir.dt.float32
BF16 = mybir.dt.bfloat16
I32 = mybir.dt.int32
AF = mybir.ActivationFunctionType
ALU = mybir.AluOpType
NPBF = ml_dtypes.bfloat16

D = 2048
T = 16384
NCORE = 8
TOWN = T // NCORE
NH, NG, HG, DK = 16, 2, 8, 128
EPS = 1e-6
FF = 5632
WIN = 512
OWN0 = 16384
TV = OWN0 + TOWN
HALO = 64
Q0 = OWN0 - HALO
NQ = TOWN + HALO
W0 = OWN0 - 640
NA = TV - W0
B0 = OWN0 - 128
NB = TV - B0
NCV = TV // 16
C_Q, C_KC, C_VC, C_KS, C_VS, C_KW, C_VW, C_GN, C_GLU, C_GM = 0, 2048, 2304, 2560, 2816, 3072, 3328, 3584, 3632, 7728
INW = 11824
EPOCH = 12000
SLOPES = [2.0 ** (-8.0 * (h + 1) / 16) for h in range(NH)]
NBLKW = 264
NBW = 272
DOFF, NDS = 272, 280
ROFF, NDC = 296, 280
QTILES = [(Q0, 64)] + [(OWN0 + 512 * i, 512) for i in range(4)]
SKIP = 200.0


def exp_width(h, nq):
    w = 64
    while w * 2 <= min(512, nq) and SLOPES[h] * (w * 2) <= 64.0:
        w *= 2
    return min(w, nq)


def n_slc_chunks(h, nq, maxc):
    return int(min(maxc, np.floor((SKIP / SLOPES[h] + nq) / 128) + 1))


def n_cmp_chunks(h, nq):
    return int(min(9, np.floor((SKIP / SLOPES[h] + nq + 15) / 2048) + 1))


def chunk_valid(kind, nq, j):
    i = np.arange(128)[:, None]
    q = np.arange(nq)[None, :]
    if kind == "cmp":
        kp = 16 * i + 31 + nq - 2048 * (j + 1)
        v = kp <= q
    else:
        kp = i + nq - 128 * (j + 1)
        dist = q - kp
        v = dist >= 0
        if kind == "win":
            v = v & (dist < WIN)
    return None if v.all() else v


def mask_index():
    idx = {}
    for nq in (64, 512):
        nwin = 8 if nq == 512 else 5
        for kind, nj in (("slc", 4), ("win", nwin), ("cmp", 1)):
            for j in range(nj):
                v = chunk_valid(kind, nq, j)
                if v is None:
                    continue
                idx[(kind, nq, j)] = v
    uniq, out = [], {}
    for key, v in idx.items():
        for n, u in enumerate(uniq):
            if u.shape == v.shape and (u == v).all():
                out[key] = n
                break
        else:
            uniq.append(v)
            out[key] = len(uniq) - 1
    return out, uniq


MASK_IDX, MASK_LIST = mask_index()
NMASK = len(MASK_LIST)


class Res:
    __slots__ = ("name", "last_w", "readers")

    def __init__(self, name=""):
        self.name = name
        self.last_w = None
        self.readers = []


class Op:
    __slots__ = ("eng", "fn", "deps", "seq", "sig", "is_dma", "chan", "needs_sig", "pos")


class Chan:
    def __init__(self, prog, name):
        self.sem = prog.new_sem("ch_" + name)
        self.count = 0
        self.last_op = None
        self.group = []


class Prog:
    ENGS = ("pe", "act", "dve", "pool", "sp")

    def __init__(self, nc, stack):
        self.nc = nc
        self.stack = stack
        self.ops = {e: [] for e in self.ENGS}
        self.seq = 0
        self.nsem = 0
        self.eng_sems = {e: [] for e in self.ENGS}
        self.chans = []
        self.uid = 0

    def new_sem(self, name):
        self.nsem += 1
        return self.stack.enter_context(self.nc.semaphore(name))

    def chan(self, name):
        c = Chan(self, name)
        self.chans.append(c)
        return c

    def _mk(self, eng, fn, reads, writes):
        op = Op()
        op.eng, op.fn, op.seq = eng, fn, self.seq
        self.seq += 1
        op.is_dma, op.chan, op.needs_sig, op.sig = False, None, False, None
        deps = []
        for r in reads:
            if r.last_w is not None:
                deps.append(r.last_w)
        for w in writes:
            if w.last_w is not None:
                deps.append(w.last_w)
            deps.extend(w.readers)
        for r in reads:
            if not getattr(op, "is_dma", False) and eng in ("pe", "act", "dve"):
                r.readers = [x for x in r.readers if x.eng != eng or x.is_dma]
            r.readers.append(op)
        for w in writes:
            w.last_w = op
            w.readers = []
        seen, dd = set(), []
        for d in deps:
            if id(d) in seen or d is op:
                continue
            seen.add(id(d))
            if eng == "pe" and d.eng == "pe" and not d.is_dma:
                continue
            dd.append(d)
        op.deps = dd
        self.ops[eng].append(op)
        return op

    def op(self, eng, fn, reads=(), writes=()):
        return self._mk(eng, fn, list(reads), list(writes))

    def dma(self, queue, chan, fn, reads=(), writes=(), cont=False):
        op = self._mk(queue, fn, list(reads), list(writes))
        op.is_dma, op.chan = True, chan
        if not cont:
            if chan.last_op is not None and chan.last_op not in op.deps:
                op.deps.append(chan.last_op)
            chan.group = []
        op.deps = [d for d in op.deps if d not in chan.group]
        chan.count += 16
        chan.group.append(op)
        for o in chan.group:
            o.sig = (chan.sem, chan.count)
        op.needs_sig = True
        chan.last_op = op
        return op

    def barrier(self):
        lasts = []
        for e in self.ENGS:
            for op in reversed(self.ops[e]):
                if not op.is_dma:
                    lasts.append(op)
                    break
        for c in self.chans:
            if c.last_op is not None:
                lasts.append(c.last_op)
        nc = self.nc
        eo = {"pe": nc.tensor, "act": nc.scalar, "dve": nc.vector, "pool": nc.gpsimd, "sp": nc.sync}
        for e in self.ENGS:
            op = self._mk(e, (lambda e=e: eo[e].nop()), [], [])
            op.deps = [d for d in lasts if not (d.eng == e and not d.is_dma)]

    def emit(self):
        nc = self.nc
        for e in self.ENGS:
            for op in self.ops[e]:
                for d in op.deps:
                    if not d.is_dma:
                        d.needs_sig = True
        for e in self.ENGS:
            cnt, sem = 0, None
            for op in self.ops[e]:
                if op.is_dma or not op.needs_sig:
                    continue
                if sem is None or cnt >= EPOCH:
                    sem = self.new_sem(f"e_{e}_{len(self.eng_sems[e])}")
                    self.eng_sems[e].append(sem)
                    cnt = 0
                cnt += 1
                op.sig = (sem, cnt)
        with nc.Block() as block:
            for e in self.ENGS:
                ops = self.ops[e]
                if not ops:
                    continue
                deco = {"pe": block.tensor, "act": block.scalar, "dve": block.vector,
                        "pool": block.gpsimd, "sp": block.sync}[e]

                def body(engobj, ops=ops):
                    known = {}
                    for op in ops:
                        for d in op.deps:
                            sem, val = d.sig
                            if known.get(id(sem), 0) >= val:
                                continue
                            engobj.wait_ge(sem, val)
                            known[id(sem)] = val
                        ins = op.fn()
                        if op.needs_sig:
                            ins.then_inc(op.sig[0], 16 if op.is_dma else 1)
                    last = {}
                    for op in ops:
                        if op.is_dma:
                            last[id(op.chan)] = op
                    for op in last.values():
                        sem, val = op.sig
                        if known.get(id(sem), 0) < val:
                            engobj.wait_ge(sem, val)
                            known[id(sem)] = val
                deco(body)


class Pool:
    def __init__(self, P, st, name, n, shape, dt, psum=False):
        self.t, self.r = [], []
        for i in range(n):
            if psum:
                self.t.append(st.enter_context(P.nc.psum_tensor(f"{name}{i}", list(shape), dt)))
            else:
                self.t.append(st.enter_context(P.nc.sbuf_tensor(f"{name}{i}", list(shape), dt)))
            self.r.append(Res(f"{name}{i}"))
        self.i = 0
        self.n = n

    def next(self):
        k = self.i % self.n
        self.i += 1
        return self.t[k], self.r[k]


class K:
    def __init__(self, dbg=None):
        self.dbg = dbg
        nc = self.nc = bass.Bass("TRN2", target_bir_lowering=False)
        self.ins = {}
        self.outs = {}

    def din(self, name, shape, dt=F32):
        t = self.nc.dram_tensor(name, list(shape), dt, kind="ExternalInput").ap()
        self.ins[name] = t
        return t

    def dscr(self, name, shape, dt):
        kind = "ExternalOutput" if (self.dbg and name in self.dbg) else "Internal"
        t = self.nc.dram_tensor(name, list(shape), dt, kind=kind).ap()
        return t

    def build(self, upto=99):
        nc = self.nc
        xv = self.din("xv", [TV, D])
        c_in = self.din("c", [1, D])
        w_ada = self.din("w_ada", [D, 6 * D])
        b_ada = self.din("b_ada", [6 * D])
        norm1_g = self.din("norm1_g", [D])
        w_in = self.din("w_in", [D, INW])
        vtok = self.din("vtok", [128, TV // 128])
        ident_in = self.din("ident", [128, 128], BF16)
        self.i_vcmp = self.din("vcmp", [128, NCV // 128])
        self.i_selb = self.din("selb", [5, 4, 128, NBW])
        self.i_selv = self.din("selv", [5, 4, 128, NBW])
        self.i_hflag = self.din("hflag", [128, 1])
        self.i_bias_s = self.din("bias_s", [128, NH, NDS])
        self.i_bias_c = self.din("bias_c", [128, NH, NDC])
        self.i_masks = self.din("masks", [128, NMASK, 512], BF16)
        self.i_esel = self.din("esel", [128, 64, 128], BF16)
        self.i_ovm = self.din("ovm", [128, 9, NBW], BF16)
        self.i_ones = self.din("ones_bf", [128, 128], BF16)
        self.i_cmp_pe = self.din("cmp_pe", [32, 128])
        self.i_w1 = [self.din("w_kc1", [4096, 256]), self.din("w_vc1", [4096, 256])]
        self.i_w2 = [self.din("w_kc2", [256, 128]), self.din("w_vc2", [256, 128])]
        self.i_wo = self.din("w_o_nsa", [D, D])
        self.i_dww = self.din("conv_dw_w", [31, D])
        self.i_dwb = self.din("conv_dw_b", [D])
        self.i_lng = self.din("conv_ln_g", [D])
        self.i_lnb = self.din("conv_ln_b", [D])
        self.i_wpw = self.din("conv_pw_w", [D, D])
        self.i_pwb = self.din("conv_pw_b", [D])
        self.i_wout = self.din("w_out", [D, D])
        self.i_n2g = self.din("norm2_g", [D])
        self.i_wup = self.din("ffn_w_up", [D, 2 * FF])
        self.i_fdw = self.din("ffn_dw_w", [3, 2 * FF])
        self.i_fdb = self.din("ffn_dw_b", [2 * FF])
        self.i_wdn = self.din("ffn_w_down", [FF, D])
        self.i_fg = self.din("final_g", [D])
        out = self.nc.dram_tensor("out", [TOWN, D], F32, kind="ExternalOutput").ap()
        self.out = out
        ada_d = self.dscr("ada_d", [6 * D], F32)
        kcT_raw = self.dscr("kcT_raw", [NG, 128, TV + 16], BF16)
        vcT_raw = self.dscr("vcT_raw", [NG, 128, TV + 16], BF16)
        kslT = self.dscr("kslT", [NG, 128, TV], BF16)
        vsl = self.dscr("vsl", [TV, NG, 130], BF16)
        QT = self.dscr("QT", [128, NH, NB], BF16)
        kwT = self.dscr("kwT", [NG, 128, NA], BF16)
        vw = self.dscr("vw", [NA, NG, 130], BF16)
        gates = self.dscr("gates", [NB, 48], F32)
        gluT = self.dscr("gluT", [128, 16, NB], BF16)
        mgT = self.dscr("mgT", [128, 32, NB], BF16)
        self.kcT = self.dscr("kcT", [NG, 128, NCV], BF16)
        self.vca = self.dscr("vca", [NCV, NG, 130], BF16)
        self.xmid = self.dscr("xmid", [NQ, D], F32)
        self.accd = self.dscr("accd", [NQ, D], F32)
        self.dumpS = self.dscr("dumpS", [128, 512], F32)
        self.impd = self.dscr("impd", [5, 4, 128, NG, NBW], F32)
        self.wb_d = {n: self.dscr(n, sh, BF16) for n, sh in (("wo_b", [D, D]), ("wpw_b", [D, D]), ("wout_b", [D, D]),
                                                           ("wup_b", [D, 2 * FF]), ("wdn_b", [FF, D]))}
        self.xv, self.kcT_raw, self.vcT_raw, self.kslT, self.vsl = xv, kcT_raw, vcT_raw, kslT, vsl
        self.QT, self.kwT, self.vw, self.gates, self.gluT, self.mgT, self.ada_d = QT, kwT, vw, gates, gluT, mgT, ada_d

        with ExitStack() as st0:
            P = self.P = Prog(nc, st0)
            self.st0 = st0
            sb = lambda name, shape, dt: st0.enter_context(nc.sbuf_tensor(name, list(shape), dt))
            ident = sb("ident_sb", [128, 128], BF16)
            r_const = Res("const")
            ch_c = P.chan("const")
            P.dma("sp", ch_c, lambda: nc.sync.dma_start(out=ident[:], in_=ident_in[:, :]), writes=[r_const])
            vtok_sb = sb("vtok_sb", [128, TV // 128], F32)
            P.dma("sp", ch_c, lambda: nc.sync.dma_start(out=vtok_sb[:], in_=vtok[:, :]), writes=[r_const], cont=True)
            ada = sb("ada_sb", [128, 96], F32)
            r_ada = Res("ada")
            s1 = sb("s1", [128, 16], F32)
            r_s1 = Res("s1")
            self.ident, self.r_const, self.vtok_sb, self.ada, self.r_ada, self.sb0 = ident, r_const, vtok_sb, ada, r_ada, sb
            self.ones = sb("ones_sb", [128, 128], BF16)
            P.dma("sp", ch_c, lambda: nc.sync.dma_start(out=self.ones[:], in_=self.i_ones[:, :]), writes=[r_const], cont=True)
            self.hfl = sb("hfl_sb", [128, 1], F32)
            P.dma("sp", ch_c, lambda: nc.sync.dma_start(out=self.hfl[:], in_=self.i_hflag[:, :]), writes=[r_const], cont=True)
            self.vcmp_sb = sb("vcmp_sb", [128, NCV // 128], F32)
            P.dma("sp", ch_c, lambda: nc.sync.dma_start(out=self.vcmp_sb[:], in_=self.i_vcmp[:, :]), writes=[r_const], cont=True)

            with ExitStack() as st:
                cT = st.enter_context(nc.sbuf_tensor("cT", [128, 16], F32))
                cact = st.enter_context(nc.sbuf_tensor("cact", [128, 16], F32))
                bT = st.enter_context(nc.sbuf_tensor("bT", [128, 96], F32))
                g1T = st.enter_context(nc.sbuf_tensor("g1T", [128, 16], F32))
                r_cT, r_cact, r_bT, r_g1T = Res(), Res(), Res(), Res()
                ch0 = P.chan("p0")
                P.dma("sp", ch0, lambda: nc.sync.dma_start(out=cT[:], in_=c_in[0, :].rearrange("(j p) -> p j", p=128),
                                                         allow_slow_non_contiguous=True), writes=[r_cT])
                P.dma("sp", ch0, lambda: nc.sync.dma_start(out=bT[:], in_=b_ada.rearrange("(f p) -> p f", p=128),
                                                         allow_slow_non_contiguous=True), writes=[r_bT], cont=True)
                P.dma("sp", ch0, lambda: nc.sync.dma_start(out=g1T[:], in_=norm1_g.rearrange("(j p) -> p j", p=128),
                                                         allow_slow_non_contiguous=True), writes=[r_g1T], cont=True)
                P.op("act", lambda: nc.scalar.activation(out=cact[:], in_=cT[:], func=AF.Silu), reads=[r_cT], writes=[r_cact])
                wpool = Pool(P, st, "wada", 2, [128, 16, 512], F32)
                chw = [P.chan("wada0"), P.chan("wada1")]
                aps = st.enter_context(nc.psum_tensor("ada_ps", [128, 96], F32))
                r_aps = Res()
                for blk in range(24):
                    wt, wr = wpool.next()
                    P.dma("sp", chw[blk % 2],
                          lambda wt=wt, blk=blk: nc.sync.dma_start(
                              out=wt[:], in_=w_ada[:, blk * 512:(blk + 1) * 512].rearrange("(k p) c -> p k c", p=128)),
                          writes=[wr])
                    for fl in range(4):
                        f = blk * 4 + fl
                        for kc in range(16):
                            P.op("pe", lambda wt=wt, fl=fl, kc=kc, f=f: nc.tensor.matmul(
                                aps[:, f:f + 1], lhsT=wt[:, kc, fl * 128:(fl + 1) * 128], rhs=cact[:, kc:kc + 1],
                                start=(kc == 0), stop=(kc == 15)), reads=[wr, r_cact], writes=[r_aps])
                P.op("dve", lambda: nc.vector.tensor_tensor(out=ada[:], in0=aps[:], in1=bT[:], op=ALU.add),
                     reads=[r_aps, r_bT], writes=[r_ada])
                P.op("dve", lambda: nc.vector.scalar_tensor_tensor(out=s1[:], in0=ada[:, 16:32], scalar=1.0, in1=g1T[:],
                                                                   op0=ALU.add, op1=ALU.mult),
                     reads=[r_ada, r_g1T], writes=[r_s1])
                r_adad = Res()
                ch_ad = P.chan("adad")
                P.dma("act", ch_ad, lambda: nc.scalar.dma_start(out=ada_d.rearrange("(f p) -> p f", p=128), in_=ada[:],
                                                              allow_slow_non_contiguous=True),
                      reads=[r_ada], writes=[r_adad])
                P.barrier()
            if upto <= 0:
                P.emit()
                return nc

            self.r_wb = Res("wb_scratch")
            if not (self.dbg and "nocast" in self.dbg):
                with ExitStack() as st:
                    c32 = Pool(P, st, "cst32_", 3, [128, 8192], F32)
                    cbf = Pool(P, st, "cstbf_", 3, [128, 8192], BF16)
                    cci = [P.chan(f"cji{i}") for i in range(3)]
                    cco = [P.chan(f"cjo{i}") for i in range(3)]
                    for ji, (sv, dv, o, w) in enumerate(self.cast_jobs()):
                        t32, r32 = c32.next()
                        tbf, rbf = cbf.next()
                        P.dma("sp", cci[ji % 3], lambda t32=t32, sv=sv, o=o, w=w: nc.sync.dma_start(out=t32[:, 0:w], in_=sv[:, o:o + w]), writes=[r32])
                        if ji % 2 == 0:
                            P.op("dve", lambda t32=t32, tbf=tbf, w=w: nc.vector.tensor_copy(out=tbf[:, 0:w], in_=t32[:, 0:w]), reads=[r32], writes=[rbf])
                        else:
                            P.op("act", lambda t32=t32, tbf=tbf, w=w: nc.scalar.copy(out=tbf[:, 0:w], in_=t32[:, 0:w]), reads=[r32], writes=[rbf])
                        P.dma("act", cco[ji % 3], lambda tbf=tbf, dv=dv, o=o, w=w: nc.scalar.dma_start(out=dv[:, o:o + w], in_=tbf[:, 0:w]),
                              reads=[rbf], writes=[self.r_wb])
                    P.barrier()
            with ExitStack() as st:
                wkv32 = Pool(P, st, "wkv32_", 2, [128, 16, 128], F32)
                wkv = st.enter_context(nc.sbuf_tensor("wkv", [128, 16, 1024], BF16))
                r_wkv = Res()
                chw = [P.chan("wkv0"), P.chan("wkv1")]
                for q in range(8):
                    wt, wr = wkv32.next()
                    P.dma("sp", chw[q % 2], lambda wt=wt, q=q: nc.sync.dma_start(
                        out=wt[:], in_=w_in[:, C_KC + q * 128:C_KC + (q + 1) * 128].rearrange("(k p) c -> p k c", p=128)),
                        writes=[wr])
                    P.op("pool", lambda wt=wt, q=q: nc.gpsimd.tensor_copy(out=wkv[:, :, q * 128:(q + 1) * 128], in_=wt[:]),
                         reads=[wr], writes=[r_wkv])
                xpool = Pool(P, st, "xt", 2, [128, 4, D], F32)
                chx = [P.chan("x0"), P.chan("x1")]
                junk = st.enter_context(nc.sbuf_tensor("junk", [128, D], BF16))
                r_junk = Res()
                sspool = Pool(P, st, "ss", 2, [128, 4], F32)
                rspool = Pool(P, st, "rs", 2, [128, 4], F32)
                xnpool = Pool(P, st, "xn", 2, [128, 4, D], BF16)
                hTpool = Pool(P, st, "hT", 2, [128, 16, 512], BF16)
                tpp = Pool(P, st, "tp", 4, [128, 512], BF16, psum=True)
                mmp = Pool(P, st, "mm", 3, [128, 512], F32, psum=True)
                stg = Pool(P, st, "stg", 2, [128, 6, 512], BF16)
                stv = Pool(P, st, "stv", 2, [128, 4, NG, 130], BF16)
                chs = [P.chan("st0"), P.chan("st1")]
                chv = [P.chan("sv0"), P.chan("sv1")]
                r_scr = Res("scr1a")
                for i in range(2):
                    P.op("dve", lambda i=i: nc.vector.memset(stv.t[i][:], 0.0), writes=[stv.r[i]])
                nblk = TV // 512
                if self.dbg and "nblk" in self.dbg:
                    nblk = self.dbg["nblk"]
                for tb in range(nblk):
                    t0 = tb * 512
                    hT, r_hT = self.norm_T(P, st, xv, t0, 4, xpool, chx[tb % 2], junk, r_junk, sspool, rspool, xnpool,
                                           hTpool, tpp, ident, r_const, s1, r_s1, ada, r_ada, 0)
                    sg, r_sg = stg.next()
                    for ci in range(6):
                        ps, r_ps = mmp.next()
                        for kc in range(16):
                            P.op("pe", lambda ps=ps, ci=ci, kc=kc, hT=hT: nc.tensor.matmul(
                                ps[:], lhsT=wkv[:, kc, ci * 128:(ci + 1) * 128], rhs=hT[:, kc, :],
                                start=(kc == 0), stop=(kc == 15)), reads=[r_wkv, r_hT], writes=[r_ps])
                        if ci % 2 == 0:
                            P.op("act", lambda ps=ps, sg=sg, ci=ci: nc.scalar.copy(out=sg[:, ci, :], in_=ps[:]),
                                 reads=[r_ps], writes=[r_sg])
                        else:
                            P.op("dve", lambda ps=ps, sg=sg, ci=ci: nc.vector.tensor_copy(out=sg[:, ci, :], in_=ps[:]),
                                 reads=[r_ps], writes=[r_sg])
                    dsts = [kcT_raw, vcT_raw, kslT]
                    for k3 in range(3):
                        P.dma("act", chs[tb % 2], lambda sg=sg, k3=k3, t0=t0: nc.scalar.dma_start(
                            out=dsts[k3][:, :, t0:t0 + 512].rearrange("g p t -> p g t"), in_=sg[:, 2 * k3:2 * k3 + 2, :]),
                            reads=[r_sg], writes=[r_scr], cont=(k3 > 0))
                    sv, r_sv = stv.next()
                    for s in range(4):
                        ps, r_ps = mmp.next()
                        for kc in range(16):
                            P.op("pe", lambda ps=ps, s=s, kc=kc, hT=hT: nc.tensor.matmul(
                                ps[:, 0:256], lhsT=hT[:, kc, s * 128:(s + 1) * 128], rhs=wkv[:, kc, 768:1024],
                                start=(kc == 0), stop=(kc == 15)), reads=[r_wkv, r_hT], writes=[r_ps])
                        tile = tb * 4 + s
                        P.op("dve", lambda ps=ps, sv=sv, s=s, tile=tile: nc.vector.tensor_scalar(
                            out=sv[:, s, :, 0:128], in0=ps[:, 0:256].rearrange("p (g d) -> p g d", g=NG),
                            scalar1=vtok_sb[:, tile:tile + 1], scalar2=None, op0=ALU.mult),
                            reads=[r_ps, r_const], writes=[r_sv])
                        P.op("dve", lambda sv=sv, s=s, tile=tile: nc.vector.tensor_copy(
                            out=sv[:, s, :, 128:130], in_=vtok_sb[:, tile:tile + 1].unsqueeze(1).to_broadcast([128, NG, 2])),
                            reads=[r_const], writes=[r_sv])
                    P.dma("act", chv[tb % 2], lambda sv=sv, t0=t0: nc.scalar.dma_start(
                        out=vsl[t0:t0 + 512, :, :].rearrange("(s p) g d -> p s g d", p=128), in_=sv[:]),
                        reads=[r_sv], writes=[r_scr])
                P.barrier()
            if upto <= 1:
                P.emit()
                return nc

            with ExitStack() as st:
                hTo = st.enter_context(nc.sbuf_tensor("hTo", [128, 16, NA], BF16))
                r_hTo = Res()
                with ExitStack() as st2:
                    xpool = Pool(P, st2, "xtb", 2, [128, 4, D], F32)
                    chx = [P.chan("xb0"), P.chan("xb1")]
                    junk = st2.enter_context(nc.sbuf_tensor("junkb", [128, D], BF16))
                    r_junk = Res()
                    sspool = Pool(P, st2, "ssb", 2, [128, 4], F32)
                    rspool = Pool(P, st2, "rsb", 2, [128, 4], F32)
                    xnpool = Pool(P, st2, "xnb", 2, [128, 4, D], BF16)
                    tpp = Pool(P, st2, "tpb", 4, [128, 512], BF16, psum=True)
                    for tb in range(6):
                        t0 = W0 + tb * 512
                        nsub = 4 if tb < 5 else 1

                        class _HP:
                            def next(self_inner):
                                return hTo[:, :, tb * 512:tb * 512 + nsub * 128], r_hTo
                        self.norm_T(P, st2, xv, t0, nsub, xpool, chx[tb % 2], junk, r_junk, sspool, rspool, xnpool,
                                    _HP(), tpp, ident, r_const, s1, r_s1, ada, r_ada, 0)
                    P.barrier()
                w32 = Pool(P, st, "w32_", 2, [128, 16, 512], F32)
                wbf = Pool(P, st, "wbf_", 2, [128, 16, 512], BF16)
                chw = [P.chan("w1b0"), P.chan("w1b1")]
                mmp = Pool(P, st, "mmb", 4, [128, 512], F32, psum=True)
                stA = Pool(P, st, "stA", 2, [128, NA], BF16)
                chst = [P.chan("stA0"), P.chan("stA1"), P.chan("stA2")]
                sgp = Pool(P, st, "sgp", 1, [128, NB], BF16)
                r_scr = Res("scr1b")
                self.wblk = 0

                def load_w(colranges):
                    wt, wr = w32.next()
                    wb, wbr = wbf.next()
                    ch = chw[self.wblk % 2]
                    self.wblk += 1
                    o = 0
                    for i, (c0, n) in enumerate(colranges):
                        P.dma("sp", ch, lambda wt=wt, c0=c0, n=n, o=o: nc.sync.dma_start(
                            out=wt[:, :, o:o + n], in_=w_in[:, c0:c0 + n].rearrange("(k p) c -> p k c", p=128)),
                            writes=[wr], cont=(i > 0))
                        o += n
                    P.op("pool", lambda wt=wt, wb=wb, o=o: nc.gpsimd.tensor_copy(out=wb[:, :, 0:o], in_=wt[:, :, 0:o]),
                         reads=[wr], writes=[wbr])
                    return wb, wbr

                def blocks(lo, hi):
                    b = []
                    t = lo
                    while t < hi:
                        n = min(512, hi - t)
                        b.append((t, n))
                        t += n
                    return b

                def fm_chunk(wb, wbr, woff, lo, hi, evac):
                    for (t, n) in blocks(lo, hi):
                        ps, r_ps = mmp.next()
                        for kc in range(16):
                            P.op("pe", lambda ps=ps, kc=kc, t=t, n=n: nc.tensor.matmul(
                                ps[:, 0:n], lhsT=wb[:, kc, woff:woff + 128], rhs=hTo[:, kc, t:t + n],
                                start=(kc == 0), stop=(kc == 15)), reads=[wbr, r_hTo], writes=[r_ps])
                        evac(ps, r_ps, t, n)

                ecount = [0]

                def copy_evac(dst, r_dst, off, func=None, scale=1.0):
                    def ev(ps, r_ps, t, n):
                        ecount[0] += 1
                        if func is None and scale == 1.0 and ecount[0] % 2 == 0:
                            P.op("dve", lambda: nc.vector.tensor_copy(out=dst[:, t - off:t - off + n], in_=ps[:, 0:n]),
                                 reads=[r_ps], writes=[r_dst])
                        else:
                            P.op("act", lambda: nc.scalar.activation(out=dst[:, t - off:t - off + n], in_=ps[:, 0:n],
                                                                     func=(func or AF.Copy), scale=scale),
                                 reads=[r_ps], writes=[r_dst])
                    return ev

                stn = [0]

                def store(dst_ap, sg, r_sg, n):
                    ch = chst[stn[0] % 3]
                    stn[0] += 1
                    P.dma("act", ch, lambda: nc.scalar.dma_start(out=dst_ap, in_=sg[:, 0:n]), reads=[r_sg], writes=[r_scr])

                OB = B0 - W0
                for qb in range(4):
                    wb, wbr = load_w([(C_Q + qb * 512, 512)])
                    for hl in range(4):
                        h = qb * 4 + hl
                        sg, r_sg = stA.next()
                        fm_chunk(wb, wbr, hl * 128, OB, NA, copy_evac(sg, r_sg, OB, scale=float(DK) ** -0.5))
                        store(QT[:, h, :], sg, r_sg, NB)
                wb, wbr = load_w([(C_KW, 256), (C_VW, 256)])
                for g in range(NG):
                    sg, r_sg = stA.next()
                    fm_chunk(wb, wbr, g * 128, 0, NA, copy_evac(sg, r_sg, 0))
                    store(kwT[g, :, :], sg, r_sg, NA)
                stv = Pool(P, st, "stvb", 2, [128, NG, 130], BF16)
                chv = [P.chan("svb0"), P.chan("svb1")]
                for i in range(2):
                    P.op("dve", lambda i=i: nc.vector.memset(stv.t[i][:], 0.0), writes=[stv.r[i]])
                for s in range(NA // 128):
                    ps, r_ps = mmp.next()
                    for kc in range(16):
                        P.op("pe", lambda ps=ps, s=s, kc=kc, wb=wb: nc.tensor.matmul(
                            ps[:, 0:256], lhsT=hTo[:, kc, s * 128:(s + 1) * 128], rhs=wb[:, kc, 256:512],
                            start=(kc == 0), stop=(kc == 15)), reads=[wbr, r_hTo], writes=[r_ps])
                    sv, r_sv = stv.next()
                    tile = W0 // 128 + s
                    P.op("dve", lambda ps=ps, sv=sv, tile=tile: nc.vector.tensor_scalar(
                        out=sv[:, :, 0:128], in0=ps[:, 0:256].rearrange("p (g d) -> p g d", g=NG),
                        scalar1=vtok_sb[:, tile:tile + 1], scalar2=None, op0=ALU.mult),
                        reads=[r_ps, r_const], writes=[r_sv])
                    P.op("pool", lambda sv=sv, tile=tile: nc.gpsimd.tensor_copy(
                        out=sv[:, :, 128:130], in_=vtok_sb[:, tile:tile + 1].unsqueeze(1).to_broadcast([128, NG, 2])),
                        reads=[r_const], writes=[r_sv])
                    P.dma("act", chv[s % 2], lambda sv=sv, s=s: nc.scalar.dma_start(
                        out=vw[s * 128:(s + 1) * 128, :, :], in_=sv[:]), reads=[r_sv], writes=[r_scr])
                wb, wbr = load_w([(C_GN, 48)])
                gst = Pool(P, st, "gst", 2, [128, 48], F32)
                chg = [P.chan("gs0"), P.chan("gs1")]
                for s in range(NB // 128):
                    ps, r_ps = mmp.next()
                    for kc in range(16):
                        P.op("pe", lambda ps=ps, s=s, kc=kc, wb=wb: nc.tensor.matmul(
                            ps[:, 0:48], lhsT=hTo[:, kc, OB + s * 128:OB + (s + 1) * 128], rhs=wb[:, kc, 0:48],
                            start=(kc == 0), stop=(kc == 15)), reads=[wbr, r_hTo], writes=[r_ps])
                    gs, r_gs = gst.next()
                    P.op("act", lambda ps=ps, gs=gs: nc.scalar.activation(out=gs[:], in_=ps[:, 0:48], func=AF.Sigmoid),
                         reads=[r_ps], writes=[r_gs])
                    P.dma("act", chg[s % 2], lambda gs=gs, s=s: nc.scalar.dma_start(
                        out=gates[s * 128:(s + 1) * 128, :], in_=gs[:]), reads=[r_gs], writes=[r_scr])
                for cb in range(8):
                    wb, wbr = load_w([(C_GLU + cb * 256, 256), (C_GLU + D + cb * 256, 256)])
                    for cl in range(2):
                        ch_ = cb * 2 + cl
                        sgm, r_sgm = sgp.next()
                        fm_chunk(wb, wbr, 256 + cl * 128, OB, NA, copy_evac(sgm, r_sgm, OB, func=AF.Sigmoid))
                        sg, r_sg = stA.next()

                        def ev(ps, r_ps, t, n, sg=sg, r_sg=r_sg, sgm=sgm, r_sgm=r_sgm):
                            P.op("dve", lambda: nc.vector.tensor_tensor(out=sg[:, t - OB:t - OB + n], in0=ps[:, 0:n],
                                                                        in1=sgm[:, t - OB:t - OB + n], op=ALU.mult),
                                 reads=[r_ps, r_sgm], writes=[r_sg])
                        fm_chunk(wb, wbr, cl * 128, OB, NA, ev)
                        P.op("dve", lambda sg=sg: nc.vector.tensor_scalar(out=sg[:, 0:OWN0 - B0], in0=sg[:, 0:OWN0 - B0], scalar1=self.hfl[:, 0:1],
                                                                        scalar2=None, op0=ALU.mult), reads=[r_sg, r_const], writes=[r_sg])
                        store(gluT[:, ch_, :], sg, r_sg, NB)
                for mb in range(8):
                    wb, wbr = load_w([(C_GM + mb * 512, 512)])
                    for cl in range(4):
                        sg, r_sg = stA.next()
                        fm_chunk(wb, wbr, cl * 128, OB, NA, copy_evac(sg, r_sg, OB, func=AF.Sigmoid))
                        store(mgT[:, mb * 4 + cl, :], sg, r_sg, NB)
                P.barrier()
            self.r_scr_all = Res("scr_all")
            if upto >= 3:
                self.phase_compress()
            if upto >= 4:
                self.phase_attn(upto)
            if upto >= 5:
                self.phase_mix()
            if upto >= 6:
                self.phase_ffn()
            P.emit()
        return nc

    def norm_T(self, P, st, src, t0, nsub, xpool, chx, junk, r_junk, sspool, rspool, xnpool, hTpool, tpp,
               ident, r_const, sc, r_sc, ada, r_ada, sh_col, xt_in=None):
        nc = self.nc
        if xt_in is None:
            xt, r_xt = xpool.next()
            P.dma("sp", chx, lambda: nc.sync.dma_start(
                out=xt[:, 0:nsub, :], in_=src[t0:t0 + nsub * 128, :].rearrange("(s p) d -> p s d", p=128)), writes=[r_xt])
        else:
            xt, r_xt = xt_in
        ss, r_ss = sspool.next()
        rs, r_rs = rspool.next()
        for s in range(nsub):
            P.op("act", lambda s=s: nc.scalar.activation(out=junk[:], in_=xt[:, s, :], func=AF.Square,
                                                         accum_out=ss[:, s:s + 1]),
                 reads=[r_xt], writes=[r_junk, r_ss])
        P.op("dve", lambda: nc.vector.tensor_scalar(out=rs[:, 0:nsub], in0=ss[:, 0:nsub], scalar1=1.0 / D, scalar2=EPS,
                                                    op0=ALU.mult, op1=ALU.add), reads=[r_ss], writes=[r_rs])
        P.op("act", lambda: nc.scalar.sqrt(out=rs[:, 0:nsub], in_=rs[:, 0:nsub]), reads=[r_rs], writes=[r_rs])
        P.op("dve", lambda: nc.vector.reciprocal(out=rs[:, 0:nsub], in_=rs[:, 0:nsub]), reads=[r_rs], writes=[r_rs])
        xn, r_xn = xnpool.next()
        for s in range(nsub):
            P.op("dve", lambda s=s: nc.vector.tensor_scalar(out=xn[:, s, :], in0=xt[:, s, :], scalar1=rs[:, s:s + 1],
                                                            scalar2=None, op0=ALU.mult),
                 reads=[r_xt, r_rs], writes=[r_xn])
        hT, r_hT = hTpool.next()
        for j in range(16):
            tp, r_tp = tpp.next()
            for s in range(nsub):
                P.op("pe", lambda tp=tp, s=s, j=j: nc.tensor.transpose(
                    out=tp[:, s * 128:(s + 1) * 128], in_=xn[:, s, j * 128:(j + 1) * 128], identity=ident[:]),
                    reads=[r_xn, r_const], writes=[r_tp])
            if j % 2 == 0:
                P.op("act", lambda tp=tp, j=j: nc.scalar.activation(
                    out=hT[:, j, 0:nsub * 128], in_=tp[:, 0:nsub * 128], func=AF.Identity,
                    scale=sc[:, j:j + 1], bias=ada[:, sh_col + j:sh_col + j + 1]),
                    reads=[r_tp, r_sc, r_ada], writes=[r_hT])
            else:
                P.op("dve", lambda tp=tp, j=j: nc.vector.tensor_scalar(
                    out=hT[:, j, 0:nsub * 128], in0=tp[:, 0:nsub * 128], scalar1=sc[:, j:j + 1],
                    scalar2=ada[:, sh_col + j:sh_col + j + 1], op0=ALU.mult, op1=ALU.add),
                    reads=[r_tp, r_sc, r_ada], writes=[r_hT])
        self.last_rs = (rs, r_rs)
        self.last_xt = (xt, r_xt)
        return hT, r_hT


    def phase_compress(self):
        nc, P = self.nc, self.P
        with ExitStack() as st:
            sbt = lambda name, shape, dt: st.enter_context(nc.sbuf_tensor(name, list(shape), dt))
            raw = sbt("c_raw", [128, TV + 16], BF16)
            R = sbt("c_R", [128, 16, NCV + 1], BF16)
            w1f = sbt("c_w1f", [128, 32, 256], F32)
            w1b = sbt("c_w1b", [128, 32, 256], BF16)
            w2f = sbt("c_w2f", [128, 2, 128], F32)
            w2b = sbt("c_w2b", [128, 2, 128], BF16)
            pef = sbt("c_pef", [128, 32], F32)
            peb = sbt("c_peb", [128, 32], BF16)
            bia = sbt("c_bia", [128, 2], F32)
            hid = sbt("c_hid", [128, 2, NCV], BF16)
            kst = sbt("c_kst", [128, NCV], BF16)
            zpad = sbt("c_zpad", [128, NG, 16], BF16)
            r_raw, r_R, r_w1f, r_w1b, r_w2f, r_w2b, r_pe, r_bia, r_hid, r_kst, r_z = [Res() for _ in range(11)]
            vst = Pool(P, st, "c_vst", 2, [128, NG, 130], BF16)
            mmp = Pool(P, st, "c_mm", 3, [128, 512], F32, psum=True)
            bps = st.enter_context(nc.psum_tensor("c_bps", [128, 2], F32))
            r_bps = Res()
            ch = {n: P.chan("c_" + n) for n in ("raw", "w1", "w2", "pe", "k", "v0", "v1", "z")}
            r_out = self.r_scr_all
            P.op("dve", lambda: nc.vector.memset(zpad[:], 0.0), writes=[r_z])
            for i, rt in enumerate((self.kcT_raw, self.vcT_raw)):
                P.dma("act", ch["z"], lambda rt=rt: nc.scalar.dma_start(
                    out=rt[:, :, TV:TV + 16].rearrange("g p t -> p g t"), in_=zpad[:]), reads=[r_z], writes=[r_out], cont=(i > 0))
            P.dma("sp", ch["pe"], lambda: nc.sync.dma_start(out=pef[:], in_=self.i_cmp_pe.rearrange("l d -> d l"),
                                                           allow_slow_non_contiguous=True), writes=[r_pe])
            P.op("dve", lambda: nc.vector.tensor_copy(out=peb[:], in_=pef[:]), reads=[r_pe], writes=[r_pe])
            for i in range(2):
                P.op("dve", lambda i=i: nc.vector.memset(vst.t[i][:], 0.0), writes=[vst.r[i]])
            for kv in range(2):
                P.dma("sp", ch["w1"], lambda kv=kv: nc.sync.dma_start(
                    out=w1f[:], in_=self.i_w1[kv].rearrange("(l d) c -> d l c", d=128)), writes=[r_w1f])
                P.op("pool", lambda: nc.gpsimd.tensor_copy(out=w1b[:], in_=w1f[:]), reads=[r_w1f], writes=[r_w1b])
                P.dma("sp", ch["w2"], lambda kv=kv: nc.sync.dma_start(
                    out=w2f[:], in_=self.i_w2[kv].rearrange("(c p) d -> p c d", p=128)), writes=[r_w2f])
                P.op("dve", lambda: nc.vector.tensor_copy(out=w2b[:], in_=w2f[:]), reads=[r_w2f], writes=[r_w2b])
                for hc in range(2):
                    for l in range(32):
                        P.op("pe", lambda hc=hc, l=l: nc.tensor.matmul(
                            bps[:, hc:hc + 1], lhsT=w1b[:, l, hc * 128:(hc + 1) * 128], rhs=peb[:, l:l + 1],
                            start=(l == 0), stop=(l == 31)), reads=[r_w1b, r_pe], writes=[r_bps])
                P.op("dve", lambda: nc.vector.tensor_copy(out=bia[:], in_=bps[:]), reads=[r_bps], writes=[r_bia])
                src = (self.kcT_raw, self.vcT_raw)[kv]
                for g in range(NG):
                    P.dma("sp", ch["raw"], lambda g=g, src=src: nc.sync.dma_start(out=raw[:], in_=src[g, :, :]),
                          reads=[r_out], writes=[r_raw])
                    P.op("pool", lambda: nc.gpsimd.tensor_copy(
                        out=R[:], in_=raw[:].rearrange("p (m l) -> p l m", l=16)), reads=[r_raw], writes=[r_R])
                    for hc in range(2):
                        for (n0, nn) in ((0, 512), (512, 512), (1024, 128)):
                            ps, r_ps = mmp.next()
                            for l in range(32):
                                P.op("pe", lambda ps=ps, hc=hc, l=l, n0=n0, nn=nn: nc.tensor.matmul(
                                    ps[:, 0:nn], lhsT=w1b[:, l, hc * 128:(hc + 1) * 128],
                                    rhs=R[:, l % 16, (l // 16) + n0:(l // 16) + n0 + nn],
                                    start=(l == 0), stop=(l == 31)), reads=[r_w1b, r_R], writes=[r_ps])
                            P.op("act", lambda ps=ps, hc=hc, n0=n0, nn=nn: nc.scalar.activation(
                                out=hid[:, hc, n0:n0 + nn], in_=ps[:, 0:nn], func=AF.Silu, bias=bia[:, hc:hc + 1]),
                                reads=[r_ps, r_bia], writes=[r_hid])
                    if kv == 0:
                        for (n0, nn) in ((0, 512), (512, 512), (1024, 128)):
                            ps, r_ps = mmp.next()
                            for hc in range(2):
                                P.op("pe", lambda ps=ps, hc=hc, n0=n0, nn=nn: nc.tensor.matmul(
                                    ps[:, 0:nn], lhsT=w2b[:, hc, :], rhs=hid[:, hc, n0:n0 + nn],
                                    start=(hc == 0), stop=(hc == 1)), reads=[r_w2b, r_hid], writes=[r_ps])
                            P.op("dve", lambda ps=ps, n0=n0, nn=nn: nc.vector.tensor_copy(out=kst[:, n0:n0 + nn], in_=ps[:, 0:nn]),
                                 reads=[r_ps], writes=[r_kst])
                        P.dma("act", ch["k"], lambda g=g: nc.scalar.dma_start(out=self.kcT[g, :, :], in_=kst[:]),
                              reads=[r_kst], writes=[r_out])
                    else:
                        for tl in range(NCV // 128):
                            ps, r_ps = mmp.next()
                            for hc in range(2):
                                P.op("pe", lambda ps=ps, hc=hc, tl=tl: nc.tensor.matmul(
                                    ps[:, 0:128], lhsT=hid[:, hc, tl * 128:(tl + 1) * 128], rhs=w2b[:, hc, :],
                                    start=(hc == 0), stop=(hc == 1)), reads=[r_w2b, r_hid], writes=[r_ps])
                            sv, r_sv = vst.next()
                            P.op("dve", lambda ps=ps, sv=sv, tl=tl, g=g: nc.vector.tensor_scalar(
                                out=sv[:, g, 0:128], in0=ps[:, 0:128], scalar1=self.vcmp_sb[:, tl:tl + 1], scalar2=None,
                                op0=ALU.mult), reads=[r_ps, self.r_const], writes=[r_sv])
                            P.op("pool", lambda sv=sv, tl=tl, g=g: nc.gpsimd.tensor_copy(
                                out=sv[:, g, 128:130], in_=self.vcmp_sb[:, tl:tl + 1].to_broadcast([128, 2])), reads=[self.r_const], writes=[r_sv])
                            P.dma("act", ch["v%d" % (tl % 2)], lambda sv=sv, tl=tl, g=g: nc.scalar.dma_start(
                                out=self.vca[tl * 128:(tl + 1) * 128, g, :], in_=sv[:, g, :]), reads=[r_sv], writes=[r_out])
            P.barrier()


    def phase_attn(self, upto):
        nc, P = self.nc, self.P
        with ExitStack() as st:
            sbt = lambda name, shape, dt: st.enter_context(nc.sbuf_tensor(name, list(shape), dt))
            bias_s = sbt("a_bs", [128, NH, NDS], F32)
            bias_c = sbt("a_bc", [128, NH, NDC], F32)
            masks = sbt("a_mk", [128, NMASK, 512], BF16)
            esel = sbt("a_es", [128, 64, 128], BF16)
            r_tab = Res("tables")
            cht = P.chan("a_tab")
            for i, (dst, src) in enumerate(((bias_s, self.i_bias_s), (bias_c, self.i_bias_c), (masks, self.i_masks), (esel, self.i_esel))):
                P.dma("sp", cht, lambda dst=dst, src=src: nc.sync.dma_start(out=dst[:], in_=src[:, :, :]), writes=[r_tab], cont=(i > 0))
            crhs = [[sbt(f"a_cr{g}_{jj}", [128, 130 + NBW], BF16) for jj in range(9)] for g in range(NG)]
            r_crhs = [[Res() for jj in range(9)] for g in range(NG)]
            ch_cr = [P.chan("a_cr0"), P.chan("a_cr1")]
            ovm = sbt("a_ovm", [128, 9, NBW], BF16)
            P.dma("sp", cht, lambda: nc.sync.dma_start(out=ovm[:], in_=self.i_ovm[:, :, :]), writes=[r_tab], cont=True)
            QTt = sbt("a_qt", [128, NH, 512], BF16)
            gat = sbt("a_gat", [128, 4, 48], F32)
            selb = sbt("a_selb", [128, 4, NBW], F32)
            selv = sbt("a_selv", [128, 4, NBW], F32)
            acc = sbt("a_acc", [128, 4, D], F32)
            imp = sbt("a_imp", [128, 4, NG, NBW], F32)
            mneg = sbt("a_mneg", [128, NG, 3, 512], BF16)
            r_qt, r_gat, r_sel, r_acc, r_imp, r_mneg = [Res() for _ in range(6)]
            ch_q = P.chan("a_q")
            kpool = Pool(P, st, "a_k", 3, [128, 2048], BF16)
            vpool = Pool(P, st, "a_v", 3, [128, 16, 130], BF16)
            chk = [P.chan(f"a_k{i}") for i in range(3)]
            chv = [P.chan(f"a_v{i}") for i in range(3)]
            ptp = Pool(P, st, "a_pt", 4, [128, 512], BF16)
            sps = Pool(P, st, "a_S", 2, [128, 512], F32, psum=True)
            ops_ = Pool(P, st, "a_o", 4, [128, 512], F32, psum=True)
            tps = Pool(P, st, "a_tp", 2, [128, 512], BF16, psum=True)
            small = Pool(P, st, "a_sm", 4, [128, 4], F32)
            sc1 = sbt("a_sc1", [128, 384], F32)
            sc2 = sbt("a_sc2", [128, 384], F32)
            m8 = sbt("a_m8", [128, 16], F32)
            mbf = sbt("a_mbf", [128, 384], BF16)
            r_sc1, r_sc2, r_m8, r_mbf = Res(), Res(), Res(), Res()
            P.op("dve", lambda: nc.vector.memset(mbf[:], 0.0), writes=[r_mbf])
            P.op("dve", lambda: nc.vector.memset(sc1[:], 0.0), writes=[r_sc1])
            ch_dbg = P.chan("a_dbg")
            r_out = self.r_scr_all
            kcount = [0]

            def do_tile(ti, q0, nq):
                self._cr_loaded = [0, 0]
                qend = q0 + nq
                nsub = max(1, nq // 128)
                rows = min(128, nq)
                nend = qend // 16
                P.dma("sp", ch_q, lambda q0=q0, nq=nq: nc.sync.dma_start(out=QTt[:, :, 0:nq], in_=self.QT[:, :, q0 - B0:q0 - B0 + nq]),
                      reads=[r_out], writes=[r_qt])
                P.dma("sp", ch_q, lambda q0=q0, nq=nq, rows=rows, nsub=nsub: nc.sync.dma_start(
                    out=gat[0:rows, 0:nsub, :], in_=self.gates[q0 - B0:q0 - B0 + nq, :].rearrange("(s p) c -> p s c", p=rows)),
                    reads=[r_out], writes=[r_gat], cont=True)
                P.dma("sp", ch_q, lambda ti=ti: nc.sync.dma_start(out=selb[:], in_=self.i_selb[ti].rearrange("s p b -> p s b")),
                      writes=[r_sel], cont=True)
                P.dma("sp", ch_q, lambda ti=ti: nc.sync.dma_start(out=selv[:], in_=self.i_selv[ti].rearrange("s p b -> p s b")),
                      writes=[r_sel], cont=True)

                def run_branch(bi, h):
                    g = h // HG
                    W = exp_width(h, nq)
                    if bi == 0:
                        nch = min(n_cmp_chunks(h, nq), nend // 128)
                    elif bi == 1:
                        nch = min(n_slc_chunks(h, nq, 132), qend // 128)
                    else:
                        nch = min(n_slc_chunks(h, nq, 8 if nq == 512 else 5), 8 if nq == 512 else 5)
                    ncol = 130 + NBW if bi == 0 else 130
                    oacc = [ops_.next() for _ in range(nsub)]
                    kt = vt = None
                    stA = {}

                    def stageB(j, pt, r_pt, rhsV, r_rhsV):
                        for s_ in range(nsub):
                            o, r_o = oacc[s_]
                            P.op("pe", lambda o=o, s_=s_, pt=pt, rhsV=rhsV, j=j: nc.tensor.matmul(
                                o[0:rows, 0:ncol], lhsT=pt[:, s_ * 128:s_ * 128 + rows], rhs=rhsV,
                                start=(j == 0), stop=(j == nch - 1)), reads=[r_pt, r_rhsV], writes=[r_o])

                    for j in range(nch):
                        if bi == 0:
                            n0 = nend - 128 * (j + 1)
                            kt, r_kt = kpool.next()
                            kc_ = kcount[0] % 3
                            kcount[0] += 1
                            P.dma("sp", chk[kc_], lambda kt=kt, n0=n0: nc.sync.dma_start(out=kt[:, 0:128], in_=self.kcT[g, :, n0:n0 + 128]),
                                  reads=[r_out], writes=[r_kt])
                            if h % HG == 0 or j >= self._cr_loaded[g]:
                                P.dma("sp", ch_cr[g], lambda n0=n0, j=j: nc.sync.dma_start(out=crhs[g][j][:, 0:130], in_=self.vca[n0:n0 + 128, g, :]),
                                      reads=[r_out], writes=[r_crhs[g][j]])
                                P.op("dve", lambda j=j: nc.vector.tensor_scalar(out=crhs[g][j][:, 130:130 + NBW], in0=ovm[:, j, :],
                                                                               scalar1=crhs[g][j][:, 128:129], scalar2=None, op0=ALU.mult),
                                     reads=[r_tab, r_crhs[g][j]], writes=[r_crhs[g][j]])
                                self._cr_loaded[g] = max(self._cr_loaded[g], j + 1) if h % HG else j + 1
                            klhs, rhsV, r_rhsV = kt[:, 0:128], crhs[g][j][:, 0:ncol], r_crhs[g][j]
                            mk = MASK_IDX.get(("cmp", nq, j))
                            bcol = lambda r, j=j: bias_c[:, h, (nq - 2048 * (j + 1) - r * W) // 64 + ROFF:(nq - 2048 * (j + 1) - r * W) // 64 + ROFF + 1]
                        else:
                            if j % 16 == 0:
                                nsup = min(16, nch - j)
                                lo = qend - 128 * (j + nsup)
                                kt, r_kt = kpool.next()
                                vt, r_vt = vpool.next()
                                kc_ = kcount[0] % 3
                                kcount[0] += 1
                                if bi == 1:
                                    ksrc, vsrc, off = self.kslT, self.vsl, 0
                                else:
                                    ksrc, vsrc, off = self.kwT, self.vw, W0
                                P.dma("sp", chk[kc_], lambda kt=kt, lo=lo, nsup=nsup, ksrc=ksrc, off=off: nc.sync.dma_start(
                                    out=kt[:, 0:128 * nsup], in_=ksrc[g, :, lo - off:lo - off + 128 * nsup]), reads=[r_out], writes=[r_kt])
                                P.dma("sp", chv[kc_], lambda vt=vt, lo=lo, nsup=nsup, vsrc=vsrc, off=off: nc.sync.dma_start(
                                    out=vt[:, 0:nsup, :], in_=vsrc[lo - off:lo - off + 128 * nsup, g, :].rearrange("(s p) d -> p s d", p=128)),
                                    reads=[r_out], writes=[r_vt])
                                sup_n = nsup
                            sl = sup_n - 1 - (j % 16)
                            klhs, rhsV, r_rhsV = kt[:, sl * 128:(sl + 1) * 128], vt[:, sl, :], r_vt
                            mk = MASK_IDX.get(("slc" if bi == 1 else "win", nq, j))
                            bcol = lambda r, j=j: bias_s[:, h, (nq - 128 * (j + 1) - r * W) // 64 + DOFF:(nq - 128 * (j + 1) - r * W) // 64 + DOFF + 1]
                        S, r_S = sps.next()
                        nmm = 1 + (mk is not None) + (bi == 1)
                        cnt = [0]

                        def mm(lhsT, rhs, reads):
                            first, last = cnt[0] == 0, cnt[0] == nmm - 1
                            cnt[0] += 1
                            P.op("pe", lambda S=S: nc.tensor.matmul(S[:, 0:nq], lhsT=lhsT, rhs=rhs, start=first, stop=last),
                                 reads=reads, writes=[r_S])
                        mm(klhs, QTt[:, h, 0:nq], [r_kt, r_qt])
                        if mk is not None:
                            mm(self.ident[:], masks[:, mk, 0:nq], [self.r_const, r_tab])
                        if bi == 1:
                            b0 = NBLKW - 2 * (j + 1)
                            mm(esel[:, (b0 % 128) // 2, :], mneg[:, g, b0 // 128, 0:nq], [r_tab, r_mneg])
                        if self.dbg and "dumpS" in self.dbg and bi == 0 and h == 0 and j == 0:
                            dS = sbt("dbg_S", [128, 512], F32)
                            r_dS = Res()
                            P.op("dve", lambda: nc.vector.tensor_copy(out=dS[:, 0:nq], in_=S[:, 0:nq]), reads=[r_S], writes=[r_dS])
                            P.dma("act", ch_dbg, lambda: nc.scalar.dma_start(out=self.dumpS[:, 0:nq], in_=dS[:, 0:nq]), reads=[r_dS], writes=[r_out])
                            raise StopIteration
                        pt, r_pt = ptp.next()
                        for r in range(nq // W):
                            P.op("act", lambda r=r, bcol=bcol, S=S, pt=pt: nc.scalar.activation(
                                out=pt[:, r * W:(r + 1) * W], in_=S[:, r * W:(r + 1) * W], func=AF.Exp, bias=bcol(r)),
                                reads=[r_S, r_tab], writes=[r_pt])
                        if j > 0:
                            stageB(j - 1, *stA.pop(j - 1))
                        stA[j] = (pt, r_pt, rhsV, r_rhsV)
                    stageB(nch - 1, *stA.pop(nch - 1))
                    for s_ in range(nsub):
                        o, r_o = oacc[s_]
                        sm, r_sm = small.next()
                        P.op("dve", lambda o=o, sm=sm: nc.vector.tensor_scalar(out=sm[0:rows, 0:1], in0=o[0:rows, 128:129], scalar1=1e-30,
                                                                             scalar2=None, op0=ALU.max), reads=[r_o], writes=[r_sm])
                        P.op("dve", lambda sm=sm: nc.vector.reciprocal(out=sm[0:rows, 1:2], in_=sm[0:rows, 0:1]), reads=[r_sm], writes=[r_sm])
                        P.op("dve", lambda sm=sm, s_=s_: nc.vector.tensor_tensor(
                            out=sm[0:rows, 2:3], in0=sm[0:rows, 1:2], in1=gat[0:rows, s_, bi * 16 + h:bi * 16 + h + 1], op=ALU.mult),
                            reads=[r_sm, r_gat], writes=[r_sm])
                        dst = acc[0:rows, s_, h * 128:(h + 1) * 128]
                        if bi == 0:
                            P.op("dve", lambda o=o, sm=sm, dst=dst: nc.vector.tensor_scalar(
                                out=dst, in0=o[0:rows, 0:128], scalar1=sm[0:rows, 2:3], scalar2=None, op0=ALU.mult),
                                reads=[r_o, r_sm], writes=[r_acc])
                            idst = imp[0:rows, s_, g, :]
                            if h % HG == 0:
                                P.op("dve", lambda o=o, sm=sm, idst=idst: nc.vector.tensor_scalar(
                                    out=idst, in0=o[0:rows, 130:130 + NBW], scalar1=sm[0:rows, 1:2], scalar2=None, op0=ALU.mult),
                                    reads=[r_o, r_sm], writes=[r_imp])
                            else:
                                P.op("dve", lambda o=o, sm=sm, idst=idst: nc.vector.scalar_tensor_tensor(
                                    out=idst, in0=o[0:rows, 130:130 + NBW], scalar=sm[0:rows, 1:2], in1=idst, op0=ALU.mult, op1=ALU.add),
                                    reads=[r_o, r_sm, r_imp], writes=[r_imp])
                        else:
                            P.op("dve", lambda o=o, sm=sm, dst=dst: nc.vector.scalar_tensor_tensor(
                                out=dst, in0=o[0:rows, 0:128], scalar=sm[0:rows, 2:3], in1=dst, op0=ALU.mult, op1=ALU.add),
                                reads=[r_o, r_sm, r_acc], writes=[r_acc])

                try:
                    for h in range(NH):
                        run_branch(0, h)
                except StopIteration:
                    return "stop"
                if self.dbg and "impd" in self.dbg:
                    P.dma("act", ch_dbg, lambda ti=ti: nc.scalar.dma_start(out=self.impd[ti].rearrange("s p g b -> p s g b"), in_=imp[:]),
                          reads=[r_imp], writes=[r_out])
                for g in range(NG):
                    tpl = [tps.next() for _ in range(3)]
                    for s_ in range(nsub):
                        P.op("dve", lambda s_=s_, g=g: nc.vector.tensor_tensor(out=sc1[0:rows, 0:NBW], in0=imp[0:rows, s_, g, :],
                                                                               in1=selb[0:rows, s_, :], op=ALU.add),
                             reads=[r_imp, r_sel], writes=[r_sc1])
                        P.op("dve", lambda s_=s_: nc.vector.tensor_tensor(out=sc1[0:rows, 0:NBW], in0=sc1[0:rows, 0:NBW],
                                                                          in1=selv[0:rows, s_, :], op=ALU.mult),
                             reads=[r_sc1, r_sel], writes=[r_sc1])
                        P.op("dve", lambda: nc.vector.max(out=m8[0:rows, 0:8], in_=sc1[0:rows, :]), reads=[r_sc1], writes=[r_m8])
                        P.op("dve", lambda: nc.vector.match_replace(out=sc2[0:rows, :], in_to_replace=m8[0:rows, 0:8],
                                                                    in_values=sc1[0:rows, :], imm_value=-1e30),
                             reads=[r_sc1, r_m8], writes=[r_sc2])
                        P.op("dve", lambda: nc.vector.max(out=m8[0:rows, 8:16], in_=sc2[0:rows, :]), reads=[r_sc2], writes=[r_m8])
                        P.op("dve", lambda: nc.vector.tensor_scalar(out=mbf[0:rows, 0:NBW], in0=sc1[0:rows, 0:NBW], scalar1=m8[0:rows, 15:16],
                                                                    scalar2=None, op0=ALU.is_ge), reads=[r_sc1, r_m8], writes=[r_mbf])
                        for bg in range(3):
                            tp, r_tp = tpl[bg]
                            P.op("pe", lambda tp=tp, bg=bg, s_=s_: nc.tensor.transpose(
                                out=tp[:, s_ * 128:s_ * 128 + rows], in_=mbf[0:rows, bg * 128:(bg + 1) * 128], identity=self.ident[0:rows, 0:rows]),
                                reads=[r_mbf, self.r_const], writes=[r_tp])
                    for bg in range(3):
                        tp, r_tp = tpl[bg]
                        P.op("act", lambda tp=tp, bg=bg, g=g: nc.scalar.activation(out=mneg[:, g, bg, 0:nq], in_=tp[:, 0:nq], func=AF.Identity,
                                                                                  scale=30000.0, bias=-30000.0),
                             reads=[r_tp], writes=[r_mneg])
                for bi in (1, 2):
                    if self.dbg and "branches" in self.dbg and bi not in self.dbg["branches"]:
                        continue
                    for h in range(NH):
                        run_branch(bi, h)
                if True:
                    P.dma("act", ch_dbg, lambda q0=q0, nq=nq, rows=rows, nsub=nsub: nc.scalar.dma_start(
                        out=self.accd[q0 - Q0:q0 - Q0 + nq, :].rearrange("(s p) d -> p s d", p=rows), in_=acc[0:rows, 0:nsub, :]),
                        reads=[r_acc], writes=[r_out])

            for ti, (q0, nq) in enumerate(QTILES):
                if self.dbg and "tiles" in self.dbg and ti not in self.dbg["tiles"]:
                    continue
                if do_tile(ti, q0, nq) == "stop":
                    return
            P.barrier()


    def cast_jobs(self):
        jobs = []
        for name, src in (("wo_b", self.i_wo), ("wpw_b", self.i_wpw), ("wout_b", self.i_wout), ("wup_b", self.i_wup), ("wdn_b", self.i_wdn)):
            dst = self.wb_d[name]
            R, C = src.shape
            sv = src.rearrange("(p a) c -> p (a c)", p=128)
            dv = dst.rearrange("(p a) c -> p (a c)", p=128)
            F = R * C // 128
            for o in range(0, F, 8192):
                jobs.append((sv, dv, o, min(8192, F - o)))
        return jobs

    def cast_setup(self, st):
        nc, P = self.nc, self.P
        self.cj = self.cast_jobs()
        self.cji = 0
        self.cpend = None
        self.c32 = Pool(P, st, "cst32_", 2, [128, 2048], F32)
        self.cbf = Pool(P, st, "cstbf_", 2, [128, 2048], BF16)
        self.cch_i = [P.chan("cji0"), P.chan("cji1")]
        self.cch_o = [P.chan("cjo0"), P.chan("cjo1")]
        self.r_wb = Res("wb_scratch")

    def cast_step(self, n):
        nc, P = self.nc, self.P
        for _ in range(n):
            if self.cji >= len(self.cj):
                break
            sv, dv, o, w = self.cj[self.cji]
            i = self.cji % 2
            self.cji += 1
            t32, r32 = self.c32.next()
            tbf, rbf = self.cbf.next()
            P.dma("sp", self.cch_i[i], lambda t32=t32, sv=sv, o=o, w=w: nc.sync.dma_start(out=t32[:, 0:w], in_=sv[:, o:o + w]), writes=[r32])
            P.op("pool", lambda t32=t32, tbf=tbf, w=w: nc.gpsimd.tensor_copy(out=tbf[:, 0:w], in_=t32[:, 0:w]), reads=[r32], writes=[rbf])
            if self.cpend is not None:
                self.cpend()
            self.cpend = (lambda i=i, tbf=tbf, dv=dv, o=o, w=w, rbf=rbf: P.dma(
                "sp", self.cch_o[i], lambda: nc.sync.dma_start(out=dv[:, o:o + w], in_=tbf[:, 0:w]), reads=[rbf], writes=[self.r_wb]))
        if self.cji >= len(self.cj) and self.cpend is not None:
            self.cpend()
            self.cpend = None

    def phase_mix(self):
        nc, P = self.nc, self.P
        with ExitStack() as st:
            sbt = lambda name, shape, dt: st.enter_context(nc.sbuf_tensor(name, list(shape), dt))
            g1bc = sbt("m_g1bc", [128, D], F32)
            dww = sbt("m_dww", [128, 16, 31], F32)
            cols = sbt("m_cols", [128, 4, 16], F32)
            r_cst = Res()
            chc = P.chan("m_c")
            P.dma("sp", chc, lambda: nc.sync.dma_start(out=g1bc[:], in_=self.ada_d[2 * D:3 * D].partition_broadcast(128)), writes=[r_cst])
            for c_ in range(16):
                P.dma("sp", chc, lambda c_=c_: nc.sync.dma_start(out=dww[:, c_, :], in_=self.i_dww[:, c_ * 128:(c_ + 1) * 128].rearrange("k p -> p k"),
                                                              allow_slow_non_contiguous=True), writes=[r_cst], cont=True)
            for i, src in enumerate((self.i_dwb, self.i_lng, self.i_lnb, self.i_pwb)):
                P.dma("sp", chc, lambda i=i, src=src: nc.sync.dma_start(out=cols[:, i, :], in_=src.rearrange("(c p) -> p c", p=128),
                                                                      allow_slow_non_contiguous=True), writes=[r_cst], cont=True)
            oT = sbt("m_oT", [128, 16, 512], BF16)
            glu = sbt("m_glu", [128, 16, 544], BF16)
            ybf = sbt("m_ybf", [128, 16, 512], BF16)
            uc = sbt("m_uc", [128, 16, 512], BF16)
            mer = sbt("m_mer", [128, 16, 512], BF16)
            r_oT, r_glu, r_ybf, r_uc, r_mer = [Res() for _ in range(5)]
            accp = Pool(P, st, "m_acc", 2, [128, D], F32)
            accb = Pool(P, st, "m_accb", 2, [128, D], BF16)
            wblk = Pool(P, st, "m_w", 2, [128, 16, 512], BF16)
            dgp = Pool(P, st, "m_dg", 2, [128, 31, 128], BF16)
            ysq = Pool(P, st, "m_ysq", 2, [128, 512], BF16)
            mgp = Pool(P, st, "m_mg", 4, [128, 512], BF16)
            tmpf = Pool(P, st, "m_tmp", 3, [128, 512], F32)
            xp = Pool(P, st, "m_x", 3, [128, 512], F32)
            stat = sbt("m_stat", [128, 3, 512], F32)
            r_stat = Res()
            chl = [P.chan(f"m_l{i}") for i in range(4)]
            chw = [P.chan("m_w0"), P.chan("m_w1")]
            chs = [P.chan(f"m_s{i}") for i in range(3)]
            mm = Pool(P, st, "m_mm", 3, [128, 512], F32, psum=True)
            sps = Pool(P, st, "m_sp", 2, [128, 512], F32, psum=True)
            tps = Pool(P, st, "m_tp", 2, [128, 512], BF16, psum=True)
            r_in, r_out = self.r_scr_all, Res("xmid")
            cnt = {"l": 0, "w": 0, "s": 0}

            def load_wblk(wsrc, cb):
                wt, wr = wblk.next()
                ch = chw[cnt["w"] % 2]
                cnt["w"] += 1
                P.dma("sp", ch, lambda: nc.sync.dma_start(out=wt[:], in_=wsrc[:, cb * 512:(cb + 1) * 512].rearrange("(k p) c -> p k c", p=128)),
                      reads=[self.r_wb], writes=[wr])
                return wt, wr

            def mix_tile(ti, q0, nq):
                nsub, rows = max(1, nq // 128), min(128, nq)
                for s_ in range(nsub):
                    at, r_at = accp.next()
                    ab, r_ab = accb.next()
                    ch = chl[cnt["l"] % 4]
                    cnt["l"] += 1
                    P.dma("sp", ch, lambda at=at, s_=s_: nc.sync.dma_start(out=at[0:rows, :], in_=self.accd[q0 - Q0 + s_ * 128:q0 - Q0 + s_ * 128 + rows, :]),
                          reads=[r_in], writes=[r_at])
                    P.op("act", lambda at=at, ab=ab: nc.scalar.copy(out=ab[0:rows, :], in_=at[0:rows, :]), reads=[r_at], writes=[r_ab])
                    for hq in range(4):
                        tp, r_tp = tps.next()
                        for hl in range(4):
                            h = hq * 4 + hl
                            P.op("pe", lambda tp=tp, hl=hl, h=h, ab=ab: nc.tensor.transpose(
                                out=tp[:, hl * 128:hl * 128 + rows], in_=ab[0:rows, h * 128:(h + 1) * 128], identity=self.ident[0:rows, 0:rows]),
                                reads=[r_ab, self.r_const], writes=[r_tp])
                        P.op("dve", lambda tp=tp, hq=hq, s_=s_: nc.vector.tensor_copy(
                            out=oT[:, hq * 4:hq * 4 + 4, s_ * 128:s_ * 128 + rows],
                            in_=tp[:, :].rearrange("p (h q) -> p h q", h=4)[:, :, 0:rows]), reads=[r_tp], writes=[r_oT])
                for cb in range(4):
                    wt, wr = load_wblk(self.wb_d["wo_b"], cb)
                    for cl in range(4):
                        c = cb * 4 + cl
                        ps, r_ps = mm.next()
                        for kc in range(16):
                            P.op("pe", lambda ps=ps, kc=kc, cl=cl, wt=wt: nc.tensor.matmul(
                                ps[:, 0:nq], lhsT=wt[:, kc, cl * 128:(cl + 1) * 128], rhs=oT[:, kc, 0:nq], start=(kc == 0), stop=(kc == 15)),
                                reads=[wr, r_oT], writes=[r_ps])
                        mg, r_mg = mgp.next()
                        ch = chl[cnt["l"] % 4]
                        cnt["l"] += 1
                        P.dma("sp", ch, lambda mg=mg, c=c: nc.sync.dma_start(out=mg[:, 0:nq], in_=self.mgT[:, c, q0 - B0:q0 - B0 + nq]),
                              reads=[r_in], writes=[r_mg])
                        P.op("dve", lambda ps=ps, mg=mg, c=c: nc.vector.tensor_tensor(out=mer[:, c, 0:nq], in0=ps[:, 0:nq], in1=mg[:, 0:nq], op=ALU.mult),
                             reads=[r_ps, r_mg], writes=[r_mer])
                P.dma("sp", chl[cnt["l"] % 4], lambda: nc.sync.dma_start(out=glu[:, :, 0:nq + 32], in_=self.gluT[:, :, q0 - 32 - B0:q0 - B0 + nq]),
                      reads=[r_in], writes=[r_glu])
                cnt["l"] += 1
                s_sum, r_ssum = sps.next()
                s_sq, r_ssq = sps.next()
                for chn in range(16):
                    dg, r_dg = dgp.next()
                    for k in range(31):
                        P.op("dve", lambda dg=dg, k=k, chn=chn: nc.vector.tensor_scalar(
                            out=dg[:, k, :], in0=self.ident[:], scalar1=dww[:, chn, k:k + 1], scalar2=None, op0=ALU.mult),
                            reads=[self.r_const, r_cst], writes=[r_dg])
                    ps, r_ps = mm.next()
                    for k in range(31):
                        P.op("pe", lambda ps=ps, dg=dg, k=k, chn=chn: nc.tensor.matmul(
                            ps[:, 0:nq], lhsT=dg[:, k, :], rhs=glu[:, chn, k + 2:k + 2 + nq], start=(k == 0), stop=(k == 30)),
                            reads=[r_dg, r_glu], writes=[r_ps])
                    P.op("act", lambda ps=ps, chn=chn: nc.scalar.activation(out=ybf[:, chn, 0:nq], in_=ps[:, 0:nq], func=AF.Identity,
                                                                          bias=cols[:, 0, chn:chn + 1]), reads=[r_ps, r_cst], writes=[r_ybf])
                    yq, r_yq = ysq.next()
                    P.op("act", lambda ps=ps, chn=chn, yq=yq: nc.scalar.activation(out=yq[:, 0:nq], in_=ps[:, 0:nq], func=AF.Square,
                                                                                 bias=cols[:, 0, chn:chn + 1]), reads=[r_ps, r_cst], writes=[r_yq])
                    P.op("pe", lambda chn=chn: nc.tensor.matmul(s_sum[:, 0:nq], lhsT=self.ones[:], rhs=ybf[:, chn, 0:nq],
                                                                start=(chn == 0), stop=(chn == 15)), reads=[r_ybf, self.r_const], writes=[r_ssum])
                    P.op("pe", lambda chn=chn, yq=yq: nc.tensor.matmul(s_sq[:, 0:nq], lhsT=self.ones[:], rhs=yq[:, 0:nq],
                                                                       start=(chn == 0), stop=(chn == 15)), reads=[r_yq, self.r_const], writes=[r_ssq])
                mean, rstd, msq = stat[:, 0, 0:nq], stat[:, 1, 0:nq], stat[:, 2, 0:nq]
                P.op("dve", lambda: nc.vector.tensor_scalar(out=mean, in0=s_sum[:, 0:nq], scalar1=1.0 / D, scalar2=None, op0=ALU.mult),
                     reads=[r_ssum], writes=[r_stat])
                P.op("dve", lambda: nc.vector.tensor_tensor(out=msq, in0=mean, in1=mean, op=ALU.mult), reads=[r_stat], writes=[r_stat])
                P.op("dve", lambda: nc.vector.scalar_tensor_tensor(out=rstd, in0=s_sq[:, 0:nq], scalar=1.0 / D, in1=msq, op0=ALU.mult, op1=ALU.subtract),
                     reads=[r_ssq, r_stat], writes=[r_stat])
                P.op("dve", lambda: nc.vector.tensor_scalar(out=rstd, in0=rstd, scalar1=EPS, scalar2=None, op0=ALU.add), reads=[r_stat], writes=[r_stat])
                P.op("act", lambda: nc.scalar.sqrt(out=rstd, in_=rstd), reads=[r_stat], writes=[r_stat])
                P.op("dve", lambda: nc.vector.reciprocal(out=rstd, in_=rstd), reads=[r_stat], writes=[r_stat])
                for chn in range(16):
                    tf, r_tf = tmpf.next()
                    P.op("dve", lambda tf=tf, chn=chn: nc.vector.tensor_tensor(out=tf[:, 0:nq], in0=ybf[:, chn, 0:nq], in1=mean, op=ALU.subtract),
                         reads=[r_ybf, r_stat], writes=[r_tf])
                    P.op("dve", lambda tf=tf: nc.vector.tensor_tensor(out=tf[:, 0:nq], in0=tf[:, 0:nq], in1=rstd, op=ALU.mult),
                         reads=[r_tf, r_stat], writes=[r_tf])
                    P.op("act", lambda tf=tf, chn=chn: nc.scalar.activation(out=uc[:, chn, 0:nq], in_=tf[:, 0:nq], func=AF.Silu,
                                                                          scale=cols[:, 1, chn:chn + 1], bias=cols[:, 2, chn:chn + 1]),
                         reads=[r_tf, r_cst], writes=[r_uc])
                for cb in range(4):
                    wt, wr = load_wblk(self.wb_d["wpw_b"], cb)
                    for cl in range(4):
                        c = cb * 4 + cl
                        ps, r_ps = mm.next()
                        for kc in range(16):
                            P.op("pe", lambda ps=ps, kc=kc, cl=cl, wt=wt: nc.tensor.matmul(
                                ps[:, 0:nq], lhsT=wt[:, kc, cl * 128:(cl + 1) * 128], rhs=uc[:, kc, 0:nq], start=(kc == 0), stop=(kc == 15)),
                                reads=[wr, r_uc], writes=[r_ps])
                        mg, r_mg = mgp.next()
                        ch = chl[cnt["l"] % 4]
                        cnt["l"] += 1
                        P.dma("sp", ch, lambda mg=mg, c=c: nc.sync.dma_start(out=mg[:, 0:nq], in_=self.mgT[:, 16 + c, q0 - B0:q0 - B0 + nq]),
                              reads=[r_in], writes=[r_mg])
                        tf, r_tf = tmpf.next()
                        P.op("dve", lambda ps=ps, mg=mg, c=c, tf=tf: nc.vector.scalar_tensor_tensor(
                            out=tf[:, 0:nq], in0=ps[:, 0:nq], scalar=cols[:, 3, c:c + 1], in1=mg[:, 0:nq], op0=ALU.add, op1=ALU.mult),
                            reads=[r_ps, r_mg, r_cst], writes=[r_tf])
                        P.op("dve", lambda c=c, tf=tf: nc.vector.tensor_tensor(out=mer[:, c, 0:nq], in0=mer[:, c, 0:nq], in1=tf[:, 0:nq], op=ALU.add),
                             reads=[r_tf, r_mer], writes=[r_mer])
                for cb in range(4):
                    wt, wr = load_wblk(self.wb_d["wout_b"], cb)
                    for s_ in range(nsub):
                        ps, r_ps = mm.next()
                        for kc in range(16):
                            P.op("pe", lambda ps=ps, kc=kc, s_=s_, wt=wt: nc.tensor.matmul(
                                ps[0:rows, :], lhsT=mer[:, kc, s_ * 128:s_ * 128 + rows], rhs=wt[:, kc, :], start=(kc == 0), stop=(kc == 15)),
                                reads=[wr, r_mer], writes=[r_ps])
                        xt, r_xt = xp.next()
                        i3 = cnt["s"] % 3
                        cnt["s"] += 1
                        r0 = q0 + s_ * 128
                        P.dma("sp", chs[i3], lambda xt=xt, r0=r0, cb=cb: nc.sync.dma_start(out=xt[0:rows, :], in_=self.xv[r0:r0 + rows, cb * 512:(cb + 1) * 512]),
                              writes=[r_xt])
                        tf, r_tf = tmpf.next()
                        P.op("dve", lambda ps=ps, tf=tf, cb=cb: nc.vector.tensor_tensor(out=tf[0:rows, :], in0=ps[0:rows, :], in1=g1bc[0:rows, cb * 512:(cb + 1) * 512],
                                                                                     op=ALU.mult), reads=[r_ps, r_cst], writes=[r_tf])
                        P.op("dve", lambda xt=xt, tf=tf: nc.vector.tensor_tensor(out=xt[0:rows, :], in0=xt[0:rows, :], in1=tf[0:rows, :], op=ALU.add),
                             reads=[r_tf, r_xt], writes=[r_xt])
                        P.dma("act", chs[i3], lambda xt=xt, r0=r0, cb=cb: nc.scalar.dma_start(
                            out=self.xmid[r0 - Q0:r0 - Q0 + rows, cb * 512:(cb + 1) * 512], in_=xt[0:rows, :]), reads=[r_xt], writes=[r_out])

            for ti, (q0, nq) in enumerate(QTILES):
                if self.dbg and "tiles" in self.dbg and ti not in self.dbg["tiles"]:
                    continue
                mix_tile(ti, q0, nq)
            P.barrier()


    def phase_ffn(self):
        nc, P = self.nc, self.P
        with ExitStack() as st:
            sbt = lambda name, shape, dt: st.enter_context(nc.sbuf_tensor(name, list(shape), dt))
            g2bc = sbt("f_g2bc", [128, D], F32)
            fgbc = sbt("f_fgbc", [128, D], F32)
            n2g = sbt("f_n2g", [128, 16], F32)
            s2 = sbt("f_s2", [128, 16], F32)
            fdw = sbt("f_fdw", [128, 3, 88], F32)
            fdb = sbt("f_fdb", [128, 88], F32)
            hfl = sbt("f_hfl", [128, 1], F32)
            r_cst, r_s2 = Res(), Res()
            chc = P.chan("f_c")
            P.dma("sp", chc, lambda: nc.sync.dma_start(out=g2bc[:], in_=self.ada_d[5 * D:6 * D].partition_broadcast(128)), writes=[r_cst])
            P.dma("sp", chc, lambda: nc.sync.dma_start(out=fgbc[:], in_=self.i_fg.partition_broadcast(128)), writes=[r_cst], cont=True)
            P.dma("sp", chc, lambda: nc.sync.dma_start(out=n2g[:], in_=self.i_n2g.rearrange("(c p) -> p c", p=128), allow_slow_non_contiguous=True),
                  writes=[r_cst], cont=True)
            for k in range(3):
                P.dma("sp", chc, lambda k=k: nc.sync.dma_start(out=fdw[:, k, :], in_=self.i_fdw[k, :].rearrange("(c p) -> p c", p=128),
                                                            allow_slow_non_contiguous=True), writes=[r_cst], cont=True)
            P.dma("sp", chc, lambda: nc.sync.dma_start(out=fdb[:], in_=self.i_fdb.rearrange("(c p) -> p c", p=128), allow_slow_non_contiguous=True),
                  writes=[r_cst], cont=True)
            P.dma("sp", chc, lambda: nc.sync.dma_start(out=hfl[:], in_=self.i_hflag[:, :]), writes=[r_cst], cont=True)
            P.op("dve", lambda: nc.vector.scalar_tensor_tensor(out=s2[:], in0=self.ada[:, 64:80], scalar=1.0, in1=n2g[:], op0=ALU.add, op1=ALU.mult),
                 reads=[self.r_ada, r_cst], writes=[r_s2])
            xpool = Pool(P, st, "f_x", 1, [128, 4, D], F32)
            chx = P.chan("f_x")
            junk = sbt("f_junk", [128, D], BF16)
            r_junk = Res()
            sspool = Pool(P, st, "f_ss", 2, [128, 4], F32)
            rspool = Pool(P, st, "f_rs", 2, [128, 4], F32)
            xnpool = Pool(P, st, "f_xn", 1, [128, 4, D], BF16)
            h2T = sbt("f_h2T", [128, 16, 514], BF16)
            r_h2T = Res()
            tpp = Pool(P, st, "f_tp", 2, [128, 512], BF16, psum=True)
            up = Pool(P, st, "f_up", 2, [128, 1024], F32, psum=True)
            dn = Pool(P, st, "f_dn", 2, [128, 512], F32, psum=True)
            wup = Pool(P, st, "f_wu", 2, [128, 16, 256], BF16)
            wdn = Pool(P, st, "f_wd", 2, [128, 44, 256], BF16)
            chwu = [P.chan("f_wu0"), P.chan("f_wu1")]
            chwd = [P.chan("f_wd0"), P.chan("f_wd1")]
            z = sbt("f_z", [128, 44, 512], BF16)
            r_z = Res()
            hh, r_hh = z[:, 0:16, 0:128], r_z
            Tp = Pool(P, st, "f_T", 3, [128, 512], F32)
            sgp = Pool(P, st, "f_sg", 1, [128, 512], BF16)
            tmp = Pool(P, st, "f_tmp", 1, [128, 256], F32)
            cho = [P.chan("f_o0"), P.chan("f_o1")]
            r_in = Res()
            wupb, wdnb = self.wb_d["wup_b"], self.wb_d["wdn_b"]
            cnt = {"u": 0, "d": 0, "o": 0}

            class HP:
                def __init__(s_, ap, res):
                    s_.ap, s_.res = ap, res

                def next(s_):
                    return s_.ap, s_.res

            self.norm_T(P, st, self.xmid, 0, 1, xpool, chx, junk, r_junk, sspool, rspool, xnpool, HP(hh, r_hh), tpp,
                        self.ident, self.r_const, s2, r_s2, self.ada, self.r_ada, 48)
            P.op("dve", lambda: nc.vector.tensor_scalar(out=h2T[:, :, 0:2], in0=hh[:, :, 62:64], scalar1=hfl[:, 0:1], scalar2=None, op0=ALU.mult),
                 reads=[r_hh, r_cst], writes=[r_h2T])

            def window(w):
                row0 = HALO + 512 * w
                if w > 0:
                    P.op("pool", lambda: nc.gpsimd.tensor_copy(out=h2T[:, :, 0:2], in_=h2T[:, :, 512:514]), reads=[r_h2T], writes=[r_h2T])
                self.norm_T(P, st, self.xmid, row0, 4, xpool, chx, junk, r_junk, sspool, rspool, xnpool, HP(h2T[:, :, 2:514], r_h2T), tpp,
                            self.ident, self.r_const, s2, r_s2, self.ada, self.r_ada, 48)
                xt, r_xt = self.last_xt
                for pb in range(44):
                    wt, wr = wup.next()
                    ch = chwu[cnt["u"] % 2]
                    cnt["u"] += 1
                    P.dma("sp", ch, lambda wt=wt, pb=pb: nc.sync.dma_start(
                        out=wt[:, :, 0:128], in_=wupb[:, pb * 128:(pb + 1) * 128].rearrange("(k p) c -> p k c", p=128)), reads=[self.r_wb], writes=[wr])
                    P.dma("sp", ch, lambda wt=wt, pb=pb: nc.sync.dma_start(
                        out=wt[:, :, 128:256], in_=wupb[:, FF + pb * 128:FF + (pb + 1) * 128].rearrange("(k p) c -> p k c", p=128)),
                        reads=[self.r_wb], writes=[wr], cont=True)
                    for cl in range(1):
                        c = pb
                        Ts = []
                        for half in range(2):
                            cc = c + 44 * half
                            woff = half * 128
                            ps, r_ps = up.next()
                            for kc in range(16):
                                P.op("pe", lambda ps=ps, kc=kc, woff=woff, wt=wt: nc.tensor.matmul(
                                    ps[:, 512:1024], lhsT=wt[:, kc, woff:woff + 128], rhs=h2T[:, kc, 2:514], start=(kc == 0), stop=(kc == 15)),
                                    reads=[wr, r_h2T], writes=[r_ps])
                            for kc in range(16):
                                P.op("pe", lambda ps=ps, kc=kc, woff=woff, wt=wt: nc.tensor.matmul(
                                    ps[:, 510:512], lhsT=wt[:, kc, woff:woff + 128], rhs=h2T[:, kc, 0:2], start=(kc == 0), stop=(kc == 15)),
                                    reads=[wr, r_h2T], writes=[r_ps])
                            T_, r_T = Tp.next()
                            P.op("act", lambda ps=ps, T_=T_, cc=cc: nc.scalar.activation(out=T_[:], in_=ps[:, 512:1024], func=AF.Identity,
                                                                                        scale=fdw[:, 2, cc:cc + 1], bias=fdb[:, cc:cc + 1]),
                                 reads=[r_ps, r_cst], writes=[r_T])
                            P.op("dve", lambda ps=ps, T_=T_, cc=cc: nc.vector.scalar_tensor_tensor(
                                out=T_[:], in0=ps[:, 511:1023], scalar=fdw[:, 1, cc:cc + 1], in1=T_[:], op0=ALU.mult, op1=ALU.add),
                                reads=[r_ps, r_T, r_cst], writes=[r_T])
                            P.op("dve", lambda ps=ps, T_=T_, cc=cc: nc.vector.scalar_tensor_tensor(
                                out=T_[:], in0=ps[:, 510:1022], scalar=fdw[:, 0, cc:cc + 1], in1=T_[:], op0=ALU.mult, op1=ALU.add),
                                reads=[r_ps, r_T, r_cst], writes=[r_T])
                            Ts.append((T_, r_T))
                        (Ta, r_Ta), (Tg, r_Tg) = Ts
                        sg, r_sg = sgp.next()
                        P.op("act", lambda Tg=Tg, sg=sg: nc.scalar.activation(out=sg[:], in_=Tg[:], func=AF.Silu), reads=[r_Tg], writes=[r_sg])
                        P.op("dve", lambda Ta=Ta, sg=sg, c=c: nc.vector.tensor_tensor(out=z[:, c, :], in0=Ta[:], in1=sg[:], op=ALU.mult),
                             reads=[r_Ta, r_sg], writes=[r_z])
                for cb in range(8):
                    wt, wr = wdn.next()
                    ch = chwd[cnt["d"] % 2]
                    cnt["d"] += 1
                    P.dma("sp", ch, lambda wt=wt, cb=cb: nc.sync.dma_start(
                        out=wt[:], in_=wdnb[:, cb * 256:(cb + 1) * 256].rearrange("(k p) c -> p k c", p=128)), reads=[self.r_wb], writes=[wr])
                    for s_ in range(4):
                        ps, r_ps = dn.next()
                        for kc in range(44):
                            P.op("pe", lambda ps=ps, kc=kc, s_=s_, wt=wt: nc.tensor.matmul(
                                ps[:, 0:256], lhsT=z[:, kc, s_ * 128:(s_ + 1) * 128], rhs=wt[:, kc, :], start=(kc == 0), stop=(kc == 43)),
                                reads=[wr, r_z], writes=[r_ps])
                        tf, r_tf = tmp.next()
                        P.op("dve", lambda ps=ps, tf=tf, cb=cb: nc.vector.tensor_tensor(out=tf[:], in0=ps[:, 0:256], in1=g2bc[:, cb * 256:(cb + 1) * 256], op=ALU.mult),
                             reads=[r_ps, r_cst], writes=[r_tf])
                        P.op("dve", lambda tf=tf, s_=s_, cb=cb, xt=xt: nc.vector.tensor_tensor(
                            out=xt[:, s_, cb * 256:(cb + 1) * 256], in0=xt[:, s_, cb * 256:(cb + 1) * 256], in1=tf[:], op=ALU.add),
                            reads=[r_tf, r_xt], writes=[r_xt])
                ss, r_ss = sspool.next()
                rs, r_rs = rspool.next()
                for s_ in range(4):
                    P.op("act", lambda s_=s_, xt=xt, ss=ss: nc.scalar.activation(out=junk[:], in_=xt[:, s_, :], func=AF.Square, accum_out=ss[:, s_:s_ + 1]),
                         reads=[r_xt], writes=[r_junk, r_ss])
                P.op("dve", lambda ss=ss, rs=rs: nc.vector.tensor_scalar(out=rs[:], in0=ss[:], scalar1=1.0 / D, scalar2=EPS, op0=ALU.mult, op1=ALU.add),
                     reads=[r_ss], writes=[r_rs])
                P.op("act", lambda rs=rs: nc.scalar.sqrt(out=rs[:], in_=rs[:]), reads=[r_rs], writes=[r_rs])
                P.op("dve", lambda rs=rs: nc.vector.reciprocal(out=rs[:], in_=rs[:]), reads=[r_rs], writes=[r_rs])
                for s_ in range(4):
                    P.op("dve", lambda s_=s_, xt=xt, rs=rs: nc.vector.scalar_tensor_tensor(
                        out=xt[:, s_, :], in0=xt[:, s_, :], scalar=rs[:, s_:s_ + 1], in1=fgbc[:], op0=ALU.mult, op1=ALU.mult),
                        reads=[r_xt, r_rs, r_cst], writes=[r_xt])
                    i2 = cnt["o"] % 2
                    cnt["o"] += 1
                    t0 = 512 * w + 128 * s_
                    P.dma("act", cho[i2], lambda s_=s_, xt=xt, t0=t0: nc.scalar.dma_start(out=self.out[t0:t0 + 128, :], in_=xt[:, s_, :]), reads=[r_xt])

            for w in range(4):
                if self.dbg and "wins" in self.dbg and w not in self.dbg["wins"]:
                    continue
                window(w)
            P.barrier()

_STATIC = {}


def static_tables():
    if _STATIC:
        return _STATIC
    sl = np.array(SLOPES, np.float64)
    i = np.arange(128, dtype=np.float64)
    bs = sl[None, :, None] * (i[:, None, None] + 64.0 * (np.arange(NDS)[None, None, :] - DOFF))
    bc = sl[None, :, None] * (16.0 * i[:, None, None] + 31.0 + 64.0 * (np.arange(NDC)[None, None, :] - ROFF))
    masks = np.zeros((128, NMASK, 512), np.float32)
    for n, v in enumerate(MASK_LIST):
        masks[:, n, :v.shape[1]] = np.where(v, 0.0, -30000.0)
    E = np.zeros((128, 64, 128), np.float32)
    for u in range(64):
        for k in range(128):
            E[2 * u + k // 64, u, k] = 1.0
    Ov = np.zeros((128, 9, NBW), np.float32)
    for jj in range(9):
        for ii in range(128):
            for b in range(NBLKW):
                dlt = ii - 128 * (jj + 1) - 4 * b + 1056
                if -1 <= dlt <= 3:
                    Ov[ii, jj, b] = 1.0
    _STATIC.update({"ident": np.eye(128, dtype=np.float32).astype(NPBF),
                    "bias_s": bs.astype(np.float32), "bias_c": bc.astype(np.float32),
                    "masks": masks.astype(NPBF), "esel": E.astype(NPBF), "ovm": Ov.astype(NPBF),
                    "ones_bf": np.ones((128, 128), np.float32).astype(NPBF)})
    return _STATIC


def host_tables(c):
    t_start = c * TOWN
    real = np.arange(TV) - OWN0 + t_start
    vt = (real >= 0).astype(np.float32)
    nv = np.arange(NCV)
    rn = nv - (OWN0 - t_start) // 16
    vc = ((rn >= 0) & (rn <= 1022) & (nv <= 1150)).astype(np.float32)
    selb = np.zeros((5, 4, 128, NBW), np.float32)
    selv = np.zeros((5, 4, 128, NBW), np.float32)
    b = np.arange(NBLKW)
    for ti, (q0, nq) in enumerate(QTILES):
        qend = q0 + nq
        jv = b + qend // 64 - NBLKW
        realb = jv - (OWN0 - t_start) // 64
        for s_ in range(max(1, nq // 128)):
            rows = min(128, nq)
            tq = q0 + 128 * s_ + np.arange(rows)
            cur = tq // 64
            valid = (realb[None, :] >= 0) & (jv[None, :] <= cur[:, None])
            forced = (realb[None, :] == 0) | (jv[None, :] == cur[:, None]) | (jv[None, :] == cur[:, None] - 1)
            selv[ti, s_, :rows, :NBLKW] = valid
            selb[ti, s_, :rows, :NBLKW] = 1.0 + 1e6 * forced
    d = {"vtok": np.ascontiguousarray(vt.reshape(TV // 128, 128).T),
         "vcmp": np.ascontiguousarray(vc.reshape(NCV // 128, 128).T),
         "selb": selb, "selv": selv,
         "hflag": np.full((128, 1), 1.0 if c > 0 else 0.0, np.float32)}
    d.update(static_tables())
    return d


def make_inputs(inputs, c):
    x = np.asarray(inputs["x"], np.float32)[0]
    t_start = c * TOWN
    xv = np.zeros((TV, D), np.float32)
    lo = OWN0 - t_start
    xv[lo:] = x[:t_start + TOWN]
    m = {"xv": xv, "c": np.asarray(inputs["c"], np.float32),
         "w_ada": np.asarray(inputs["w_ada"], np.float32)[0], "b_ada": np.asarray(inputs["b_ada"], np.float32)[0],
         "norm1_g": np.asarray(inputs["norm1_g"], np.float32)[0], "w_in": np.asarray(inputs["w_in"], np.float32)[0]}
    for n in ("cmp_pe", "w_kc1", "w_kc2", "w_vc1", "w_vc2", "w_o_nsa", "conv_dw_w", "conv_dw_b", "conv_ln_g", "conv_ln_b",
              "conv_pw_w", "conv_pw_b", "w_out", "norm2_g", "ffn_w_up", "ffn_dw_w", "ffn_dw_b", "ffn_w_down"):
        m[n] = np.asarray(inputs[n], np.float32)[0]
    m["final_g"] = np.asarray(inputs["final_g"], np.float32)
    m.update(host_tables(c))
    return m


def kernel(**inputs):
    k = K()
    nc = k.build()
    in_maps = [{n: v for n, v in make_inputs(inputs, c).items() if n in k.ins} for c in range(NCORE)]
    res = run_bass_kernel_spmd(nc, in_maps, core_ids=list(range(NCORE)))
    outs = [np.asarray(res.results[c]["out"], np.float32) for c in range(NCORE)]
    return np.concatenate(outs, axis=0)[None]
```

```python
from contextlib import ExitStack
import numpy as np
import ml_dtypes
import concourse.bass as bass
import concourse.mybir as mybir
from concourse.bass_utils import run_bass_kernel_spmd

F32 = mybir.dt.float32
BF16 = mybir.dt.bfloat16
I32 = mybir.dt.int32
AF = mybir.ActivationFunctionType
ALU = mybir.AluOpType
NPBF = ml_dtypes.bfloat16

D = 2048
T = 16384
NCORE = 8
TOWN = T // NCORE
NH, NG, HG, DK = 16, 2, 8, 128
EPS = 1e-6
FF = 5632
WIN = 512
OWN0 = 16384
TV = OWN0 + TOWN
HALO = 64
Q0 = OWN0 - HALO
NQ = TOWN + HALO
W0 = OWN0 - 640
NA = TV - W0
B0 = OWN0 - 128
NB = TV - B0
NCV = TV // 16
C_Q, C_KC, C_VC, C_KS, C_VS, C_KW, C_VW, C_GN, C_GLU, C_GM = 0, 2048, 2304, 2560, 2816, 3072, 3328, 3584, 3632, 7728
INW = 11824
EPOCH = 12000
SLOPES = [2.0 ** (-8.0 * (h + 1) / 16) for h in range(NH)]
NBLKW = 264
NBW = 272
DOFF, NDS = 272, 280
ROFF, NDC = 296, 280
QTILES = [(Q0, 64)] + [(OWN0 + 512 * i, 512) for i in range(4)]
SKIP = 100.0


def exp_width(h, nq):
    w = 64
    while w * 2 <= min(512, nq) and SLOPES[h] * (w * 2) <= 64.0:
        w *= 2
    return min(w, nq)


def n_slc_chunks(h, nq, maxc):
    return int(min(maxc, np.floor((SKIP / SLOPES[h] + nq) / 128) + 1))


def n_cmp_chunks(h, nq):
    return int(min(9, np.floor((SKIP / SLOPES[h] + nq + 15) / 2048) + 1))


def chunk_valid(kind, nq, j):
    i = np.arange(128)[:, None]
    q = np.arange(nq)[None, :]
    if kind == "cmp":
        kp = 16 * i + 31 + nq - 2048 * (j + 1)
        v = kp <= q
    else:
        kp = i + nq - 128 * (j + 1)
        dist = q - kp
        v = dist >= 0
        if kind == "win":
            v = v & (dist < WIN)
    return None if v.all() else v


def mask_index():
    idx = {}
    for nq in (64, 512):
        nwin = 8 if nq == 512 else 5
        for kind, nj in (("slc", 4), ("win", nwin), ("cmp", 1)):
            for j in range(nj):
                v = chunk_valid(kind, nq, j)
                if v is None:
                    continue
                idx[(kind, nq, j)] = v
    uniq, out = [], {}
    for key, v in idx.items():
        for n, u in enumerate(uniq):
            if u.shape == v.shape and (u == v).all():
                out[key] = n
                break
        else:
            uniq.append(v)
            out[key] = len(uniq) - 1
    return out, uniq


MASK_IDX, MASK_LIST = mask_index()
NMASK = len(MASK_LIST)


class Res:
    __slots__ = ("name", "last_w", "readers")

    def __init__(self, name=""):
        self.name = name
        self.last_w = None
        self.readers = []


class Op:
    __slots__ = ("eng", "fn", "deps", "seq", "sig", "is_dma", "chan", "needs_sig", "pos")


class Chan:
    def __init__(self, prog, name):
        self.sem = prog.new_sem("ch_" + name)
        self.count = 0
        self.last_op = None
        self.group = []


class Prog:
    ENGS = ("pe", "act", "dve", "pool", "sp")

    def __init__(self, nc, stack):
        self.nc = nc
        self.stack = stack
        self.ops = {e: [] for e in self.ENGS}
        self.seq = 0
        self.nsem = 0
        self.eng_sems = {e: [] for e in self.ENGS}
        self.chans = []
        self.uid = 0

    def new_sem(self, name):
        self.nsem += 1
        return self.stack.enter_context(self.nc.semaphore(name))

    def chan(self, name):
        c = Chan(self, name)
        self.chans.append(c)
        return c

    def _mk(self, eng, fn, reads, writes):
        op = Op()
        op.eng, op.fn, op.seq = eng, fn, self.seq
        self.seq += 1
        op.is_dma, op.chan, op.needs_sig, op.sig = False, None, False, None
        deps = []
        for r in reads:
            if r.last_w is not None:
                deps.append(r.last_w)
        for w in writes:
            if w.last_w is not None:
                deps.append(w.last_w)
            deps.extend(w.readers)
        for r in reads:
            if not getattr(op, "is_dma", False) and eng in ("pe", "act", "dve"):
                r.readers = [x for x in r.readers if x.eng != eng or x.is_dma]
            r.readers.append(op)
        for w in writes:
            w.last_w = op
            w.readers = []
        seen, dd = set(), []
        for d in deps:
            if id(d) in seen or d is op:
                continue
            seen.add(id(d))
            if eng == "pe" and d.eng == "pe" and not d.is_dma:
                continue
            dd.append(d)
        op.deps = dd
        self.ops[eng].append(op)
        return op

    def op(self, eng, fn, reads=(), writes=()):
        return self._mk(eng, fn, list(reads), list(writes))

    def dma(self, queue, chan, fn, reads=(), writes=(), cont=False):
        op = self._mk(queue, fn, list(reads), list(writes))
        op.is_dma, op.chan = True, chan
        if not cont:
            if chan.last_op is not None and chan.last_op not in op.deps:
                op.deps.append(chan.last_op)
            chan.group = []
        op.deps = [d for d in op.deps if d not in chan.group]
        chan.count += 16
        chan.group.append(op)
        for o in chan.group:
            o.sig = (chan.sem, chan.count)
        op.needs_sig = True
        chan.last_op = op
        return op

    def barrier(self):
        lasts = []
        for e in self.ENGS:
            for op in reversed(self.ops[e]):
                if not op.is_dma:
                    lasts.append(op)
                    break
        for c in self.chans:
            if c.last_op is not None:
                lasts.append(c.last_op)
        nc = self.nc
        eo = {"pe": nc.tensor, "act": nc.scalar, "dve": nc.vector, "pool": nc.gpsimd, "sp": nc.sync}
        for e in self.ENGS:
            op = self._mk(e, (lambda e=e: eo[e].nop()), [], [])
            op.deps = [d for d in lasts if not (d.eng == e and not d.is_dma)]

    def emit(self):
        nc = self.nc
        for e in self.ENGS:
            for op in self.ops[e]:
                for d in op.deps:
                    if not d.is_dma:
                        d.needs_sig = True
        for e in self.ENGS:
            cnt, sem = 0, None
            for op in self.ops[e]:
                if op.is_dma or not op.needs_sig:
                    continue
                if sem is None or cnt >= EPOCH:
                    sem = self.new_sem(f"e_{e}_{len(self.eng_sems[e])}")
                    self.eng_sems[e].append(sem)
                    cnt = 0
                cnt += 1
                op.sig = (sem, cnt)
        with nc.Block() as block:
            for e in self.ENGS:
                ops = self.ops[e]
                if not ops:
                    continue
                deco = {"pe": block.tensor, "act": block.scalar, "dve": block.vector,
                        "pool": block.gpsimd, "sp": block.sync}[e]

                def body(engobj, ops=ops):
                    known = {}
                    for op in ops:
                        for d in op.deps:
                            sem, val = d.sig
                            if known.get(id(sem), 0) >= val:
                                continue
                            engobj.wait_ge(sem, val)
                            known[id(sem)] = val
                        ins = op.fn()
                        if op.needs_sig:
                            ins.then_inc(op.sig[0], 16 if op.is_dma else 1)
                    last = {}
                    for op in ops:
                        if op.is_dma:
                            last[id(op.chan)] = op
                    for op in last.values():
                        sem, val = op.sig
                        if known.get(id(sem), 0) < val:
                            engobj.wait_ge(sem, val)
                            known[id(sem)] = val
                deco(body)


class Pool:
    def __init__(self, P, st, name, n, shape, dt, psum=False):
        self.t, self.r = [], []
        for i in range(n):
            if psum:
                self.t.append(st.enter_context(P.nc.psum_tensor(f"{name}{i}", list(shape), dt)))
            else:
                self.t.append(st.enter_context(P.nc.sbuf_tensor(f"{name}{i}", list(shape), dt)))
            self.r.append(Res(f"{name}{i}"))
        self.i = 0
        self.n = n

    def next(self):
        k = self.i % self.n
        self.i += 1
        return self.t[k], self.r[k]


class K:
    def __init__(self, dbg=None):
        self.dbg = dbg
        nc = self.nc = bass.Bass("TRN2", target_bir_lowering=False)
        self.ins = {}
        self.outs = {}

    def din(self, name, shape, dt=F32):
        t = self.nc.dram_tensor(name, list(shape), dt, kind="ExternalInput").ap()
        self.ins[name] = t
        return t

    def dscr(self, name, shape, dt):
        kind = "ExternalOutput" if (self.dbg and name in self.dbg) else "Internal"
        t = self.nc.dram_tensor(name, list(shape), dt, kind=kind).ap()
        return t

    def build(self, upto=99):
        nc = self.nc
        xv = self.din("xv", [TV, D])
        c_in = self.din("c", [1, D])
        w_ada = self.din("w_ada", [D, 6 * D])
        b_ada = self.din("b_ada", [6 * D])
        norm1_g = self.din("norm1_g", [D])
        w_in = self.din("w_in", [D, INW])
        vtok = self.din("vtok", [128, TV // 128])
        ident_in = self.din("ident", [128, 128], BF16)
        self.i_vcmp = self.din("vcmp", [128, NCV // 128])
        self.i_selb = self.din("selb", [5, 4, 128, NBW])
        self.i_selv = self.din("selv", [5, 4, 128, NBW])
        self.i_hflag = self.din("hflag", [128, 1])
        self.i_bias_s = self.din("bias_s", [128, NH, NDS])
        self.i_bias_c = self.din("bias_c", [128, NH, NDC])
        self.i_masks = self.din("masks", [128, NMASK, 512], BF16)
        self.i_esel = self.din("esel", [128, 64, 128], BF16)
        self.i_ovm = self.din("ovm", [128, 9, NBW], BF16)
        self.i_ones = self.din("ones_bf", [128, 128], BF16)
        self.i_cmp_pe = self.din("cmp_pe", [32, 128])
        self.i_w1 = [self.din("w_kc1", [4096, 256]), self.din("w_vc1", [4096, 256])]
        self.i_w2 = [self.din("w_kc2", [256, 128]), self.din("w_vc2", [256, 128])]
        self.i_wo = self.din("w_o_nsa", [D, D])
        self.i_dww = self.din("conv_dw_w", [31, D])
        self.i_dwb = self.din("conv_dw_b", [D])
        self.i_lng = self.din("conv_ln_g", [D])
        self.i_lnb = self.din("conv_ln_b", [D])
        self.i_wpw = self.din("conv_pw_w", [D, D])
        self.i_pwb = self.din("conv_pw_b", [D])
        self.i_wout = self.din("w_out", [D, D])
        self.i_n2g = self.din("norm2_g", [D])
        self.i_wup = self.din("ffn_w_up", [D, 2 * FF])
        self.i_fdw = self.din("ffn_dw_w", [3, 2 * FF])
        self.i_fdb = self.din("ffn_dw_b", [2 * FF])
        self.i_wdn = self.din("ffn_w_down", [FF, D])
        self.i_fg = self.din("final_g", [D])
        out = self.nc.dram_tensor("out", [TOWN, D], F32, kind="ExternalOutput").ap()
        self.out = out
        ada_d = self.dscr("ada_d", [6 * D], F32)
        kcT_raw = self.dscr("kcT_raw", [NG, 128, TV + 16], BF16)
        vcT_raw = self.dscr("vcT_raw", [NG, 128, TV + 16], BF16)
        kslT = self.dscr("kslT", [NG, 128, TV], BF16)
        vsl = self.dscr("vsl", [TV, NG, 130], BF16)
        QT = self.dscr("QT", [128, NH, NB], BF16)
        kwT = self.dscr("kwT", [NG, 128, NA], BF16)
        vw = self.dscr("vw", [NA, NG, 130], BF16)
        gates = self.dscr("gates", [NB, 48], F32)
        gluT = self.dscr("gluT", [128, 16, NB], BF16)
        mgT = self.dscr("mgT", [128, 32, NB], BF16)
        self.kcT = self.dscr("kcT", [NG, 128, NCV], BF16)
        self.vca = self.dscr("vca", [NCV, NG, 130], BF16)
        self.xmid = self.dscr("xmid", [NQ, D], F32)
        self.accd = self.dscr("accd", [NQ, D], F32)
        self.dumpS = self.dscr("dumpS", [128, 512], F32)
        self.impd = self.dscr("impd", [5, 4, 128, NG, NBW], F32)
        self.wb_d = {n: self.dscr(n, sh, BF16) for n, sh in (("wo_b", [D, D]), ("wpw_b", [D, D]), ("wout_b", [D, D]),
                                                           ("wup_b", [D, 2 * FF]), ("wdn_b", [FF, D]))}
        self.xv, self.kcT_raw, self.vcT_raw, self.kslT, self.vsl = xv, kcT_raw, vcT_raw, kslT, vsl
        self.QT, self.kwT, self.vw, self.gates, self.gluT, self.mgT, self.ada_d = QT, kwT, vw, gates, gluT, mgT, ada_d

        with ExitStack() as st0:
            P = self.P = Prog(nc, st0)
            self.st0 = st0
            sb = lambda name, shape, dt: st0.enter_context(nc.sbuf_tensor(name, list(shape), dt))
            ident = sb("ident_sb", [128, 128], BF16)
            r_const = Res("const")
            ch_c = P.chan("const")
            P.dma("sp", ch_c, lambda: nc.sync.dma_start(out=ident[:], in_=ident_in[:, :]), writes=[r_const])
            vtok_sb = sb("vtok_sb", [128, TV // 128], F32)
            P.dma("sp", ch_c, lambda: nc.sync.dma_start(out=vtok_sb[:], in_=vtok[:, :]), writes=[r_const], cont=True)
            ada = sb("ada_sb", [128, 96], F32)
            r_ada = Res("ada")
            s1 = sb("s1", [128, 16], F32)
            r_s1 = Res("s1")
            self.ident, self.r_const, self.vtok_sb, self.ada, self.r_ada, self.sb0 = ident, r_const, vtok_sb, ada, r_ada, sb
            self.ones = sb("ones_sb", [128, 128], BF16)
            P.dma("sp", ch_c, lambda: nc.sync.dma_start(out=self.ones[:], in_=self.i_ones[:, :]), writes=[r_const], cont=True)
            self.hfl = sb("hfl_sb", [128, 1], F32)
            P.dma("sp", ch_c, lambda: nc.sync.dma_start(out=self.hfl[:], in_=self.i_hflag[:, :]), writes=[r_const], cont=True)
            self.vcmp_sb = sb("vcmp_sb", [128, NCV // 128], F32)
            P.dma("sp", ch_c, lambda: nc.sync.dma_start(out=self.vcmp_sb[:], in_=self.i_vcmp[:, :]), writes=[r_const], cont=True)

            with ExitStack() as st:
                cT = st.enter_context(nc.sbuf_tensor("cT", [128, 16], F32))
                cact = st.enter_context(nc.sbuf_tensor("cact", [128, 16], F32))
                bT = st.enter_context(nc.sbuf_tensor("bT", [128, 96], F32))
                g1T = st.enter_context(nc.sbuf_tensor("g1T", [128, 16], F32))
                r_cT, r_cact, r_bT, r_g1T = Res(), Res(), Res(), Res()
                ch0 = P.chan("p0")
                P.dma("sp", ch0, lambda: nc.sync.dma_start(out=cT[:], in_=c_in[0, :].rearrange("(j p) -> p j", p=128),
                                                         allow_slow_non_contiguous=True), writes=[r_cT])
                P.dma("sp", ch0, lambda: nc.sync.dma_start(out=bT[:], in_=b_ada.rearrange("(f p) -> p f", p=128),
                                                         allow_slow_non_contiguous=True), writes=[r_bT], cont=True)
                P.dma("sp", ch0, lambda: nc.sync.dma_start(out=g1T[:], in_=norm1_g.rearrange("(j p) -> p j", p=128),
                                                         allow_slow_non_contiguous=True), writes=[r_g1T], cont=True)
                P.op("act", lambda: nc.scalar.activation(out=cact[:], in_=cT[:], func=AF.Silu), reads=[r_cT], writes=[r_cact])
                wpool = Pool(P, st, "wada", 2, [128, 16, 512], F32)
                chw = [P.chan("wada0"), P.chan("wada1")]
                aps = st.enter_context(nc.psum_tensor("ada_ps", [128, 96], F32))
                r_aps = Res()
                for blk in range(24):
                    wt, wr = wpool.next()
                    P.dma("sp", chw[blk % 2],
                          lambda wt=wt, blk=blk: nc.sync.dma_start(
                              out=wt[:], in_=w_ada[:, blk * 512:(blk + 1) * 512].rearrange("(k p) c -> p k c", p=128)),
                          writes=[wr])
                    for fl in range(4):
                        f = blk * 4 + fl
                        for kc in range(16):
                            P.op("pe", lambda wt=wt, fl=fl, kc=kc, f=f: nc.tensor.matmul(
                                aps[:, f:f + 1], lhsT=wt[:, kc, fl * 128:(fl + 1) * 128], rhs=cact[:, kc:kc + 1],
                                start=(kc == 0), stop=(kc == 15)), reads=[wr, r_cact], writes=[r_aps])
                P.op("dve", lambda: nc.vector.tensor_tensor(out=ada[:], in0=aps[:], in1=bT[:], op=ALU.add),
                     reads=[r_aps, r_bT], writes=[r_ada])
                P.op("dve", lambda: nc.vector.scalar_tensor_tensor(out=s1[:], in0=ada[:, 16:32], scalar=1.0, in1=g1T[:],
                                                                   op0=ALU.add, op1=ALU.mult),
                     reads=[r_ada, r_g1T], writes=[r_s1])
                r_adad = Res()
                ch_ad = P.chan("adad")
                P.dma("act", ch_ad, lambda: nc.scalar.dma_start(out=ada_d.rearrange("(f p) -> p f", p=128), in_=ada[:],
                                                              allow_slow_non_contiguous=True),
                      reads=[r_ada], writes=[r_adad])
                P.barrier()
            if upto <= 0:
                P.emit()
                return nc

            self.r_wb = Res("wb_scratch")
            if not (self.dbg and "nocast" in self.dbg):
                with ExitStack() as st:
                    c32 = Pool(P, st, "cst32_", 3, [128, 8192], F32)
                    cbf = Pool(P, st, "cstbf_", 3, [128, 8192], BF16)
                    cci = [P.chan(f"cji{i}") for i in range(3)]
                    cco = [P.chan(f"cjo{i}") for i in range(3)]
                    for ji, (sv, dv, o, w) in enumerate(self.cast_jobs()):
                        t32, r32 = c32.next()
                        tbf, rbf = cbf.next()
                        P.dma("sp", cci[ji % 3], lambda t32=t32, sv=sv, o=o, w=w: nc.sync.dma_start(out=t32[:, 0:w], in_=sv[:, o:o + w]), writes=[r32])
                        if ji % 2 == 0:
                            P.op("dve", lambda t32=t32, tbf=tbf, w=w: nc.vector.tensor_copy(out=tbf[:, 0:w], in_=t32[:, 0:w]), reads=[r32], writes=[rbf])
                        else:
                            P.op("act", lambda t32=t32, tbf=tbf, w=w: nc.scalar.copy(out=tbf[:, 0:w], in_=t32[:, 0:w]), reads=[r32], writes=[rbf])
                        P.dma("act", cco[ji % 3], lambda tbf=tbf, dv=dv, o=o, w=w: nc.scalar.dma_start(out=dv[:, o:o + w], in_=tbf[:, 0:w]),
                              reads=[rbf], writes=[self.r_wb])
                    P.barrier()
            with ExitStack() as st:
                wkv32 = Pool(P, st, "wkv32_", 2, [128, 16, 128], F32)
                wkv = st.enter_context(nc.sbuf_tensor("wkv", [128, 16, 1024], BF16))
                r_wkv = Res()
                chw = [P.chan("wkv0"), P.chan("wkv1")]
                for q in range(8):
                    wt, wr = wkv32.next()
                    P.dma("sp", chw[q % 2], lambda wt=wt, q=q: nc.sync.dma_start(
                        out=wt[:], in_=w_in[:, C_KC + q * 128:C_KC + (q + 1) * 128].rearrange("(k p) c -> p k c", p=128)),
                        writes=[wr])
                    P.op("pool", lambda wt=wt, q=q: nc.gpsimd.tensor_copy(out=wkv[:, :, q * 128:(q + 1) * 128], in_=wt[:]),
                         reads=[wr], writes=[r_wkv])
                xpool = Pool(P, st, "xt", 2, [128, 4, D], F32)
                chx = [P.chan("x0"), P.chan("x1")]
                junk = st.enter_context(nc.sbuf_tensor("junk", [128, D], BF16))
                r_junk = Res()
                sspool = Pool(P, st, "ss", 2, [128, 4], F32)
                rspool = Pool(P, st, "rs", 2, [128, 4], F32)
                xnpool = Pool(P, st, "xn", 2, [128, 4, D], BF16)
                hTpool = Pool(P, st, "hT", 2, [128, 16, 512], BF16)
                tpp = Pool(P, st, "tp", 4, [128, 512], BF16, psum=True)
                mmp = Pool(P, st, "mm", 3, [128, 512], F32, psum=True)
                stg = Pool(P, st, "stg", 2, [128, 6, 512], BF16)
                stv = Pool(P, st, "stv", 2, [128, 4, NG, 130], BF16)
                chs = [P.chan("st0"), P.chan("st1")]
                chv = [P.chan("sv0"), P.chan("sv1")]
                r_scr = Res("scr1a")
                for i in range(2):
                    P.op("dve", lambda i=i: nc.vector.memset(stv.t[i][:], 0.0), writes=[stv.r[i]])
                nblk = TV // 512
                if self.dbg and "nblk" in self.dbg:
                    nblk = self.dbg["nblk"]
                for tb in range(nblk):
                    t0 = tb * 512
                    hT, r_hT = self.norm_T(P, st, xv, t0, 4, xpool, chx[tb % 2], junk, r_junk, sspool, rspool, xnpool,
                                           hTpool, tpp, ident, r_const, s1, r_s1, ada, r_ada, 0)
                    sg, r_sg = stg.next()
                    for ci in range(6):
                        ps, r_ps = mmp.next()
                        for kc in range(16):
                            P.op("pe", lambda ps=ps, ci=ci, kc=kc, hT=hT: nc.tensor.matmul(
                                ps[:], lhsT=wkv[:, kc, ci * 128:(ci + 1) * 128], rhs=hT[:, kc, :],
                                start=(kc == 0), stop=(kc == 15)), reads=[r_wkv, r_hT], writes=[r_ps])
                        if ci % 2 == 0:
                            P.op("act", lambda ps=ps, sg=sg, ci=ci: nc.scalar.copy(out=sg[:, ci, :], in_=ps[:]),
                                 reads=[r_ps], writes=[r_sg])
                        else:
                            P.op("dve", lambda ps=ps, sg=sg, ci=ci: nc.vector.tensor_copy(out=sg[:, ci, :], in_=ps[:]),
                                 reads=[r_ps], writes=[r_sg])
                    dsts = [kcT_raw, vcT_raw, kslT]
                    for k3 in range(3):
                        P.dma("act", chs[tb % 2], lambda sg=sg, k3=k3, t0=t0: nc.scalar.dma_start(
                            out=dsts[k3][:, :, t0:t0 + 512].rearrange("g p t -> p g t"), in_=sg[:, 2 * k3:2 * k3 + 2, :]),
                            reads=[r_sg], writes=[r_scr], cont=(k3 > 0))
                    sv, r_sv = stv.next()
                    for s in range(4):
                        ps, r_ps = mmp.next()
                        for kc in range(16):
                            P.op("pe", lambda ps=ps, s=s, kc=kc, hT=hT: nc.tensor.matmul(
                                ps[:, 0:256], lhsT=hT[:, kc, s * 128:(s + 1) * 128], rhs=wkv[:, kc, 768:1024],
                                start=(kc == 0), stop=(kc == 15)), reads=[r_wkv, r_hT], writes=[r_ps])
                        tile = tb * 4 + s
                        P.op("dve", lambda ps=ps, sv=sv, s=s, tile=tile: nc.vector.tensor_scalar(
                            out=sv[:, s, :, 0:128], in0=ps[:, 0:256].rearrange("p (g d) -> p g d", g=NG),
                            scalar1=vtok_sb[:, tile:tile + 1], scalar2=None, op0=ALU.mult),
                            reads=[r_ps, r_const], writes=[r_sv])
                        P.op("dve", lambda sv=sv, s=s, tile=tile: nc.vector.tensor_copy(
                            out=sv[:, s, :, 128:130], in_=vtok_sb[:, tile:tile + 1].unsqueeze(1).to_broadcast([128, NG, 2])),
                            reads=[r_const], writes=[r_sv])
                    P.dma("act", chv[tb % 2], lambda sv=sv, t0=t0: nc.scalar.dma_start(
                        out=vsl[t0:t0 + 512, :, :].rearrange("(s p) g d -> p s g d", p=128), in_=sv[:]),
                        reads=[r_sv], writes=[r_scr])
                P.barrier()
            if upto <= 1:
                P.emit()
                return nc

            with ExitStack() as st:
                hTo = st.enter_context(nc.sbuf_tensor("hTo", [128, 16, NA], BF16))
                r_hTo = Res()
                with ExitStack() as st2:
                    xpool = Pool(P, st2, "xtb", 2, [128, 4, D], F32)
                    chx = [P.chan("xb0"), P.chan("xb1")]
                    junk = st2.enter_context(nc.sbuf_tensor("junkb", [128, D], BF16))
                    r_junk = Res()
                    sspool = Pool(P, st2, "ssb", 2, [128, 4], F32)
                    rspool = Pool(P, st2, "rsb", 2, [128, 4], F32)
                    xnpool = Pool(P, st2, "xnb", 2, [128, 4, D], BF16)
                    tpp = Pool(P, st2, "tpb", 4, [128, 512], BF16, psum=True)
                    for tb in range(6):
                        t0 = W0 + tb * 512
                        nsub = 4 if tb < 5 else 1

                        class _HP:
                            def next(self_inner):
                                return hTo[:, :, tb * 512:tb * 512 + nsub * 128], r_hTo
                        self.norm_T(P, st2, xv, t0, nsub, xpool, chx[tb % 2], junk, r_junk, sspool, rspool, xnpool,
                                    _HP(), tpp, ident, r_const, s1, r_s1, ada, r_ada, 0)
                    P.barrier()
                w32 = Pool(P, st, "w32_", 2, [128, 16, 512], F32)
                wbf = Pool(P, st, "wbf_", 2, [128, 16, 512], BF16)
                chw = [P.chan("w1b0"), P.chan("w1b1")]
                mmp = Pool(P, st, "mmb", 4, [128, 512], F32, psum=True)
                stA = Pool(P, st, "stA", 2, [128, NA], BF16)
                chst = [P.chan("stA0"), P.chan("stA1"), P.chan("stA2")]
                sgp = Pool(P, st, "sgp", 1, [128, NB], BF16)
                r_scr = Res("scr1b")
                self.wblk = 0

                def load_w(colranges):
                    wt, wr = w32.next()
                    wb, wbr = wbf.next()
                    ch = chw[self.wblk % 2]
                    self.wblk += 1
                    o = 0
                    for i, (c0, n) in enumerate(colranges):
                        P.dma("sp", ch, lambda wt=wt, c0=c0, n=n, o=o: nc.sync.dma_start(
                            out=wt[:, :, o:o + n], in_=w_in[:, c0:c0 + n].rearrange("(k p) c -> p k c", p=128)),
                            writes=[wr], cont=(i > 0))
                        o += n
                    P.op("pool", lambda wt=wt, wb=wb, o=o: nc.gpsimd.tensor_copy(out=wb[:, :, 0:o], in_=wt[:, :, 0:o]),
                         reads=[wr], writes=[wbr])
                    return wb, wbr

                def blocks(lo, hi):
                    b = []
                    t = lo
                    while t < hi:
                        n = min(512, hi - t)
                        b.append((t, n))
                        t += n
                    return b

                def fm_chunk(wb, wbr, woff, lo, hi, evac):
                    for (t, n) in blocks(lo, hi):
                        ps, r_ps = mmp.next()
                        for kc in range(16):
                            P.op("pe", lambda ps=ps, kc=kc, t=t, n=n: nc.tensor.matmul(
                                ps[:, 0:n], lhsT=wb[:, kc, woff:woff + 128], rhs=hTo[:, kc, t:t + n],
                                start=(kc == 0), stop=(kc == 15)), reads=[wbr, r_hTo], writes=[r_ps])
                        evac(ps, r_ps, t, n)

                ecount = [0]

                def copy_evac(dst, r_dst, off, func=None, scale=1.0):
                    def ev(ps, r_ps, t, n):
                        ecount[0] += 1
                        if func is None and scale == 1.0 and ecount[0] % 2 == 0:
                            P.op("dve", lambda: nc.vector.tensor_copy(out=dst[:, t - off:t - off + n], in_=ps[:, 0:n]),
                                 reads=[r_ps], writes=[r_dst])
                        else:
                            P.op("act", lambda: nc.scalar.activation(out=dst[:, t - off:t - off + n], in_=ps[:, 0:n],
                                                                     func=(func or AF.Copy), scale=scale),
                                 reads=[r_ps], writes=[r_dst])
                    return ev

                stn = [0]

                def store(dst_ap, sg, r_sg, n):
                    ch = chst[stn[0] % 3]
                    stn[0] += 1
                    P.dma("act", ch, lambda: nc.scalar.dma_start(out=dst_ap, in_=sg[:, 0:n]), reads=[r_sg], writes=[r_scr])

                OB = B0 - W0
                for qb in range(4):
                    wb, wbr = load_w([(C_Q + qb * 512, 512)])
                    for hl in range(4):
                        h = qb * 4 + hl
                        sg, r_sg = stA.next()
                        fm_chunk(wb, wbr, hl * 128, OB, NA, copy_evac(sg, r_sg, OB, scale=float(DK) ** -0.5))
                        store(QT[:, h, :], sg, r_sg, NB)
                wb, wbr = load_w([(C_KW, 256), (C_VW, 256)])
                for g in range(NG):
                    sg, r_sg = stA.next()
                    fm_chunk(wb, wbr, g * 128, 0, NA, copy_evac(sg, r_sg, 0))
                    store(kwT[g, :, :], sg, r_sg, NA)
                stv = Pool(P, st, "stvb", 2, [128, NG, 130], BF16)
                chv = [P.chan("svb0"), P.chan("svb1")]
                for i in range(2):
                    P.op("dve", lambda i=i: nc.vector.memset(stv.t[i][:], 0.0), writes=[stv.r[i]])
                for s in range(NA // 128):
                    ps, r_ps = mmp.next()
                    for kc in range(16):
                        P.op("pe", lambda ps=ps, s=s, kc=kc, wb=wb: nc.tensor.matmul(
                            ps[:, 0:256], lhsT=hTo[:, kc, s * 128:(s + 1) * 128], rhs=wb[:, kc, 256:512],
                            start=(kc == 0), stop=(kc == 15)), reads=[wbr, r_hTo], writes=[r_ps])
                    sv, r_sv = stv.next()
                    tile = W0 // 128 + s
                    P.op("dve", lambda ps=ps, sv=sv, tile=tile: nc.vector.tensor_scalar(
                        out=sv[:, :, 0:128], in0=ps[:, 0:256].rearrange("p (g d) -> p g d", g=NG),
                        scalar1=vtok_sb[:, tile:tile + 1], scalar2=None, op0=ALU.mult),
                        reads=[r_ps, r_const], writes=[r_sv])
                    P.op("pool", lambda sv=sv, tile=tile: nc.gpsimd.tensor_copy(
                        out=sv[:, :, 128:130], in_=vtok_sb[:, tile:tile + 1].unsqueeze(1).to_broadcast([128, NG, 2])),
                        reads=[r_const], writes=[r_sv])
                    P.dma("act", chv[s % 2], lambda sv=sv, s=s: nc.scalar.dma_start(
                        out=vw[s * 128:(s + 1) * 128, :, :], in_=sv[:]), reads=[r_sv], writes=[r_scr])
                wb, wbr = load_w([(C_GN, 48)])
                gst = Pool(P, st, "gst", 2, [128, 48], F32)
                chg = [P.chan("gs0"), P.chan("gs1")]
                for s in range(NB // 128):
                    ps, r_ps = mmp.next()
                    for kc in range(16):
                        P.op("pe", lambda ps=ps, s=s, kc=kc, wb=wb: nc.tensor.matmul(
                            ps[:, 0:48], lhsT=hTo[:, kc, OB + s * 128:OB + (s + 1) * 128], rhs=wb[:, kc, 0:48],
                            start=(kc == 0), stop=(kc == 15)), reads=[wbr, r_hTo], writes=[r_ps])
                    gs, r_gs = gst.next()
                    P.op("act", lambda ps=ps, gs=gs: nc.scalar.activation(out=gs[:], in_=ps[:, 0:48], func=AF.Sigmoid),
                         reads=[r_ps], writes=[r_gs])
                    P.dma("act", chg[s % 2], lambda gs=gs, s=s: nc.scalar.dma_start(
                        out=gates[s * 128:(s + 1) * 128, :], in_=gs[:]), reads=[r_gs], writes=[r_scr])
                for cb in range(8):
                    wb, wbr = load_w([(C_GLU + cb * 256, 256), (C_GLU + D + cb * 256, 256)])
                    for cl in range(2):
                        ch_ = cb * 2 + cl
                        sgm, r_sgm = sgp.next()
                        fm_chunk(wb, wbr, 256 + cl * 128, OB, NA, copy_evac(sgm, r_sgm, OB, func=AF.Sigmoid))
                        sg, r_sg = stA.next()

                        def ev(ps, r_ps, t, n, sg=sg, r_sg=r_sg, sgm=sgm, r_sgm=r_sgm):
                            P.op("dve", lambda: nc.vector.tensor_tensor(out=sg[:, t - OB:t - OB + n], in0=ps[:, 0:n],
                                                                        in1=sgm[:, t - OB:t - OB + n], op=ALU.mult),
                                 reads=[r_ps, r_sgm], writes=[r_sg])
                        fm_chunk(wb, wbr, cl * 128, OB, NA, ev)
                        P.op("dve", lambda sg=sg: nc.vector.tensor_scalar(out=sg[:, 0:OWN0 - B0], in0=sg[:, 0:OWN0 - B0], scalar1=self.hfl[:, 0:1],
                                                                        scalar2=None, op0=ALU.mult), reads=[r_sg, r_const], writes=[r_sg])
                        store(gluT[:, ch_, :], sg, r_sg, NB)
                for mb in range(8):
                    wb, wbr = load_w([(C_GM + mb * 512, 512)])
                    for cl in range(4):
                        sg, r_sg = stA.next()
                        fm_chunk(wb, wbr, cl * 128, OB, NA, copy_evac(sg, r_sg, OB, func=AF.Sigmoid))
                        store(mgT[:, mb * 4 + cl, :], sg, r_sg, NB)
                P.barrier()
            self.r_scr_all = Res("scr_all")
            if upto >= 3:
                self.phase_compress()
            if upto >= 4:
                self.phase_attn(upto)
            if upto >= 5:
                self.phase_mix()
            if upto >= 6:
                self.phase_ffn()
            P.emit()
        return nc

    def norm_T(self, P, st, src, t0, nsub, xpool, chx, junk, r_junk, sspool, rspool, xnpool, hTpool, tpp,
               ident, r_const, sc, r_sc, ada, r_ada, sh_col, xt_in=None):
        nc = self.nc
        if xt_in is None:
            xt, r_xt = xpool.next()
            P.dma("sp", chx, lambda: nc.sync.dma_start(
                out=xt[:, 0:nsub, :], in_=src[t0:t0 + nsub * 128, :].rearrange("(s p) d -> p s d", p=128)), writes=[r_xt])
        else:
            xt, r_xt = xt_in
        ss, r_ss = sspool.next()
        rs, r_rs = rspool.next()
        for s in range(nsub):
            P.op("act", lambda s=s: nc.scalar.activation(out=junk[:], in_=xt[:, s, :], func=AF.Square,
                                                         accum_out=ss[:, s:s + 1]),
                 reads=[r_xt], writes=[r_junk, r_ss])
        P.op("dve", lambda: nc.vector.tensor_scalar(out=rs[:, 0:nsub], in0=ss[:, 0:nsub], scalar1=1.0 / D, scalar2=EPS,
                                                    op0=ALU.mult, op1=ALU.add), reads=[r_ss], writes=[r_rs])
        P.op("act", lambda: nc.scalar.sqrt(out=rs[:, 0:nsub], in_=rs[:, 0:nsub]), reads=[r_rs], writes=[r_rs])
        P.op("dve", lambda: nc.vector.reciprocal(out=rs[:, 0:nsub], in_=rs[:, 0:nsub]), reads=[r_rs], writes=[r_rs])
        xn, r_xn = xnpool.next()
        for s in range(nsub):
            P.op("dve", lambda s=s: nc.vector.tensor_scalar(out=xn[:, s, :], in0=xt[:, s, :], scalar1=rs[:, s:s + 1],
                                                            scalar2=None, op0=ALU.mult),
                 reads=[r_xt, r_rs], writes=[r_xn])
        hT, r_hT = hTpool.next()
        for j in range(16):
            tp, r_tp = tpp.next()
            for s in range(nsub):
                P.op("pe", lambda tp=tp, s=s, j=j: nc.tensor.transpose(
                    out=tp[:, s * 128:(s + 1) * 128], in_=xn[:, s, j * 128:(j + 1) * 128], identity=ident[:]),
                    reads=[r_xn, r_const], writes=[r_tp])
            if j % 2 == 0:
                P.op("act", lambda tp=tp, j=j: nc.scalar.activation(
                    out=hT[:, j, 0:nsub * 128], in_=tp[:, 0:nsub * 128], func=AF.Identity,
                    scale=sc[:, j:j + 1], bias=ada[:, sh_col + j:sh_col + j + 1]),
                    reads=[r_tp, r_sc, r_ada], writes=[r_hT])
            else:
                P.op("dve", lambda tp=tp, j=j: nc.vector.tensor_scalar(
                    out=hT[:, j, 0:nsub * 128], in0=tp[:, 0:nsub * 128], scalar1=sc[:, j:j + 1],
                    scalar2=ada[:, sh_col + j:sh_col + j + 1], op0=ALU.mult, op1=ALU.add),
                    reads=[r_tp, r_sc, r_ada], writes=[r_hT])
        self.last_rs = (rs, r_rs)
        self.last_xt = (xt, r_xt)
        return hT, r_hT


    def phase_compress(self):
        nc, P = self.nc, self.P
        with ExitStack() as st:
            sbt = lambda name, shape, dt: st.enter_context(nc.sbuf_tensor(name, list(shape), dt))
            raw = sbt("c_raw", [128, TV + 16], BF16)
            R = sbt("c_R", [128, 16, NCV + 1], BF16)
            w1f = sbt("c_w1f", [128, 32, 256], F32)
            w1b = sbt("c_w1b", [128, 32, 256], BF16)
            w2f = sbt("c_w2f", [128, 2, 128], F32)
            w2b = sbt("c_w2b", [128, 2, 128], BF16)
            pef = sbt("c_pef", [128, 32], F32)
            peb = sbt("c_peb", [128, 32], BF16)
            bia = sbt("c_bia", [128, 2], F32)
            hid = sbt("c_hid", [128, 2, NCV], BF16)
            kst = sbt("c_kst", [128, NCV], BF16)
            zpad = sbt("c_zpad", [128, NG, 16], BF16)
            r_raw, r_R, r_w1f, r_w1b, r_w2f, r_w2b, r_pe, r_bia, r_hid, r_kst, r_z = [Res() for _ in range(11)]
            vst = Pool(P, st, "c_vst", 2, [128, NG, 130], BF16)
            mmp = Pool(P, st, "c_mm", 3, [128, 512], F32, psum=True)
            bps = st.enter_context(nc.psum_tensor("c_bps", [128, 2], F32))
            r_bps = Res()
            ch = {n: P.chan("c_" + n) for n in ("raw", "w1", "w2", "pe", "k", "v0", "v1", "z")}
            r_out = self.r_scr_all
            P.op("dve", lambda: nc.vector.memset(zpad[:], 0.0), writes=[r_z])
            for i, rt in enumerate((self.kcT_raw, self.vcT_raw)):
                P.dma("act", ch["z"], lambda rt=rt: nc.scalar.dma_start(
                    out=rt[:, :, TV:TV + 16].rearrange("g p t -> p g t"), in_=zpad[:]), reads=[r_z], writes=[r_out], cont=(i > 0))
            P.dma("sp", ch["pe"], lambda: nc.sync.dma_start(out=pef[:], in_=self.i_cmp_pe.rearrange("l d -> d l"),
                                                           allow_slow_non_contiguous=True), writes=[r_pe])
            P.op("dve", lambda: nc.vector.tensor_copy(out=peb[:], in_=pef[:]), reads=[r_pe], writes=[r_pe])
            for i in range(2):
                P.op("dve", lambda i=i: nc.vector.memset(vst.t[i][:], 0.0), writes=[vst.r[i]])
            for kv in range(2):
                P.dma("sp", ch["w1"], lambda kv=kv: nc.sync.dma_start(
                    out=w1f[:], in_=self.i_w1[kv].rearrange("(l d) c -> d l c", d=128)), writes=[r_w1f])
                P.op("pool", lambda: nc.gpsimd.tensor_copy(out=w1b[:], in_=w1f[:]), reads=[r_w1f], writes=[r_w1b])
                P.dma("sp", ch["w2"], lambda kv=kv: nc.sync.dma_start(
                    out=w2f[:], in_=self.i_w2[kv].rearrange("(c p) d -> p c d", p=128)), writes=[r_w2f])
                P.op("dve", lambda: nc.vector.tensor_copy(out=w2b[:], in_=w2f[:]), reads=[r_w2f], writes=[r_w2b])
                for hc in range(2):
                    for l in range(32):
                        P.op("pe", lambda hc=hc, l=l: nc.tensor.matmul(
                            bps[:, hc:hc + 1], lhsT=w1b[:, l, hc * 128:(hc + 1) * 128], rhs=peb[:, l:l + 1],
                            start=(l == 0), stop=(l == 31)), reads=[r_w1b, r_pe], writes=[r_bps])
                P.op("dve", lambda: nc.vector.tensor_copy(out=bia[:], in_=bps[:]), reads=[r_bps], writes=[r_bia])
                src = (self.kcT_raw, self.vcT_raw)[kv]
                for g in range(NG):
                    P.dma("sp", ch["raw"], lambda g=g, src=src: nc.sync.dma_start(out=raw[:], in_=src[g, :, :]),
                          reads=[r_out], writes=[r_raw])
                    P.op("pool", lambda: nc.gpsimd.tensor_copy(
                        out=R[:], in_=raw[:].rearrange("p (m l) -> p l m", l=16)), reads=[r_raw], writes=[r_R])
                    for hc in range(2):
                        for (n0, nn) in ((0, 512), (512, 512), (1024, 128)):
                            ps, r_ps = mmp.next()
                            for l in range(32):
                                P.op("pe", lambda ps=ps, hc=hc, l=l, n0=n0, nn=nn: nc.tensor.matmul(
                                    ps[:, 0:nn], lhsT=w1b[:, l, hc * 128:(hc + 1) * 128],
                                    rhs=R[:, l % 16, (l // 16) + n0:(l // 16) + n0 + nn],
                                    start=(l == 0), stop=(l == 31)), reads=[r_w1b, r_R], writes=[r_ps])
                            P.op("act", lambda ps=ps, hc=hc, n0=n0, nn=nn: nc.scalar.activation(
                                out=hid[:, hc, n0:n0 + nn], in_=ps[:, 0:nn], func=AF.Silu, bias=bia[:, hc:hc + 1]),
                                reads=[r_ps, r_bia], writes=[r_hid])
                    if kv == 0:
                        for (n0, nn) in ((0, 512), (512, 512), (1024, 128)):
                            ps, r_ps = mmp.next()
                            for hc in range(2):
                                P.op("pe", lambda ps=ps, hc=hc, n0=n0, nn=nn: nc.tensor.matmul(
                                    ps[:, 0:nn], lhsT=w2b[:, hc, :], rhs=hid[:, hc, n0:n0 + nn],
                                    start=(hc == 0), stop=(hc == 1)), reads=[r_w2b, r_hid], writes=[r_ps])
                            P.op("dve", lambda ps=ps, n0=n0, nn=nn: nc.vector.tensor_copy(out=kst[:, n0:n0 + nn], in_=ps[:, 0:nn]),
                                 reads=[r_ps], writes=[r_kst])
                        P.dma("act", ch["k"], lambda g=g: nc.scalar.dma_start(out=self.kcT[g, :, :], in_=kst[:]),
                              reads=[r_kst], writes=[r_out])
                    else:
                        for tl in range(NCV // 128):
                            ps, r_ps = mmp.next()
                            for hc in range(2):
                                P.op("pe", lambda ps=ps, hc=hc, tl=tl: nc.tensor.matmul(
                                    ps[:, 0:128], lhsT=hid[:, hc, tl * 128:(tl + 1) * 128], rhs=w2b[:, hc, :],
                                    start=(hc == 0), stop=(hc == 1)), reads=[r_w2b, r_hid], writes=[r_ps])
                            sv, r_sv = vst.next()
                            P.op("dve", lambda ps=ps, sv=sv, tl=tl, g=g: nc.vector.tensor_scalar(
                                out=sv[:, g, 0:128], in0=ps[:, 0:128], scalar1=self.vcmp_sb[:, tl:tl + 1], scalar2=None,
                                op0=ALU.mult), reads=[r_ps, self.r_const], writes=[r_sv])
                            P.op("pool", lambda sv=sv, tl=tl, g=g: nc.gpsimd.tensor_copy(
                                out=sv[:, g, 128:130], in_=self.vcmp_sb[:, tl:tl + 1].to_broadcast([128, 2])), reads=[self.r_const], writes=[r_sv])
                            P.dma("act", ch["v%d" % (tl % 2)], lambda sv=sv, tl=tl, g=g: nc.scalar.dma_start(
                                out=self.vca[tl * 128:(tl + 1) * 128, g, :], in_=sv[:, g, :]), reads=[r_sv], writes=[r_out])
            P.barrier()


    def phase_attn(self, upto):
        nc, P = self.nc, self.P
        with ExitStack() as st:
            sbt = lambda name, shape, dt: st.enter_context(nc.sbuf_tensor(name, list(shape), dt))
            bias_s = sbt("a_bs", [128, NH, NDS], F32)
            bias_c = sbt("a_bc", [128, NH, NDC], F32)
            masks = sbt("a_mk", [128, NMASK, 512], BF16)
            esel = sbt("a_es", [128, 64, 128], BF16)
            r_tab = Res("tables")
            cht = P.chan("a_tab")
            for i, (dst, src) in enumerate(((bias_s, self.i_bias_s), (bias_c, self.i_bias_c), (masks, self.i_masks), (esel, self.i_esel))):
                P.dma("sp", cht, lambda dst=dst, src=src: nc.sync.dma_start(out=dst[:], in_=src[:, :, :]), writes=[r_tab], cont=(i > 0))
            crhs = [[sbt(f"a_cr{g}_{jj}", [128, 130 + NBW], BF16) for jj in range(9)] for g in range(NG)]
            r_crhs = [[Res() for jj in range(9)] for g in range(NG)]
            ch_cr = [P.chan("a_cr0"), P.chan("a_cr1")]
            ovm = sbt("a_ovm", [128, 9, NBW], BF16)
            P.dma("sp", cht, lambda: nc.sync.dma_start(out=ovm[:], in_=self.i_ovm[:, :, :]), writes=[r_tab], cont=True)
            QTt = sbt("a_qt", [128, NH, 512], BF16)
            gat = sbt("a_gat", [128, 4, 48], F32)
            selb = sbt("a_selb", [128, 4, NBW], F32)
            selv = sbt("a_selv", [128, 4, NBW], F32)
            acc = sbt("a_acc", [128, 4, D], F32)
            imp = sbt("a_imp", [128, 4, NG, NBW], F32)
            mneg = sbt("a_mneg", [128, NG, 3, 512], BF16)
            r_qt, r_gat, r_sel, r_acc, r_imp, r_mneg = [Res() for _ in range(6)]
            ch_q = P.chan("a_q")
            kpool = Pool(P, st, "a_k", 3, [128, 2048], BF16)
            vpool = Pool(P, st, "a_v", 3, [128, 16, 130], BF16)
            chk = [P.chan(f"a_k{i}") for i in range(3)]
            chv = [P.chan(f"a_v{i}") for i in range(3)]
            ptp = Pool(P, st, "a_pt", 4, [128, 512], BF16)
            sps = Pool(P, st, "a_S", 2, [128, 512], F32, psum=True)
            ops_ = Pool(P, st, "a_o", 4, [128, 512], F32, psum=True)
            tps = Pool(P, st, "a_tp", 2, [128, 512], BF16, psum=True)
            small = Pool(P, st, "a_sm", 4, [128, 4], F32)
            sc1 = sbt("a_sc1", [128, 384], F32)
            sc2 = sbt("a_sc2", [128, 384], F32)
            m8 = sbt("a_m8", [128, 16], F32)
            mbf = sbt("a_mbf", [128, 8, 384], BF16)
            r_sc1, r_sc2, r_m8, r_mbf = Res(), Res(), Res(), Res()
            P.op("dve", lambda: nc.vector.memset(mbf[:], 0.0), writes=[r_mbf])
            P.op("dve", lambda: nc.vector.memset(sc1[:], 0.0), writes=[r_sc1])
            ch_dbg = P.chan("a_dbg")
            r_out = self.r_scr_all
            kcount = [0]

            def do_tile(ti, q0, nq):
                self._cr_loaded = [0, 0]
                qend = q0 + nq
                nsub = max(1, nq // 128)
                rows = min(128, nq)
                nend = qend // 16
                P.dma("sp", ch_q, lambda q0=q0, nq=nq: nc.sync.dma_start(out=QTt[:, :, 0:nq], in_=self.QT[:, :, q0 - B0:q0 - B0 + nq]),
                      reads=[r_out], writes=[r_qt])
                P.dma("sp", ch_q, lambda q0=q0, nq=nq, rows=rows, nsub=nsub: nc.sync.dma_start(
                    out=gat[0:rows, 0:nsub, :], in_=self.gates[q0 - B0:q0 - B0 + nq, :].rearrange("(s p) c -> p s c", p=rows)),
                    reads=[r_out], writes=[r_gat], cont=True)
                P.dma("sp", ch_q, lambda ti=ti: nc.sync.dma_start(out=selb[:], in_=self.i_selb[ti].rearrange("s p b -> p s b")),
                      writes=[r_sel], cont=True)
                P.dma("sp", ch_q, lambda ti=ti: nc.sync.dma_start(out=selv[:], in_=self.i_selv[ti].rearrange("s p b -> p s b")),
                      writes=[r_sel], cont=True)

                def run_branch(bi, h):
                    g = h // HG
                    W = exp_width(h, nq)
                    if bi == 0:
                        nch = min(n_cmp_chunks(h, nq), nend // 128)
                    elif bi == 1:
                        nch = min(n_slc_chunks(h, nq, 132), qend // 128)
                    else:
                        nch = min(n_slc_chunks(h, nq, 8 if nq == 512 else 5), 8 if nq == 512 else 5)
                    ncol = 130 + NBW if bi == 0 else 130
                    oacc = [ops_.next() for _ in range(nsub)]
                    kt = vt = None
                    stA = {}

                    def stageB(j, pt, r_pt, rhsV, r_rhsV):
                        for s_ in range(nsub):
                            o, r_o = oacc[s_]
                            P.op("pe", lambda o=o, s_=s_, pt=pt, rhsV=rhsV, j=j: nc.tensor.matmul(
                                o[0:rows, 0:ncol], lhsT=pt[:, s_ * 128:s_ * 128 + rows], rhs=rhsV,
                                start=(j == 0), stop=(j == nch - 1)), reads=[r_pt, r_rhsV], writes=[r_o])

                    for j in range(nch):
                        if bi == 0:
                            n0 = nend - 128 * (j + 1)
                            kt, r_kt = kpool.next()
                            kc_ = kcount[0] % 3
                            kcount[0] += 1
                            P.dma("sp", chk[kc_], lambda kt=kt, n0=n0: nc.sync.dma_start(out=kt[:, 0:128], in_=self.kcT[g, :, n0:n0 + 128]),
                                  reads=[r_out], writes=[r_kt])
                            if h % HG == 0 or j >= self._cr_loaded[g]:
                                P.dma("sp", ch_cr[g], lambda n0=n0, j=j: nc.sync.dma_start(out=crhs[g][j][:, 0:130], in_=self.vca[n0:n0 + 128, g, :]),
                                      reads=[r_out], writes=[r_crhs[g][j]])
                                P.op("dve", lambda j=j: nc.vector.tensor_scalar(out=crhs[g][j][:, 130:130 + NBW], in0=ovm[:, j, :],
                                                                               scalar1=crhs[g][j][:, 128:129], scalar2=None, op0=ALU.mult),
                                     reads=[r_tab, r_crhs[g][j]], writes=[r_crhs[g][j]])
                                self._cr_loaded[g] = max(self._cr_loaded[g], j + 1) if h % HG else j + 1
                            klhs, rhsV, r_rhsV = kt[:, 0:128], crhs[g][j][:, 0:ncol], r_crhs[g][j]
                            mk = MASK_IDX.get(("cmp", nq, j))
                            bcol = lambda r, j=j: bias_c[:, h, (nq - 2048 * (j + 1) - r * W) // 64 + ROFF:(nq - 2048 * (j + 1) - r * W) // 64 + ROFF + 1]
                        else:
                            if j % 16 == 0:
                                nsup = min(16, nch - j)
                                lo = qend - 128 * (j + nsup)
                                kt, r_kt = kpool.next()
                                vt, r_vt = vpool.next()
                                kc_ = kcount[0] % 3
                                kcount[0] += 1
                                if bi == 1:
                                    ksrc, vsrc, off = self.kslT, self.vsl, 0
                                else:
                                    ksrc, vsrc, off = self.kwT, self.vw, W0
                                P.dma("sp", chk[kc_], lambda kt=kt, lo=lo, nsup=nsup, ksrc=ksrc, off=off: nc.sync.dma_start(
                                    out=kt[:, 0:128 * nsup], in_=ksrc[g, :, lo - off:lo - off + 128 * nsup]), reads=[r_out], writes=[r_kt])
                                P.dma("sp", chv[kc_], lambda vt=vt, lo=lo, nsup=nsup, vsrc=vsrc, off=off: nc.sync.dma_start(
                                    out=vt[:, 0:nsup, :], in_=vsrc[lo - off:lo - off + 128 * nsup, g, :].rearrange("(s p) d -> p s d", p=128)),
                                    reads=[r_out], writes=[r_vt])
                                sup_n = nsup
                            sl = sup_n - 1 - (j % 16)
                            klhs, rhsV, r_rhsV = kt[:, sl * 128:(sl + 1) * 128], vt[:, sl, :], r_vt
                            mk = MASK_IDX.get(("slc" if bi == 1 else "win", nq, j))
                            bcol = lambda r, j=j: bias_s[:, h, (nq - 128 * (j + 1) - r * W) // 64 + DOFF:(nq - 128 * (j + 1) - r * W) // 64 + DOFF + 1]
                        S, r_S = sps.next()
                        nmm = 1 + (mk is not None) + (bi == 1)
                        cnt = [0]

                        def mm(lhsT, rhs, reads):
                            first, last = cnt[0] == 0, cnt[0] == nmm - 1
                            cnt[0] += 1
                            P.op("pe", lambda S=S: nc.tensor.matmul(S[:, 0:nq], lhsT=lhsT, rhs=rhs, start=first, stop=last),
                                 reads=reads, writes=[r_S])
                        mm(klhs, QTt[:, h, 0:nq], [r_kt, r_qt])
                        if mk is not None:
                            mm(self.ident[:], masks[:, mk, 0:nq], [self.r_const, r_tab])
                        if bi == 1:
                            b0 = NBLKW - 2 * (j + 1)
                            mm(esel[:, (b0 % 128) // 2, :], mneg[:, g, b0 // 128, 0:nq], [r_tab, r_mneg])
                        if self.dbg and "dumpS" in self.dbg and bi == 0 and h == 0 and j == 0:
                            dS = sbt("dbg_S", [128, 512], F32)
                            r_dS = Res()
                            P.op("dve", lambda: nc.vector.tensor_copy(out=dS[:, 0:nq], in_=S[:, 0:nq]), reads=[r_S], writes=[r_dS])
                            P.dma("act", ch_dbg, lambda: nc.scalar.dma_start(out=self.dumpS[:, 0:nq], in_=dS[:, 0:nq]), reads=[r_dS], writes=[r_out])
                            raise StopIteration
                        pt, r_pt = ptp.next()
                        for r in range(nq // W):
                            P.op("act", lambda r=r, bcol=bcol, S=S, pt=pt: nc.scalar.activation(
                                out=pt[:, r * W:(r + 1) * W], in_=S[:, r * W:(r + 1) * W], func=AF.Exp, bias=bcol(r)),
                                reads=[r_S, r_tab], writes=[r_pt])
                        if j > 0:
                            stageB(j - 1, *stA.pop(j - 1))
                        stA[j] = (pt, r_pt, rhsV, r_rhsV)
                    stageB(nch - 1, *stA.pop(nch - 1))
                    for s_ in range(nsub):
                        o, r_o = oacc[s_]
                        sm, r_sm = small.next()
                        P.op("dve", lambda o=o, sm=sm: nc.vector.tensor_scalar(out=sm[0:rows, 0:1], in0=o[0:rows, 128:129], scalar1=1e-30,
                                                                             scalar2=None, op0=ALU.max), reads=[r_o], writes=[r_sm])
                        P.op("dve", lambda sm=sm: nc.vector.reciprocal(out=sm[0:rows, 1:2], in_=sm[0:rows, 0:1]), reads=[r_sm], writes=[r_sm])
                        P.op("dve", lambda sm=sm, s_=s_: nc.vector.tensor_tensor(
                            out=sm[0:rows, 2:3], in0=sm[0:rows, 1:2], in1=gat[0:rows, s_, bi * 16 + h:bi * 16 + h + 1], op=ALU.mult),
                            reads=[r_sm, r_gat], writes=[r_sm])
                        dst = acc[0:rows, s_, h * 128:(h + 1) * 128]
                        if bi == 0:
                            P.op("dve", lambda o=o, sm=sm, dst=dst: nc.vector.tensor_scalar(
                                out=dst, in0=o[0:rows, 0:128], scalar1=sm[0:rows, 2:3], scalar2=None, op0=ALU.mult),
                                reads=[r_o, r_sm], writes=[r_acc])
                            idst = imp[0:rows, s_, g, :]
                            if h % HG == 0:
                                P.op("dve", lambda o=o, sm=sm, idst=idst: nc.vector.tensor_scalar(
                                    out=idst, in0=o[0:rows, 130:130 + NBW], scalar1=sm[0:rows, 1:2], scalar2=None, op0=ALU.mult),
                                    reads=[r_o, r_sm], writes=[r_imp])
                            else:
                                P.op("dve", lambda o=o, sm=sm, idst=idst: nc.vector.scalar_tensor_tensor(
                                    out=idst, in0=o[0:rows, 130:130 + NBW], scalar=sm[0:rows, 1:2], in1=idst, op0=ALU.mult, op1=ALU.add),
                                    reads=[r_o, r_sm, r_imp], writes=[r_imp])
                        else:
                            P.op("dve", lambda o=o, sm=sm, dst=dst: nc.vector.scalar_tensor_tensor(
                                out=dst, in0=o[0:rows, 0:128], scalar=sm[0:rows, 2:3], in1=dst, op0=ALU.mult, op1=ALU.add),
                                reads=[r_o, r_sm, r_acc], writes=[r_acc])

                try:
                    for h in range(NH):
                        run_branch(0, h)
                except StopIteration:
                    return "stop"
                if self.dbg and "impd" in self.dbg:
                    P.dma("act", ch_dbg, lambda ti=ti: nc.scalar.dma_start(out=self.impd[ti].rearrange("s p g b -> p s g b"), in_=imp[:]),
                          reads=[r_imp], writes=[r_out])
                for g in range(NG):
                    for s_ in range(nsub):
                        P.op("dve", lambda s_=s_, g=g: nc.vector.tensor_tensor(out=sc1[0:rows, 0:NBW], in0=imp[0:rows, s_, g, :],
                                                                               in1=selb[0:rows, s_, :], op=ALU.add),
                             reads=[r_imp, r_sel], writes=[r_sc1])
                        P.op("dve", lambda s_=s_: nc.vector.tensor_tensor(out=sc1[0:rows, 0:NBW], in0=sc1[0:rows, 0:NBW],
                                                                          in1=selv[0:rows, s_, :], op=ALU.mult),
                             reads=[r_sc1, r_sel], writes=[r_sc1])
                        P.op("dve", lambda: nc.vector.max(out=m8[0:rows, 0:8], in_=sc1[0:rows, :]), reads=[r_sc1], writes=[r_m8])
                        P.op("dve", lambda: nc.vector.match_replace(out=sc2[0:rows, :], in_to_replace=m8[0:rows, 0:8],
                                                                    in_values=sc1[0:rows, :], imm_value=-1e30),
                             reads=[r_sc1, r_m8], writes=[r_sc2])
                        P.op("dve", lambda: nc.vector.max(out=m8[0:rows, 8:16], in_=sc2[0:rows, :]), reads=[r_sc2], writes=[r_m8])
                        P.op("dve", lambda g=g, s_=s_: nc.vector.tensor_scalar(out=mbf[0:rows, g * 4 + s_, 0:NBW], in0=sc1[0:rows, 0:NBW], scalar1=m8[0:rows, 15:16],
                                                                            scalar2=None, op0=ALU.is_ge), reads=[r_sc1, r_m8], writes=[r_mbf])
                if not (self.dbg and "branches" in self.dbg and 2 not in self.dbg["branches"]):
                    for h in range(NH):
                        run_branch(2, h)
                for g in range(NG):
                    tpl = [tps.next() for _ in range(3)]
                    for s_ in range(nsub):
                        for bg in range(3):
                            tp, r_tp = tpl[bg]
                            P.op("pe", lambda tp=tp, bg=bg, s_=s_, g=g: nc.tensor.transpose(
                                out=tp[:, s_ * 128:s_ * 128 + rows], in_=mbf[0:rows, g * 4 + s_, bg * 128:(bg + 1) * 128], identity=self.ident[0:rows, 0:rows]),
                                reads=[r_mbf, self.r_const], writes=[r_tp])
                    for bg in range(3):
                        tp, r_tp = tpl[bg]
                        P.op("act", lambda tp=tp, bg=bg, g=g: nc.scalar.activation(out=mneg[:, g, bg, 0:nq], in_=tp[:, 0:nq], func=AF.Identity,
                                                                                  scale=30000.0, bias=-30000.0),
                             reads=[r_tp], writes=[r_mneg])
                for bi in (1,):
                    if self.dbg and "branches" in self.dbg and bi not in self.dbg["branches"]:
                        continue
                    for h in range(NH):
                        run_branch(bi, h)
                if True:
                    P.dma("act", ch_dbg, lambda q0=q0, nq=nq, rows=rows, nsub=nsub: nc.scalar.dma_start(
                        out=self.accd[q0 - Q0:q0 - Q0 + nq, :].rearrange("(s p) d -> p s d", p=rows), in_=acc[0:rows, 0:nsub, :]),
                        reads=[r_acc], writes=[r_out])

            for ti, (q0, nq) in enumerate(QTILES):
                if self.dbg and "tiles" in self.dbg and ti not in self.dbg["tiles"]:
                    continue
                if do_tile(ti, q0, nq) == "stop":
                    return
            P.barrier()


    def cast_jobs(self):
        jobs = []
        for name, src in (("wo_b", self.i_wo), ("wpw_b", self.i_wpw), ("wout_b", self.i_wout), ("wup_b", self.i_wup), ("wdn_b", self.i_wdn)):
            dst = self.wb_d[name]
            R, C = src.shape
            sv = src.rearrange("(p a) c -> p (a c)", p=128)
            dv = dst.rearrange("(p a) c -> p (a c)", p=128)
            F = R * C // 128
            for o in range(0, F, 8192):
                jobs.append((sv, dv, o, min(8192, F - o)))
        return jobs

    def cast_setup(self, st):
        nc, P = self.nc, self.P
        self.cj = self.cast_jobs()
        self.cji = 0
        self.cpend = None
        self.c32 = Pool(P, st, "cst32_", 2, [128, 2048], F32)
        self.cbf = Pool(P, st, "cstbf_", 2, [128, 2048], BF16)
        self.cch_i = [P.chan("cji0"), P.chan("cji1")]
        self.cch_o = [P.chan("cjo0"), P.chan("cjo1")]
        self.r_wb = Res("wb_scratch")

    def cast_step(self, n):
        nc, P = self.nc, self.P
        for _ in range(n):
            if self.cji >= len(self.cj):
                break
            sv, dv, o, w = self.cj[self.cji]
            i = self.cji % 2
            self.cji += 1
            t32, r32 = self.c32.next()
            tbf, rbf = self.cbf.next()
            P.dma("sp", self.cch_i[i], lambda t32=t32, sv=sv, o=o, w=w: nc.sync.dma_start(out=t32[:, 0:w], in_=sv[:, o:o + w]), writes=[r32])
            P.op("pool", lambda t32=t32, tbf=tbf, w=w: nc.gpsimd.tensor_copy(out=tbf[:, 0:w], in_=t32[:, 0:w]), reads=[r32], writes=[rbf])
            if self.cpend is not None:
                self.cpend()
            self.cpend = (lambda i=i, tbf=tbf, dv=dv, o=o, w=w, rbf=rbf: P.dma(
                "sp", self.cch_o[i], lambda: nc.sync.dma_start(out=dv[:, o:o + w], in_=tbf[:, 0:w]), reads=[rbf], writes=[self.r_wb]))
        if self.cji >= len(self.cj) and self.cpend is not None:
            self.cpend()
            self.cpend = None

    def phase_mix(self):
        nc, P = self.nc, self.P
        with ExitStack() as st:
            sbt = lambda name, shape, dt: st.enter_context(nc.sbuf_tensor(name, list(shape), dt))
            g1bc = sbt("m_g1bc", [128, D], F32)
            dww = sbt("m_dww", [128, 16, 31], F32)
            cols = sbt("m_cols", [128, 4, 16], F32)
            r_cst = Res()
            chc = P.chan("m_c")
            P.dma("sp", chc, lambda: nc.sync.dma_start(out=g1bc[:], in_=self.ada_d[2 * D:3 * D].partition_broadcast(128)), writes=[r_cst])
            for c_ in range(16):
                P.dma("sp", chc, lambda c_=c_: nc.sync.dma_start(out=dww[:, c_, :], in_=self.i_dww[:, c_ * 128:(c_ + 1) * 128].rearrange("k p -> p k"),
                                                              allow_slow_non_contiguous=True), writes=[r_cst], cont=True)
            for i, src in enumerate((self.i_dwb, self.i_lng, self.i_lnb, self.i_pwb)):
                P.dma("sp", chc, lambda i=i, src=src: nc.sync.dma_start(out=cols[:, i, :], in_=src.rearrange("(c p) -> p c", p=128),
                                                                      allow_slow_non_contiguous=True), writes=[r_cst], cont=True)
            oT = sbt("m_oT", [128, 16, 512], BF16)
            glu = sbt("m_glu", [128, 16, 544], BF16)
            ybf = sbt("m_ybf", [128, 16, 512], BF16)
            uc = sbt("m_uc", [128, 16, 512], BF16)
            mer = sbt("m_mer", [128, 16, 512], BF16)
            r_oT, r_glu, r_ybf, r_uc, r_mer = [Res() for _ in range(5)]
            accp = Pool(P, st, "m_acc", 2, [128, D], F32)
            accb = Pool(P, st, "m_accb", 2, [128, D], BF16)
            wblk = Pool(P, st, "m_w", 2, [128, 16, 512], BF16)
            dgp = Pool(P, st, "m_dg", 2, [128, 31, 128], BF16)
            ysq = Pool(P, st, "m_ysq", 2, [128, 512], BF16)
            mgp = Pool(P, st, "m_mg", 4, [128, 512], BF16)
            tmpf = Pool(P, st, "m_tmp", 3, [128, 512], F32)
            xp = Pool(P, st, "m_x", 3, [128, 512], F32)
            stat = sbt("m_stat", [128, 3, 512], F32)
            r_stat = Res()
            chl = [P.chan(f"m_l{i}") for i in range(4)]
            chw = [P.chan("m_w0"), P.chan("m_w1")]
            chs = [P.chan(f"m_s{i}") for i in range(3)]
            mm = Pool(P, st, "m_mm", 3, [128, 512], F32, psum=True)
            sps = Pool(P, st, "m_sp", 2, [128, 512], F32, psum=True)
            tps = Pool(P, st, "m_tp", 2, [128, 512], BF16, psum=True)
            r_in, r_out = self.r_scr_all, Res("xmid")
            cnt = {"l": 0, "w": 0, "s": 0}

            def load_wblk(wsrc, cb):
                wt, wr = wblk.next()
                ch = chw[cnt["w"] % 2]
                cnt["w"] += 1
                P.dma("sp", ch, lambda: nc.sync.dma_start(out=wt[:], in_=wsrc[:, cb * 512:(cb + 1) * 512].rearrange("(k p) c -> p k c", p=128)),
                      reads=[self.r_wb], writes=[wr])
                return wt, wr

            def mix_tile(ti, q0, nq):
                nsub, rows = max(1, nq // 128), min(128, nq)
                for s_ in range(nsub):
                    at, r_at = accp.next()
                    ab, r_ab = accb.next()
                    ch = chl[cnt["l"] % 4]
                    cnt["l"] += 1
                    P.dma("sp", ch, lambda at=at, s_=s_: nc.sync.dma_start(out=at[0:rows, :], in_=self.accd[q0 - Q0 + s_ * 128:q0 - Q0 + s_ * 128 + rows, :]),
                          reads=[r_in], writes=[r_at])
                    P.op("act", lambda at=at, ab=ab: nc.scalar.copy(out=ab[0:rows, :], in_=at[0:rows, :]), reads=[r_at], writes=[r_ab])
                    for hq in range(4):
                        tp, r_tp = tps.next()
                        for hl in range(4):
                            h = hq * 4 + hl
                            P.op("pe", lambda tp=tp, hl=hl, h=h, ab=ab: nc.tensor.transpose(
                                out=tp[:, hl * 128:hl * 128 + rows], in_=ab[0:rows, h * 128:(h + 1) * 128], identity=self.ident[0:rows, 0:rows]),
                                reads=[r_ab, self.r_const], writes=[r_tp])
                        P.op("dve", lambda tp=tp, hq=hq, s_=s_: nc.vector.tensor_copy(
                            out=oT[:, hq * 4:hq * 4 + 4, s_ * 128:s_ * 128 + rows],
                            in_=tp[:, :].rearrange("p (h q) -> p h q", h=4)[:, :, 0:rows]), reads=[r_tp], writes=[r_oT])
                for cb in range(4):
                    wt, wr = load_wblk(self.wb_d["wo_b"], cb)
                    for cl in range(4):
                        c = cb * 4 + cl
                        ps, r_ps = mm.next()
                        for kc in range(16):
                            P.op("pe", lambda ps=ps, kc=kc, cl=cl, wt=wt: nc.tensor.matmul(
                                ps[:, 0:nq], lhsT=wt[:, kc, cl * 128:(cl + 1) * 128], rhs=oT[:, kc, 0:nq], start=(kc == 0), stop=(kc == 15)),
                                reads=[wr, r_oT], writes=[r_ps])
                        mg, r_mg = mgp.next()
                        ch = chl[cnt["l"] % 4]
                        cnt["l"] += 1
                        P.dma("sp", ch, lambda mg=mg, c=c: nc.sync.dma_start(out=mg[:, 0:nq], in_=self.mgT[:, c, q0 - B0:q0 - B0 + nq]),
                              reads=[r_in], writes=[r_mg])
                        P.op("dve", lambda ps=ps, mg=mg, c=c: nc.vector.tensor_tensor(out=mer[:, c, 0:nq], in0=ps[:, 0:nq], in1=mg[:, 0:nq], op=ALU.mult),
                             reads=[r_ps, r_mg], writes=[r_mer])
                P.dma("sp", chl[cnt["l"] % 4], lambda: nc.sync.dma_start(out=glu[:, :, 0:nq + 32], in_=self.gluT[:, :, q0 - 32 - B0:q0 - B0 + nq]),
                      reads=[r_in], writes=[r_glu])
                cnt["l"] += 1
                s_sum, r_ssum = sps.next()
                s_sq, r_ssq = sps.next()
                for chn in range(16):
                    dg, r_dg = dgp.next()
                    for k in range(31):
                        P.op("dve", lambda dg=dg, k=k, chn=chn: nc.vector.tensor_scalar(
                            out=dg[:, k, :], in0=self.ident[:], scalar1=dww[:, chn, k:k + 1], scalar2=None, op0=ALU.mult),
                            reads=[self.r_const, r_cst], writes=[r_dg])
                    ps, r_ps = mm.next()
                    for k in range(31):
                        P.op("pe", lambda ps=ps, dg=dg, k=k, chn=chn: nc.tensor.matmul(
                            ps[:, 0:nq], lhsT=dg[:, k, :], rhs=glu[:, chn, k + 2:k + 2 + nq], start=(k == 0), stop=(k == 30)),
                            reads=[r_dg, r_glu], writes=[r_ps])
                    P.op("act", lambda ps=ps, chn=chn: nc.scalar.activation(out=ybf[:, chn, 0:nq], in_=ps[:, 0:nq], func=AF.Identity,
                                                                          bias=cols[:, 0, chn:chn + 1]), reads=[r_ps, r_cst], writes=[r_ybf])
                    yq, r_yq = ysq.next()
                    P.op("act", lambda ps=ps, chn=chn, yq=yq: nc.scalar.activation(out=yq[:, 0:nq], in_=ps[:, 0:nq], func=AF.Square,
                                                                                 bias=cols[:, 0, chn:chn + 1]), reads=[r_ps, r_cst], writes=[r_yq])
                    P.op("pe", lambda chn=chn: nc.tensor.matmul(s_sum[:, 0:nq], lhsT=self.ones[:], rhs=ybf[:, chn, 0:nq],
                                                                start=(chn == 0), stop=(chn == 15)), reads=[r_ybf, self.r_const], writes=[r_ssum])
                    P.op("pe", lambda chn=chn, yq=yq: nc.tensor.matmul(s_sq[:, 0:nq], lhsT=self.ones[:], rhs=yq[:, 0:nq],
                                                                       start=(chn == 0), stop=(chn == 15)), reads=[r_yq, self.r_const], writes=[r_ssq])
                mean, rstd, msq = stat[:, 0, 0:nq], stat[:, 1, 0:nq], stat[:, 2, 0:nq]
                P.op("dve", lambda: nc.vector.tensor_scalar(out=mean, in0=s_sum[:, 0:nq], scalar1=1.0 / D, scalar2=None, op0=ALU.mult),
                     reads=[r_ssum], writes=[r_stat])
                P.op("dve", lambda: nc.vector.tensor_tensor(out=msq, in0=mean, in1=mean, op=ALU.mult), reads=[r_stat], writes=[r_stat])
                P.op("dve", lambda: nc.vector.scalar_tensor_tensor(out=rstd, in0=s_sq[:, 0:nq], scalar=1.0 / D, in1=msq, op0=ALU.mult, op1=ALU.subtract),
                     reads=[r_ssq, r_stat], writes=[r_stat])
                P.op("dve", lambda: nc.vector.tensor_scalar(out=rstd, in0=rstd, scalar1=EPS, scalar2=None, op0=ALU.add), reads=[r_stat], writes=[r_stat])
                P.op("act", lambda: nc.scalar.sqrt(out=rstd, in_=rstd), reads=[r_stat], writes=[r_stat])
                P.op("dve", lambda: nc.vector.reciprocal(out=rstd, in_=rstd), reads=[r_stat], writes=[r_stat])
                for chn in range(16):
                    tf, r_tf = tmpf.next()
                    P.op("dve", lambda tf=tf, chn=chn: nc.vector.tensor_tensor(out=tf[:, 0:nq], in0=ybf[:, chn, 0:nq], in1=mean, op=ALU.subtract),
                         reads=[r_ybf, r_stat], writes=[r_tf])
                    P.op("dve", lambda tf=tf: nc.vector.tensor_tensor(out=tf[:, 0:nq], in0=tf[:, 0:nq], in1=rstd, op=ALU.mult),
                         reads=[r_tf, r_stat], writes=[r_tf])
                    P.op("act", lambda tf=tf, chn=chn: nc.scalar.activation(out=uc[:, chn, 0:nq], in_=tf[:, 0:nq], func=AF.Silu,
                                                                          scale=cols[:, 1, chn:chn + 1], bias=cols[:, 2, chn:chn + 1]),
                         reads=[r_tf, r_cst], writes=[r_uc])
                for cb in range(4):
                    wt, wr = load_wblk(self.wb_d["wpw_b"], cb)
                    for cl in range(4):
                        c = cb * 4 + cl
                        ps, r_ps = mm.next()
                        for kc in range(16):
                            P.op("pe", lambda ps=ps, kc=kc, cl=cl, wt=wt: nc.tensor.matmul(
                                ps[:, 0:nq], lhsT=wt[:, kc, cl * 128:(cl + 1) * 128], rhs=uc[:, kc, 0:nq], start=(kc == 0), stop=(kc == 15)),
                                reads=[wr, r_uc], writes=[r_ps])
                        mg, r_mg = mgp.next()
                        ch = chl[cnt["l"] % 4]
                        cnt["l"] += 1
                        P.dma("sp", ch, lambda mg=mg, c=c: nc.sync.dma_start(out=mg[:, 0:nq], in_=self.mgT[:, 16 + c, q0 - B0:q0 - B0 + nq]),
                              reads=[r_in], writes=[r_mg])
                        tf, r_tf = tmpf.next()
                        P.op("dve", lambda ps=ps, mg=mg, c=c, tf=tf: nc.vector.scalar_tensor_tensor(
                            out=tf[:, 0:nq], in0=ps[:, 0:nq], scalar=cols[:, 3, c:c + 1], in1=mg[:, 0:nq], op0=ALU.add, op1=ALU.mult),
                            reads=[r_ps, r_mg, r_cst], writes=[r_tf])
                        P.op("dve", lambda c=c, tf=tf: nc.vector.tensor_tensor(out=mer[:, c, 0:nq], in0=mer[:, c, 0:nq], in1=tf[:, 0:nq], op=ALU.add),
                             reads=[r_tf, r_mer], writes=[r_mer])
                for cb in range(4):
                    wt, wr = load_wblk(self.wb_d["wout_b"], cb)
                    for s_ in range(nsub):
                        ps, r_ps = mm.next()
                        for kc in range(16):
                            P.op("pe", lambda ps=ps, kc=kc, s_=s_, wt=wt: nc.tensor.matmul(
                                ps[0:rows, :], lhsT=mer[:, kc, s_ * 128:s_ * 128 + rows], rhs=wt[:, kc, :], start=(kc == 0), stop=(kc == 15)),
                                reads=[wr, r_mer], writes=[r_ps])
                        xt, r_xt = xp.next()
                        i3 = cnt["s"] % 3
                        cnt["s"] += 1
                        r0 = q0 + s_ * 128
                        P.dma("sp", chs[i3], lambda xt=xt, r0=r0, cb=cb: nc.sync.dma_start(out=xt[0:rows, :], in_=self.xv[r0:r0 + rows, cb * 512:(cb + 1) * 512]),
                              writes=[r_xt])
                        tf, r_tf = tmpf.next()
                        P.op("dve", lambda ps=ps, tf=tf, cb=cb: nc.vector.tensor_tensor(out=tf[0:rows, :], in0=ps[0:rows, :], in1=g1bc[0:rows, cb * 512:(cb + 1) * 512],
                                                                                     op=ALU.mult), reads=[r_ps, r_cst], writes=[r_tf])
                        P.op("dve", lambda xt=xt, tf=tf: nc.vector.tensor_tensor(out=xt[0:rows, :], in0=xt[0:rows, :], in1=tf[0:rows, :], op=ALU.add),
                             reads=[r_tf, r_xt], writes=[r_xt])
                        P.dma("act", chs[i3], lambda xt=xt, r0=r0, cb=cb: nc.scalar.dma_start(
                            out=self.xmid[r0 - Q0:r0 - Q0 + rows, cb * 512:(cb + 1) * 512], in_=xt[0:rows, :]), reads=[r_xt], writes=[r_out])

            for ti, (q0, nq) in enumerate(QTILES):
                if self.dbg and "tiles" in self.dbg and ti not in self.dbg["tiles"]:
                    continue
                mix_tile(ti, q0, nq)
            P.barrier()


    def phase_ffn(self):
        nc, P = self.nc, self.P
        with ExitStack() as st:
            sbt = lambda name, shape, dt: st.enter_context(nc.sbuf_tensor(name, list(shape), dt))
            g2bc = sbt("f_g2bc", [128, D], F32)
            fgbc = sbt("f_fgbc", [128, D], F32)
            n2g = sbt("f_n2g", [128, 16], F32)
            s2 = sbt("f_s2", [128, 16], F32)
            fdw = sbt("f_fdw", [128, 3, 88], F32)
            fdb = sbt("f_fdb", [128, 88], F32)
            hfl = sbt("f_hfl", [128, 1], F32)
            r_cst, r_s2 = Res(), Res()
            chc = P.chan("f_c")
            P.dma("sp", chc, lambda: nc.sync.dma_start(out=g2bc[:], in_=self.ada_d[5 * D:6 * D].partition_broadcast(128)), writes=[r_cst])
            P.dma("sp", chc, lambda: nc.sync.dma_start(out=fgbc[:], in_=self.i_fg.partition_broadcast(128)), writes=[r_cst], cont=True)
            P.dma("sp", chc, lambda: nc.sync.dma_start(out=n2g[:], in_=self.i_n2g.rearrange("(c p) -> p c", p=128), allow_slow_non_contiguous=True),
                  writes=[r_cst], cont=True)
            for k in range(3):
                P.dma("sp", chc, lambda k=k: nc.sync.dma_start(out=fdw[:, k, :], in_=self.i_fdw[k, :].rearrange("(c p) -> p c", p=128),
                                                            allow_slow_non_contiguous=True), writes=[r_cst], cont=True)
            P.dma("sp", chc, lambda: nc.sync.dma_start(out=fdb[:], in_=self.i_fdb.rearrange("(c p) -> p c", p=128), allow_slow_non_contiguous=True),
                  writes=[r_cst], cont=True)
            P.dma("sp", chc, lambda: nc.sync.dma_start(out=hfl[:], in_=self.i_hflag[:, :]), writes=[r_cst], cont=True)
            P.op("dve", lambda: nc.vector.scalar_tensor_tensor(out=s2[:], in0=self.ada[:, 64:80], scalar=1.0, in1=n2g[:], op0=ALU.add, op1=ALU.mult),
                 reads=[self.r_ada, r_cst], writes=[r_s2])
            xpool = Pool(P, st, "f_x", 1, [128, 4, D], F32)
            chx = P.chan("f_x")
            junk = sbt("f_junk", [128, D], BF16)
            r_junk = Res()
            sspool = Pool(P, st, "f_ss", 2, [128, 4], F32)
            rspool = Pool(P, st, "f_rs", 2, [128, 4], F32)
            xnpool = Pool(P, st, "f_xn", 1, [128, 4, D], BF16)
            h2T = sbt("f_h2T", [128, 16, 514], BF16)
            r_h2T = Res()
            tpp = Pool(P, st, "f_tp", 2, [128, 512], BF16, psum=True)
            up = Pool(P, st, "f_up", 2, [128, 1024], F32, psum=True)
            dn = Pool(P, st, "f_dn", 2, [128, 512], F32, psum=True)
            wup = Pool(P, st, "f_wu", 2, [128, 16, 256], BF16)
            wdn = Pool(P, st, "f_wd", 2, [128, 44, 256], BF16)
            chwu = [P.chan("f_wu0"), P.chan("f_wu1")]
            chwd = [P.chan("f_wd0"), P.chan("f_wd1")]
            z = sbt("f_z", [128, 44, 512], BF16)
            r_z = Res()
            hh, r_hh = z[:, 0:16, 0:128], r_z
            Tp = Pool(P, st, "f_T", 3, [128, 512], F32)
            sgp = Pool(P, st, "f_sg", 1, [128, 512], BF16)
            tmp = Pool(P, st, "f_tmp", 1, [128, 256], F32)
            cho = [P.chan("f_o0"), P.chan("f_o1")]
            r_in = Res()
            wupb, wdnb = self.wb_d["wup_b"], self.wb_d["wdn_b"]
            cnt = {"u": 0, "d": 0, "o": 0}

            class HP:
                def __init__(s_, ap, res):
                    s_.ap, s_.res = ap, res

                def next(s_):
                    return s_.ap, s_.res

            self.norm_T(P, st, self.xmid, 0, 1, xpool, chx, junk, r_junk, sspool, rspool, xnpool, HP(hh, r_hh), tpp,
                        self.ident, self.r_const, s2, r_s2, self.ada, self.r_ada, 48)
            P.op("dve", lambda: nc.vector.tensor_scalar(out=h2T[:, :, 0:2], in0=hh[:, :, 62:64], scalar1=hfl[:, 0:1], scalar2=None, op0=ALU.mult),
                 reads=[r_hh, r_cst], writes=[r_h2T])

            def window(w):
                row0 = HALO + 512 * w
                if w > 0:
                    P.op("pool", lambda: nc.gpsimd.tensor_copy(out=h2T[:, :, 0:2], in_=h2T[:, :, 512:514]), reads=[r_h2T], writes=[r_h2T])
                self.norm_T(P, st, self.xmid, row0, 4, xpool, chx, junk, r_junk, sspool, rspool, xnpool, HP(h2T[:, :, 2:514], r_h2T), tpp,
                            self.ident, self.r_const, s2, r_s2, self.ada, self.r_ada, 48)
                xt, r_xt = self.last_xt
                for pb in range(44):
                    wt, wr = wup.next()
                    ch = chwu[cnt["u"] % 2]
                    cnt["u"] += 1
                    P.dma("sp", ch, lambda wt=wt, pb=pb: nc.sync.dma_start(
                        out=wt[:, :, 0:128], in_=wupb[:, pb * 128:(pb + 1) * 128].rearrange("(k p) c -> p k c", p=128)), reads=[self.r_wb], writes=[wr])
                    P.dma("sp", ch, lambda wt=wt, pb=pb: nc.sync.dma_start(
                        out=wt[:, :, 128:256], in_=wupb[:, FF + pb * 128:FF + (pb + 1) * 128].rearrange("(k p) c -> p k c", p=128)),
                        reads=[self.r_wb], writes=[wr], cont=True)
                    for cl in range(1):
                        c = pb
                        Ts = []
                        for half in range(2):
                            cc = c + 44 * half
                            woff = half * 128
                            ps, r_ps = up.next()
                            for kc in range(16):
                                P.op("pe", lambda ps=ps, kc=kc, woff=woff, wt=wt: nc.tensor.matmul(
                                    ps[:, 512:1024], lhsT=wt[:, kc, woff:woff + 128], rhs=h2T[:, kc, 2:514], start=(kc == 0), stop=(kc == 15)),
                                    reads=[wr, r_h2T], writes=[r_ps])
                            for kc in range(16):
                                P.op("pe", lambda ps=ps, kc=kc, woff=woff, wt=wt: nc.tensor.matmul(
                                    ps[:, 510:512], lhsT=wt[:, kc, woff:woff + 128], rhs=h2T[:, kc, 0:2], start=(kc == 0), stop=(kc == 15)),
                                    reads=[wr, r_h2T], writes=[r_ps])
                            T_, r_T = Tp.next()
                            P.op("act", lambda ps=ps, T_=T_, cc=cc: nc.scalar.activation(out=T_[:], in_=ps[:, 512:1024], func=AF.Identity,
                                                                                        scale=fdw[:, 2, cc:cc + 1], bias=fdb[:, cc:cc + 1]),
                                 reads=[r_ps, r_cst], writes=[r_T])
                            P.op("dve", lambda ps=ps, T_=T_, cc=cc: nc.vector.scalar_tensor_tensor(
                                out=T_[:], in0=ps[:, 511:1023], scalar=fdw[:, 1, cc:cc + 1], in1=T_[:], op0=ALU.mult, op1=ALU.add),
                                reads=[r_ps, r_T, r_cst], writes=[r_T])
                            P.op("dve", lambda ps=ps, T_=T_, cc=cc: nc.vector.scalar_tensor_tensor(
                                out=T_[:], in0=ps[:, 510:1022], scalar=fdw[:, 0, cc:cc + 1], in1=T_[:], op0=ALU.mult, op1=ALU.add),
                                reads=[r_ps, r_T, r_cst], writes=[r_T])
                            Ts.append((T_, r_T))
                        (Ta, r_Ta), (Tg, r_Tg) = Ts
                        sg, r_sg = sgp.next()
                        P.op("act", lambda Tg=Tg, sg=sg: nc.scalar.activation(out=sg[:], in_=Tg[:], func=AF.Silu), reads=[r_Tg], writes=[r_sg])
                        P.op("dve", lambda Ta=Ta, sg=sg, c=c: nc.vector.tensor_tensor(out=z[:, c, :], in0=Ta[:], in1=sg[:], op=ALU.mult),
                             reads=[r_Ta, r_sg], writes=[r_z])
                for cb in range(8):
                    wt, wr = wdn.next()
                    ch = chwd[cnt["d"] % 2]
                    cnt["d"] += 1
                    P.dma("sp", ch, lambda wt=wt, cb=cb: nc.sync.dma_start(
                        out=wt[:], in_=wdnb[:, cb * 256:(cb + 1) * 256].rearrange("(k p) c -> p k c", p=128)), reads=[self.r_wb], writes=[wr])
                    for s_ in range(4):
                        ps, r_ps = dn.next()
                        for kc in range(44):
                            P.op("pe", lambda ps=ps, kc=kc, s_=s_, wt=wt: nc.tensor.matmul(
                                ps[:, 0:256], lhsT=z[:, kc, s_ * 128:(s_ + 1) * 128], rhs=wt[:, kc, :], start=(kc == 0), stop=(kc == 43)),
                                reads=[wr, r_z], writes=[r_ps])
                        tf, r_tf = tmp.next()
                        P.op("dve", lambda ps=ps, tf=tf, cb=cb: nc.vector.tensor_tensor(out=tf[:], in0=ps[:, 0:256], in1=g2bc[:, cb * 256:(cb + 1) * 256], op=ALU.mult),
                             reads=[r_ps, r_cst], writes=[r_tf])
                        P.op("dve", lambda tf=tf, s_=s_, cb=cb, xt=xt: nc.vector.tensor_tensor(
                            out=xt[:, s_, cb * 256:(cb + 1) * 256], in0=xt[:, s_, cb * 256:(cb + 1) * 256], in1=tf[:], op=ALU.add),
                            reads=[r_tf, r_xt], writes=[r_xt])
                ss, r_ss = sspool.next()
                rs, r_rs = rspool.next()
                for s_ in range(4):
                    P.op("act", lambda s_=s_, xt=xt, ss=ss: nc.scalar.activation(out=junk[:], in_=xt[:, s_, :], func=AF.Square, accum_out=ss[:, s_:s_ + 1]),
                         reads=[r_xt], writes=[r_junk, r_ss])
                P.op("dve", lambda ss=ss, rs=rs: nc.vector.tensor_scalar(out=rs[:], in0=ss[:], scalar1=1.0 / D, scalar2=EPS, op0=ALU.mult, op1=ALU.add),
                     reads=[r_ss], writes=[r_rs])
                P.op("act", lambda rs=rs: nc.scalar.sqrt(out=rs[:], in_=rs[:]), reads=[r_rs], writes=[r_rs])
                P.op("dve", lambda rs=rs: nc.vector.reciprocal(out=rs[:], in_=rs[:]), reads=[r_rs], writes=[r_rs])
                for s_ in range(4):
                    P.op("dve", lambda s_=s_, xt=xt, rs=rs: nc.vector.scalar_tensor_tensor(
                        out=xt[:, s_, :], in0=xt[:, s_, :], scalar=rs[:, s_:s_ + 1], in1=fgbc[:], op0=ALU.mult, op1=ALU.mult),
                        reads=[r_xt, r_rs, r_cst], writes=[r_xt])
                    i2 = cnt["o"] % 2
                    cnt["o"] += 1
                    t0 = 512 * w + 128 * s_
                    P.dma("act", cho[i2], lambda s_=s_, xt=xt, t0=t0: nc.scalar.dma_start(out=self.out[t0:t0 + 128, :], in_=xt[:, s_, :]), reads=[r_xt])

            for w in range(4):
                if self.dbg and "wins" in self.dbg and w not in self.dbg["wins"]:
                    continue
                window(w)
            P.barrier()

_STATIC = {}


def static_tables():
    if _STATIC:
        return _STATIC
    sl = np.array(SLOPES, np.float64)
    i = np.arange(128, dtype=np.float64)
    bs = sl[None, :, None] * (i[:, None, None] + 64.0 * (np.arange(NDS)[None, None, :] - DOFF))
    bc = sl[None, :, None] * (16.0 * i[:, None, None] + 31.0 + 64.0 * (np.arange(NDC)[None, None, :] - ROFF))
    masks = np.zeros((128, NMASK, 512), np.float32)
    for n, v in enumerate(MASK_LIST):
        masks[:, n, :v.shape[1]] = np.where(v, 0.0, -30000.0)
    E = np.zeros((128, 64, 128), np.float32)
    for u in range(64):
        for k in range(128):
            E[2 * u + k // 64, u, k] = 1.0
    Ov = np.zeros((128, 9, NBW), np.float32)
    for jj in range(9):
        for ii in range(128):
            for b in range(NBLKW):
                dlt = ii - 128 * (jj + 1) - 4 * b + 1056
                if -1 <= dlt <= 3:
                    Ov[ii, jj, b] = 1.0
    _STATIC.update({"ident": np.eye(128, dtype=np.float32).astype(NPBF),
                    "bias_s": bs.astype(np.float32), "bias_c": bc.astype(np.float32),
                    "masks": masks.astype(NPBF), "esel": E.astype(NPBF), "ovm": Ov.astype(NPBF),
                    "ones_bf": np.ones((128, 128), np.float32).astype(NPBF)})
    return _STATIC


def host_tables(c):
    t_start = c * TOWN
    real = np.arange(TV) - OWN0 + t_start
    vt = (real >= 0).astype(np.float32)
    nv = np.arange(NCV)
    rn = nv - (OWN0 - t_start) // 16
    vc = ((rn >= 0) & (rn <= 1022) & (nv <= 1150)).astype(np.float32)
    selb = np.zeros((5, 4, 128, NBW), np.float32)
    selv = np.zeros((5, 4, 128, NBW), np.float32)
    b = np.arange(NBLKW)
    for ti, (q0, nq) in enumerate(QTILES):
        qend = q0 + nq
        jv = b + qend // 64 - NBLKW
        realb = jv - (OWN0 - t_start) // 64
        for s_ in range(max(1, nq // 128)):
            rows = min(128, nq)
            tq = q0 + 128 * s_ + np.arange(rows)
            cur = tq // 64
            valid = (realb[None, :] >= 0) & (jv[None, :] <= cur[:, None])
            forced = (realb[None, :] == 0) | (jv[None, :] == cur[:, None]) | (jv[None, :] == cur[:, None] - 1)
            selv[ti, s_, :rows, :NBLKW] = valid
            selb[ti, s_, :rows, :NBLKW] = 1.0 + 1e6 * forced
    d = {"vtok": np.ascontiguousarray(vt.reshape(TV // 128, 128).T),
         "vcmp": np.ascontiguousarray(vc.reshape(NCV // 128, 128).T),
         "selb": selb, "selv": selv,
         "hflag": np.full((128, 1), 1.0 if c > 0 else 0.0, np.float32)}
    d.update(static_tables())
    return d


def make_inputs(inputs, c):
    x = np.asarray(inputs["x"], np.float32)[0]
    t_start = c * TOWN
    xv = np.zeros((TV, D), np.float32)
    lo = OWN0 - t_start
    xv[lo:] = x[:t_start + TOWN]
    m = {"xv": xv, "c": np.asarray(inputs["c"], np.float32),
         "w_ada": np.asarray(inputs["w_ada"], np.float32)[0], "b_ada": np.asarray(inputs["b_ada"], np.float32)[0],
         "norm1_g": np.asarray(inputs["norm1_g"], np.float32)[0], "w_in": np.asarray(inputs["w_in"], np.float32)[0]}
    for n in ("cmp_pe", "w_kc1", "w_kc2", "w_vc1", "w_vc2", "w_o_nsa", "conv_dw_w", "conv_dw_b", "conv_ln_g", "conv_ln_b",
              "conv_pw_w", "conv_pw_b", "w_out", "norm2_g", "ffn_w_up", "ffn_dw_w", "ffn_dw_b", "ffn_w_down"):
        m[n] = np.asarray(inputs[n], np.float32)[0]
    m["final_g"] = np.asarray(inputs["final_g"], np.float32)
    m.update(host_tables(c))
    return m


def kernel(**inputs):
    k = K()
    nc = k.build()
    in_maps = [{n: v for n, v in make_inputs(inputs, c).items() if n in k.ins} for c in range(NCORE)]
    res = run_bass_kernel_spmd(nc, in_maps, core_ids=list(range(NCORE)))
    outs = [np.asarray(res.results[c]["out"], np.float32) for c in range(NCORE)]
    return np.concatenate(outs, axis=0)[None]
```

```python
from contextlib import ExitStack
import numpy as np
import ml_dtypes
import concourse.bass as bass
import concourse.mybir as mybir
from concourse.bass_utils import run_bass_kernel_spmd

F32 = mybir.dt.float32
BF16 = mybir.dt.bfloat16
I32 = mybir.dt.int32
AF = mybir.ActivationFunctionType
ALU = mybir.AluOpType
NPBF = ml_dtypes.bfloat16

D = 2048
T = 16384
NCORE = 8
TOWN = T // NCORE
NH, NG, HG, DK = 16, 2, 8, 128
EPS = 1e-6
FF = 5632
WIN = 512
OWN0 = 16384
TV = OWN0 + TOWN
HALO = 64
Q0 = OWN0 - HALO
NQ = TOWN + HALO
W0 = OWN0 - 640
NA = TV - W0
B0 = OWN0 - 128
NB = TV - B0
NCV = TV // 16
C_Q, C_KC, C_VC, C_KS, C_VS, C_KW, C_VW, C_GN, C_GLU, C_GM = 0, 2048, 2304, 2560, 2816, 3072, 3328, 3584, 3632, 7728
INW = 11824
EPOCH = 12000
SLOPES = [2.0 ** (-8.0 * (h + 1) / 16) for h in range(NH)]
NBLKW = 264
NBW = 272
DOFF, NDS = 272, 280
ROFF, NDC = 296, 280
QTILES = [(Q0, 64)] + [(OWN0 + 512 * i, 512) for i in range(4)]
SKIP = 64.0


def exp_width(h, nq):
    w = 64
    while w * 2 <= min(512, nq) and SLOPES[h] * (w * 2) <= 64.0:
        w *= 2
    return min(w, nq)


def n_slc_chunks(h, nq, maxc):
    return int(min(maxc, np.floor((SKIP / SLOPES[h] + nq) / 128) + 1))


def n_cmp_chunks(h, nq):
    return int(min(9, np.floor((SKIP / SLOPES[h] + nq + 15) / 2048) + 1))


def chunk_valid(kind, nq, j):
    i = np.arange(128)[:, None]
    q = np.arange(nq)[None, :]
    if kind == "cmp":
        kp = 16 * i + 31 + nq - 2048 * (j + 1)
        v = kp <= q
    else:
        kp = i + nq - 128 * (j + 1)
        dist = q - kp
        v = dist >= 0
        if kind == "win":
            v = v & (dist < WIN)
    return None if v.all() else v


def mask_index():
    idx = {}
    for nq in (64, 512):
        nwin = 8 if nq == 512 else 5
        for kind, nj in (("slc", 4), ("win", nwin), ("cmp", 1)):
            for j in range(nj):
                v = chunk_valid(kind, nq, j)
                if v is None:
                    continue
                idx[(kind, nq, j)] = v
    uniq, out = [], {}
    for key, v in idx.items():
        for n, u in enumerate(uniq):
            if u.shape == v.shape and (u == v).all():
                out[key] = n
                break
        else:
            uniq.append(v)
            out[key] = len(uniq) - 1
    return out, uniq


MASK_IDX, MASK_LIST = mask_index()
NMASK = len(MASK_LIST)


class Res:
    __slots__ = ("name", "last_w", "readers")

    def __init__(self, name=""):
        self.name = name
        self.last_w = None
        self.readers = []


class Op:
    __slots__ = ("eng", "fn", "deps", "seq", "sig", "is_dma", "chan", "needs_sig", "pos")


class Chan:
    def __init__(self, prog, name):
        self.sem = prog.new_sem("ch_" + name)
        self.count = 0
        self.last_op = None
        self.group = []


class Prog:
    ENGS = ("pe", "act", "dve", "pool", "sp")

    def __init__(self, nc, stack):
        self.nc = nc
        self.stack = stack
        self.ops = {e: [] for e in self.ENGS}
        self.seq = 0
        self.nsem = 0
        self.eng_sems = {e: [] for e in self.ENGS}
        self.chans = []
        self.uid = 0

    def new_sem(self, name):
        self.nsem += 1
        return self.stack.enter_context(self.nc.semaphore(name))

    def chan(self, name):
        c = Chan(self, name)
        self.chans.append(c)
        return c

    def _mk(self, eng, fn, reads, writes):
        op = Op()
        op.eng, op.fn, op.seq = eng, fn, self.seq
        self.seq += 1
        op.is_dma, op.chan, op.needs_sig, op.sig = False, None, False, None
        deps = []
        for r in reads:
            if r.last_w is not None:
                deps.append(r.last_w)
        for w in writes:
            if w.last_w is not None:
                deps.append(w.last_w)
            deps.extend(w.readers)
        for r in reads:
            if not getattr(op, "is_dma", False) and eng in ("pe", "act", "dve"):
                r.readers = [x for x in r.readers if x.eng != eng or x.is_dma]
            r.readers.append(op)
        for w in writes:
            w.last_w = op
            w.readers = []
        seen, dd = set(), []
        for d in deps:
            if id(d) in seen or d is op:
                continue
            seen.add(id(d))
            if eng == "pe" and d.eng == "pe" and not d.is_dma:
                continue
            dd.append(d)
        op.deps = dd
        self.ops[eng].append(op)
        return op

    def op(self, eng, fn, reads=(), writes=()):
        return self._mk(eng, fn, list(reads), list(writes))

    def dma(self, queue, chan, fn, reads=(), writes=(), cont=False):
        op = self._mk(queue, fn, list(reads), list(writes))
        op.is_dma, op.chan = True, chan
        if not cont:
            if chan.last_op is not None and chan.last_op not in op.deps:
                op.deps.append(chan.last_op)
            chan.group = []
        op.deps = [d for d in op.deps if d not in chan.group]
        chan.count += 16
        chan.group.append(op)
        for o in chan.group:
            o.sig = (chan.sem, chan.count)
        op.needs_sig = True
        chan.last_op = op
        return op

    def barrier(self):
        lasts = []
        for e in self.ENGS:
            for op in reversed(self.ops[e]):
                if not op.is_dma:
                    lasts.append(op)
                    break
        for c in self.chans:
            if c.last_op is not None:
                lasts.append(c.last_op)
        nc = self.nc
        eo = {"pe": nc.tensor, "act": nc.scalar, "dve": nc.vector, "pool": nc.gpsimd, "sp": nc.sync}
        for e in self.ENGS:
            op = self._mk(e, (lambda e=e: eo[e].nop()), [], [])
            op.deps = [d for d in lasts if not (d.eng == e and not d.is_dma)]

    def emit(self):
        nc = self.nc
        for e in self.ENGS:
            for op in self.ops[e]:
                for d in op.deps:
                    if not d.is_dma:
                        d.needs_sig = True
        for e in self.ENGS:
            cnt, sem = 0, None
            for op in self.ops[e]:
                if op.is_dma or not op.needs_sig:
                    continue
                if sem is None or cnt >= EPOCH:
                    sem = self.new_sem(f"e_{e}_{len(self.eng_sems[e])}")
                    self.eng_sems[e].append(sem)
                    cnt = 0
                cnt += 1
                op.sig = (sem, cnt)
        with nc.Block() as block:
            for e in self.ENGS:
                ops = self.ops[e]
                if not ops:
                    continue
                deco = {"pe": block.tensor, "act": block.scalar, "dve": block.vector,
                        "pool": block.gpsimd, "sp": block.sync}[e]

                def body(engobj, ops=ops):
                    known = {}
                    for op in ops:
                        for d in op.deps:
                            sem, val = d.sig
                            if known.get(id(sem), 0) >= val:
                                continue
                            engobj.wait_ge(sem, val)
                            known[id(sem)] = val
                        ins = op.fn()
                        if op.needs_sig:
                            ins.then_inc(op.sig[0], 16 if op.is_dma else 1)
                    last = {}
                    for op in ops:
                        if op.is_dma:
                            last[id(op.chan)] = op
                    for op in last.values():
                        sem, val = op.sig
                        if known.get(id(sem), 0) < val:
                            engobj.wait_ge(sem, val)
                            known[id(sem)] = val
                deco(body)


class Pool:
    def __init__(self, P, st, name, n, shape, dt, psum=False):
        self.t, self.r = [], []
        for i in range(n):
            if psum:
                self.t.append(st.enter_context(P.nc.psum_tensor(f"{name}{i}", list(shape), dt)))
            else:
                self.t.append(st.enter_context(P.nc.sbuf_tensor(f"{name}{i}", list(shape), dt)))
            self.r.append(Res(f"{name}{i}"))
        self.i = 0
        self.n = n

    def next(self):
        k = self.i % self.n
        self.i += 1
        return self.t[k], self.r[k]


class K:
    def __init__(self, dbg=None):
        self.dbg = dbg
        nc = self.nc = bass.Bass("TRN2", target_bir_lowering=False)
        self.ins = {}
        self.outs = {}

    def din(self, name, shape, dt=F32):
        t = self.nc.dram_tensor(name, list(shape), dt, kind="ExternalInput").ap()
        self.ins[name] = t
        return t

    def dscr(self, name, shape, dt):
        kind = "ExternalOutput" if (self.dbg and name in self.dbg) else "Internal"
        t = self.nc.dram_tensor(name, list(shape), dt, kind=kind).ap()
        return t

    def build(self, upto=99):
        nc = self.nc
        xv = self.din("xv", [TV, D])
        c_in = self.din("c", [1, D])
        w_ada = self.din("w_ada", [D, 6 * D])
        b_ada = self.din("b_ada", [6 * D])
        norm1_g = self.din("norm1_g", [D])
        w_in = self.din("w_in", [D, INW])
        vtok = self.din("vtok", [128, TV // 128])
        ident_in = self.din("ident", [128, 128], BF16)
        self.i_vcmp = self.din("vcmp", [128, NCV // 128])
        self.i_selb = self.din("selb", [5, 4, 128, NBW])
        self.i_selv = self.din("selv", [5, 4, 128, NBW])
        self.i_hflag = self.din("hflag", [128, 1])
        self.i_bias_s = self.din("bias_s", [128, NH, NDS])
        self.i_bias_c = self.din("bias_c", [128, NH, NDC])
        self.i_masks = self.din("masks", [128, NMASK, 512], BF16)
        self.i_esel = self.din("esel", [128, 64, 128], BF16)
        self.i_ovm = self.din("ovm", [128, 9, NBW], BF16)
        self.i_ones = self.din("ones_bf", [128, 128], BF16)
        self.i_cmp_pe = self.din("cmp_pe", [32, 128])
        self.i_w1 = [self.din("w_kc1", [4096, 256]), self.din("w_vc1", [4096, 256])]
        self.i_w2 = [self.din("w_kc2", [256, 128]), self.din("w_vc2", [256, 128])]
        self.i_wo = self.din("w_o_nsa", [D, D])
        self.i_dww = self.din("conv_dw_w", [31, D])
        self.i_dwb = self.din("conv_dw_b", [D])
        self.i_lng = self.din("conv_ln_g", [D])
        self.i_lnb = self.din("conv_ln_b", [D])
        self.i_wpw = self.din("conv_pw_w", [D, D])
        self.i_pwb = self.din("conv_pw_b", [D])
        self.i_wout = self.din("w_out", [D, D])
        self.i_n2g = self.din("norm2_g", [D])
        self.i_wup = self.din("ffn_w_up", [D, 2 * FF])
        self.i_fdw = self.din("ffn_dw_w", [3, 2 * FF])
        self.i_fdb = self.din("ffn_dw_b", [2 * FF])
        self.i_wdn = self.din("ffn_w_down", [FF, D])
        self.i_fg = self.din("final_g", [D])
        out = self.nc.dram_tensor("out", [TOWN, D], F32, kind="ExternalOutput").ap()
        self.out = out
        ada_d = self.dscr("ada_d", [6 * D], F32)
        kcT_raw = self.dscr("kcT_raw", [NG, 128, TV + 16], BF16)
        vcT_raw = self.dscr("vcT_raw", [NG, 128, TV + 16], BF16)
        kslT = self.dscr("kslT", [NG, 128, TV], BF16)
        vsl = self.dscr("vsl", [TV, NG, 130], BF16)
        QT = self.dscr("QT", [128, NH, NB], BF16)
        kwT = self.dscr("kwT", [NG, 128, NA], BF16)
        vw = self.dscr("vw", [NA, NG, 130], BF16)
        gates = self.dscr("gates", [NB, 48], F32)
        gluT = self.dscr("gluT", [128, 16, NB], BF16)
        mgT = self.dscr("mgT", [128, 32, NB], BF16)
        self.kcT = self.dscr("kcT", [NG, 128, NCV], BF16)
        self.vca = self.dscr("vca", [NCV, NG, 130], BF16)
        self.xmid = self.dscr("xmid", [NQ, D], F32)
        self.accd = self.dscr("accd", [NQ, D], F32)
        self.dumpS = self.dscr("dumpS", [128, 512], F32)
        self.impd = self.dscr("impd", [5, 4, 128, NG, NBW], F32)
        self.wb_d = {n: self.dscr(n, sh, BF16) for n, sh in (("wo_b", [D, D]), ("wpw_b", [D, D]), ("wout_b", [D, D]),
                                                           ("wup_b", [D, 2 * FF]), ("wdn_b", [FF, D]))}
        self.xv, self.kcT_raw, self.vcT_raw, self.kslT, self.vsl = xv, kcT_raw, vcT_raw, kslT, vsl
        self.QT, self.kwT, self.vw, self.gates, self.gluT, self.mgT, self.ada_d = QT, kwT, vw, gates, gluT, mgT, ada_d

        with ExitStack() as st0:
            P = self.P = Prog(nc, st0)
            self.st0 = st0
            sb = lambda name, shape, dt: st0.enter_context(nc.sbuf_tensor(name, list(shape), dt))
            ident = sb("ident_sb", [128, 128], BF16)
            r_const = Res("const")
            ch_c = P.chan("const")
            P.dma("sp", ch_c, lambda: nc.sync.dma_start(out=ident[:], in_=ident_in[:, :]), writes=[r_const])
            vtok_sb = sb("vtok_sb", [128, TV // 128], F32)
            P.dma("sp", ch_c, lambda: nc.sync.dma_start(out=vtok_sb[:], in_=vtok[:, :]), writes=[r_const], cont=True)
            ada = sb("ada_sb", [128, 96], F32)
            r_ada = Res("ada")
            s1 = sb("s1", [128, 16], F32)
            r_s1 = Res("s1")
            self.ident, self.r_const, self.vtok_sb, self.ada, self.r_ada, self.sb0 = ident, r_const, vtok_sb, ada, r_ada, sb
            self.ones = sb("ones_sb", [128, 128], BF16)
            P.dma("sp", ch_c, lambda: nc.sync.dma_start(out=self.ones[:], in_=self.i_ones[:, :]), writes=[r_const], cont=True)
            self.hfl = sb("hfl_sb", [128, 1], F32)
            P.dma("sp", ch_c, lambda: nc.sync.dma_start(out=self.hfl[:], in_=self.i_hflag[:, :]), writes=[r_const], cont=True)
            self.vcmp_sb = sb("vcmp_sb", [128, NCV // 128], F32)
            P.dma("sp", ch_c, lambda: nc.sync.dma_start(out=self.vcmp_sb[:], in_=self.i_vcmp[:, :]), writes=[r_const], cont=True)

            with ExitStack() as st:
                cT = st.enter_context(nc.sbuf_tensor("cT", [128, 16], F32))
                cact = st.enter_context(nc.sbuf_tensor("cact", [128, 16], F32))
                bT = st.enter_context(nc.sbuf_tensor("bT", [128, 96], F32))
                g1T = st.enter_context(nc.sbuf_tensor("g1T", [128, 16], F32))
                r_cT, r_cact, r_bT, r_g1T = Res(), Res(), Res(), Res()
                ch0 = P.chan("p0")
                P.dma("sp", ch0, lambda: nc.sync.dma_start(out=cT[:], in_=c_in[0, :].rearrange("(j p) -> p j", p=128),
                                                         allow_slow_non_contiguous=True), writes=[r_cT])
                P.dma("sp", ch0, lambda: nc.sync.dma_start(out=bT[:], in_=b_ada.rearrange("(f p) -> p f", p=128),
                                                         allow_slow_non_contiguous=True), writes=[r_bT], cont=True)
                P.dma("sp", ch0, lambda: nc.sync.dma_start(out=g1T[:], in_=norm1_g.rearrange("(j p) -> p j", p=128),
                                                         allow_slow_non_contiguous=True), writes=[r_g1T], cont=True)
                P.op("act", lambda: nc.scalar.activation(out=cact[:], in_=cT[:], func=AF.Silu), reads=[r_cT], writes=[r_cact])
                wpool = Pool(P, st, "wada", 2, [128, 16, 512], F32)
                chw = [P.chan("wada0"), P.chan("wada1")]
                aps = st.enter_context(nc.psum_tensor("ada_ps", [128, 96], F32))
                r_aps = Res()
                for blk in range(24):
                    wt, wr = wpool.next()
                    P.dma("sp", chw[blk % 2],
                          lambda wt=wt, blk=blk: nc.sync.dma_start(
                              out=wt[:], in_=w_ada[:, blk * 512:(blk + 1) * 512].rearrange("(k p) c -> p k c", p=128)),
                          writes=[wr])
                    for fl in range(4):
                        f = blk * 4 + fl
                        for kc in range(16):
                            P.op("pe", lambda wt=wt, fl=fl, kc=kc, f=f: nc.tensor.matmul(
                                aps[:, f:f + 1], lhsT=wt[:, kc, fl * 128:(fl + 1) * 128], rhs=cact[:, kc:kc + 1],
                                start=(kc == 0), stop=(kc == 15)), reads=[wr, r_cact], writes=[r_aps])
                P.op("dve", lambda: nc.vector.tensor_tensor(out=ada[:], in0=aps[:], in1=bT[:], op=ALU.add),
                     reads=[r_aps, r_bT], writes=[r_ada])
                P.op("dve", lambda: nc.vector.scalar_tensor_tensor(out=s1[:], in0=ada[:, 16:32], scalar=1.0, in1=g1T[:],
                                                                   op0=ALU.add, op1=ALU.mult),
                     reads=[r_ada, r_g1T], writes=[r_s1])
                r_adad = Res()
                ch_ad = P.chan("adad")
                P.dma("act", ch_ad, lambda: nc.scalar.dma_start(out=ada_d.rearrange("(f p) -> p f", p=128), in_=ada[:],
                                                              allow_slow_non_contiguous=True),
                      reads=[r_ada], writes=[r_adad])
                P.barrier()
            if upto <= 0:
                P.emit()
                return nc

            self.r_wb = Res("wb_scratch")
            if not (self.dbg and "nocast" in self.dbg):
                with ExitStack() as st:
                    c32 = Pool(P, st, "cst32_", 3, [128, 8192], F32)
                    cbf = Pool(P, st, "cstbf_", 3, [128, 8192], BF16)
                    cci = [P.chan(f"cji{i}") for i in range(3)]
                    cco = [P.chan(f"cjo{i}") for i in range(3)]
                    for ji, (sv, dv, o, w) in enumerate(self.cast_jobs()):
                        t32, r32 = c32.next()
                        tbf, rbf = cbf.next()
                        P.dma("sp", cci[ji % 3], lambda t32=t32, sv=sv, o=o, w=w: nc.sync.dma_start(out=t32[:, 0:w], in_=sv[:, o:o + w]), writes=[r32])
                        if ji % 2 == 0:
                            P.op("dve", lambda t32=t32, tbf=tbf, w=w: nc.vector.tensor_copy(out=tbf[:, 0:w], in_=t32[:, 0:w]), reads=[r32], writes=[rbf])
                        else:
                            P.op("act", lambda t32=t32, tbf=tbf, w=w: nc.scalar.copy(out=tbf[:, 0:w], in_=t32[:, 0:w]), reads=[r32], writes=[rbf])
                        P.dma("act", cco[ji % 3], lambda tbf=tbf, dv=dv, o=o, w=w: nc.scalar.dma_start(out=dv[:, o:o + w], in_=tbf[:, 0:w]),
                              reads=[rbf], writes=[self.r_wb])
                    P.barrier()
            with ExitStack() as st:
                wkv32 = Pool(P, st, "wkv32_", 2, [128, 16, 128], F32)
                wkv = st.enter_context(nc.sbuf_tensor("wkv", [128, 16, 1024], BF16))
                r_wkv = Res()
                chw = [P.chan("wkv0"), P.chan("wkv1")]
                for q in range(8):
                    wt, wr = wkv32.next()
                    P.dma("sp", chw[q % 2], lambda wt=wt, q=q: nc.sync.dma_start(
                        out=wt[:], in_=w_in[:, C_KC + q * 128:C_KC + (q + 1) * 128].rearrange("(k p) c -> p k c", p=128)),
                        writes=[wr])
                    P.op("pool", lambda wt=wt, q=q: nc.gpsimd.tensor_copy(out=wkv[:, :, q * 128:(q + 1) * 128], in_=wt[:]),
                         reads=[wr], writes=[r_wkv])
                xpool = Pool(P, st, "xt", 2, [128, 4, D], F32)
                chx = [P.chan("x0"), P.chan("x1")]
                junk = st.enter_context(nc.sbuf_tensor("junk", [128, D], BF16))
                r_junk = Res()
                sspool = Pool(P, st, "ss", 2, [128, 4], F32)
                rspool = Pool(P, st, "rs", 2, [128, 4], F32)
                xnpool = Pool(P, st, "xn", 2, [128, 4, D], BF16)
                hTpool = Pool(P, st, "hT", 2, [128, 16, 512], BF16)
                tpp = Pool(P, st, "tp", 4, [128, 512], BF16, psum=True)
                mmp = Pool(P, st, "mm", 3, [128, 512], F32, psum=True)
                stg = Pool(P, st, "stg", 2, [128, 6, 512], BF16)
                stv = Pool(P, st, "stv", 2, [128, 4, NG, 130], BF16)
                chs = [P.chan("st0"), P.chan("st1")]
                chv = [P.chan("sv0"), P.chan("sv1")]
                r_scr = Res("scr1a")
                for i in range(2):
                    P.op("dve", lambda i=i: nc.vector.memset(stv.t[i][:], 0.0), writes=[stv.r[i]])
                nblk = TV // 512
                if self.dbg and "nblk" in self.dbg:
                    nblk = self.dbg["nblk"]
                for tb in range(nblk):
                    t0 = tb * 512
                    hT, r_hT = self.norm_T(P, st, xv, t0, 4, xpool, chx[tb % 2], junk, r_junk, sspool, rspool, xnpool,
                                           hTpool, tpp, ident, r_const, s1, r_s1, ada, r_ada, 0)
                    sg, r_sg = stg.next()
                    for ci in range(6):
                        ps, r_ps = mmp.next()
                        for kc in range(16):
                            P.op("pe", lambda ps=ps, ci=ci, kc=kc, hT=hT: nc.tensor.matmul(
                                ps[:], lhsT=wkv[:, kc, ci * 128:(ci + 1) * 128], rhs=hT[:, kc, :],
                                start=(kc == 0), stop=(kc == 15)), reads=[r_wkv, r_hT], writes=[r_ps])
                        if ci % 2 == 0:
                            P.op("act", lambda ps=ps, sg=sg, ci=ci: nc.scalar.copy(out=sg[:, ci, :], in_=ps[:]),
                                 reads=[r_ps], writes=[r_sg])
                        else:
                            P.op("dve", lambda ps=ps, sg=sg, ci=ci: nc.vector.tensor_copy(out=sg[:, ci, :], in_=ps[:]),
                                 reads=[r_ps], writes=[r_sg])
                    dsts = [kcT_raw, vcT_raw, kslT]
                    for k3 in range(3):
                        P.dma("act", chs[tb % 2], lambda sg=sg, k3=k3, t0=t0: nc.scalar.dma_start(
                            out=dsts[k3][:, :, t0:t0 + 512].rearrange("g p t -> p g t"), in_=sg[:, 2 * k3:2 * k3 + 2, :]),
                            reads=[r_sg], writes=[r_scr], cont=(k3 > 0))
                    sv, r_sv = stv.next()
                    for s in range(4):
                        ps, r_ps = mmp.next()
                        for kc in range(16):
                            P.op("pe", lambda ps=ps, s=s, kc=kc, hT=hT: nc.tensor.matmul(
                                ps[:, 0:256], lhsT=hT[:, kc, s * 128:(s + 1) * 128], rhs=wkv[:, kc, 768:1024],
                                start=(kc == 0), stop=(kc == 15)), reads=[r_wkv, r_hT], writes=[r_ps])
                        tile = tb * 4 + s
                        P.op("dve", lambda ps=ps, sv=sv, s=s, tile=tile: nc.vector.tensor_scalar(
                            out=sv[:, s, :, 0:128], in0=ps[:, 0:256].rearrange("p (g d) -> p g d", g=NG),
                            scalar1=vtok_sb[:, tile:tile + 1], scalar2=None, op0=ALU.mult),
                            reads=[r_ps, r_const], writes=[r_sv])
                        P.op("dve", lambda sv=sv, s=s, tile=tile: nc.vector.tensor_copy(
                            out=sv[:, s, :, 128:130], in_=vtok_sb[:, tile:tile + 1].unsqueeze(1).to_broadcast([128, NG, 2])),
                            reads=[r_const], writes=[r_sv])
                    P.dma("act", chv[tb % 2], lambda sv=sv, t0=t0: nc.scalar.dma_start(
                        out=vsl[t0:t0 + 512, :, :].rearrange("(s p) g d -> p s g d", p=128), in_=sv[:]),
                        reads=[r_sv], writes=[r_scr])
                P.barrier()
            if upto <= 1:
                P.emit()
                return nc

            with ExitStack() as st:
                hTo = st.enter_context(nc.sbuf_tensor("hTo", [128, 16, NA], BF16))
                r_hTo = Res()
                with ExitStack() as st2:
                    xpool = Pool(P, st2, "xtb", 2, [128, 4, D], F32)
                    chx = [P.chan("xb0"), P.chan("xb1")]
                    junk = st2.enter_context(nc.sbuf_tensor("junkb", [128, D], BF16))
                    r_junk = Res()
                    sspool = Pool(P, st2, "ssb", 2, [128, 4], F32)
                    rspool = Pool(P, st2, "rsb", 2, [128, 4], F32)
                    xnpool = Pool(P, st2, "xnb", 2, [128, 4, D], BF16)
                    tpp = Pool(P, st2, "tpb", 4, [128, 512], BF16, psum=True)
                    for tb in range(6):
                        t0 = W0 + tb * 512
                        nsub = 4 if tb < 5 else 1

                        class _HP:
                            def next(self_inner):
                                return hTo[:, :, tb * 512:tb * 512 + nsub * 128], r_hTo
                        self.norm_T(P, st2, xv, t0, nsub, xpool, chx[tb % 2], junk, r_junk, sspool, rspool, xnpool,
                                    _HP(), tpp, ident, r_const, s1, r_s1, ada, r_ada, 0)
                    P.barrier()
                w32 = Pool(P, st, "w32_", 2, [128, 16, 512], F32)
                wbf = Pool(P, st, "wbf_", 2, [128, 16, 512], BF16)
                chw = [P.chan("w1b0"), P.chan("w1b1")]
                mmp = Pool(P, st, "mmb", 4, [128, 512], F32, psum=True)
                stA = Pool(P, st, "stA", 2, [128, NA], BF16)
                chst = [P.chan("stA0"), P.chan("stA1"), P.chan("stA2")]
                sgp = Pool(P, st, "sgp", 1, [128, NB], BF16)
                r_scr = Res("scr1b")
                self.wblk = 0

                def load_w(colranges):
                    wt, wr = w32.next()
                    wb, wbr = wbf.next()
                    ch = chw[self.wblk % 2]
                    self.wblk += 1
                    o = 0
                    for i, (c0, n) in enumerate(colranges):
                        P.dma("sp", ch, lambda wt=wt, c0=c0, n=n, o=o: nc.sync.dma_start(
                            out=wt[:, :, o:o + n], in_=w_in[:, c0:c0 + n].rearrange("(k p) c -> p k c", p=128)),
                            writes=[wr], cont=(i > 0))
                        o += n
                    P.op("pool", lambda wt=wt, wb=wb, o=o: nc.gpsimd.tensor_copy(out=wb[:, :, 0:o], in_=wt[:, :, 0:o]),
                         reads=[wr], writes=[wbr])
                    return wb, wbr

                def blocks(lo, hi):
                    b = []
                    t = lo
                    while t < hi:
                        n = min(512, hi - t)
                        b.append((t, n))
                        t += n
                    return b

                def fm_chunk(wb, wbr, woff, lo, hi, evac):
                    for (t, n) in blocks(lo, hi):
                        ps, r_ps = mmp.next()
                        for kc in range(16):
                            P.op("pe", lambda ps=ps, kc=kc, t=t, n=n: nc.tensor.matmul(
                                ps[:, 0:n], lhsT=wb[:, kc, woff:woff + 128], rhs=hTo[:, kc, t:t + n],
                                start=(kc == 0), stop=(kc == 15)), reads=[wbr, r_hTo], writes=[r_ps])
                        evac(ps, r_ps, t, n)

                ecount = [0]

                def copy_evac(dst, r_dst, off, func=None, scale=1.0):
                    def ev(ps, r_ps, t, n):
                        ecount[0] += 1
                        if func is None and scale == 1.0 and ecount[0] % 2 == 0:
                            P.op("dve", lambda: nc.vector.tensor_copy(out=dst[:, t - off:t - off + n], in_=ps[:, 0:n]),
                                 reads=[r_ps], writes=[r_dst])
                        else:
                            P.op("act", lambda: nc.scalar.activation(out=dst[:, t - off:t - off + n], in_=ps[:, 0:n],
                                                                     func=(func or AF.Copy), scale=scale),
                                 reads=[r_ps], writes=[r_dst])
                    return ev

                stn = [0]

                def store(dst_ap, sg, r_sg, n):
                    ch = chst[stn[0] % 3]
                    stn[0] += 1
                    P.dma("act", ch, lambda: nc.scalar.dma_start(out=dst_ap, in_=sg[:, 0:n]), reads=[r_sg], writes=[r_scr])

                OB = B0 - W0
                for qb in range(4):
                    wb, wbr = load_w([(C_Q + qb * 512, 512)])
                    for hl in range(4):
                        h = qb * 4 + hl
                        sg, r_sg = stA.next()
                        fm_chunk(wb, wbr, hl * 128, OB, NA, copy_evac(sg, r_sg, OB, scale=float(DK) ** -0.5))
                        store(QT[:, h, :], sg, r_sg, NB)
                wb, wbr = load_w([(C_KW, 256), (C_VW, 256)])
                for g in range(NG):
                    sg, r_sg = stA.next()
                    fm_chunk(wb, wbr, g * 128, 0, NA, copy_evac(sg, r_sg, 0))
                    store(kwT[g, :, :], sg, r_sg, NA)
                stv = Pool(P, st, "stvb", 2, [128, NG, 130], BF16)
                chv = [P.chan("svb0"), P.chan("svb1")]
                for i in range(2):
                    P.op("dve", lambda i=i: nc.vector.memset(stv.t[i][:], 0.0), writes=[stv.r[i]])
                for s in range(NA // 128):
                    ps, r_ps = mmp.next()
                    for kc in range(16):
                        P.op("pe", lambda ps=ps, s=s, kc=kc, wb=wb: nc.tensor.matmul(
                            ps[:, 0:256], lhsT=hTo[:, kc, s * 128:(s + 1) * 128], rhs=wb[:, kc, 256:512],
                            start=(kc == 0), stop=(kc == 15)), reads=[wbr, r_hTo], writes=[r_ps])
                    sv, r_sv = stv.next()
                    tile = W0 // 128 + s
                    P.op("dve", lambda ps=ps, sv=sv, tile=tile: nc.vector.tensor_scalar(
                        out=sv[:, :, 0:128], in0=ps[:, 0:256].rearrange("p (g d) -> p g d", g=NG),
                        scalar1=vtok_sb[:, tile:tile + 1], scalar2=None, op0=ALU.mult),
                        reads=[r_ps, r_const], writes=[r_sv])
                    P.op("pool", lambda sv=sv, tile=tile: nc.gpsimd.tensor_copy(
                        out=sv[:, :, 128:130], in_=vtok_sb[:, tile:tile + 1].unsqueeze(1).to_broadcast([128, NG, 2])),
                        reads=[r_const], writes=[r_sv])
                    P.dma("act", chv[s % 2], lambda sv=sv, s=s: nc.scalar.dma_start(
                        out=vw[s * 128:(s + 1) * 128, :, :], in_=sv[:]), reads=[r_sv], writes=[r_scr])
                wb, wbr = load_w([(C_GN, 48)])
                gst = Pool(P, st, "gst", 2, [128, 48], F32)
                chg = [P.chan("gs0"), P.chan("gs1")]
                for s in range(NB // 128):
                    ps, r_ps = mmp.next()
                    for kc in range(16):
                        P.op("pe", lambda ps=ps, s=s, kc=kc, wb=wb: nc.tensor.matmul(
                            ps[:, 0:48], lhsT=hTo[:, kc, OB + s * 128:OB + (s + 1) * 128], rhs=wb[:, kc, 0:48],
                            start=(kc == 0), stop=(kc == 15)), reads=[wbr, r_hTo], writes=[r_ps])
                    gs, r_gs = gst.next()
                    P.op("act", lambda ps=ps, gs=gs: nc.scalar.activation(out=gs[:], in_=ps[:, 0:48], func=AF.Sigmoid),
                         reads=[r_ps], writes=[r_gs])
                    P.dma("act", chg[s % 2], lambda gs=gs, s=s: nc.scalar.dma_start(
                        out=gates[s * 128:(s + 1) * 128, :], in_=gs[:]), reads=[r_gs], writes=[r_scr])
                for cb in range(8):
                    wb, wbr = load_w([(C_GLU + cb * 256, 256), (C_GLU + D + cb * 256, 256)])
                    for cl in range(2):
                        ch_ = cb * 2 + cl
                        sgm, r_sgm = sgp.next()
                        fm_chunk(wb, wbr, 256 + cl * 128, OB, NA, copy_evac(sgm, r_sgm, OB, func=AF.Sigmoid))
                        sg, r_sg = stA.next()

                        def ev(ps, r_ps, t, n, sg=sg, r_sg=r_sg, sgm=sgm, r_sgm=r_sgm):
                            P.op("dve", lambda: nc.vector.tensor_tensor(out=sg[:, t - OB:t - OB + n], in0=ps[:, 0:n],
                                                                        in1=sgm[:, t - OB:t - OB + n], op=ALU.mult),
                                 reads=[r_ps, r_sgm], writes=[r_sg])
                        fm_chunk(wb, wbr, cl * 128, OB, NA, ev)
                        P.op("dve", lambda sg=sg: nc.vector.tensor_scalar(out=sg[:, 0:OWN0 - B0], in0=sg[:, 0:OWN0 - B0], scalar1=self.hfl[:, 0:1],
                                                                        scalar2=None, op0=ALU.mult), reads=[r_sg, r_const], writes=[r_sg])
                        store(gluT[:, ch_, :], sg, r_sg, NB)
                for mb in range(8):
                    wb, wbr = load_w([(C_GM + mb * 512, 512)])
                    for cl in range(4):
                        sg, r_sg = stA.next()
                        fm_chunk(wb, wbr, cl * 128, OB, NA, copy_evac(sg, r_sg, OB, func=AF.Sigmoid))
                        store(mgT[:, mb * 4 + cl, :], sg, r_sg, NB)
                P.barrier()
            self.r_scr_all = Res("scr_all")
            if upto >= 3:
                self.phase_compress()
            if upto >= 4:
                self.phase_attn(upto)
            if upto >= 5:
                self.phase_mix()
            if upto >= 6:
                self.phase_ffn()
            P.emit()
        return nc

    def norm_T(self, P, st, src, t0, nsub, xpool, chx, junk, r_junk, sspool, rspool, xnpool, hTpool, tpp,
               ident, r_const, sc, r_sc, ada, r_ada, sh_col, xt_in=None):
        nc = self.nc
        if xt_in is None:
            xt, r_xt = xpool.next()
            P.dma("sp", chx, lambda: nc.sync.dma_start(
                out=xt[:, 0:nsub, :], in_=src[t0:t0 + nsub * 128, :].rearrange("(s p) d -> p s d", p=128)), writes=[r_xt])
        else:
            xt, r_xt = xt_in
        ss, r_ss = sspool.next()
        rs, r_rs = rspool.next()
        for s in range(nsub):
            P.op("act", lambda s=s: nc.scalar.activation(out=junk[:], in_=xt[:, s, :], func=AF.Square,
                                                         accum_out=ss[:, s:s + 1]),
                 reads=[r_xt], writes=[r_junk, r_ss])
        P.op("dve", lambda: nc.vector.tensor_scalar(out=rs[:, 0:nsub], in0=ss[:, 0:nsub], scalar1=1.0 / D, scalar2=EPS,
                                                    op0=ALU.mult, op1=ALU.add), reads=[r_ss], writes=[r_rs])
        P.op("act", lambda: nc.scalar.sqrt(out=rs[:, 0:nsub], in_=rs[:, 0:nsub]), reads=[r_rs], writes=[r_rs])
        P.op("dve", lambda: nc.vector.reciprocal(out=rs[:, 0:nsub], in_=rs[:, 0:nsub]), reads=[r_rs], writes=[r_rs])
        xn, r_xn = xnpool.next()
        for s in range(nsub):
            P.op("dve", lambda s=s: nc.vector.tensor_scalar(out=xn[:, s, :], in0=xt[:, s, :], scalar1=rs[:, s:s + 1],
                                                            scalar2=None, op0=ALU.mult),
                 reads=[r_xt, r_rs], writes=[r_xn])
        hT, r_hT = hTpool.next()
        for j in range(16):
            tp, r_tp = tpp.next()
            for s in range(nsub):
                P.op("pe", lambda tp=tp, s=s, j=j: nc.tensor.transpose(
                    out=tp[:, s * 128:(s + 1) * 128], in_=xn[:, s, j * 128:(j + 1) * 128], identity=ident[:]),
                    reads=[r_xn, r_const], writes=[r_tp])
            if j % 2 == 0:
                P.op("act", lambda tp=tp, j=j: nc.scalar.activation(
                    out=hT[:, j, 0:nsub * 128], in_=tp[:, 0:nsub * 128], func=AF.Identity,
                    scale=sc[:, j:j + 1], bias=ada[:, sh_col + j:sh_col + j + 1]),
                    reads=[r_tp, r_sc, r_ada], writes=[r_hT])
            else:
                P.op("dve", lambda tp=tp, j=j: nc.vector.tensor_scalar(
                    out=hT[:, j, 0:nsub * 128], in0=tp[:, 0:nsub * 128], scalar1=sc[:, j:j + 1],
                    scalar2=ada[:, sh_col + j:sh_col + j + 1], op0=ALU.mult, op1=ALU.add),
                    reads=[r_tp, r_sc, r_ada], writes=[r_hT])
        self.last_rs = (rs, r_rs)
        self.last_xt = (xt, r_xt)
        return hT, r_hT


    def phase_compress(self):
        nc, P = self.nc, self.P
        with ExitStack() as st:
            sbt = lambda name, shape, dt: st.enter_context(nc.sbuf_tensor(name, list(shape), dt))
            raw = sbt("c_raw", [128, TV + 16], BF16)
            R = sbt("c_R", [128, 16, NCV + 1], BF16)
            w1f = sbt("c_w1f", [128, 32, 256], F32)
            w1b = sbt("c_w1b", [128, 32, 256], BF16)
            w2f = sbt("c_w2f", [128, 2, 128], F32)
            w2b = sbt("c_w2b", [128, 2, 128], BF16)
            pef = sbt("c_pef", [128, 32], F32)
            peb = sbt("c_peb", [128, 32], BF16)
            bia = sbt("c_bia", [128, 2], F32)
            hid = sbt("c_hid", [128, 2, NCV], BF16)
            kst = sbt("c_kst", [128, NCV], BF16)
            zpad = sbt("c_zpad", [128, NG, 16], BF16)
            r_raw, r_R, r_w1f, r_w1b, r_w2f, r_w2b, r_pe, r_bia, r_hid, r_kst, r_z = [Res() for _ in range(11)]
            vst = Pool(P, st, "c_vst", 2, [128, NG, 130], BF16)
            mmp = Pool(P, st, "c_mm", 3, [128, 512], F32, psum=True)
            bps = st.enter_context(nc.psum_tensor("c_bps", [128, 2], F32))
            r_bps = Res()
            ch = {n: P.chan("c_" + n) for n in ("raw", "w1", "w2", "pe", "k", "v0", "v1", "z")}
            r_out = self.r_scr_all
            P.op("dve", lambda: nc.vector.memset(zpad[:], 0.0), writes=[r_z])
            for i, rt in enumerate((self.kcT_raw, self.vcT_raw)):
                P.dma("act", ch["z"], lambda rt=rt: nc.scalar.dma_start(
                    out=rt[:, :, TV:TV + 16].rearrange("g p t -> p g t"), in_=zpad[:]), reads=[r_z], writes=[r_out], cont=(i > 0))
            P.dma("sp", ch["pe"], lambda: nc.sync.dma_start(out=pef[:], in_=self.i_cmp_pe.rearrange("l d -> d l"),
                                                           allow_slow_non_contiguous=True), writes=[r_pe])
            P.op("dve", lambda: nc.vector.tensor_copy(out=peb[:], in_=pef[:]), reads=[r_pe], writes=[r_pe])
            for i in range(2):
                P.op("dve", lambda i=i: nc.vector.memset(vst.t[i][:], 0.0), writes=[vst.r[i]])
            for kv in range(2):
                P.dma("sp", ch["w1"], lambda kv=kv: nc.sync.dma_start(
                    out=w1f[:], in_=self.i_w1[kv].rearrange("(l d) c -> d l c", d=128)), writes=[r_w1f])
                P.op("pool", lambda: nc.gpsimd.tensor_copy(out=w1b[:], in_=w1f[:]), reads=[r_w1f], writes=[r_w1b])
                P.dma("sp", ch["w2"], lambda kv=kv: nc.sync.dma_start(
                    out=w2f[:], in_=self.i_w2[kv].rearrange("(c p) d -> p c d", p=128)), writes=[r_w2f])
                P.op("dve", lambda: nc.vector.tensor_copy(out=w2b[:], in_=w2f[:]), reads=[r_w2f], writes=[r_w2b])
                for hc in range(2):
                    for l in range(32):
                        P.op("pe", lambda hc=hc, l=l: nc.tensor.matmul(
                            bps[:, hc:hc + 1], lhsT=w1b[:, l, hc * 128:(hc + 1) * 128], rhs=peb[:, l:l + 1],
                            start=(l == 0), stop=(l == 31)), reads=[r_w1b, r_pe], writes=[r_bps])
                P.op("dve", lambda: nc.vector.tensor_copy(out=bia[:], in_=bps[:]), reads=[r_bps], writes=[r_bia])
                src = (self.kcT_raw, self.vcT_raw)[kv]
                for g in range(NG):
                    P.dma("sp", ch["raw"], lambda g=g, src=src: nc.sync.dma_start(out=raw[:], in_=src[g, :, :]),
                          reads=[r_out], writes=[r_raw])
                    P.op("pool", lambda: nc.gpsimd.tensor_copy(
                        out=R[:], in_=raw[:].rearrange("p (m l) -> p l m", l=16)), reads=[r_raw], writes=[r_R])
                    for hc in range(2):
                        for (n0, nn) in ((0, 512), (512, 512), (1024, 128)):
                            ps, r_ps = mmp.next()
                            for l in range(32):
                                P.op("pe", lambda ps=ps, hc=hc, l=l, n0=n0, nn=nn: nc.tensor.matmul(
                                    ps[:, 0:nn], lhsT=w1b[:, l, hc * 128:(hc + 1) * 128],
                                    rhs=R[:, l % 16, (l // 16) + n0:(l // 16) + n0 + nn],
                                    start=(l == 0), stop=(l == 31)), reads=[r_w1b, r_R], writes=[r_ps])
                            P.op("act", lambda ps=ps, hc=hc, n0=n0, nn=nn: nc.scalar.activation(
                                out=hid[:, hc, n0:n0 + nn], in_=ps[:, 0:nn], func=AF.Silu, bias=bia[:, hc:hc + 1]),
                                reads=[r_ps, r_bia], writes=[r_hid])
                    if kv == 0:
                        for (n0, nn) in ((0, 512), (512, 512), (1024, 128)):
                            ps, r_ps = mmp.next()
                            for hc in range(2):
                                P.op("pe", lambda ps=ps, hc=hc, n0=n0, nn=nn: nc.tensor.matmul(
                                    ps[:, 0:nn], lhsT=w2b[:, hc, :], rhs=hid[:, hc, n0:n0 + nn],
                                    start=(hc == 0), stop=(hc == 1)), reads=[r_w2b, r_hid], writes=[r_ps])
                            P.op("dve", lambda ps=ps, n0=n0, nn=nn: nc.vector.tensor_copy(out=kst[:, n0:n0 + nn], in_=ps[:, 0:nn]),
                                 reads=[r_ps], writes=[r_kst])
                        P.dma("act", ch["k"], lambda g=g: nc.scalar.dma_start(out=self.kcT[g, :, :], in_=kst[:]),
                              reads=[r_kst], writes=[r_out])
                    else:
                        for tl in range(NCV // 128):
                            ps, r_ps = mmp.next()
                            for hc in range(2):
                                P.op("pe", lambda ps=ps, hc=hc, tl=tl: nc.tensor.matmul(
                                    ps[:, 0:128], lhsT=hid[:, hc, tl * 128:(tl + 1) * 128], rhs=w2b[:, hc, :],
                                    start=(hc == 0), stop=(hc == 1)), reads=[r_w2b, r_hid], writes=[r_ps])
                            sv, r_sv = vst.next()
                            P.op("dve", lambda ps=ps, sv=sv, tl=tl, g=g: nc.vector.tensor_scalar(
                                out=sv[:, g, 0:128], in0=ps[:, 0:128], scalar1=self.vcmp_sb[:, tl:tl + 1], scalar2=None,
                                op0=ALU.mult), reads=[r_ps, self.r_const], writes=[r_sv])
                            P.op("pool", lambda sv=sv, tl=tl, g=g: nc.gpsimd.tensor_copy(
                                out=sv[:, g, 128:130], in_=self.vcmp_sb[:, tl:tl + 1].to_broadcast([128, 2])), reads=[self.r_const], writes=[r_sv])
                            P.dma("act", ch["v%d" % (tl % 2)], lambda sv=sv, tl=tl, g=g: nc.scalar.dma_start(
                                out=self.vca[tl * 128:(tl + 1) * 128, g, :], in_=sv[:, g, :]), reads=[r_sv], writes=[r_out])
            P.barrier()


    def phase_attn(self, upto):
        nc, P = self.nc, self.P
        with ExitStack() as st:
            sbt = lambda name, shape, dt: st.enter_context(nc.sbuf_tensor(name, list(shape), dt))
            bias_s = sbt("a_bs", [128, NH, NDS], F32)
            bias_c = sbt("a_bc", [128, NH, NDC], F32)
            masks = sbt("a_mk", [128, NMASK, 512], BF16)
            esel = sbt("a_es", [128, 64, 128], BF16)
            r_tab = Res("tables")
            cht = P.chan("a_tab")
            for i, (dst, src) in enumerate(((bias_s, self.i_bias_s), (bias_c, self.i_bias_c), (masks, self.i_masks), (esel, self.i_esel))):
                P.dma("sp", cht, lambda dst=dst, src=src: nc.sync.dma_start(out=dst[:], in_=src[:, :, :]), writes=[r_tab], cont=(i > 0))
            crhs = [[sbt(f"a_cr{g}_{jj}", [128, 130 + NBW], BF16) for jj in range(9)] for g in range(NG)]
            r_crhs = [[Res() for jj in range(9)] for g in range(NG)]
            ch_cr = [P.chan("a_cr0"), P.chan("a_cr1")]
            ovm = sbt("a_ovm", [128, 9, NBW], BF16)
            P.dma("sp", cht, lambda: nc.sync.dma_start(out=ovm[:], in_=self.i_ovm[:, :, :]), writes=[r_tab], cont=True)
            QTt = sbt("a_qt", [128, NH, 512], BF16)
            gat = sbt("a_gat", [128, 4, 48], F32)
            selb = sbt("a_selb", [128, 4, NBW], F32)
            selv = sbt("a_selv", [128, 4, NBW], F32)
            acc = sbt("a_acc", [128, 4, D], F32)
            imp = sbt("a_imp", [128, 4, NG, NBW], F32)
            mneg = sbt("a_mneg", [128, NG, 3, 512], BF16)
            r_qt, r_gat, r_sel, r_acc, r_imp, r_mneg = [Res() for _ in range(6)]
            ch_q = P.chan("a_q")
            kpool = Pool(P, st, "a_k", 3, [128, 2048], BF16)
            vpool = Pool(P, st, "a_v", 3, [128, 16, 130], BF16)
            chk = [P.chan(f"a_k{i}") for i in range(3)]
            chv = [P.chan(f"a_v{i}") for i in range(3)]
            ptp = Pool(P, st, "a_pt", 4, [128, 512], BF16)
            sps = Pool(P, st, "a_S", 2, [128, 512], F32, psum=True)
            ops_ = Pool(P, st, "a_o", 4, [128, 512], F32, psum=True)
            tps = Pool(P, st, "a_tp", 2, [128, 512], BF16, psum=True)
            small = Pool(P, st, "a_sm", 4, [128, 4], F32)
            osb = Pool(P, st, "a_osb", 8, [128, 130 + NBW], F32)
            sc1 = sbt("a_sc1", [128, 384], F32)
            sc2 = sbt("a_sc2", [128, 384], F32)
            m8 = sbt("a_m8", [128, 16], F32)
            mbf = sbt("a_mbf", [128, 8, 384], BF16)
            r_sc1, r_sc2, r_m8, r_mbf = Res(), Res(), Res(), Res()
            P.op("dve", lambda: nc.vector.memset(mbf[:], 0.0), writes=[r_mbf])
            P.op("dve", lambda: nc.vector.memset(sc1[:], 0.0), writes=[r_sc1])
            ch_dbg = P.chan("a_dbg")
            r_out = self.r_scr_all
            kcount = [0]

            def do_tile(ti, q0, nq):
                self._cr_loaded = [0, 0]
                qend = q0 + nq
                nsub = max(1, nq // 128)
                rows = min(128, nq)
                nend = qend // 16
                P.dma("sp", ch_q, lambda q0=q0, nq=nq: nc.sync.dma_start(out=QTt[:, :, 0:nq], in_=self.QT[:, :, q0 - B0:q0 - B0 + nq]),
                      reads=[r_out], writes=[r_qt])
                P.dma("sp", ch_q, lambda q0=q0, nq=nq, rows=rows, nsub=nsub: nc.sync.dma_start(
                    out=gat[0:rows, 0:nsub, :], in_=self.gates[q0 - B0:q0 - B0 + nq, :].rearrange("(s p) c -> p s c", p=rows)),
                    reads=[r_out], writes=[r_gat], cont=True)
                P.dma("sp", ch_q, lambda ti=ti: nc.sync.dma_start(out=selb[:], in_=self.i_selb[ti].rearrange("s p b -> p s b")),
                      writes=[r_sel], cont=True)
                P.dma("sp", ch_q, lambda ti=ti: nc.sync.dma_start(out=selv[:], in_=self.i_selv[ti].rearrange("s p b -> p s b")),
                      writes=[r_sel], cont=True)

                def run_branch(bi, h):
                    g = h // HG
                    W = exp_width(h, nq)
                    if bi == 0:
                        nch = min(n_cmp_chunks(h, nq), nend // 128)
                    elif bi == 1:
                        nch = min(n_slc_chunks(h, nq, 132), qend // 128)
                    else:
                        nch = min(n_slc_chunks(h, nq, 8 if nq == 512 else 5), 8 if nq == 512 else 5)
                    ncol = 130 + NBW if bi == 0 else 130
                    oacc = [ops_.next() for _ in range(nsub)]
                    kt = vt = None
                    stA = {}

                    def stageB(j, pt, r_pt, rhsV, r_rhsV):
                        for s_ in range(nsub):
                            o, r_o = oacc[s_]
                            P.op("pe", lambda o=o, s_=s_, pt=pt, rhsV=rhsV, j=j: nc.tensor.matmul(
                                o[0:rows, 0:ncol], lhsT=pt[:, s_ * 128:s_ * 128 + rows], rhs=rhsV,
                                start=(j == 0), stop=(j == nch - 1)), reads=[r_pt, r_rhsV], writes=[r_o])

                    for j in range(nch):
                        if bi == 0:
                            n0 = nend - 128 * (j + 1)
                            kt, r_kt = kpool.next()
                            kc_ = kcount[0] % 3
                            kcount[0] += 1
                            P.dma("sp", chk[kc_], lambda kt=kt, n0=n0: nc.sync.dma_start(out=kt[:, 0:128], in_=self.kcT[g, :, n0:n0 + 128]),
                                  reads=[r_out], writes=[r_kt])
                            if h % HG == 0 or j >= self._cr_loaded[g]:
                                P.dma("sp", ch_cr[g], lambda n0=n0, j=j: nc.sync.dma_start(out=crhs[g][j][:, 0:130], in_=self.vca[n0:n0 + 128, g, :]),
                                      reads=[r_out], writes=[r_crhs[g][j]])
                                P.op("dve", lambda j=j: nc.vector.tensor_scalar(out=crhs[g][j][:, 130:130 + NBW], in0=ovm[:, j, :],
                                                                               scalar1=crhs[g][j][:, 128:129], scalar2=None, op0=ALU.mult),
                                     reads=[r_tab, r_crhs[g][j]], writes=[r_crhs[g][j]])
                                self._cr_loaded[g] = max(self._cr_loaded[g], j + 1) if h % HG else j + 1
                            klhs, rhsV, r_rhsV = kt[:, 0:128], crhs[g][j][:, 0:ncol], r_crhs[g][j]
                            mk = MASK_IDX.get(("cmp", nq, j))
                            bcol = lambda r, j=j: bias_c[:, h, (nq - 2048 * (j + 1) - r * W) // 64 + ROFF:(nq - 2048 * (j + 1) - r * W) // 64 + ROFF + 1]
                        else:
                            if j % 16 == 0:
                                nsup = min(16, nch - j)
                                lo = qend - 128 * (j + nsup)
                                kt, r_kt = kpool.next()
                                vt, r_vt = vpool.next()
                                kc_ = kcount[0] % 3
                                kcount[0] += 1
                                if bi == 1:
                                    ksrc, vsrc, off = self.kslT, self.vsl, 0
                                else:
                                    ksrc, vsrc, off = self.kwT, self.vw, W0
                                P.dma("sp", chk[kc_], lambda kt=kt, lo=lo, nsup=nsup, ksrc=ksrc, off=off: nc.sync.dma_start(
                                    out=kt[:, 0:128 * nsup], in_=ksrc[g, :, lo - off:lo - off + 128 * nsup]), reads=[r_out], writes=[r_kt])
                                P.dma("sp", chv[kc_], lambda vt=vt, lo=lo, nsup=nsup, vsrc=vsrc, off=off: nc.sync.dma_start(
                                    out=vt[:, 0:nsup, :], in_=vsrc[lo - off:lo - off + 128 * nsup, g, :].rearrange("(s p) d -> p s d", p=128)),
                                    reads=[r_out], writes=[r_vt])
                                sup_n = nsup
                            sl = sup_n - 1 - (j % 16)
                            klhs, rhsV, r_rhsV = kt[:, sl * 128:(sl + 1) * 128], vt[:, sl, :], r_vt
                            mk = MASK_IDX.get(("slc" if bi == 1 else "win", nq, j))
                            bcol = lambda r, j=j: bias_s[:, h, (nq - 128 * (j + 1) - r * W) // 64 + DOFF:(nq - 128 * (j + 1) - r * W) // 64 + DOFF + 1]
                        S, r_S = sps.next()
                        nmm = 1 + (mk is not None) + (bi == 1)
                        cnt = [0]

                        def mm(lhsT, rhs, reads):
                            first, last = cnt[0] == 0, cnt[0] == nmm - 1
                            cnt[0] += 1
                            P.op("pe", lambda S=S: nc.tensor.matmul(S[:, 0:nq], lhsT=lhsT, rhs=rhs, start=first, stop=last),
                                 reads=reads, writes=[r_S])
                        mm(klhs, QTt[:, h, 0:nq], [r_kt, r_qt])
                        if mk is not None:
                            mm(self.ident[:], masks[:, mk, 0:nq], [self.r_const, r_tab])
                        if bi == 1:
                            b0 = NBLKW - 2 * (j + 1)
                            mm(esel[:, (b0 % 128) // 2, :], mneg[:, g, b0 // 128, 0:nq], [r_tab, r_mneg])
                        if self.dbg and "dumpS" in self.dbg and bi == 0 and h == 0 and j == 0:
                            dS = sbt("dbg_S", [128, 512], F32)
                            r_dS = Res()
                            P.op("dve", lambda: nc.vector.tensor_copy(out=dS[:, 0:nq], in_=S[:, 0:nq]), reads=[r_S], writes=[r_dS])
                            P.dma("act", ch_dbg, lambda: nc.scalar.dma_start(out=self.dumpS[:, 0:nq], in_=dS[:, 0:nq]), reads=[r_dS], writes=[r_out])
                            raise StopIteration
                        pt, r_pt = ptp.next()
                        for r in range(nq // W):
                            P.op("act", lambda r=r, bcol=bcol, S=S, pt=pt: nc.scalar.activation(
                                out=pt[:, r * W:(r + 1) * W], in_=S[:, r * W:(r + 1) * W], func=AF.Exp, bias=bcol(r)),
                                reads=[r_S, r_tab], writes=[r_pt])
                        if j > 0:
                            stageB(j - 1, *stA.pop(j - 1))
                        stA[j] = (pt, r_pt, rhsV, r_rhsV)
                    stageB(nch - 1, *stA.pop(nch - 1))
                    evac = []
                    for s_ in range(nsub):
                        o, r_o = oacc[s_]
                        ob_, r_ob = osb.next()
                        P.op("act", lambda o=o, ob_=ob_: nc.scalar.copy(out=ob_[0:rows, 0:ncol], in_=o[0:rows, 0:ncol]), reads=[r_o], writes=[r_ob])
                        evac.append((ob_, r_ob))
                    for s_ in range(nsub):
                        o, r_o = evac[s_]
                        sm, r_sm = small.next()
                        P.op("dve", lambda o=o, sm=sm: nc.vector.tensor_scalar(out=sm[0:rows, 0:1], in0=o[0:rows, 128:129], scalar1=1e-30,
                                                                             scalar2=None, op0=ALU.max), reads=[r_o], writes=[r_sm])
                        P.op("dve", lambda sm=sm: nc.vector.reciprocal(out=sm[0:rows, 1:2], in_=sm[0:rows, 0:1]), reads=[r_sm], writes=[r_sm])
                        P.op("dve", lambda sm=sm, s_=s_: nc.vector.tensor_tensor(
                            out=sm[0:rows, 2:3], in0=sm[0:rows, 1:2], in1=gat[0:rows, s_, bi * 16 + h:bi * 16 + h + 1], op=ALU.mult),
                            reads=[r_sm, r_gat], writes=[r_sm])
                        dst = acc[0:rows, s_, h * 128:(h + 1) * 128]
                        if bi == 0:
                            P.op("dve", lambda o=o, sm=sm, dst=dst: nc.vector.tensor_scalar(
                                out=dst, in0=o[0:rows, 0:128], scalar1=sm[0:rows, 2:3], scalar2=None, op0=ALU.mult),
                                reads=[r_o, r_sm], writes=[r_acc])
                            idst = imp[0:rows, s_, g, :]
                            if h % HG == 0:
                                P.op("dve", lambda o=o, sm=sm, idst=idst: nc.vector.tensor_scalar(
                                    out=idst, in0=o[0:rows, 130:130 + NBW], scalar1=sm[0:rows, 1:2], scalar2=None, op0=ALU.mult),
                                    reads=[r_o, r_sm], writes=[r_imp])
                            else:
                                P.op("dve", lambda o=o, sm=sm, idst=idst: nc.vector.scalar_tensor_tensor(
                                    out=idst, in0=o[0:rows, 130:130 + NBW], scalar=sm[0:rows, 1:2], in1=idst, op0=ALU.mult, op1=ALU.add),
                                    reads=[r_o, r_sm, r_imp], writes=[r_imp])
                        else:
                            P.op("dve", lambda o=o, sm=sm, dst=dst: nc.vector.scalar_tensor_tensor(
                                out=dst, in0=o[0:rows, 0:128], scalar=sm[0:rows, 2:3], in1=dst, op0=ALU.mult, op1=ALU.add),
                                reads=[r_o, r_sm, r_acc], writes=[r_acc])

                try:
                    for h in range(NH):
                        run_branch(0, h)
                except StopIteration:
                    return "stop"
                if self.dbg and "impd" in self.dbg:
                    P.dma("act", ch_dbg, lambda ti=ti: nc.scalar.dma_start(out=self.impd[ti].rearrange("s p g b -> p s g b"), in_=imp[:]),
                          reads=[r_imp], writes=[r_out])
                for g in range(NG):
                    for s_ in range(nsub):
                        P.op("dve", lambda s_=s_, g=g: nc.vector.tensor_tensor(out=sc1[0:rows, 0:NBW], in0=imp[0:rows, s_, g, :],
                                                                               in1=selb[0:rows, s_, :], op=ALU.add),
                             reads=[r_imp, r_sel], writes=[r_sc1])
                        P.op("dve", lambda s_=s_: nc.vector.tensor_tensor(out=sc1[0:rows, 0:NBW], in0=sc1[0:rows, 0:NBW],
                                                                          in1=selv[0:rows, s_, :], op=ALU.mult),
                             reads=[r_sc1, r_sel], writes=[r_sc1])
                        P.op("dve", lambda: nc.vector.max(out=m8[0:rows, 0:8], in_=sc1[0:rows, :]), reads=[r_sc1], writes=[r_m8])
                        P.op("dve", lambda: nc.vector.match_replace(out=sc2[0:rows, :], in_to_replace=m8[0:rows, 0:8],
                                                                    in_values=sc1[0:rows, :], imm_value=-1e30),
                             reads=[r_sc1, r_m8], writes=[r_sc2])
                        P.op("dve", lambda: nc.vector.max(out=m8[0:rows, 8:16], in_=sc2[0:rows, :]), reads=[r_sc2], writes=[r_m8])
                        P.op("dve", lambda g=g, s_=s_: nc.vector.tensor_scalar(out=mbf[0:rows, g * 4 + s_, 0:NBW], in0=sc1[0:rows, 0:NBW], scalar1=m8[0:rows, 15:16],
                                                                            scalar2=None, op0=ALU.is_ge), reads=[r_sc1, r_m8], writes=[r_mbf])
                if not (self.dbg and "branches" in self.dbg and 2 not in self.dbg["branches"]):
                    for h in range(NH):
                        run_branch(2, h)
                for g in range(NG):
                    tpl = [tps.next() for _ in range(3)]
                    for s_ in range(nsub):
                        for bg in range(3):
                            tp, r_tp = tpl[bg]
                            P.op("pe", lambda tp=tp, bg=bg, s_=s_, g=g: nc.tensor.transpose(
                                out=tp[:, s_ * 128:s_ * 128 + rows], in_=mbf[0:rows, g * 4 + s_, bg * 128:(bg + 1) * 128], identity=self.ident[0:rows, 0:rows]),
                                reads=[r_mbf, self.r_const], writes=[r_tp])
                    for bg in range(3):
                        tp, r_tp = tpl[bg]
                        P.op("act", lambda tp=tp, bg=bg, g=g: nc.scalar.activation(out=mneg[:, g, bg, 0:nq], in_=tp[:, 0:nq], func=AF.Identity,
                                                                                  scale=30000.0, bias=-30000.0),
                             reads=[r_tp], writes=[r_mneg])
                for bi in (1,):
                    if self.dbg and "branches" in self.dbg and bi not in self.dbg["branches"]:
                        continue
                    for h in range(NH):
                        run_branch(bi, h)
                if True:
                    P.dma("act", ch_dbg, lambda q0=q0, nq=nq, rows=rows, nsub=nsub: nc.scalar.dma_start(
                        out=self.accd[q0 - Q0:q0 - Q0 + nq, :].rearrange("(s p) d -> p s d", p=rows), in_=acc[0:rows, 0:nsub, :]),
                        reads=[r_acc], writes=[r_out])

            for ti, (q0, nq) in enumerate(QTILES):
                if self.dbg and "tiles" in self.dbg and ti not in self.dbg["tiles"]:
                    continue
                if do_tile(ti, q0, nq) == "stop":
                    return
            P.barrier()


    def cast_jobs(self):
        jobs = []
        for name, src in (("wo_b", self.i_wo), ("wpw_b", self.i_wpw), ("wout_b", self.i_wout), ("wup_b", self.i_wup), ("wdn_b", self.i_wdn)):
            dst = self.wb_d[name]
            R, C = src.shape
            sv = src.rearrange("(p a) c -> p (a c)", p=128)
            dv = dst.rearrange("(p a) c -> p (a c)", p=128)
            F = R * C // 128
            for o in range(0, F, 8192):
                jobs.append((sv, dv, o, min(8192, F - o)))
        return jobs

    def cast_setup(self, st):
        nc, P = self.nc, self.P
        self.cj = self.cast_jobs()
        self.cji = 0
        self.cpend = None
        self.c32 = Pool(P, st, "cst32_", 2, [128, 2048], F32)
        self.cbf = Pool(P, st, "cstbf_", 2, [128, 2048], BF16)
        self.cch_i = [P.chan("cji0"), P.chan("cji1")]
        self.cch_o = [P.chan("cjo0"), P.chan("cjo1")]
        self.r_wb = Res("wb_scratch")

    def cast_step(self, n):
        nc, P = self.nc, self.P
        for _ in range(n):
            if self.cji >= len(self.cj):
                break
            sv, dv, o, w = self.cj[self.cji]
            i = self.cji % 2
            self.cji += 1
            t32, r32 = self.c32.next()
            tbf, rbf = self.cbf.next()
            P.dma("sp", self.cch_i[i], lambda t32=t32, sv=sv, o=o, w=w: nc.sync.dma_start(out=t32[:, 0:w], in_=sv[:, o:o + w]), writes=[r32])
            P.op("pool", lambda t32=t32, tbf=tbf, w=w: nc.gpsimd.tensor_copy(out=tbf[:, 0:w], in_=t32[:, 0:w]), reads=[r32], writes=[rbf])
            if self.cpend is not None:
                self.cpend()
            self.cpend = (lambda i=i, tbf=tbf, dv=dv, o=o, w=w, rbf=rbf: P.dma(
                "sp", self.cch_o[i], lambda: nc.sync.dma_start(out=dv[:, o:o + w], in_=tbf[:, 0:w]), reads=[rbf], writes=[self.r_wb]))
        if self.cji >= len(self.cj) and self.cpend is not None:
            self.cpend()
            self.cpend = None

    def phase_mix(self):
        nc, P = self.nc, self.P
        with ExitStack() as st:
            sbt = lambda name, shape, dt: st.enter_context(nc.sbuf_tensor(name, list(shape), dt))
            g1bc = sbt("m_g1bc", [128, D], F32)
            dww = sbt("m_dww", [128, 16, 31], F32)
            cols = sbt("m_cols", [128, 4, 16], F32)
            r_cst = Res()
            chc = P.chan("m_c")
            P.dma("sp", chc, lambda: nc.sync.dma_start(out=g1bc[:], in_=self.ada_d[2 * D:3 * D].partition_broadcast(128)), writes=[r_cst])
            for c_ in range(16):
                P.dma("sp", chc, lambda c_=c_: nc.sync.dma_start(out=dww[:, c_, :], in_=self.i_dww[:, c_ * 128:(c_ + 1) * 128].rearrange("k p -> p k"),
                                                              allow_slow_non_contiguous=True), writes=[r_cst], cont=True)
            for i, src in enumerate((self.i_dwb, self.i_lng, self.i_lnb, self.i_pwb)):
                P.dma("sp", chc, lambda i=i, src=src: nc.sync.dma_start(out=cols[:, i, :], in_=src.rearrange("(c p) -> p c", p=128),
                                                                      allow_slow_non_contiguous=True), writes=[r_cst], cont=True)
            oT = sbt("m_oT", [128, 16, 512], BF16)
            glu = sbt("m_glu", [128, 16, 544], BF16)
            ybf = sbt("m_ybf", [128, 16, 512], BF16)
            uc = sbt("m_uc", [128, 16, 512], BF16)
            mer = sbt("m_mer", [128, 16, 512], BF16)
            r_oT, r_glu, r_ybf, r_uc, r_mer = [Res() for _ in range(5)]
            accp = Pool(P, st, "m_acc", 2, [128, D], F32)
            accb = Pool(P, st, "m_accb", 2, [128, D], BF16)
            wblk = Pool(P, st, "m_w", 2, [128, 16, 512], BF16)
            dgp = Pool(P, st, "m_dg", 2, [128, 31, 128], BF16)
            ysq = Pool(P, st, "m_ysq", 2, [128, 512], BF16)
            mgp = Pool(P, st, "m_mg", 4, [128, 512], BF16)
            tmpf = Pool(P, st, "m_tmp", 3, [128, 512], F32)
            xp = Pool(P, st, "m_x", 3, [128, 512], F32)
            stat = sbt("m_stat", [128, 3, 512], F32)
            r_stat = Res()
            chl = [P.chan(f"m_l{i}") for i in range(4)]
            chw = [P.chan("m_w0"), P.chan("m_w1")]
            chs = [P.chan(f"m_s{i}") for i in range(3)]
            mm = Pool(P, st, "m_mm", 3, [128, 512], F32, psum=True)
            sps = Pool(P, st, "m_sp", 2, [128, 512], F32, psum=True)
            tps = Pool(P, st, "m_tp", 2, [128, 512], BF16, psum=True)
            r_in, r_out = self.r_scr_all, Res("xmid")
            cnt = {"l": 0, "w": 0, "s": 0}

            def load_wblk(wsrc, cb):
                wt, wr = wblk.next()
                ch = chw[cnt["w"] % 2]
                cnt["w"] += 1
                P.dma("sp", ch, lambda: nc.sync.dma_start(out=wt[:], in_=wsrc[:, cb * 512:(cb + 1) * 512].rearrange("(k p) c -> p k c", p=128)),
                      reads=[self.r_wb], writes=[wr])
                return wt, wr

            def mix_tile(ti, q0, nq):
                nsub, rows = max(1, nq // 128), min(128, nq)
                for s_ in range(nsub):
                    at, r_at = accp.next()
                    ab, r_ab = accb.next()
                    ch = chl[cnt["l"] % 4]
                    cnt["l"] += 1
                    P.dma("sp", ch, lambda at=at, s_=s_: nc.sync.dma_start(out=at[0:rows, :], in_=self.accd[q0 - Q0 + s_ * 128:q0 - Q0 + s_ * 128 + rows, :]),
                          reads=[r_in], writes=[r_at])
                    P.op("act", lambda at=at, ab=ab: nc.scalar.copy(out=ab[0:rows, :], in_=at[0:rows, :]), reads=[r_at], writes=[r_ab])
                    for hq in range(4):
                        tp, r_tp = tps.next()
                        for hl in range(4):
                            h = hq * 4 + hl
                            P.op("pe", lambda tp=tp, hl=hl, h=h, ab=ab: nc.tensor.transpose(
                                out=tp[:, hl * 128:hl * 128 + rows], in_=ab[0:rows, h * 128:(h + 1) * 128], identity=self.ident[0:rows, 0:rows]),
                                reads=[r_ab, self.r_const], writes=[r_tp])
                        P.op("dve", lambda tp=tp, hq=hq, s_=s_: nc.vector.tensor_copy(
                            out=oT[:, hq * 4:hq * 4 + 4, s_ * 128:s_ * 128 + rows],
                            in_=tp[:, :].rearrange("p (h q) -> p h q", h=4)[:, :, 0:rows]), reads=[r_tp], writes=[r_oT])
                for cb in range(4):
                    wt, wr = load_wblk(self.wb_d["wo_b"], cb)
                    for cl in range(4):
                        c = cb * 4 + cl
                        ps, r_ps = mm.next()
                        for kc in range(16):
                            P.op("pe", lambda ps=ps, kc=kc, cl=cl, wt=wt: nc.tensor.matmul(
                                ps[:, 0:nq], lhsT=wt[:, kc, cl * 128:(cl + 1) * 128], rhs=oT[:, kc, 0:nq], start=(kc == 0), stop=(kc == 15)),
                                reads=[wr, r_oT], writes=[r_ps])
                        mg, r_mg = mgp.next()
                        ch = chl[cnt["l"] % 4]
                        cnt["l"] += 1
                        P.dma("sp", ch, lambda mg=mg, c=c: nc.sync.dma_start(out=mg[:, 0:nq], in_=self.mgT[:, c, q0 - B0:q0 - B0 + nq]),
                              reads=[r_in], writes=[r_mg])
                        P.op("dve", lambda ps=ps, mg=mg, c=c: nc.vector.tensor_tensor(out=mer[:, c, 0:nq], in0=ps[:, 0:nq], in1=mg[:, 0:nq], op=ALU.mult),
                             reads=[r_ps, r_mg], writes=[r_mer])
                P.dma("sp", chl[cnt["l"] % 4], lambda: nc.sync.dma_start(out=glu[:, :, 0:nq + 32], in_=self.gluT[:, :, q0 - 32 - B0:q0 - B0 + nq]),
                      reads=[r_in], writes=[r_glu])
                cnt["l"] += 1
                s_sum, r_ssum = sps.next()
                s_sq, r_ssq = sps.next()
                for chn in range(16):
                    dg, r_dg = dgp.next()
                    for k in range(31):
                        P.op("dve", lambda dg=dg, k=k, chn=chn: nc.vector.tensor_scalar(
                            out=dg[:, k, :], in0=self.ident[:], scalar1=dww[:, chn, k:k + 1], scalar2=None, op0=ALU.mult),
                            reads=[self.r_const, r_cst], writes=[r_dg])
                    ps, r_ps = mm.next()
                    for k in range(31):
                        P.op("pe", lambda ps=ps, dg=dg, k=k, chn=chn: nc.tensor.matmul(
                            ps[:, 0:nq], lhsT=dg[:, k, :], rhs=glu[:, chn, k + 2:k + 2 + nq], start=(k == 0), stop=(k == 30)),
                            reads=[r_dg, r_glu], writes=[r_ps])
                    P.op("act", lambda ps=ps, chn=chn: nc.scalar.activation(out=ybf[:, chn, 0:nq], in_=ps[:, 0:nq], func=AF.Identity,
                                                                          bias=cols[:, 0, chn:chn + 1]), reads=[r_ps, r_cst], writes=[r_ybf])
                    yq, r_yq = ysq.next()
                    P.op("act", lambda ps=ps, chn=chn, yq=yq: nc.scalar.activation(out=yq[:, 0:nq], in_=ps[:, 0:nq], func=AF.Square,
                                                                                 bias=cols[:, 0, chn:chn + 1]), reads=[r_ps, r_cst], writes=[r_yq])
                    P.op("pe", lambda chn=chn: nc.tensor.matmul(s_sum[:, 0:nq], lhsT=self.ones[:], rhs=ybf[:, chn, 0:nq],
                                                                start=(chn == 0), stop=(chn == 15)), reads=[r_ybf, self.r_const], writes=[r_ssum])
                    P.op("pe", lambda chn=chn, yq=yq: nc.tensor.matmul(s_sq[:, 0:nq], lhsT=self.ones[:], rhs=yq[:, 0:nq],
                                                                       start=(chn == 0), stop=(chn == 15)), reads=[r_yq, self.r_const], writes=[r_ssq])
                mean, rstd, msq = stat[:, 0, 0:nq], stat[:, 1, 0:nq], stat[:, 2, 0:nq]
                P.op("dve", lambda: nc.vector.tensor_scalar(out=mean, in0=s_sum[:, 0:nq], scalar1=1.0 / D, scalar2=None, op0=ALU.mult),
                     reads=[r_ssum], writes=[r_stat])
                P.op("dve", lambda: nc.vector.tensor_tensor(out=msq, in0=mean, in1=mean, op=ALU.mult), reads=[r_stat], writes=[r_stat])
                P.op("dve", lambda: nc.vector.scalar_tensor_tensor(out=rstd, in0=s_sq[:, 0:nq], scalar=1.0 / D, in1=msq, op0=ALU.mult, op1=ALU.subtract),
                     reads=[r_ssq, r_stat], writes=[r_stat])
                P.op("dve", lambda: nc.vector.tensor_scalar(out=rstd, in0=rstd, scalar1=EPS, scalar2=None, op0=ALU.add), reads=[r_stat], writes=[r_stat])
                P.op("act", lambda: nc.scalar.sqrt(out=rstd, in_=rstd), reads=[r_stat], writes=[r_stat])
                P.op("dve", lambda: nc.vector.reciprocal(out=rstd, in_=rstd), reads=[r_stat], writes=[r_stat])
                for chn in range(16):
                    tf, r_tf = tmpf.next()
                    P.op("dve", lambda tf=tf, chn=chn: nc.vector.tensor_tensor(out=tf[:, 0:nq], in0=ybf[:, chn, 0:nq], in1=mean, op=ALU.subtract),
                         reads=[r_ybf, r_stat], writes=[r_tf])
                    P.op("dve", lambda tf=tf: nc.vector.tensor_tensor(out=tf[:, 0:nq], in0=tf[:, 0:nq], in1=rstd, op=ALU.mult),
                         reads=[r_tf, r_stat], writes=[r_tf])
                    P.op("act", lambda tf=tf, chn=chn: nc.scalar.activation(out=uc[:, chn, 0:nq], in_=tf[:, 0:nq], func=AF.Silu,
                                                                          scale=cols[:, 1, chn:chn + 1], bias=cols[:, 2, chn:chn + 1]),
                         reads=[r_tf, r_cst], writes=[r_uc])
                for cb in range(4):
                    wt, wr = load_wblk(self.wb_d["wpw_b"], cb)
                    for cl in range(4):
                        c = cb * 4 + cl
                        ps, r_ps = mm.next()
                        for kc in range(16):
                            P.op("pe", lambda ps=ps, kc=kc, cl=cl, wt=wt: nc.tensor.matmul(
                                ps[:, 0:nq], lhsT=wt[:, kc, cl * 128:(cl + 1) * 128], rhs=uc[:, kc, 0:nq], start=(kc == 0), stop=(kc == 15)),
                                reads=[wr, r_uc], writes=[r_ps])
                        mg, r_mg = mgp.next()
                        ch = chl[cnt["l"] % 4]
                        cnt["l"] += 1
                        P.dma("sp", ch, lambda mg=mg, c=c: nc.sync.dma_start(out=mg[:, 0:nq], in_=self.mgT[:, 16 + c, q0 - B0:q0 - B0 + nq]),
                              reads=[r_in], writes=[r_mg])
                        tf, r_tf = tmpf.next()
                        P.op("dve", lambda ps=ps, mg=mg, c=c, tf=tf: nc.vector.scalar_tensor_tensor(
                            out=tf[:, 0:nq], in0=ps[:, 0:nq], scalar=cols[:, 3, c:c + 1], in1=mg[:, 0:nq], op0=ALU.add, op1=ALU.mult),
                            reads=[r_ps, r_mg, r_cst], writes=[r_tf])
                        P.op("dve", lambda c=c, tf=tf: nc.vector.tensor_tensor(out=mer[:, c, 0:nq], in0=mer[:, c, 0:nq], in1=tf[:, 0:nq], op=ALU.add),
                             reads=[r_tf, r_mer], writes=[r_mer])
                for cb in range(4):
                    wt, wr = load_wblk(self.wb_d["wout_b"], cb)
                    for s_ in range(nsub):
                        ps, r_ps = mm.next()
                        for kc in range(16):
                            P.op("pe", lambda ps=ps, kc=kc, s_=s_, wt=wt: nc.tensor.matmul(
                                ps[0:rows, :], lhsT=mer[:, kc, s_ * 128:s_ * 128 + rows], rhs=wt[:, kc, :], start=(kc == 0), stop=(kc == 15)),
                                reads=[wr, r_mer], writes=[r_ps])
                        xt, r_xt = xp.next()
                        i3 = cnt["s"] % 3
                        cnt["s"] += 1
                        r0 = q0 + s_ * 128
                        P.dma("sp", chs[i3], lambda xt=xt, r0=r0, cb=cb: nc.sync.dma_start(out=xt[0:rows, :], in_=self.xv[r0:r0 + rows, cb * 512:(cb + 1) * 512]),
                              writes=[r_xt])
                        tf, r_tf = tmpf.next()
                        P.op("dve", lambda ps=ps, tf=tf, cb=cb: nc.vector.tensor_tensor(out=tf[0:rows, :], in0=ps[0:rows, :], in1=g1bc[0:rows, cb * 512:(cb + 1) * 512],
                                                                                     op=ALU.mult), reads=[r_ps, r_cst], writes=[r_tf])
                        P.op("dve", lambda xt=xt, tf=tf: nc.vector.tensor_tensor(out=xt[0:rows, :], in0=xt[0:rows, :], in1=tf[0:rows, :], op=ALU.add),
                             reads=[r_tf, r_xt], writes=[r_xt])
                        P.dma("act", chs[i3], lambda xt=xt, r0=r0, cb=cb: nc.scalar.dma_start(
                            out=self.xmid[r0 - Q0:r0 - Q0 + rows, cb * 512:(cb + 1) * 512], in_=xt[0:rows, :]), reads=[r_xt], writes=[r_out])

            for ti, (q0, nq) in enumerate(QTILES):
                if self.dbg and "tiles" in self.dbg and ti not in self.dbg["tiles"]:
                    continue
                mix_tile(ti, q0, nq)
            P.barrier()


    def phase_ffn(self):
        nc, P = self.nc, self.P
        with ExitStack() as st:
            sbt = lambda name, shape, dt: st.enter_context(nc.sbuf_tensor(name, list(shape), dt))
            g2bc = sbt("f_g2bc", [128, D], F32)
            fgbc = sbt("f_fgbc", [128, D], F32)
            n2g = sbt("f_n2g", [128, 16], F32)
            s2 = sbt("f_s2", [128, 16], F32)
            fdw = sbt("f_fdw", [128, 3, 88], F32)
            fdb = sbt("f_fdb", [128, 88], F32)
            hfl = sbt("f_hfl", [128, 1], F32)
            r_cst, r_s2 = Res(), Res()
            chc = P.chan("f_c")
            P.dma("sp", chc, lambda: nc.sync.dma_start(out=g2bc[:], in_=self.ada_d[5 * D:6 * D].partition_broadcast(128)), writes=[r_cst])
            P.dma("sp", chc, lambda: nc.sync.dma_start(out=fgbc[:], in_=self.i_fg.partition_broadcast(128)), writes=[r_cst], cont=True)
            P.dma("sp", chc, lambda: nc.sync.dma_start(out=n2g[:], in_=self.i_n2g.rearrange("(c p) -> p c", p=128), allow_slow_non_contiguous=True),
                  writes=[r_cst], cont=True)
            for k in range(3):
                P.dma("sp", chc, lambda k=k: nc.sync.dma_start(out=fdw[:, k, :], in_=self.i_fdw[k, :].rearrange("(c p) -> p c", p=128),
                                                            allow_slow_non_contiguous=True), writes=[r_cst], cont=True)
            P.dma("sp", chc, lambda: nc.sync.dma_start(out=fdb[:], in_=self.i_fdb.rearrange("(c p) -> p c", p=128), allow_slow_non_contiguous=True),
                  writes=[r_cst], cont=True)
            P.dma("sp", chc, lambda: nc.sync.dma_start(out=hfl[:], in_=self.i_hflag[:, :]), writes=[r_cst], cont=True)
            P.op("dve", lambda: nc.vector.scalar_tensor_tensor(out=s2[:], in0=self.ada[:, 64:80], scalar=1.0, in1=n2g[:], op0=ALU.add, op1=ALU.mult),
                 reads=[self.r_ada, r_cst], writes=[r_s2])
            xpool = Pool(P, st, "f_x", 1, [128, 4, D], F32)
            chx = P.chan("f_x")
            junk = sbt("f_junk", [128, D], BF16)
            r_junk = Res()
            sspool = Pool(P, st, "f_ss", 2, [128, 4], F32)
            rspool = Pool(P, st, "f_rs", 2, [128, 4], F32)
            xnpool = Pool(P, st, "f_xn", 1, [128, 4, D], BF16)
            h2T = sbt("f_h2T", [128, 16, 514], BF16)
            r_h2T = Res()
            tpp = Pool(P, st, "f_tp", 2, [128, 512], BF16, psum=True)
            up = Pool(P, st, "f_up", 2, [128, 1024], F32, psum=True)
            dn = Pool(P, st, "f_dn", 2, [128, 512], F32, psum=True)
            wup = Pool(P, st, "f_wu", 2, [128, 16, 256], BF16)
            wdn = Pool(P, st, "f_wd", 2, [128, 44, 256], BF16)
            chwu = [P.chan("f_wu0"), P.chan("f_wu1")]
            chwd = [P.chan("f_wd0"), P.chan("f_wd1")]
            z = sbt("f_z", [128, 44, 512], BF16)
            r_z = Res()
            hh, r_hh = z[:, 0:16, 0:128], r_z
            Tp = Pool(P, st, "f_T", 3, [128, 512], F32)
            sgp = Pool(P, st, "f_sg", 1, [128, 512], BF16)
            tmp = Pool(P, st, "f_tmp", 1, [128, 256], F32)
            cho = [P.chan("f_o0"), P.chan("f_o1")]
            r_in = Res()
            wupb, wdnb = self.wb_d["wup_b"], self.wb_d["wdn_b"]
            cnt = {"u": 0, "d": 0, "o": 0}

            class HP:
                def __init__(s_, ap, res):
                    s_.ap, s_.res = ap, res

                def next(s_):
                    return s_.ap, s_.res

            self.norm_T(P, st, self.xmid, 0, 1, xpool, chx, junk, r_junk, sspool, rspool, xnpool, HP(hh, r_hh), tpp,
                        self.ident, self.r_const, s2, r_s2, self.ada, self.r_ada, 48)
            P.op("dve", lambda: nc.vector.tensor_scalar(out=h2T[:, :, 0:2], in0=hh[:, :, 62:64], scalar1=hfl[:, 0:1], scalar2=None, op0=ALU.mult),
                 reads=[r_hh, r_cst], writes=[r_h2T])

            def window(w):
                row0 = HALO + 512 * w
                if w > 0:
                    P.op("pool", lambda: nc.gpsimd.tensor_copy(out=h2T[:, :, 0:2], in_=h2T[:, :, 512:514]), reads=[r_h2T], writes=[r_h2T])
                self.norm_T(P, st, self.xmid, row0, 4, xpool, chx, junk, r_junk, sspool, rspool, xnpool, HP(h2T[:, :, 2:514], r_h2T), tpp,
                            self.ident, self.r_const, s2, r_s2, self.ada, self.r_ada, 48)
                xt, r_xt = self.last_xt
                for pb in range(44):
                    wt, wr = wup.next()
                    ch = chwu[cnt["u"] % 2]
                    cnt["u"] += 1
                    P.dma("sp", ch, lambda wt=wt, pb=pb: nc.sync.dma_start(
                        out=wt[:, :, 0:128], in_=wupb[:, pb * 128:(pb + 1) * 128].rearrange("(k p) c -> p k c", p=128)), reads=[self.r_wb], writes=[wr])
                    P.dma("sp", ch, lambda wt=wt, pb=pb: nc.sync.dma_start(
                        out=wt[:, :, 128:256], in_=wupb[:, FF + pb * 128:FF + (pb + 1) * 128].rearrange("(k p) c -> p k c", p=128)),
                        reads=[self.r_wb], writes=[wr], cont=True)
                    for cl in range(1):
                        c = pb
                        Ts = []
                        for half in range(2):
                            cc = c + 44 * half
                            woff = half * 128
                            ps, r_ps = up.next()
                            for kc in range(16):
                                P.op("pe", lambda ps=ps, kc=kc, woff=woff, wt=wt: nc.tensor.matmul(
                                    ps[:, 512:1024], lhsT=wt[:, kc, woff:woff + 128], rhs=h2T[:, kc, 2:514], start=(kc == 0), stop=(kc == 15)),
                                    reads=[wr, r_h2T], writes=[r_ps])
                            for kc in range(16):
                                P.op("pe", lambda ps=ps, kc=kc, woff=woff, wt=wt: nc.tensor.matmul(
                                    ps[:, 510:512], lhsT=wt[:, kc, woff:woff + 128], rhs=h2T[:, kc, 0:2], start=(kc == 0), stop=(kc == 15)),
                                    reads=[wr, r_h2T], writes=[r_ps])
                            T_, r_T = Tp.next()
                            P.op("act", lambda ps=ps, T_=T_, cc=cc: nc.scalar.activation(out=T_[:], in_=ps[:, 512:1024], func=AF.Identity,
                                                                                        scale=fdw[:, 2, cc:cc + 1], bias=fdb[:, cc:cc + 1]),
                                 reads=[r_ps, r_cst], writes=[r_T])
                            P.op("dve", lambda ps=ps, T_=T_, cc=cc: nc.vector.scalar_tensor_tensor(
                                out=T_[:], in0=ps[:, 511:1023], scalar=fdw[:, 1, cc:cc + 1], in1=T_[:], op0=ALU.mult, op1=ALU.add),
                                reads=[r_ps, r_T, r_cst], writes=[r_T])
                            P.op("dve", lambda ps=ps, T_=T_, cc=cc: nc.vector.scalar_tensor_tensor(
                                out=T_[:], in0=ps[:, 510:1022], scalar=fdw[:, 0, cc:cc + 1], in1=T_[:], op0=ALU.mult, op1=ALU.add),
                                reads=[r_ps, r_T, r_cst], writes=[r_T])
                            Ts.append((T_, r_T))
                        (Ta, r_Ta), (Tg, r_Tg) = Ts
                        sg, r_sg = sgp.next()
                        P.op("act", lambda Tg=Tg, sg=sg: nc.scalar.activation(out=sg[:], in_=Tg[:], func=AF.Silu), reads=[r_Tg], writes=[r_sg])
                        P.op("dve", lambda Ta=Ta, sg=sg, c=c: nc.vector.tensor_tensor(out=z[:, c, :], in0=Ta[:], in1=sg[:], op=ALU.mult),
                             reads=[r_Ta, r_sg], writes=[r_z])
                for cb in range(8):
                    wt, wr = wdn.next()
                    ch = chwd[cnt["d"] % 2]
                    cnt["d"] += 1
                    P.dma("sp", ch, lambda wt=wt, cb=cb: nc.sync.dma_start(
                        out=wt[:], in_=wdnb[:, cb * 256:(cb + 1) * 256].rearrange("(k p) c -> p k c", p=128)), reads=[self.r_wb], writes=[wr])
                    for s_ in range(4):
                        ps, r_ps = dn.next()
                        for kc in range(44):
                            P.op("pe", lambda ps=ps, kc=kc, s_=s_, wt=wt: nc.tensor.matmul(
                                ps[:, 0:256], lhsT=z[:, kc, s_ * 128:(s_ + 1) * 128], rhs=wt[:, kc, :], start=(kc == 0), stop=(kc == 43)),
                                reads=[wr, r_z], writes=[r_ps])
                        tf, r_tf = tmp.next()
                        P.op("dve", lambda ps=ps, tf=tf, cb=cb: nc.vector.tensor_tensor(out=tf[:], in0=ps[:, 0:256], in1=g2bc[:, cb * 256:(cb + 1) * 256], op=ALU.mult),
                             reads=[r_ps, r_cst], writes=[r_tf])
                        P.op("dve", lambda tf=tf, s_=s_, cb=cb, xt=xt: nc.vector.tensor_tensor(
                            out=xt[:, s_, cb * 256:(cb + 1) * 256], in0=xt[:, s_, cb * 256:(cb + 1) * 256], in1=tf[:], op=ALU.add),
                            reads=[r_tf, r_xt], writes=[r_xt])
                ss, r_ss = sspool.next()
                rs, r_rs = rspool.next()
                for s_ in range(4):
                    P.op("act", lambda s_=s_, xt=xt, ss=ss: nc.scalar.activation(out=junk[:], in_=xt[:, s_, :], func=AF.Square, accum_out=ss[:, s_:s_ + 1]),
                         reads=[r_xt], writes=[r_junk, r_ss])
                P.op("dve", lambda ss=ss, rs=rs: nc.vector.tensor_scalar(out=rs[:], in0=ss[:], scalar1=1.0 / D, scalar2=EPS, op0=ALU.mult, op1=ALU.add),
                     reads=[r_ss], writes=[r_rs])
                P.op("act", lambda rs=rs: nc.scalar.sqrt(out=rs[:], in_=rs[:]), reads=[r_rs], writes=[r_rs])
                P.op("dve", lambda rs=rs: nc.vector.reciprocal(out=rs[:], in_=rs[:]), reads=[r_rs], writes=[r_rs])
                for s_ in range(4):
                    P.op("dve", lambda s_=s_, xt=xt, rs=rs: nc.vector.scalar_tensor_tensor(
                        out=xt[:, s_, :], in0=xt[:, s_, :], scalar=rs[:, s_:s_ + 1], in1=fgbc[:], op0=ALU.mult, op1=ALU.mult),
                        reads=[r_xt, r_rs, r_cst], writes=[r_xt])
                    i2 = cnt["o"] % 2
                    cnt["o"] += 1
                    t0 = 512 * w + 128 * s_
                    P.dma("act", cho[i2], lambda s_=s_, xt=xt, t0=t0: nc.scalar.dma_start(out=self.out[t0:t0 + 128, :], in_=xt[:, s_, :]), reads=[r_xt])

            for w in range(4):
                if self.dbg and "wins" in self.dbg and w not in self.dbg["wins"]:
                    continue
                window(w)
            P.barrier()

_STATIC = {}


def static_tables():
    if _STATIC:
        return _STATIC
    sl = np.array(SLOPES, np.float64)
    i = np.arange(128, dtype=np.float64)
    bs = sl[None, :, None] * (i[:, None, None] + 64.0 * (np.arange(NDS)[None, None, :] - DOFF))
    bc = sl[None, :, None] * (16.0 * i[:, None, None] + 31.0 + 64.0 * (np.arange(NDC)[None, None, :] - ROFF))
    masks = np.zeros((128, NMASK, 512), np.float32)
    for n, v in enumerate(MASK_LIST):
        masks[:, n, :v.shape[1]] = np.where(v, 0.0, -30000.0)
    E = np.zeros((128, 64, 128), np.float32)
    for u in range(64):
        for k in range(128):
            E[2 * u + k // 64, u, k] = 1.0
    Ov = np.zeros((128, 9, NBW), np.float32)
    for jj in range(9):
        for ii in range(128):
            for b in range(NBLKW):
                dlt = ii - 128 * (jj + 1) - 4 * b + 1056
                if -1 <= dlt <= 3:
                    Ov[ii, jj, b] = 1.0
    _STATIC.update({"ident": np.eye(128, dtype=np.float32).astype(NPBF),
                    "bias_s": bs.astype(np.float32), "bias_c": bc.astype(np.float32),
                    "masks": masks.astype(NPBF), "esel": E.astype(NPBF), "ovm": Ov.astype(NPBF),
                    "ones_bf": np.ones((128, 128), np.float32).astype(NPBF)})
    return _STATIC


def host_tables(c):
    t_start = c * TOWN
    real = np.arange(TV) - OWN0 + t_start
    vt = (real >= 0).astype(np.float32)
    nv = np.arange(NCV)
    rn = nv - (OWN0 - t_start) // 16
    vc = ((rn >= 0) & (rn <= 1022) & (nv <= 1150)).astype(np.float32)
    selb = np.zeros((5, 4, 128, NBW), np.float32)
    selv = np.zeros((5, 4, 128, NBW), np.float32)
    b = np.arange(NBLKW)
    for ti, (q0, nq) in enumerate(QTILES):
        qend = q0 + nq
        jv = b + qend // 64 - NBLKW
        realb = jv - (OWN0 - t_start) // 64
        for s_ in range(max(1, nq // 128)):
            rows = min(128, nq)
            tq = q0 + 128 * s_ + np.arange(rows)
            cur = tq // 64
            valid = (realb[None, :] >= 0) & (jv[None, :] <= cur[:, None])
            forced = (realb[None, :] == 0) | (jv[None, :] == cur[:, None]) | (jv[None, :] == cur[:, None] - 1)
            selv[ti, s_, :rows, :NBLKW] = valid
            selb[ti, s_, :rows, :NBLKW] = 1.0 + 1e6 * forced
    d = {"vtok": np.ascontiguousarray(vt.reshape(TV // 128, 128).T),
         "vcmp": np.ascontiguousarray(vc.reshape(NCV // 128, 128).T),
         "selb": selb, "selv": selv,
         "hflag": np.full((128, 1), 1.0 if c > 0 else 0.0, np.float32)}
    d.update(static_tables())
    return d


def make_inputs(inputs, c):
    x = np.asarray(inputs["x"], np.float32)[0]
    t_start = c * TOWN
    xv = np.zeros((TV, D), np.float32)
    lo = OWN0 - t_start
    xv[lo:] = x[:t_start + TOWN]
    m = {"xv": xv, "c": np.asarray(inputs["c"], np.float32),
         "w_ada": np.asarray(inputs["w_ada"], np.float32)[0], "b_ada": np.asarray(inputs["b_ada"], np.float32)[0],
         "norm1_g": np.asarray(inputs["norm1_g"], np.float32)[0], "w_in": np.asarray(inputs["w_in"], np.float32)[0]}
    for n in ("cmp_pe", "w_kc1", "w_kc2", "w_vc1", "w_vc2", "w_o_nsa", "conv_dw_w", "conv_dw_b", "conv_ln_g", "conv_ln_b",
              "conv_pw_w", "conv_pw_b", "w_out", "norm2_g", "ffn_w_up", "ffn_dw_w", "ffn_dw_b", "ffn_w_down"):
        m[n] = np.asarray(inputs[n], np.float32)[0]
    m["final_g"] = np.asarray(inputs["final_g"], np.float32)
    m.update(host_tables(c))
    return m


def kernel(**inputs):
    k = K()
    nc = k.build()
    in_maps = [{n: v for n, v in make_inputs(inputs, c).items() if n in k.ins} for c in range(NCORE)]
    res = run_bass_kernel_spmd(nc, in_maps, core_ids=list(range(NCORE)))
    outs = [np.asarray(res.results[c]["out"], np.float32) for c in range(NCORE)]
    return np.concatenate(outs, axis=0)[None]
```

```python
from contextlib import ExitStack
import numpy as np
import ml_dtypes
import concourse.bass as bass
import concourse.mybir as mybir
from concourse.bass_utils import run_bass_kernel_spmd

F32 = mybir.dt.float32
BF16 = mybir.dt.bfloat16
I32 = mybir.dt.int32
AF = mybir.ActivationFunctionType
ALU = mybir.AluOpType
NPBF = ml_dtypes.bfloat16

D = 2048
T = 16384
NCORE = 8
TOWN = T // NCORE
NH, NG, HG, DK = 16, 2, 8, 128
EPS = 1e-6
FF = 5632
WIN = 512
OWN0 = 16384
TV = OWN0 + TOWN
HALO = 64
Q0 = OWN0 - HALO
NQ = TOWN + HALO
W0 = OWN0 - 640
NA = TV - W0
B0 = OWN0 - 128
NB = TV - B0
NCV = TV // 16
C_Q, C_KC, C_VC, C_KS, C_VS, C_KW, C_VW, C_GN, C_GLU, C_GM = 0, 2048, 2304, 2560, 2816, 3072, 3328, 3584, 3632, 7728
INW = 11824
EPOCH = 12000
SLOPES = [2.0 ** (-8.0 * (h + 1) / 16) for h in range(NH)]
NBLKW = 264
NBW = 272
DOFF, NDS = 272, 280
ROFF, NDC = 296, 280
QTILES = [(Q0, 64)] + [(OWN0 + 512 * i, 512) for i in range(4)]
SKIP = 64.0


def exp_width(h, nq):
    w = 64
    while w * 2 <= min(512, nq) and SLOPES[h] * (w * 2) <= 64.0:
        w *= 2
    return min(w, nq)


def n_slc_chunks(h, nq, maxc):
    return int(min(maxc, np.floor((SKIP / SLOPES[h] + nq) / 128) + 1))


def n_cmp_chunks(h, nq):
    return int(min(9, np.floor((SKIP / SLOPES[h] + nq + 15) / 2048) + 1))


def chunk_valid(kind, nq, j):
    i = np.arange(128)[:, None]
    q = np.arange(nq)[None, :]
    if kind == "cmp":
        kp = 16 * i + 31 + nq - 2048 * (j + 1)
        v = kp <= q
    else:
        kp = i + nq - 128 * (j + 1)
        dist = q - kp
        v = dist >= 0
        if kind == "win":
            v = v & (dist < WIN)
    return None if v.all() else v


def mask_index():
    idx = {}
    for nq in (64, 512):
        nwin = 8 if nq == 512 else 5
        for kind, nj in (("slc", 4), ("win", nwin), ("cmp", 1)):
            for j in range(nj):
                v = chunk_valid(kind, nq, j)
                if v is None:
                    continue
                idx[(kind, nq, j)] = v
    uniq, out = [], {}
    for key, v in idx.items():
        for n, u in enumerate(uniq):
            if u.shape == v.shape and (u == v).all():
                out[key] = n
                break
        else:
            uniq.append(v)
            out[key] = len(uniq) - 1
    return out, uniq


MASK_IDX, MASK_LIST = mask_index()
NMASK = len(MASK_LIST)


class Res:
    __slots__ = ("name", "last_w", "readers")

    def __init__(self, name=""):
        self.name = name
        self.last_w = None
        self.readers = []


class Op:
    __slots__ = ("eng", "fn", "deps", "seq", "sig", "is_dma", "chan", "needs_sig", "pos")


class Chan:
    def __init__(self, prog, name):
        self.sem = prog.new_sem("ch_" + name)
        self.count = 0
        self.last_op = None
        self.group = []


class Prog:
    ENGS = ("pe", "act", "dve", "pool", "sp")

    def __init__(self, nc, stack):
        self.nc = nc
        self.stack = stack
        self.ops = {e: [] for e in self.ENGS}
        self.seq = 0
        self.nsem = 0
        self.eng_sems = {e: [] for e in self.ENGS}
        self.chans = []
        self.uid = 0

    def new_sem(self, name):
        self.nsem += 1
        return self.stack.enter_context(self.nc.semaphore(name))

    def chan(self, name):
        c = Chan(self, name)
        self.chans.append(c)
        return c

    def _mk(self, eng, fn, reads, writes):
        op = Op()
        op.eng, op.fn, op.seq = eng, fn, self.seq
        self.seq += 1
        op.is_dma, op.chan, op.needs_sig, op.sig = False, None, False, None
        deps = []
        for r in reads:
            if r.last_w is not None:
                deps.append(r.last_w)
        for w in writes:
            if w.last_w is not None:
                deps.append(w.last_w)
            deps.extend(w.readers)
        for r in reads:
            if not getattr(op, "is_dma", False) and eng in ("pe", "act", "dve"):
                r.readers = [x for x in r.readers if x.eng != eng or x.is_dma]
            r.readers.append(op)
        for w in writes:
            w.last_w = op
            w.readers = []
        seen, dd = set(), []
        for d in deps:
            if id(d) in seen or d is op:
                continue
            seen.add(id(d))
            if eng == "pe" and d.eng == "pe" and not d.is_dma:
                continue
            dd.append(d)
        op.deps = dd
        self.ops[eng].append(op)
        return op

    def op(self, eng, fn, reads=(), writes=()):
        return self._mk(eng, fn, list(reads), list(writes))

    def dma(self, queue, chan, fn, reads=(), writes=(), cont=False):
        op = self._mk(queue, fn, list(reads), list(writes))
        op.is_dma, op.chan = True, chan
        if not cont:
            if chan.last_op is not None and chan.last_op not in op.deps:
                op.deps.append(chan.last_op)
            chan.group = []
        op.deps = [d for d in op.deps if d not in chan.group]
        chan.count += 16
        chan.group.append(op)
        for o in chan.group:
            o.sig = (chan.sem, chan.count)
        op.needs_sig = True
        chan.last_op = op
        return op

    def barrier(self):
        lasts = []
        for e in self.ENGS:
            for op in reversed(self.ops[e]):
                if not op.is_dma:
                    lasts.append(op)
                    break
        for c in self.chans:
            if c.last_op is not None:
                lasts.append(c.last_op)
        nc = self.nc
        eo = {"pe": nc.tensor, "act": nc.scalar, "dve": nc.vector, "pool": nc.gpsimd, "sp": nc.sync}
        for e in self.ENGS:
            op = self._mk(e, (lambda e=e: eo[e].nop()), [], [])
            op.deps = [d for d in lasts if not (d.eng == e and not d.is_dma)]

    def emit(self):
        nc = self.nc
        for e in self.ENGS:
            for op in self.ops[e]:
                for d in op.deps:
                    if not d.is_dma:
                        d.needs_sig = True
        for e in self.ENGS:
            cnt, sem = 0, None
            for op in self.ops[e]:
                if op.is_dma or not op.needs_sig:
                    continue
                if sem is None or cnt >= EPOCH:
                    sem = self.new_sem(f"e_{e}_{len(self.eng_sems[e])}")
                    self.eng_sems[e].append(sem)
                    cnt = 0
                cnt += 1
                op.sig = (sem, cnt)
        with nc.Block() as block:
            for e in self.ENGS:
                ops = self.ops[e]
                if not ops:
                    continue
                deco = {"pe": block.tensor, "act": block.scalar, "dve": block.vector,
                        "pool": block.gpsimd, "sp": block.sync}[e]

                def body(engobj, ops=ops):
                    known = {}
                    for op in ops:
                        for d in op.deps:
                            sem, val = d.sig
                            if known.get(id(sem), 0) >= val:
                                continue
                            engobj.wait_ge(sem, val)
                            known[id(sem)] = val
                        ins = op.fn()
                        if op.needs_sig:
                            ins.then_inc(op.sig[0], 16 if op.is_dma else 1)
                    last = {}
                    for op in ops:
                        if op.is_dma:
                            last[id(op.chan)] = op
                    for op in last.values():
                        sem, val = op.sig
                        if known.get(id(sem), 0) < val:
                            engobj.wait_ge(sem, val)
                            known[id(sem)] = val
                deco(body)


class Pool:
    def __init__(self, P, st, name, n, shape, dt, psum=False):
        self.t, self.r = [], []
        for i in range(n):
            if psum:
                self.t.append(st.enter_context(P.nc.psum_tensor(f"{name}{i}", list(shape), dt)))
            else:
                self.t.append(st.enter_context(P.nc.sbuf_tensor(f"{name}{i}", list(shape), dt)))
            self.r.append(Res(f"{name}{i}"))
        self.i = 0
        self.n = n

    def next(self):
        k = self.i % self.n
        self.i += 1
        return self.t[k], self.r[k]


class K:
    def __init__(self, dbg=None):
        self.dbg = dbg
        nc = self.nc = bass.Bass("TRN2", target_bir_lowering=False)
        self.ins = {}
        self.outs = {}

    def din(self, name, shape, dt=F32):
        t = self.nc.dram_tensor(name, list(shape), dt, kind="ExternalInput").ap()
        self.ins[name] = t
        return t

    def dscr(self, name, shape, dt):
        kind = "ExternalOutput" if (self.dbg and name in self.dbg) else "Internal"
        t = self.nc.dram_tensor(name, list(shape), dt, kind=kind).ap()
        return t

    def build(self, upto=99):
        nc = self.nc
        xv = self.din("xv", [TV, D])
        c_in = self.din("c", [1, D])
        w_ada = self.din("w_ada", [D, 6 * D])
        b_ada = self.din("b_ada", [6 * D])
        norm1_g = self.din("norm1_g", [D])
        w_in = self.din("w_in", [D, INW])
        vtok = self.din("vtok", [128, TV // 128])
        ident_in = self.din("ident", [128, 128], BF16)
        self.i_vcmp = self.din("vcmp", [128, NCV // 128])
        self.i_selb = self.din("selb", [5, 4, 128, NBW])
        self.i_selv = self.din("selv", [5, 4, 128, NBW])
        self.i_hflag = self.din("hflag", [128, 1])
        self.i_bias_s = self.din("bias_s", [128, NH, NDS])
        self.i_bias_c = self.din("bias_c", [128, NH, NDC])
        self.i_masks = self.din("masks", [128, NMASK, 512], BF16)
        self.i_esel = self.din("esel", [128, 64, 128], BF16)
        self.i_ovm = self.din("ovm", [128, 9, NBW], BF16)
        self.i_ones = self.din("ones_bf", [128, 128], BF16)
        self.i_cmp_pe = self.din("cmp_pe", [32, 128])
        self.i_w1 = [self.din("w_kc1", [4096, 256]), self.din("w_vc1", [4096, 256])]
        self.i_w2 = [self.din("w_kc2", [256, 128]), self.din("w_vc2", [256, 128])]
        self.i_wo = self.din("w_o_nsa", [D, D])
        self.i_dww = self.din("conv_dw_w", [31, D])
        self.i_dwb = self.din("conv_dw_b", [D])
        self.i_lng = self.din("conv_ln_g", [D])
        self.i_lnb = self.din("conv_ln_b", [D])
        self.i_wpw = self.din("conv_pw_w", [D, D])
        self.i_pwb = self.din("conv_pw_b", [D])
        self.i_wout = self.din("w_out", [D, D])
        self.i_n2g = self.din("norm2_g", [D])
        self.i_wup = self.din("ffn_w_up", [D, 2 * FF])
        self.i_fdw = self.din("ffn_dw_w", [3, 2 * FF])
        self.i_fdb = self.din("ffn_dw_b", [2 * FF])
        self.i_wdn = self.din("ffn_w_down", [FF, D])
        self.i_fg = self.din("final_g", [D])
        out = self.nc.dram_tensor("out", [TOWN, D], F32, kind="ExternalOutput").ap()
        self.out = out
        ada_d = self.dscr("ada_d", [6 * D], F32)
        kcT_raw = self.dscr("kcT_raw", [NG, 128, TV + 16], BF16)
        vcT_raw = self.dscr("vcT_raw", [NG, 128, TV + 16], BF16)
        kslT = self.dscr("kslT", [NG, 128, TV], BF16)
        vsl = self.dscr("vsl", [TV, NG, 130], BF16)
        QT = self.dscr("QT", [128, NH, NB], BF16)
        kwT = self.dscr("kwT", [NG, 128, NA], BF16)
        vw = self.dscr("vw", [NA, NG, 130], BF16)
        gates = self.dscr("gates", [NB, 48], F32)
        gluT = self.dscr("gluT", [128, 16, NB], BF16)
        mgT = self.dscr("mgT", [128, 32, NB], BF16)
        self.kcT = self.dscr("kcT", [NG, 128, NCV], BF16)
        self.vca = self.dscr("vca", [NCV, NG, 130], BF16)
        self.xmid = self.dscr("xmid", [NQ, D], F32)
        self.accd = self.dscr("accd", [NQ, D], F32)
        self.dumpS = self.dscr("dumpS", [128, 512], F32)
        self.impd = self.dscr("impd", [5, 4, 128, NG, NBW], F32)
        self.wb_d = {n: self.dscr(n, sh, BF16) for n, sh in (("wo_b", [D, D]), ("wpw_b", [D, D]), ("wout_b", [D, D]),
                                                           ("wup_b", [D, 2 * FF]), ("wdn_b", [FF, D]))}
        self.xv, self.kcT_raw, self.vcT_raw, self.kslT, self.vsl = xv, kcT_raw, vcT_raw, kslT, vsl
        self.QT, self.kwT, self.vw, self.gates, self.gluT, self.mgT, self.ada_d = QT, kwT, vw, gates, gluT, mgT, ada_d

        with ExitStack() as st0:
            P = self.P = Prog(nc, st0)
            self.st0 = st0
            sb = lambda name, shape, dt: st0.enter_context(nc.sbuf_tensor(name, list(shape), dt))
            ident = sb("ident_sb", [128, 128], BF16)
            r_const = Res("const")
            ch_c = P.chan("const")
            P.dma("sp", ch_c, lambda: nc.sync.dma_start(out=ident[:], in_=ident_in[:, :]), writes=[r_const])
            vtok_sb = sb("vtok_sb", [128, TV // 128], F32)
            P.dma("sp", ch_c, lambda: nc.sync.dma_start(out=vtok_sb[:], in_=vtok[:, :]), writes=[r_const], cont=True)
            ada = sb("ada_sb", [128, 96], F32)
            r_ada = Res("ada")
            s1 = sb("s1", [128, 16], F32)
            r_s1 = Res("s1")
            self.ident, self.r_const, self.vtok_sb, self.ada, self.r_ada, self.sb0 = ident, r_const, vtok_sb, ada, r_ada, sb
            self.ones = sb("ones_sb", [128, 128], BF16)
            P.dma("sp", ch_c, lambda: nc.sync.dma_start(out=self.ones[:], in_=self.i_ones[:, :]), writes=[r_const], cont=True)
            self.hfl = sb("hfl_sb", [128, 1], F32)
            P.dma("sp", ch_c, lambda: nc.sync.dma_start(out=self.hfl[:], in_=self.i_hflag[:, :]), writes=[r_const], cont=True)
            self.vcmp_sb = sb("vcmp_sb", [128, NCV // 128], F32)
            P.dma("sp", ch_c, lambda: nc.sync.dma_start(out=self.vcmp_sb[:], in_=self.i_vcmp[:, :]), writes=[r_const], cont=True)

            with ExitStack() as st:
                cT = st.enter_context(nc.sbuf_tensor("cT", [128, 16], F32))
                cact = st.enter_context(nc.sbuf_tensor("cact", [128, 16], F32))
                bT = st.enter_context(nc.sbuf_tensor("bT", [128, 96], F32))
                g1T = st.enter_context(nc.sbuf_tensor("g1T", [128, 16], F32))
                r_cT, r_cact, r_bT, r_g1T = Res(), Res(), Res(), Res()
                ch0 = P.chan("p0")
                P.dma("sp", ch0, lambda: nc.sync.dma_start(out=cT[:], in_=c_in[0, :].rearrange("(j p) -> p j", p=128),
                                                         allow_slow_non_contiguous=True), writes=[r_cT])
                P.dma("sp", ch0, lambda: nc.sync.dma_start(out=bT[:], in_=b_ada.rearrange("(f p) -> p f", p=128),
                                                         allow_slow_non_contiguous=True), writes=[r_bT], cont=True)
                P.dma("sp", ch0, lambda: nc.sync.dma_start(out=g1T[:], in_=norm1_g.rearrange("(j p) -> p j", p=128),
                                                         allow_slow_non_contiguous=True), writes=[r_g1T], cont=True)
                P.op("act", lambda: nc.scalar.activation(out=cact[:], in_=cT[:], func=AF.Silu), reads=[r_cT], writes=[r_cact])
                wpool = Pool(P, st, "wada", 2, [128, 16, 512], F32)
                chw = [P.chan("wada0"), P.chan("wada1")]
                aps = st.enter_context(nc.psum_tensor("ada_ps", [128, 96], F32))
                r_aps = Res()
                for blk in range(24):
                    wt, wr = wpool.next()
                    P.dma("sp", chw[blk % 2],
                          lambda wt=wt, blk=blk: nc.sync.dma_start(
                              out=wt[:], in_=w_ada[:, blk * 512:(blk + 1) * 512].rearrange("(k p) c -> p k c", p=128)),
                          writes=[wr])
                    for fl in range(4):
                        f = blk * 4 + fl
                        for kc in range(16):
                            P.op("pe", lambda wt=wt, fl=fl, kc=kc, f=f: nc.tensor.matmul(
                                aps[:, f:f + 1], lhsT=wt[:, kc, fl * 128:(fl + 1) * 128], rhs=cact[:, kc:kc + 1],
                                start=(kc == 0), stop=(kc == 15)), reads=[wr, r_cact], writes=[r_aps])
                P.op("dve", lambda: nc.vector.tensor_tensor(out=ada[:], in0=aps[:], in1=bT[:], op=ALU.add),
                     reads=[r_aps, r_bT], writes=[r_ada])
                P.op("dve", lambda: nc.vector.scalar_tensor_tensor(out=s1[:], in0=ada[:, 16:32], scalar=1.0, in1=g1T[:],
                                                                   op0=ALU.add, op1=ALU.mult),
                     reads=[r_ada, r_g1T], writes=[r_s1])
                r_adad = Res()
                ch_ad = P.chan("adad")
                P.dma("act", ch_ad, lambda: nc.scalar.dma_start(out=ada_d.rearrange("(f p) -> p f", p=128), in_=ada[:],
                                                              allow_slow_non_contiguous=True),
                      reads=[r_ada], writes=[r_adad])
                P.barrier()
            if upto <= 0:
                P.emit()
                return nc

            self.r_wb = Res("wb_scratch")
            if not (self.dbg and "nocast" in self.dbg):
                with ExitStack() as st:
                    c32 = Pool(P, st, "cst32_", 3, [128, 8192], F32)
                    cbf = Pool(P, st, "cstbf_", 3, [128, 8192], BF16)
                    cci = [P.chan(f"cji{i}") for i in range(3)]
                    cco = [P.chan(f"cjo{i}") for i in range(3)]
                    for ji, (sv, dv, o, w) in enumerate(self.cast_jobs()):
                        t32, r32 = c32.next()
                        tbf, rbf = cbf.next()
                        P.dma("sp", cci[ji % 3], lambda t32=t32, sv=sv, o=o, w=w: nc.sync.dma_start(out=t32[:, 0:w], in_=sv[:, o:o + w]), writes=[r32])
                        if ji % 2 == 0:
                            P.op("dve", lambda t32=t32, tbf=tbf, w=w: nc.vector.tensor_copy(out=tbf[:, 0:w], in_=t32[:, 0:w]), reads=[r32], writes=[rbf])
                        else:
                            P.op("act", lambda t32=t32, tbf=tbf, w=w: nc.scalar.copy(out=tbf[:, 0:w], in_=t32[:, 0:w]), reads=[r32], writes=[rbf])
                        P.dma("act", cco[ji % 3], lambda tbf=tbf, dv=dv, o=o, w=w: nc.scalar.dma_start(out=dv[:, o:o + w], in_=tbf[:, 0:w]),
                              reads=[rbf], writes=[self.r_wb])
                    P.barrier()
            with ExitStack() as st:
                wkv32 = Pool(P, st, "wkv32_", 2, [128, 16, 128], F32)
                wkv = st.enter_context(nc.sbuf_tensor("wkv", [128, 16, 1024], BF16))
                r_wkv = Res()
                chw = [P.chan("wkv0"), P.chan("wkv1")]
                for q in range(8):
                    wt, wr = wkv32.next()
                    P.dma("sp", chw[q % 2], lambda wt=wt, q=q: nc.sync.dma_start(
                        out=wt[:], in_=w_in[:, C_KC + q * 128:C_KC + (q + 1) * 128].rearrange("(k p) c -> p k c", p=128)),
                        writes=[wr])
                    P.op("pool", lambda wt=wt, q=q: nc.gpsimd.tensor_copy(out=wkv[:, :, q * 128:(q + 1) * 128], in_=wt[:]),
                         reads=[wr], writes=[r_wkv])
                xpool = Pool(P, st, "xt", 2, [128, 4, D], F32)
                chx = [P.chan("x0"), P.chan("x1")]
                junk = st.enter_context(nc.sbuf_tensor("junk", [128, D], BF16))
                r_junk = Res()
                sspool = Pool(P, st, "ss", 2, [128, 4], F32)
                rspool = Pool(P, st, "rs", 2, [128, 4], F32)
                xnpool = Pool(P, st, "xn", 2, [128, 4, D], BF16)
                hTpool = Pool(P, st, "hT", 2, [128, 16, 512], BF16)
                tpp = Pool(P, st, "tp", 4, [128, 512], BF16, psum=True)
                mmp = Pool(P, st, "mm", 3, [128, 512], F32, psum=True)
                stg = Pool(P, st, "stg", 2, [128, 6, 512], BF16)
                stv = Pool(P, st, "stv", 2, [128, 4, NG, 130], BF16)
                chs = [P.chan("st0"), P.chan("st1")]
                chv = [P.chan("sv0"), P.chan("sv1")]
                r_scr = Res("scr1a")
                for i in range(2):
                    P.op("dve", lambda i=i: nc.vector.memset(stv.t[i][:], 0.0), writes=[stv.r[i]])
                nblk = TV // 512
                if self.dbg and "nblk" in self.dbg:
                    nblk = self.dbg["nblk"]
                pend = []

                def start_block(tb):
                    xn_, r_xn_ = self.norm_prep(P, xv, tb * 512, 4, xpool, chx[tb % 2], junk, r_junk, sspool, rspool, xnpool)
                    hT_, r_hT_ = hTpool.next()
                    return hT_, r_hT_, self.norm_groups(P, xn_, r_xn_, hT_, r_hT_, 4, tpp, s1, r_s1, 0)

                def pump(n):
                    for _ in range(n):
                        if pend:
                            pend.pop(0)()

                nxt = start_block(0)
                for tb in range(nblk):
                    t0 = tb * 512
                    hT, r_hT, grps = nxt
                    pend.extend(grps)
                    pump(16)
                    if tb + 1 < nblk:
                        nxt = start_block(tb + 1)
                        pend.extend(nxt[2])
                        nxt = (nxt[0], nxt[1], [])
                    sg, r_sg = stg.next()
                    for ci in range(6):
                        ps, r_ps = mmp.next()
                        for kc in range(16):
                            P.op("pe", lambda ps=ps, ci=ci, kc=kc, hT=hT: nc.tensor.matmul(
                                ps[:], lhsT=wkv[:, kc, ci * 128:(ci + 1) * 128], rhs=hT[:, kc, :],
                                start=(kc == 0), stop=(kc == 15)), reads=[r_wkv, r_hT], writes=[r_ps])
                        if ci % 2 == 0:
                            P.op("act", lambda ps=ps, sg=sg, ci=ci: nc.scalar.copy(out=sg[:, ci, :], in_=ps[:]),
                                 reads=[r_ps], writes=[r_sg])
                        else:
                            P.op("dve", lambda ps=ps, sg=sg, ci=ci: nc.vector.tensor_copy(out=sg[:, ci, :], in_=ps[:]),
                                 reads=[r_ps], writes=[r_sg])
                        pump(2)
                    dsts = [kcT_raw, vcT_raw, kslT]
                    for k3 in range(3):
                        P.dma("act", chs[tb % 2], lambda sg=sg, k3=k3, t0=t0: nc.scalar.dma_start(
                            out=dsts[k3][:, :, t0:t0 + 512].rearrange("g p t -> p g t"), in_=sg[:, 2 * k3:2 * k3 + 2, :]),
                            reads=[r_sg], writes=[r_scr], cont=(k3 > 0))
                    sv, r_sv = stv.next()
                    for s in range(4):
                        ps, r_ps = mmp.next()
                        for kc in range(16):
                            P.op("pe", lambda ps=ps, s=s, kc=kc, hT=hT: nc.tensor.matmul(
                                ps[:, 0:256], lhsT=hT[:, kc, s * 128:(s + 1) * 128], rhs=wkv[:, kc, 768:1024],
                                start=(kc == 0), stop=(kc == 15)), reads=[r_wkv, r_hT], writes=[r_ps])
                        pump(1)
                        tile = tb * 4 + s
                        P.op("dve", lambda ps=ps, sv=sv, s=s, tile=tile: nc.vector.tensor_scalar(
                            out=sv[:, s, :, 0:128], in0=ps[:, 0:256].rearrange("p (g d) -> p g d", g=NG),
                            scalar1=vtok_sb[:, tile:tile + 1], scalar2=None, op0=ALU.mult),
                            reads=[r_ps, r_const], writes=[r_sv])
                        P.op("dve", lambda sv=sv, s=s, tile=tile: nc.vector.tensor_copy(
                            out=sv[:, s, :, 128:130], in_=vtok_sb[:, tile:tile + 1].unsqueeze(1).to_broadcast([128, NG, 2])),
                            reads=[r_const], writes=[r_sv])
                    P.dma("act", chv[tb % 2], lambda sv=sv, t0=t0: nc.scalar.dma_start(
                        out=vsl[t0:t0 + 512, :, :].rearrange("(s p) g d -> p s g d", p=128), in_=sv[:]),
                        reads=[r_sv], writes=[r_scr])
                P.barrier()
            if upto <= 1:
                P.emit()
                return nc

            with ExitStack() as st:
                hTo = st.enter_context(nc.sbuf_tensor("hTo", [128, 16, NA], BF16))
                r_hTo = Res()
                with ExitStack() as st2:
                    xpool = Pool(P, st2, "xtb", 2, [128, 4, D], F32)
                    chx = [P.chan("xb0"), P.chan("xb1")]
                    junk = st2.enter_context(nc.sbuf_tensor("junkb", [128, D], BF16))
                    r_junk = Res()
                    sspool = Pool(P, st2, "ssb", 2, [128, 4], F32)
                    rspool = Pool(P, st2, "rsb", 2, [128, 4], F32)
                    xnpool = Pool(P, st2, "xnb", 2, [128, 4, D], BF16)
                    tpp = Pool(P, st2, "tpb", 4, [128, 512], BF16, psum=True)
                    for tb in range(6):
                        t0 = W0 + tb * 512
                        nsub = 4 if tb < 5 else 1

                        class _HP:
                            def next(self_inner):
                                return hTo[:, :, tb * 512:tb * 512 + nsub * 128], r_hTo
                        self.norm_T(P, st2, xv, t0, nsub, xpool, chx[tb % 2], junk, r_junk, sspool, rspool, xnpool,
                                    _HP(), tpp, ident, r_const, s1, r_s1, ada, r_ada, 0)
                    P.barrier()
                w32 = Pool(P, st, "w32_", 2, [128, 16, 512], F32)
                wbf = Pool(P, st, "wbf_", 2, [128, 16, 512], BF16)
                chw = [P.chan("w1b0"), P.chan("w1b1")]
                mmp = Pool(P, st, "mmb", 4, [128, 512], F32, psum=True)
                stA = Pool(P, st, "stA", 2, [128, NA], BF16)
                chst = [P.chan("stA0"), P.chan("stA1"), P.chan("stA2")]
                sgp = Pool(P, st, "sgp", 1, [128, NB], BF16)
                r_scr = Res("scr1b")
                self.wblk = 0

                def load_w(colranges):
                    wt, wr = w32.next()
                    wb, wbr = wbf.next()
                    ch = chw[self.wblk % 2]
                    self.wblk += 1
                    o = 0
                    for i, (c0, n) in enumerate(colranges):
                        P.dma("sp", ch, lambda wt=wt, c0=c0, n=n, o=o: nc.sync.dma_start(
                            out=wt[:, :, o:o + n], in_=w_in[:, c0:c0 + n].rearrange("(k p) c -> p k c", p=128)),
                            writes=[wr], cont=(i > 0))
                        o += n
                    P.op("pool", lambda wt=wt, wb=wb, o=o: nc.gpsimd.tensor_copy(out=wb[:, :, 0:o], in_=wt[:, :, 0:o]),
                         reads=[wr], writes=[wbr])
                    return wb, wbr

                def blocks(lo, hi):
                    b = []
                    t = lo
                    while t < hi:
                        n = min(512, hi - t)
                        b.append((t, n))
                        t += n
                    return b

                def fm_chunk(wb, wbr, woff, lo, hi, evac):
                    for (t, n) in blocks(lo, hi):
                        ps, r_ps = mmp.next()
                        for kc in range(16):
                            P.op("pe", lambda ps=ps, kc=kc, t=t, n=n: nc.tensor.matmul(
                                ps[:, 0:n], lhsT=wb[:, kc, woff:woff + 128], rhs=hTo[:, kc, t:t + n],
                                start=(kc == 0), stop=(kc == 15)), reads=[wbr, r_hTo], writes=[r_ps])
                        evac(ps, r_ps, t, n)

                ecount = [0]

                def copy_evac(dst, r_dst, off, func=None, scale=1.0):
                    def ev(ps, r_ps, t, n):
                        ecount[0] += 1
                        if func is None and scale == 1.0 and ecount[0] % 2 == 0:
                            P.op("dve", lambda: nc.vector.tensor_copy(out=dst[:, t - off:t - off + n], in_=ps[:, 0:n]),
                                 reads=[r_ps], writes=[r_dst])
                        else:
                            P.op("act", lambda: nc.scalar.activation(out=dst[:, t - off:t - off + n], in_=ps[:, 0:n],
                                                                     func=(func or AF.Copy), scale=scale),
                                 reads=[r_ps], writes=[r_dst])
                    return ev

                stn = [0]

                def store(dst_ap, sg, r_sg, n):
                    ch = chst[stn[0] % 3]
                    stn[0] += 1
                    P.dma("act", ch, lambda: nc.scalar.dma_start(out=dst_ap, in_=sg[:, 0:n]), reads=[r_sg], writes=[r_scr])

                OB = B0 - W0
                for qb in range(4):
                    wb, wbr = load_w([(C_Q + qb * 512, 512)])
                    for hl in range(4):
                        h = qb * 4 + hl
                        sg, r_sg = stA.next()
                        fm_chunk(wb, wbr, hl * 128, OB, NA, copy_evac(sg, r_sg, OB, scale=float(DK) ** -0.5))
                        store(QT[:, h, :], sg, r_sg, NB)
                wb, wbr = load_w([(C_KW, 256), (C_VW, 256)])
                for g in range(NG):
                    sg, r_sg = stA.next()
                    fm_chunk(wb, wbr, g * 128, 0, NA, copy_evac(sg, r_sg, 0))
                    store(kwT[g, :, :], sg, r_sg, NA)
                stv = Pool(P, st, "stvb", 2, [128, NG, 130], BF16)
                chv = [P.chan("svb0"), P.chan("svb1")]
                for i in range(2):
                    P.op("dve", lambda i=i: nc.vector.memset(stv.t[i][:], 0.0), writes=[stv.r[i]])
                for s in range(NA // 128):
                    ps, r_ps = mmp.next()
                    for kc in range(16):
                        P.op("pe", lambda ps=ps, s=s, kc=kc, wb=wb: nc.tensor.matmul(
                            ps[:, 0:256], lhsT=hTo[:, kc, s * 128:(s + 1) * 128], rhs=wb[:, kc, 256:512],
                            start=(kc == 0), stop=(kc == 15)), reads=[wbr, r_hTo], writes=[r_ps])
                    sv, r_sv = stv.next()
                    tile = W0 // 128 + s
                    P.op("dve", lambda ps=ps, sv=sv, tile=tile: nc.vector.tensor_scalar(
                        out=sv[:, :, 0:128], in0=ps[:, 0:256].rearrange("p (g d) -> p g d", g=NG),
                        scalar1=vtok_sb[:, tile:tile + 1], scalar2=None, op0=ALU.mult),
                        reads=[r_ps, r_const], writes=[r_sv])
                    P.op("pool", lambda sv=sv, tile=tile: nc.gpsimd.tensor_copy(
                        out=sv[:, :, 128:130], in_=vtok_sb[:, tile:tile + 1].unsqueeze(1).to_broadcast([128, NG, 2])),
                        reads=[r_const], writes=[r_sv])
                    P.dma("act", chv[s % 2], lambda sv=sv, s=s: nc.scalar.dma_start(
                        out=vw[s * 128:(s + 1) * 128, :, :], in_=sv[:]), reads=[r_sv], writes=[r_scr])
                wb, wbr = load_w([(C_GN, 48)])
                gst = Pool(P, st, "gst", 2, [128, 48], F32)
                chg = [P.chan("gs0"), P.chan("gs1")]
                for s in range(NB // 128):
                    ps, r_ps = mmp.next()
                    for kc in range(16):
                        P.op("pe", lambda ps=ps, s=s, kc=kc, wb=wb: nc.tensor.matmul(
                            ps[:, 0:48], lhsT=hTo[:, kc, OB + s * 128:OB + (s + 1) * 128], rhs=wb[:, kc, 0:48],
                            start=(kc == 0), stop=(kc == 15)), reads=[wbr, r_hTo], writes=[r_ps])
                    gs, r_gs = gst.next()
                    P.op("act", lambda ps=ps, gs=gs: nc.scalar.activation(out=gs[:], in_=ps[:, 0:48], func=AF.Sigmoid),
                         reads=[r_ps], writes=[r_gs])
                    P.dma("act", chg[s % 2], lambda gs=gs, s=s: nc.scalar.dma_start(
                        out=gates[s * 128:(s + 1) * 128, :], in_=gs[:]), reads=[r_gs], writes=[r_scr])
                for cb in range(8):
                    wb, wbr = load_w([(C_GLU + cb * 256, 256), (C_GLU + D + cb * 256, 256)])
                    for cl in range(2):
                        ch_ = cb * 2 + cl
                        sgm, r_sgm = sgp.next()
                        fm_chunk(wb, wbr, 256 + cl * 128, OB, NA, copy_evac(sgm, r_sgm, OB, func=AF.Sigmoid))
                        sg, r_sg = stA.next()

                        def ev(ps, r_ps, t, n, sg=sg, r_sg=r_sg, sgm=sgm, r_sgm=r_sgm):
                            P.op("dve", lambda: nc.vector.tensor_tensor(out=sg[:, t - OB:t - OB + n], in0=ps[:, 0:n],
                                                                        in1=sgm[:, t - OB:t - OB + n], op=ALU.mult),
                                 reads=[r_ps, r_sgm], writes=[r_sg])
                        fm_chunk(wb, wbr, cl * 128, OB, NA, ev)
                        P.op("dve", lambda sg=sg: nc.vector.tensor_scalar(out=sg[:, 0:OWN0 - B0], in0=sg[:, 0:OWN0 - B0], scalar1=self.hfl[:, 0:1],
                                                                        scalar2=None, op0=ALU.mult), reads=[r_sg, r_const], writes=[r_sg])
                        store(gluT[:, ch_, :], sg, r_sg, NB)
                for mb in range(8):
                    wb, wbr = load_w([(C_GM + mb * 512, 512)])
                    for cl in range(4):
                        sg, r_sg = stA.next()
                        fm_chunk(wb, wbr, cl * 128, OB, NA, copy_evac(sg, r_sg, OB, func=AF.Sigmoid))
                        store(mgT[:, mb * 4 + cl, :], sg, r_sg, NB)
                P.barrier()
            self.r_scr_all = Res("scr_all")
            if upto >= 3:
                self.phase_compress()
            if upto >= 4:
                self.phase_attn(upto)
            if upto >= 5:
                self.phase_mix()
            if upto >= 6:
                self.phase_ffn()
            P.emit()
        return nc

    def norm_prep(self, P, src, t0, nsub, xpool, chx, junk, r_junk, sspool, rspool, xnpool):
        nc = self.nc
        xt, r_xt = xpool.next()
        P.dma("sp", chx, lambda: nc.sync.dma_start(
            out=xt[:, 0:nsub, :], in_=src[t0:t0 + nsub * 128, :].rearrange("(s p) d -> p s d", p=128)), writes=[r_xt])
        ss, r_ss = sspool.next()
        rs, r_rs = rspool.next()
        for s in range(nsub):
            P.op("act", lambda s=s: nc.scalar.activation(out=junk[:], in_=xt[:, s, :], func=AF.Square, accum_out=ss[:, s:s + 1]),
                 reads=[r_xt], writes=[r_junk, r_ss])
        P.op("dve", lambda: nc.vector.tensor_scalar(out=rs[:, 0:nsub], in0=ss[:, 0:nsub], scalar1=1.0 / D, scalar2=EPS,
                                                    op0=ALU.mult, op1=ALU.add), reads=[r_ss], writes=[r_rs])
        P.op("act", lambda: nc.scalar.sqrt(out=rs[:, 0:nsub], in_=rs[:, 0:nsub]), reads=[r_rs], writes=[r_rs])
        P.op("dve", lambda: nc.vector.reciprocal(out=rs[:, 0:nsub], in_=rs[:, 0:nsub]), reads=[r_rs], writes=[r_rs])
        xn, r_xn = xnpool.next()
        for s in range(nsub):
            P.op("dve", lambda s=s: nc.vector.tensor_scalar(out=xn[:, s, :], in0=xt[:, s, :], scalar1=rs[:, s:s + 1],
                                                            scalar2=None, op0=ALU.mult), reads=[r_xt, r_rs], writes=[r_xn])
        return xn, r_xn

    def norm_groups(self, P, xn, r_xn, hT, r_hT, nsub, tpp, sc, r_sc, sh_col):
        nc = self.nc
        ident, r_const, ada, r_ada = self.ident, self.r_const, self.ada, self.r_ada

        def grp(j):
            tp, r_tp = tpp.next()
            for s in range(nsub):
                P.op("pe", lambda s=s: nc.tensor.transpose(out=tp[:, s * 128:(s + 1) * 128], in_=xn[:, s, j * 128:(j + 1) * 128], identity=ident[:]),
                     reads=[r_xn, r_const], writes=[r_tp])
            if j % 2 == 0:
                P.op("act", lambda: nc.scalar.activation(out=hT[:, j, 0:nsub * 128], in_=tp[:, 0:nsub * 128], func=AF.Identity,
                                                         scale=sc[:, j:j + 1], bias=ada[:, sh_col + j:sh_col + j + 1]),
                     reads=[r_tp, r_sc, r_ada], writes=[r_hT])
            else:
                P.op("dve", lambda: nc.vector.tensor_scalar(out=hT[:, j, 0:nsub * 128], in0=tp[:, 0:nsub * 128], scalar1=sc[:, j:j + 1],
                                                            scalar2=ada[:, sh_col + j:sh_col + j + 1], op0=ALU.mult, op1=ALU.add),
                     reads=[r_tp, r_sc, r_ada], writes=[r_hT])
        return [(lambda j=j: grp(j)) for j in range(16)]

    def norm_T(self, P, st, src, t0, nsub, xpool, chx, junk, r_junk, sspool, rspool, xnpool, hTpool, tpp,
               ident, r_const, sc, r_sc, ada, r_ada, sh_col, xt_in=None):
        nc = self.nc
        if xt_in is None:
            xt, r_xt = xpool.next()
            P.dma("sp", chx, lambda: nc.sync.dma_start(
                out=xt[:, 0:nsub, :], in_=src[t0:t0 + nsub * 128, :].rearrange("(s p) d -> p s d", p=128)), writes=[r_xt])
        else:
            xt, r_xt = xt_in
        ss, r_ss = sspool.next()
        rs, r_rs = rspool.next()
        for s in range(nsub):
            P.op("act", lambda s=s: nc.scalar.activation(out=junk[:], in_=xt[:, s, :], func=AF.Square,
                                                         accum_out=ss[:, s:s + 1]),
                 reads=[r_xt], writes=[r_junk, r_ss])
        P.op("dve", lambda: nc.vector.tensor_scalar(out=rs[:, 0:nsub], in0=ss[:, 0:nsub], scalar1=1.0 / D, scalar2=EPS,
                                                    op0=ALU.mult, op1=ALU.add), reads=[r_ss], writes=[r_rs])
        P.op("act", lambda: nc.scalar.sqrt(out=rs[:, 0:nsub], in_=rs[:, 0:nsub]), reads=[r_rs], writes=[r_rs])
        P.op("dve", lambda: nc.vector.reciprocal(out=rs[:, 0:nsub], in_=rs[:, 0:nsub]), reads=[r_rs], writes=[r_rs])
        xn, r_xn = xnpool.next()
        for s in range(nsub):
            P.op("dve", lambda s=s: nc.vector.tensor_scalar(out=xn[:, s, :], in0=xt[:, s, :], scalar1=rs[:, s:s + 1],
                                                            scalar2=None, op0=ALU.mult),
                 reads=[r_xt, r_rs], writes=[r_xn])
        hT, r_hT = hTpool.next()
        for j in range(16):
            tp, r_tp = tpp.next()
            for s in range(nsub):
                P.op("pe", lambda tp=tp, s=s, j=j: nc.tensor.transpose(
                    out=tp[:, s * 128:(s + 1) * 128], in_=xn[:, s, j * 128:(j + 1) * 128], identity=ident[:]),
                    reads=[r_xn, r_const], writes=[r_tp])
            if j % 2 == 0:
                P.op("act", lambda tp=tp, j=j: nc.scalar.activation(
                    out=hT[:, j, 0:nsub * 128], in_=tp[:, 0:nsub * 128], func=AF.Identity,
                    scale=sc[:, j:j + 1], bias=ada[:, sh_col + j:sh_col + j + 1]),
                    reads=[r_tp, r_sc, r_ada], writes=[r_hT])
            else:
                P.op("dve", lambda tp=tp, j=j: nc.vector.tensor_scalar(
                    out=hT[:, j, 0:nsub * 128], in0=tp[:, 0:nsub * 128], scalar1=sc[:, j:j + 1],
                    scalar2=ada[:, sh_col + j:sh_col + j + 1], op0=ALU.mult, op1=ALU.add),
                    reads=[r_tp, r_sc, r_ada], writes=[r_hT])
        self.last_rs = (rs, r_rs)
        self.last_xt = (xt, r_xt)
        return hT, r_hT


    def phase_compress(self):
        nc, P = self.nc, self.P
        with ExitStack() as st:
            sbt = lambda name, shape, dt: st.enter_context(nc.sbuf_tensor(name, list(shape), dt))
            raw = sbt("c_raw", [128, TV + 16], BF16)
            R = sbt("c_R", [128, 16, NCV + 1], BF16)
            w1f = sbt("c_w1f", [128, 32, 256], F32)
            w1b = sbt("c_w1b", [128, 32, 256], BF16)
            w2f = sbt("c_w2f", [128, 2, 128], F32)
            w2b = sbt("c_w2b", [128, 2, 128], BF16)
            pef = sbt("c_pef", [128, 32], F32)
            peb = sbt("c_peb", [128, 32], BF16)
            bia = sbt("c_bia", [128, 2], F32)
            hid = sbt("c_hid", [128, 2, NCV], BF16)
            kst = sbt("c_kst", [128, NCV], BF16)
            zpad = sbt("c_zpad", [128, NG, 16], BF16)
            r_raw, r_R, r_w1f, r_w1b, r_w2f, r_w2b, r_pe, r_bia, r_hid, r_kst, r_z = [Res() for _ in range(11)]
            vst = Pool(P, st, "c_vst", 2, [128, NG, 130], BF16)
            mmp = Pool(P, st, "c_mm", 3, [128, 512], F32, psum=True)
            bps = st.enter_context(nc.psum_tensor("c_bps", [128, 2], F32))
            r_bps = Res()
            ch = {n: P.chan("c_" + n) for n in ("raw", "w1", "w2", "pe", "k", "v0", "v1", "z")}
            r_out = self.r_scr_all
            P.op("dve", lambda: nc.vector.memset(zpad[:], 0.0), writes=[r_z])
            for i, rt in enumerate((self.kcT_raw, self.vcT_raw)):
                P.dma("act", ch["z"], lambda rt=rt: nc.scalar.dma_start(
                    out=rt[:, :, TV:TV + 16].rearrange("g p t -> p g t"), in_=zpad[:]), reads=[r_z], writes=[r_out], cont=(i > 0))
            P.dma("sp", ch["pe"], lambda: nc.sync.dma_start(out=pef[:], in_=self.i_cmp_pe.rearrange("l d -> d l"),
                                                           allow_slow_non_contiguous=True), writes=[r_pe])
            P.op("dve", lambda: nc.vector.tensor_copy(out=peb[:], in_=pef[:]), reads=[r_pe], writes=[r_pe])
            for i in range(2):
                P.op("dve", lambda i=i: nc.vector.memset(vst.t[i][:], 0.0), writes=[vst.r[i]])
            for kv in range(2):
                P.dma("sp", ch["w1"], lambda kv=kv: nc.sync.dma_start(
                    out=w1f[:], in_=self.i_w1[kv].rearrange("(l d) c -> d l c", d=128)), writes=[r_w1f])
                P.op("pool", lambda: nc.gpsimd.tensor_copy(out=w1b[:], in_=w1f[:]), reads=[r_w1f], writes=[r_w1b])
                P.dma("sp", ch["w2"], lambda kv=kv: nc.sync.dma_start(
                    out=w2f[:], in_=self.i_w2[kv].rearrange("(c p) d -> p c d", p=128)), writes=[r_w2f])
                P.op("dve", lambda: nc.vector.tensor_copy(out=w2b[:], in_=w2f[:]), reads=[r_w2f], writes=[r_w2b])
                for hc in range(2):
                    for l in range(32):
                        P.op("pe", lambda hc=hc, l=l: nc.tensor.matmul(
                            bps[:, hc:hc + 1], lhsT=w1b[:, l, hc * 128:(hc + 1) * 128], rhs=peb[:, l:l + 1],
                            start=(l == 0), stop=(l == 31)), reads=[r_w1b, r_pe], writes=[r_bps])
                P.op("dve", lambda: nc.vector.tensor_copy(out=bia[:], in_=bps[:]), reads=[r_bps], writes=[r_bia])
                src = (self.kcT_raw, self.vcT_raw)[kv]
                for g in range(NG):
                    P.dma("sp", ch["raw"], lambda g=g, src=src: nc.sync.dma_start(out=raw[:], in_=src[g, :, :]),
                          reads=[r_out], writes=[r_raw])
                    P.op("dve", lambda: nc.vector.tensor_copy(
                        out=R[:], in_=raw[:].rearrange("p (m l) -> p l m", l=16)), reads=[r_raw], writes=[r_R])
                    for hc in range(2):
                        for (n0, nn) in ((0, 512), (512, 512), (1024, 128)):
                            ps, r_ps = mmp.next()
                            for l in range(32):
                                P.op("pe", lambda ps=ps, hc=hc, l=l, n0=n0, nn=nn: nc.tensor.matmul(
                                    ps[:, 0:nn], lhsT=w1b[:, l, hc * 128:(hc + 1) * 128],
                                    rhs=R[:, l % 16, (l // 16) + n0:(l // 16) + n0 + nn],
                                    start=(l == 0), stop=(l == 31)), reads=[r_w1b, r_R], writes=[r_ps])
                            P.op("act", lambda ps=ps, hc=hc, n0=n0, nn=nn: nc.scalar.activation(
                                out=hid[:, hc, n0:n0 + nn], in_=ps[:, 0:nn], func=AF.Silu, bias=bia[:, hc:hc + 1]),
                                reads=[r_ps, r_bia], writes=[r_hid])
                    if kv == 0:
                        for (n0, nn) in ((0, 512), (512, 512), (1024, 128)):
                            ps, r_ps = mmp.next()
                            for hc in range(2):
                                P.op("pe", lambda ps=ps, hc=hc, n0=n0, nn=nn: nc.tensor.matmul(
                                    ps[:, 0:nn], lhsT=w2b[:, hc, :], rhs=hid[:, hc, n0:n0 + nn],
                                    start=(hc == 0), stop=(hc == 1)), reads=[r_w2b, r_hid], writes=[r_ps])
                            P.op("dve", lambda ps=ps, n0=n0, nn=nn: nc.vector.tensor_copy(out=kst[:, n0:n0 + nn], in_=ps[:, 0:nn]),
                                 reads=[r_ps], writes=[r_kst])
                        P.dma("act", ch["k"], lambda g=g: nc.scalar.dma_start(out=self.kcT[g, :, :], in_=kst[:]),
                              reads=[r_kst], writes=[r_out])
                    else:
                        for tl in range(NCV // 128):
                            ps, r_ps = mmp.next()
                            for hc in range(2):
                                P.op("pe", lambda ps=ps, hc=hc, tl=tl: nc.tensor.matmul(
                                    ps[:, 0:128], lhsT=hid[:, hc, tl * 128:(tl + 1) * 128], rhs=w2b[:, hc, :],
                                    start=(hc == 0), stop=(hc == 1)), reads=[r_w2b, r_hid], writes=[r_ps])
                            sv, r_sv = vst.next()
                            P.op("dve", lambda ps=ps, sv=sv, tl=tl, g=g: nc.vector.tensor_scalar(
                                out=sv[:, g, 0:128], in0=ps[:, 0:128], scalar1=self.vcmp_sb[:, tl:tl + 1], scalar2=None,
                                op0=ALU.mult), reads=[r_ps, self.r_const], writes=[r_sv])
                            P.op("pool", lambda sv=sv, tl=tl, g=g: nc.gpsimd.tensor_copy(
                                out=sv[:, g, 128:130], in_=self.vcmp_sb[:, tl:tl + 1].to_broadcast([128, 2])), reads=[self.r_const], writes=[r_sv])
                            P.dma("act", ch["v%d" % (tl % 2)], lambda sv=sv, tl=tl, g=g: nc.scalar.dma_start(
                                out=self.vca[tl * 128:(tl + 1) * 128, g, :], in_=sv[:, g, :]), reads=[r_sv], writes=[r_out])
            P.barrier()


    def phase_attn(self, upto):
        nc, P = self.nc, self.P
        with ExitStack() as st:
            sbt = lambda name, shape, dt: st.enter_context(nc.sbuf_tensor(name, list(shape), dt))
            bias_s = sbt("a_bs", [128, NH, NDS], F32)
            bias_c = sbt("a_bc", [128, NH, NDC], F32)
            masks = sbt("a_mk", [128, NMASK, 512], BF16)
            esel = sbt("a_es", [128, 64, 128], BF16)
            r_tab = Res("tables")
            cht = P.chan("a_tab")
            for i, (dst, src) in enumerate(((bias_s, self.i_bias_s), (bias_c, self.i_bias_c), (masks, self.i_masks), (esel, self.i_esel))):
                P.dma("sp", cht, lambda dst=dst, src=src: nc.sync.dma_start(out=dst[:], in_=src[:, :, :]), writes=[r_tab], cont=(i > 0))
            crhs = [[sbt(f"a_cr{g}_{jj}", [128, 130 + NBW], BF16) for jj in range(9)] for g in range(NG)]
            r_crhs = [[Res() for jj in range(9)] for g in range(NG)]
            ch_cr = [P.chan("a_cr0"), P.chan("a_cr1")]
            ovm = sbt("a_ovm", [128, 9, NBW], BF16)
            P.dma("sp", cht, lambda: nc.sync.dma_start(out=ovm[:], in_=self.i_ovm[:, :, :]), writes=[r_tab], cont=True)
            QTt = sbt("a_qt", [128, NH, 512], BF16)
            gat = sbt("a_gat", [128, 4, 48], F32)
            selb = sbt("a_selb", [128, 4, NBW], F32)
            selv = sbt("a_selv", [128, 4, NBW], F32)
            acc = sbt("a_acc", [128, 4, D], F32)
            imp = sbt("a_imp", [128, 4, NG, NBW], F32)
            mneg = sbt("a_mneg", [128, NG, 3, 512], BF16)
            r_qt, r_gat, r_sel, r_acc, r_imp, r_mneg = [Res() for _ in range(6)]
            ch_q = P.chan("a_q")
            kpool = Pool(P, st, "a_k", 3, [128, 2048], BF16)
            vpool = Pool(P, st, "a_v", 3, [128, 16, 130], BF16)
            chk = [P.chan(f"a_k{i}") for i in range(3)]
            chv = [P.chan(f"a_v{i}") for i in range(3)]
            ptp = Pool(P, st, "a_pt", 4, [128, 512], BF16)
            sps = Pool(P, st, "a_S", 2, [128, 512], F32, psum=True)
            ops_ = Pool(P, st, "a_o", 4, [128, 512], F32, psum=True)
            tps = Pool(P, st, "a_tp", 2, [128, 512], BF16, psum=True)
            small = Pool(P, st, "a_sm", 4, [128, 4], F32)
            osb = Pool(P, st, "a_osb", 8, [128, 130 + NBW], F32)
            sc1 = sbt("a_sc1", [128, 384], F32)
            sc2 = sbt("a_sc2", [128, 384], F32)
            m8 = sbt("a_m8", [128, 16], F32)
            mbf = sbt("a_mbf", [128, 8, 384], BF16)
            r_sc1, r_sc2, r_m8, r_mbf = Res(), Res(), Res(), Res()
            P.op("dve", lambda: nc.vector.memset(mbf[:], 0.0), writes=[r_mbf])
            P.op("dve", lambda: nc.vector.memset(sc1[:], 0.0), writes=[r_sc1])
            ch_dbg = P.chan("a_dbg")
            r_out = self.r_scr_all
            kcount = [0]

            def do_tile(ti, q0, nq):
                self._cr_loaded = [0, 0]
                qend = q0 + nq
                nsub = max(1, nq // 128)
                rows = min(128, nq)
                nend = qend // 16
                P.dma("sp", ch_q, lambda q0=q0, nq=nq: nc.sync.dma_start(out=QTt[:, :, 0:nq], in_=self.QT[:, :, q0 - B0:q0 - B0 + nq]),
                      reads=[r_out], writes=[r_qt])
                P.dma("sp", ch_q, lambda q0=q0, nq=nq, rows=rows, nsub=nsub: nc.sync.dma_start(
                    out=gat[0:rows, 0:nsub, :], in_=self.gates[q0 - B0:q0 - B0 + nq, :].rearrange("(s p) c -> p s c", p=rows)),
                    reads=[r_out], writes=[r_gat], cont=True)
                P.dma("sp", ch_q, lambda ti=ti: nc.sync.dma_start(out=selb[:], in_=self.i_selb[ti].rearrange("s p b -> p s b")),
                      writes=[r_sel], cont=True)
                P.dma("sp", ch_q, lambda ti=ti: nc.sync.dma_start(out=selv[:], in_=self.i_selv[ti].rearrange("s p b -> p s b")),
                      writes=[r_sel], cont=True)

                def run_branch(bi, h):
                    g = h // HG
                    W = exp_width(h, nq)
                    if bi == 0:
                        nch = min(n_cmp_chunks(h, nq), nend // 128)
                    elif bi == 1:
                        nch = min(n_slc_chunks(h, nq, 132), qend // 128)
                    else:
                        nch = min(n_slc_chunks(h, nq, 8 if nq == 512 else 5), 8 if nq == 512 else 5)
                    ncol = 130 + NBW if bi == 0 else 130
                    oacc = [ops_.next() for _ in range(nsub)]
                    kt = vt = None
                    stA = {}

                    def stageB(j, pt, r_pt, rhsV, r_rhsV):
                        for s_ in range(nsub):
                            o, r_o = oacc[s_]
                            P.op("pe", lambda o=o, s_=s_, pt=pt, rhsV=rhsV, j=j: nc.tensor.matmul(
                                o[0:rows, 0:ncol], lhsT=pt[:, s_ * 128:s_ * 128 + rows], rhs=rhsV,
                                start=(j == 0), stop=(j == nch - 1)), reads=[r_pt, r_rhsV], writes=[r_o])

                    for j in range(nch):
                        if bi == 0:
                            n0 = nend - 128 * (j + 1)
                            kt, r_kt = kpool.next()
                            kc_ = kcount[0] % 3
                            kcount[0] += 1
                            P.dma("sp", chk[kc_], lambda kt=kt, n0=n0: nc.sync.dma_start(out=kt[:, 0:128], in_=self.kcT[g, :, n0:n0 + 128]),
                                  reads=[r_out], writes=[r_kt])
                            if h % HG == 0 or j >= self._cr_loaded[g]:
                                P.dma("sp", ch_cr[g], lambda n0=n0, j=j: nc.sync.dma_start(out=crhs[g][j][:, 0:130], in_=self.vca[n0:n0 + 128, g, :]),
                                      reads=[r_out], writes=[r_crhs[g][j]])
                                P.op("dve", lambda j=j: nc.vector.tensor_scalar(out=crhs[g][j][:, 130:130 + NBW], in0=ovm[:, j, :],
                                                                               scalar1=crhs[g][j][:, 128:129], scalar2=None, op0=ALU.mult),
                                     reads=[r_tab, r_crhs[g][j]], writes=[r_crhs[g][j]])
                                self._cr_loaded[g] = max(self._cr_loaded[g], j + 1) if h % HG else j + 1
                            klhs, rhsV, r_rhsV = kt[:, 0:128], crhs[g][j][:, 0:ncol], r_crhs[g][j]
                            mk = MASK_IDX.get(("cmp", nq, j))
                            bcol = lambda r, j=j: bias_c[:, h, (nq - 2048 * (j + 1) - r * W) // 64 + ROFF:(nq - 2048 * (j + 1) - r * W) // 64 + ROFF + 1]
                        else:
                            if j % 16 == 0:
                                nsup = min(16, nch - j)
                                lo = qend - 128 * (j + nsup)
                                kt, r_kt = kpool.next()
                                vt, r_vt = vpool.next()
                                kc_ = kcount[0] % 3
                                kcount[0] += 1
                                if bi == 1:
                                    ksrc, vsrc, off = self.kslT, self.vsl, 0
                                else:
                                    ksrc, vsrc, off = self.kwT, self.vw, W0
                                P.dma("sp", chk[kc_], lambda kt=kt, lo=lo, nsup=nsup, ksrc=ksrc, off=off: nc.sync.dma_start(
                                    out=kt[:, 0:128 * nsup], in_=ksrc[g, :, lo - off:lo - off + 128 * nsup]), reads=[r_out], writes=[r_kt])
                                P.dma("sp", chv[kc_], lambda vt=vt, lo=lo, nsup=nsup, vsrc=vsrc, off=off: nc.sync.dma_start(
                                    out=vt[:, 0:nsup, :], in_=vsrc[lo - off:lo - off + 128 * nsup, g, :].rearrange("(s p) d -> p s d", p=128)),
                                    reads=[r_out], writes=[r_vt])
                                sup_n = nsup
                            sl = sup_n - 1 - (j % 16)
                            klhs, rhsV, r_rhsV = kt[:, sl * 128:(sl + 1) * 128], vt[:, sl, :], r_vt
                            mk = MASK_IDX.get(("slc" if bi == 1 else "win", nq, j))
                            bcol = lambda r, j=j: bias_s[:, h, (nq - 128 * (j + 1) - r * W) // 64 + DOFF:(nq - 128 * (j + 1) - r * W) // 64 + DOFF + 1]
                        S, r_S = sps.next()
                        nmm = 1 + (mk is not None) + (bi == 1)
                        cnt = [0]

                        def mm(lhsT, rhs, reads):
                            first, last = cnt[0] == 0, cnt[0] == nmm - 1
                            cnt[0] += 1
                            P.op("pe", lambda S=S: nc.tensor.matmul(S[:, 0:nq], lhsT=lhsT, rhs=rhs, start=first, stop=last),
                                 reads=reads, writes=[r_S])
                        mm(klhs, QTt[:, h, 0:nq], [r_kt, r_qt])
                        if mk is not None:
                            mm(self.ident[:], masks[:, mk, 0:nq], [self.r_const, r_tab])
                        if bi == 1:
                            b0 = NBLKW - 2 * (j + 1)
                            mm(esel[:, (b0 % 128) // 2, :], mneg[:, g, b0 // 128, 0:nq], [r_tab, r_mneg])
                        if self.dbg and "dumpS" in self.dbg and bi == 0 and h == 0 and j == 0:
                            dS = sbt("dbg_S", [128, 512], F32)
                            r_dS = Res()
                            P.op("dve", lambda: nc.vector.tensor_copy(out=dS[:, 0:nq], in_=S[:, 0:nq]), reads=[r_S], writes=[r_dS])
                            P.dma("act", ch_dbg, lambda: nc.scalar.dma_start(out=self.dumpS[:, 0:nq], in_=dS[:, 0:nq]), reads=[r_dS], writes=[r_out])
                            raise StopIteration
                        pt, r_pt = ptp.next()
                        for r in range(nq // W):
                            P.op("act", lambda r=r, bcol=bcol, S=S, pt=pt: nc.scalar.activation(
                                out=pt[:, r * W:(r + 1) * W], in_=S[:, r * W:(r + 1) * W], func=AF.Exp, bias=bcol(r)),
                                reads=[r_S, r_tab], writes=[r_pt])
                        if j > 0:
                            stageB(j - 1, *stA.pop(j - 1))
                        stA[j] = (pt, r_pt, rhsV, r_rhsV)
                    stageB(nch - 1, *stA.pop(nch - 1))
                    evac = []
                    for s_ in range(nsub):
                        o, r_o = oacc[s_]
                        ob_, r_ob = osb.next()
                        P.op("act", lambda o=o, ob_=ob_: nc.scalar.copy(out=ob_[0:rows, 0:ncol], in_=o[0:rows, 0:ncol]), reads=[r_o], writes=[r_ob])
                        evac.append((ob_, r_ob))
                    for s_ in range(nsub):
                        o, r_o = evac[s_]
                        sm, r_sm = small.next()
                        P.op("dve", lambda o=o, sm=sm: nc.vector.tensor_scalar(out=sm[0:rows, 0:1], in0=o[0:rows, 128:129], scalar1=1e-30,
                                                                             scalar2=None, op0=ALU.max), reads=[r_o], writes=[r_sm])
                        P.op("dve", lambda sm=sm: nc.vector.reciprocal(out=sm[0:rows, 1:2], in_=sm[0:rows, 0:1]), reads=[r_sm], writes=[r_sm])
                        P.op("dve", lambda sm=sm, s_=s_: nc.vector.tensor_tensor(
                            out=sm[0:rows, 2:3], in0=sm[0:rows, 1:2], in1=gat[0:rows, s_, bi * 16 + h:bi * 16 + h + 1], op=ALU.mult),
                            reads=[r_sm, r_gat], writes=[r_sm])
                        dst = acc[0:rows, s_, h * 128:(h + 1) * 128]
                        if bi == 0:
                            P.op("dve", lambda o=o, sm=sm, dst=dst: nc.vector.tensor_scalar(
                                out=dst, in0=o[0:rows, 0:128], scalar1=sm[0:rows, 2:3], scalar2=None, op0=ALU.mult),
                                reads=[r_o, r_sm], writes=[r_acc])
                            idst = imp[0:rows, s_, g, :]
                            if h % HG == 0:
                                P.op("dve", lambda o=o, sm=sm, idst=idst: nc.vector.tensor_scalar(
                                    out=idst, in0=o[0:rows, 130:130 + NBW], scalar1=sm[0:rows, 1:2], scalar2=None, op0=ALU.mult),
                                    reads=[r_o, r_sm], writes=[r_imp])
                            else:
                                P.op("dve", lambda o=o, sm=sm, idst=idst: nc.vector.scalar_tensor_tensor(
                                    out=idst, in0=o[0:rows, 130:130 + NBW], scalar=sm[0:rows, 1:2], in1=idst, op0=ALU.mult, op1=ALU.add),
                                    reads=[r_o, r_sm, r_imp], writes=[r_imp])
                        else:
                            P.op("dve", lambda o=o, sm=sm, dst=dst: nc.vector.scalar_tensor_tensor(
                                out=dst, in0=o[0:rows, 0:128], scalar=sm[0:rows, 2:3], in1=dst, op0=ALU.mult, op1=ALU.add),
                                reads=[r_o, r_sm, r_acc], writes=[r_acc])

                try:
                    for h in range(NH):
                        run_branch(0, h)
                except StopIteration:
                    return "stop"
                if self.dbg and "impd" in self.dbg:
                    P.dma("act", ch_dbg, lambda ti=ti: nc.scalar.dma_start(out=self.impd[ti].rearrange("s p g b -> p s g b"), in_=imp[:]),
                          reads=[r_imp], writes=[r_out])
                for g in range(NG):
                    for s_ in range(nsub):
                        P.op("dve", lambda s_=s_, g=g: nc.vector.tensor_tensor(out=sc1[0:rows, 0:NBW], in0=imp[0:rows, s_, g, :],
                                                                               in1=selb[0:rows, s_, :], op=ALU.add),
                             reads=[r_imp, r_sel], writes=[r_sc1])
                        P.op("dve", lambda s_=s_: nc.vector.tensor_tensor(out=sc1[0:rows, 0:NBW], in0=sc1[0:rows, 0:NBW],
                                                                          in1=selv[0:rows, s_, :], op=ALU.mult),
                             reads=[r_sc1, r_sel], writes=[r_sc1])
                        P.op("dve", lambda: nc.vector.max(out=m8[0:rows, 0:8], in_=sc1[0:rows, :]), reads=[r_sc1], writes=[r_m8])
                        P.op("dve", lambda: nc.vector.match_replace(out=sc2[0:rows, :], in_to_replace=m8[0:rows, 0:8],
                                                                    in_values=sc1[0:rows, :], imm_value=-1e30),
                             reads=[r_sc1, r_m8], writes=[r_sc2])
                        P.op("dve", lambda: nc.vector.max(out=m8[0:rows, 8:16], in_=sc2[0:rows, :]), reads=[r_sc2], writes=[r_m8])
                        P.op("dve", lambda g=g, s_=s_: nc.vector.tensor_scalar(out=mbf[0:rows, g * 4 + s_, 0:NBW], in0=sc1[0:rows, 0:NBW], scalar1=m8[0:rows, 15:16],
                                                                            scalar2=None, op0=ALU.is_ge), reads=[r_sc1, r_m8], writes=[r_mbf])
                if not (self.dbg and "branches" in self.dbg and 2 not in self.dbg["branches"]):
                    for h in range(NH):
                        run_branch(2, h)
                for g in range(NG):
                    tpl = [tps.next() for _ in range(3)]
                    for s_ in range(nsub):
                        for bg in range(3):
                            tp, r_tp = tpl[bg]
                            P.op("pe", lambda tp=tp, bg=bg, s_=s_, g=g: nc.tensor.transpose(
                                out=tp[:, s_ * 128:s_ * 128 + rows], in_=mbf[0:rows, g * 4 + s_, bg * 128:(bg + 1) * 128], identity=self.ident[0:rows, 0:rows]),
                                reads=[r_mbf, self.r_const], writes=[r_tp])
                    for bg in range(3):
                        tp, r_tp = tpl[bg]
                        P.op("act", lambda tp=tp, bg=bg, g=g: nc.scalar.activation(out=mneg[:, g, bg, 0:nq], in_=tp[:, 0:nq], func=AF.Identity,
                                                                                  scale=30000.0, bias=-30000.0),
                             reads=[r_tp], writes=[r_mneg])
                for bi in (1,):
                    if self.dbg and "branches" in self.dbg and bi not in self.dbg["branches"]:
                        continue
                    for h in range(NH):
                        run_branch(bi, h)
                if True:
                    P.dma("act", ch_dbg, lambda q0=q0, nq=nq, rows=rows, nsub=nsub: nc.scalar.dma_start(
                        out=self.accd[q0 - Q0:q0 - Q0 + nq, :].rearrange("(s p) d -> p s d", p=rows), in_=acc[0:rows, 0:nsub, :]),
                        reads=[r_acc], writes=[r_out])

            for ti, (q0, nq) in enumerate(QTILES):
                if self.dbg and "tiles" in self.dbg and ti not in self.dbg["tiles"]:
                    continue
                if do_tile(ti, q0, nq) == "stop":
                    return
            P.barrier()


    def cast_jobs(self):
        jobs = []
        for name, src in (("wo_b", self.i_wo), ("wpw_b", self.i_wpw), ("wout_b", self.i_wout), ("wup_b", self.i_wup), ("wdn_b", self.i_wdn)):
            dst = self.wb_d[name]
            R, C = src.shape
            sv = src.rearrange("(p a) c -> p (a c)", p=128)
            dv = dst.rearrange("(p a) c -> p (a c)", p=128)
            F = R * C // 128
            for o in range(0, F, 8192):
                jobs.append((sv, dv, o, min(8192, F - o)))
        return jobs

    def cast_setup(self, st):
        nc, P = self.nc, self.P
        self.cj = self.cast_jobs()
        self.cji = 0
        self.cpend = None
        self.c32 = Pool(P, st, "cst32_", 2, [128, 2048], F32)
        self.cbf = Pool(P, st, "cstbf_", 2, [128, 2048], BF16)
        self.cch_i = [P.chan("cji0"), P.chan("cji1")]
        self.cch_o = [P.chan("cjo0"), P.chan("cjo1")]
        self.r_wb = Res("wb_scratch")

    def cast_step(self, n):
        nc, P = self.nc, self.P
        for _ in range(n):
            if self.cji >= len(self.cj):
                break
            sv, dv, o, w = self.cj[self.cji]
            i = self.cji % 2
            self.cji += 1
            t32, r32 = self.c32.next()
            tbf, rbf = self.cbf.next()
            P.dma("sp", self.cch_i[i], lambda t32=t32, sv=sv, o=o, w=w: nc.sync.dma_start(out=t32[:, 0:w], in_=sv[:, o:o + w]), writes=[r32])
            P.op("pool", lambda t32=t32, tbf=tbf, w=w: nc.gpsimd.tensor_copy(out=tbf[:, 0:w], in_=t32[:, 0:w]), reads=[r32], writes=[rbf])
            if self.cpend is not None:
                self.cpend()
            self.cpend = (lambda i=i, tbf=tbf, dv=dv, o=o, w=w, rbf=rbf: P.dma(
                "sp", self.cch_o[i], lambda: nc.sync.dma_start(out=dv[:, o:o + w], in_=tbf[:, 0:w]), reads=[rbf], writes=[self.r_wb]))
        if self.cji >= len(self.cj) and self.cpend is not None:
            self.cpend()
            self.cpend = None

    def phase_mix(self):
        nc, P = self.nc, self.P
        with ExitStack() as st:
            sbt = lambda name, shape, dt: st.enter_context(nc.sbuf_tensor(name, list(shape), dt))
            g1bc = sbt("m_g1bc", [128, D], F32)
            dww = sbt("m_dww", [128, 16, 31], F32)
            cols = sbt("m_cols", [128, 4, 16], F32)
            r_cst = Res()
            chc = P.chan("m_c")
            P.dma("sp", chc, lambda: nc.sync.dma_start(out=g1bc[:], in_=self.ada_d[2 * D:3 * D].partition_broadcast(128)), writes=[r_cst])
            for c_ in range(16):
                P.dma("sp", chc, lambda c_=c_: nc.sync.dma_start(out=dww[:, c_, :], in_=self.i_dww[:, c_ * 128:(c_ + 1) * 128].rearrange("k p -> p k"),
                                                              allow_slow_non_contiguous=True), writes=[r_cst], cont=True)
            for i, src in enumerate((self.i_dwb, self.i_lng, self.i_lnb, self.i_pwb)):
                P.dma("sp", chc, lambda i=i, src=src: nc.sync.dma_start(out=cols[:, i, :], in_=src.rearrange("(c p) -> p c", p=128),
                                                                      allow_slow_non_contiguous=True), writes=[r_cst], cont=True)
            oT = sbt("m_oT", [128, 16, 512], BF16)
            glu = sbt("m_glu", [128, 16, 544], BF16)
            ybf = sbt("m_ybf", [128, 16, 512], BF16)
            uc = sbt("m_uc", [128, 16, 512], BF16)
            mer = sbt("m_mer", [128, 16, 512], BF16)
            r_oT, r_glu, r_ybf, r_uc, r_mer = [Res() for _ in range(5)]
            accp = Pool(P, st, "m_acc", 2, [128, D], F32)
            accb = Pool(P, st, "m_accb", 2, [128, D], BF16)
            wblk = Pool(P, st, "m_w", 2, [128, 16, 512], BF16)
            dgp = Pool(P, st, "m_dg", 2, [128, 31, 128], BF16)
            ysq = Pool(P, st, "m_ysq", 2, [128, 512], BF16)
            mgp = Pool(P, st, "m_mg", 4, [128, 512], BF16)
            tmpf = Pool(P, st, "m_tmp", 3, [128, 512], F32)
            xp = Pool(P, st, "m_x", 3, [128, 512], F32)
            stat = sbt("m_stat", [128, 3, 512], F32)
            r_stat = Res()
            chl = [P.chan(f"m_l{i}") for i in range(4)]
            chw = [P.chan("m_w0"), P.chan("m_w1")]
            chs = [P.chan(f"m_s{i}") for i in range(3)]
            mm = Pool(P, st, "m_mm", 3, [128, 512], F32, psum=True)
            sps = Pool(P, st, "m_sp", 2, [128, 512], F32, psum=True)
            tps = Pool(P, st, "m_tp", 2, [128, 512], BF16, psum=True)
            r_in, r_out = self.r_scr_all, Res("xmid")
            cnt = {"l": 0, "w": 0, "s": 0}

            def load_wblk(wsrc, cb):
                wt, wr = wblk.next()
                ch = chw[cnt["w"] % 2]
                cnt["w"] += 1
                P.dma("sp", ch, lambda: nc.sync.dma_start(out=wt[:], in_=wsrc[:, cb * 512:(cb + 1) * 512].rearrange("(k p) c -> p k c", p=128)),
                      reads=[self.r_wb], writes=[wr])
                return wt, wr

            def mix_tile(ti, q0, nq):
                nsub, rows = max(1, nq // 128), min(128, nq)
                for s_ in range(nsub):
                    at, r_at = accp.next()
                    ab, r_ab = accb.next()
                    ch = chl[cnt["l"] % 4]
                    cnt["l"] += 1
                    P.dma("sp", ch, lambda at=at, s_=s_: nc.sync.dma_start(out=at[0:rows, :], in_=self.accd[q0 - Q0 + s_ * 128:q0 - Q0 + s_ * 128 + rows, :]),
                          reads=[r_in], writes=[r_at])
                    P.op("act", lambda at=at, ab=ab: nc.scalar.copy(out=ab[0:rows, :], in_=at[0:rows, :]), reads=[r_at], writes=[r_ab])
                    for hq in range(4):
                        tp, r_tp = tps.next()
                        for hl in range(4):
                            h = hq * 4 + hl
                            P.op("pe", lambda tp=tp, hl=hl, h=h, ab=ab: nc.tensor.transpose(
                                out=tp[:, hl * 128:hl * 128 + rows], in_=ab[0:rows, h * 128:(h + 1) * 128], identity=self.ident[0:rows, 0:rows]),
                                reads=[r_ab, self.r_const], writes=[r_tp])
                        P.op("dve", lambda tp=tp, hq=hq, s_=s_: nc.vector.tensor_copy(
                            out=oT[:, hq * 4:hq * 4 + 4, s_ * 128:s_ * 128 + rows],
                            in_=tp[:, :].rearrange("p (h q) -> p h q", h=4)[:, :, 0:rows]), reads=[r_tp], writes=[r_oT])
                for cb in range(4):
                    wt, wr = load_wblk(self.wb_d["wo_b"], cb)
                    for cl in range(4):
                        c = cb * 4 + cl
                        ps, r_ps = mm.next()
                        for kc in range(16):
                            P.op("pe", lambda ps=ps, kc=kc, cl=cl, wt=wt: nc.tensor.matmul(
                                ps[:, 0:nq], lhsT=wt[:, kc, cl * 128:(cl + 1) * 128], rhs=oT[:, kc, 0:nq], start=(kc == 0), stop=(kc == 15)),
                                reads=[wr, r_oT], writes=[r_ps])
                        mg, r_mg = mgp.next()
                        ch = chl[cnt["l"] % 4]
                        cnt["l"] += 1
                        P.dma("sp", ch, lambda mg=mg, c=c: nc.sync.dma_start(out=mg[:, 0:nq], in_=self.mgT[:, c, q0 - B0:q0 - B0 + nq]),
                              reads=[r_in], writes=[r_mg])
                        P.op("dve", lambda ps=ps, mg=mg, c=c: nc.vector.tensor_tensor(out=mer[:, c, 0:nq], in0=ps[:, 0:nq], in1=mg[:, 0:nq], op=ALU.mult),
                             reads=[r_ps, r_mg], writes=[r_mer])
                P.dma("sp", chl[cnt["l"] % 4], lambda: nc.sync.dma_start(out=glu[:, :, 0:nq + 32], in_=self.gluT[:, :, q0 - 32 - B0:q0 - B0 + nq]),
                      reads=[r_in], writes=[r_glu])
                cnt["l"] += 1
                s_sum, r_ssum = sps.next()
                s_sq, r_ssq = sps.next()
                for chn in range(16):
                    dg, r_dg = dgp.next()
                    P.op("dve", lambda dg=dg, chn=chn: nc.vector.tensor_tensor(
                        out=dg[:], in0=self.ident[:].unsqueeze(1).to_broadcast([128, 31, 128]),
                        in1=dww[:, chn, :].unsqueeze(2).to_broadcast([128, 31, 128]), op=ALU.mult),
                        reads=[self.r_const, r_cst], writes=[r_dg])
                    ps, r_ps = mm.next()
                    for k in range(31):
                        P.op("pe", lambda ps=ps, dg=dg, k=k, chn=chn: nc.tensor.matmul(
                            ps[:, 0:nq], lhsT=dg[:, k, :], rhs=glu[:, chn, k + 2:k + 2 + nq], start=(k == 0), stop=(k == 30)),
                            reads=[r_dg, r_glu], writes=[r_ps])
                    P.op("act", lambda ps=ps, chn=chn: nc.scalar.activation(out=ybf[:, chn, 0:nq], in_=ps[:, 0:nq], func=AF.Identity,
                                                                          bias=cols[:, 0, chn:chn + 1]), reads=[r_ps, r_cst], writes=[r_ybf])
                    yq, r_yq = ysq.next()
                    P.op("act", lambda ps=ps, chn=chn, yq=yq: nc.scalar.activation(out=yq[:, 0:nq], in_=ps[:, 0:nq], func=AF.Square,
                                                                                 bias=cols[:, 0, chn:chn + 1]), reads=[r_ps, r_cst], writes=[r_yq])
                    P.op("pe", lambda chn=chn: nc.tensor.matmul(s_sum[:, 0:nq], lhsT=self.ones[:], rhs=ybf[:, chn, 0:nq],
                                                                start=(chn == 0), stop=(chn == 15)), reads=[r_ybf, self.r_const], writes=[r_ssum])
                    P.op("pe", lambda chn=chn, yq=yq: nc.tensor.matmul(s_sq[:, 0:nq], lhsT=self.ones[:], rhs=yq[:, 0:nq],
                                                                       start=(chn == 0), stop=(chn == 15)), reads=[r_yq, self.r_const], writes=[r_ssq])
                mean, rstd, msq = stat[:, 0, 0:nq], stat[:, 1, 0:nq], stat[:, 2, 0:nq]
                P.op("dve", lambda: nc.vector.tensor_scalar(out=mean, in0=s_sum[:, 0:nq], scalar1=1.0 / D, scalar2=None, op0=ALU.mult),
                     reads=[r_ssum], writes=[r_stat])
                P.op("dve", lambda: nc.vector.tensor_tensor(out=msq, in0=mean, in1=mean, op=ALU.mult), reads=[r_stat], writes=[r_stat])
                P.op("dve", lambda: nc.vector.scalar_tensor_tensor(out=rstd, in0=s_sq[:, 0:nq], scalar=1.0 / D, in1=msq, op0=ALU.mult, op1=ALU.subtract),
                     reads=[r_ssq, r_stat], writes=[r_stat])
                P.op("dve", lambda: nc.vector.tensor_scalar(out=rstd, in0=rstd, scalar1=EPS, scalar2=None, op0=ALU.add), reads=[r_stat], writes=[r_stat])
                P.op("act", lambda: nc.scalar.sqrt(out=rstd, in_=rstd), reads=[r_stat], writes=[r_stat])
                P.op("dve", lambda: nc.vector.reciprocal(out=rstd, in_=rstd), reads=[r_stat], writes=[r_stat])
                for chn in range(16):
                    tf, r_tf = tmpf.next()
                    P.op("dve", lambda tf=tf, chn=chn: nc.vector.tensor_tensor(out=tf[:, 0:nq], in0=ybf[:, chn, 0:nq], in1=mean, op=ALU.subtract),
                         reads=[r_ybf, r_stat], writes=[r_tf])
                    P.op("dve", lambda tf=tf: nc.vector.tensor_tensor(out=tf[:, 0:nq], in0=tf[:, 0:nq], in1=rstd, op=ALU.mult),
                         reads=[r_tf, r_stat], writes=[r_tf])
                    P.op("act", lambda tf=tf, chn=chn: nc.scalar.activation(out=uc[:, chn, 0:nq], in_=tf[:, 0:nq], func=AF.Silu,
                                                                          scale=cols[:, 1, chn:chn + 1], bias=cols[:, 2, chn:chn + 1]),
                         reads=[r_tf, r_cst], writes=[r_uc])
                for cb in range(4):
                    wt, wr = load_wblk(self.wb_d["wpw_b"], cb)
                    for cl in range(4):
                        c = cb * 4 + cl
                        ps, r_ps = mm.next()
                        for kc in range(16):
                            P.op("pe", lambda ps=ps, kc=kc, cl=cl, wt=wt: nc.tensor.matmul(
                                ps[:, 0:nq], lhsT=wt[:, kc, cl * 128:(cl + 1) * 128], rhs=uc[:, kc, 0:nq], start=(kc == 0), stop=(kc == 15)),
                                reads=[wr, r_uc], writes=[r_ps])
                        mg, r_mg = mgp.next()
                        ch = chl[cnt["l"] % 4]
                        cnt["l"] += 1
                        P.dma("sp", ch, lambda mg=mg, c=c: nc.sync.dma_start(out=mg[:, 0:nq], in_=self.mgT[:, 16 + c, q0 - B0:q0 - B0 + nq]),
                              reads=[r_in], writes=[r_mg])
                        tf, r_tf = tmpf.next()
                        P.op("dve", lambda ps=ps, mg=mg, c=c, tf=tf: nc.vector.scalar_tensor_tensor(
                            out=tf[:, 0:nq], in0=ps[:, 0:nq], scalar=cols[:, 3, c:c + 1], in1=mg[:, 0:nq], op0=ALU.add, op1=ALU.mult),
                            reads=[r_ps, r_mg, r_cst], writes=[r_tf])
                        P.op("dve", lambda c=c, tf=tf: nc.vector.tensor_tensor(out=mer[:, c, 0:nq], in0=mer[:, c, 0:nq], in1=tf[:, 0:nq], op=ALU.add),
                             reads=[r_tf, r_mer], writes=[r_mer])
                for cb in range(4):
                    wt, wr = load_wblk(self.wb_d["wout_b"], cb)
                    for s_ in range(nsub):
                        ps, r_ps = mm.next()
                        for kc in range(16):
                            P.op("pe", lambda ps=ps, kc=kc, s_=s_, wt=wt: nc.tensor.matmul(
                                ps[0:rows, :], lhsT=mer[:, kc, s_ * 128:s_ * 128 + rows], rhs=wt[:, kc, :], start=(kc == 0), stop=(kc == 15)),
                                reads=[wr, r_mer], writes=[r_ps])
                        xt, r_xt = xp.next()
                        i3 = cnt["s"] % 3
                        cnt["s"] += 1
                        r0 = q0 + s_ * 128
                        P.dma("sp", chs[i3], lambda xt=xt, r0=r0, cb=cb: nc.sync.dma_start(out=xt[0:rows, :], in_=self.xv[r0:r0 + rows, cb * 512:(cb + 1) * 512]),
                              writes=[r_xt])
                        tf, r_tf = tmpf.next()
                        P.op("dve", lambda ps=ps, tf=tf, cb=cb: nc.vector.tensor_tensor(out=tf[0:rows, :], in0=ps[0:rows, :], in1=g1bc[0:rows, cb * 512:(cb + 1) * 512],
                                                                                     op=ALU.mult), reads=[r_ps, r_cst], writes=[r_tf])
                        P.op("dve", lambda xt=xt, tf=tf: nc.vector.tensor_tensor(out=xt[0:rows, :], in0=xt[0:rows, :], in1=tf[0:rows, :], op=ALU.add),
                             reads=[r_tf, r_xt], writes=[r_xt])
                        P.dma("act", chs[i3], lambda xt=xt, r0=r0, cb=cb: nc.scalar.dma_start(
                            out=self.xmid[r0 - Q0:r0 - Q0 + rows, cb * 512:(cb + 1) * 512], in_=xt[0:rows, :]), reads=[r_xt], writes=[r_out])

            for ti, (q0, nq) in enumerate(QTILES):
                if self.dbg and "tiles" in self.dbg and ti not in self.dbg["tiles"]:
                    continue
                mix_tile(ti, q0, nq)
            P.barrier()


    def phase_ffn(self):
        nc, P = self.nc, self.P
        with ExitStack() as st:
            sbt = lambda name, shape, dt: st.enter_context(nc.sbuf_tensor(name, list(shape), dt))
            g2bc = sbt("f_g2bc", [128, D], F32)
            fgbc = sbt("f_fgbc", [128, D], F32)
            n2g = sbt("f_n2g", [128, 16], F32)
            s2 = sbt("f_s2", [128, 16], F32)
            fdw = sbt("f_fdw", [128, 3, 88], F32)
            fdb = sbt("f_fdb", [128, 88], F32)
            hfl = sbt("f_hfl", [128, 1], F32)
            r_cst, r_s2 = Res(), Res()
            chc = P.chan("f_c")
            P.dma("sp", chc, lambda: nc.sync.dma_start(out=g2bc[:], in_=self.ada_d[5 * D:6 * D].partition_broadcast(128)), writes=[r_cst])
            P.dma("sp", chc, lambda: nc.sync.dma_start(out=fgbc[:], in_=self.i_fg.partition_broadcast(128)), writes=[r_cst], cont=True)
            P.dma("sp", chc, lambda: nc.sync.dma_start(out=n2g[:], in_=self.i_n2g.rearrange("(c p) -> p c", p=128), allow_slow_non_contiguous=True),
                  writes=[r_cst], cont=True)
            for k in range(3):
                P.dma("sp", chc, lambda k=k: nc.sync.dma_start(out=fdw[:, k, :], in_=self.i_fdw[k, :].rearrange("(c p) -> p c", p=128),
                                                            allow_slow_non_contiguous=True), writes=[r_cst], cont=True)
            P.dma("sp", chc, lambda: nc.sync.dma_start(out=fdb[:], in_=self.i_fdb.rearrange("(c p) -> p c", p=128), allow_slow_non_contiguous=True),
                  writes=[r_cst], cont=True)
            P.dma("sp", chc, lambda: nc.sync.dma_start(out=hfl[:], in_=self.i_hflag[:, :]), writes=[r_cst], cont=True)
            P.op("dve", lambda: nc.vector.scalar_tensor_tensor(out=s2[:], in0=self.ada[:, 64:80], scalar=1.0, in1=n2g[:], op0=ALU.add, op1=ALU.mult),
                 reads=[self.r_ada, r_cst], writes=[r_s2])
            xpool = Pool(P, st, "f_x", 1, [128, 4, D], F32)
            chx = P.chan("f_x")
            junk = sbt("f_junk", [128, D], BF16)
            r_junk = Res()
            sspool = Pool(P, st, "f_ss", 2, [128, 4], F32)
            rspool = Pool(P, st, "f_rs", 2, [128, 4], F32)
            xnpool = Pool(P, st, "f_xn", 1, [128, 4, D], BF16)
            h2T = sbt("f_h2T", [128, 16, 514], BF16)
            r_h2T = Res()
            tpp = Pool(P, st, "f_tp", 2, [128, 512], BF16, psum=True)
            up = Pool(P, st, "f_up", 2, [128, 1024], F32, psum=True)
            dn = Pool(P, st, "f_dn", 2, [128, 512], F32, psum=True)
            wup = Pool(P, st, "f_wu", 2, [128, 16, 256], BF16)
            wdn = Pool(P, st, "f_wd", 2, [128, 44, 256], BF16)
            chwu = [P.chan("f_wu0"), P.chan("f_wu1")]
            chwd = [P.chan("f_wd0"), P.chan("f_wd1")]
            z = sbt("f_z", [128, 44, 512], BF16)
            r_z = Res()
            hh, r_hh = z[:, 0:16, 0:128], r_z
            Tp = Pool(P, st, "f_T", 3, [128, 512], F32)
            sgp = Pool(P, st, "f_sg", 1, [128, 512], BF16)
            tmp = Pool(P, st, "f_tmp", 1, [128, 256], F32)
            cho = [P.chan("f_o0"), P.chan("f_o1")]
            r_in = Res()
            wupb, wdnb = self.wb_d["wup_b"], self.wb_d["wdn_b"]
            cnt = {"u": 0, "d": 0, "o": 0}

            class HP:
                def __init__(s_, ap, res):
                    s_.ap, s_.res = ap, res

                def next(s_):
                    return s_.ap, s_.res

            self.norm_T(P, st, self.xmid, 0, 1, xpool, chx, junk, r_junk, sspool, rspool, xnpool, HP(hh, r_hh), tpp,
                        self.ident, self.r_const, s2, r_s2, self.ada, self.r_ada, 48)
            P.op("dve", lambda: nc.vector.tensor_scalar(out=h2T[:, :, 0:2], in0=hh[:, :, 62:64], scalar1=hfl[:, 0:1], scalar2=None, op0=ALU.mult),
                 reads=[r_hh, r_cst], writes=[r_h2T])

            def window(w):
                row0 = HALO + 512 * w
                if w > 0:
                    P.op("pool", lambda: nc.gpsimd.tensor_copy(out=h2T[:, :, 0:2], in_=h2T[:, :, 512:514]), reads=[r_h2T], writes=[r_h2T])
                self.norm_T(P, st, self.xmid, row0, 4, xpool, chx, junk, r_junk, sspool, rspool, xnpool, HP(h2T[:, :, 2:514], r_h2T), tpp,
                            self.ident, self.r_const, s2, r_s2, self.ada, self.r_ada, 48)
                xt, r_xt = self.last_xt
                for pb in range(44):
                    wt, wr = wup.next()
                    ch = chwu[cnt["u"] % 2]
                    cnt["u"] += 1
                    P.dma("sp", ch, lambda wt=wt, pb=pb: nc.sync.dma_start(
                        out=wt[:, :, 0:128], in_=wupb[:, pb * 128:(pb + 1) * 128].rearrange("(k p) c -> p k c", p=128)), reads=[self.r_wb], writes=[wr])
                    P.dma("sp", ch, lambda wt=wt, pb=pb: nc.sync.dma_start(
                        out=wt[:, :, 128:256], in_=wupb[:, FF + pb * 128:FF + (pb + 1) * 128].rearrange("(k p) c -> p k c", p=128)),
                        reads=[self.r_wb], writes=[wr], cont=True)
                    for cl in range(1):
                        c = pb
                        Ts = []
                        for half in range(2):
                            cc = c + 44 * half
                            woff = half * 128
                            ps, r_ps = up.next()
                            for kc in range(16):
                                P.op("pe", lambda ps=ps, kc=kc, woff=woff, wt=wt: nc.tensor.matmul(
                                    ps[:, 512:1024], lhsT=wt[:, kc, woff:woff + 128], rhs=h2T[:, kc, 2:514], start=(kc == 0), stop=(kc == 15)),
                                    reads=[wr, r_h2T], writes=[r_ps])
                            for kc in range(16):
                                P.op("pe", lambda ps=ps, kc=kc, woff=woff, wt=wt: nc.tensor.matmul(
                                    ps[:, 510:512], lhsT=wt[:, kc, woff:woff + 128], rhs=h2T[:, kc, 0:2], start=(kc == 0), stop=(kc == 15)),
                                    reads=[wr, r_h2T], writes=[r_ps])
                            T_, r_T = Tp.next()
                            P.op("act", lambda ps=ps, T_=T_, cc=cc: nc.scalar.activation(out=T_[:], in_=ps[:, 512:1024], func=AF.Identity,
                                                                                        scale=fdw[:, 2, cc:cc + 1], bias=fdb[:, cc:cc + 1]),
                                 reads=[r_ps, r_cst], writes=[r_T])
                            P.op("dve", lambda ps=ps, T_=T_, cc=cc: nc.vector.scalar_tensor_tensor(
                                out=T_[:], in0=ps[:, 511:1023], scalar=fdw[:, 1, cc:cc + 1], in1=T_[:], op0=ALU.mult, op1=ALU.add),
                                reads=[r_ps, r_T, r_cst], writes=[r_T])
                            P.op("dve", lambda ps=ps, T_=T_, cc=cc: nc.vector.scalar_tensor_tensor(
                                out=T_[:], in0=ps[:, 510:1022], scalar=fdw[:, 0, cc:cc + 1], in1=T_[:], op0=ALU.mult, op1=ALU.add),
                                reads=[r_ps, r_T, r_cst], writes=[r_T])
                            Ts.append((T_, r_T))
                        (Ta, r_Ta), (Tg, r_Tg) = Ts
                        sg, r_sg = sgp.next()
                        P.op("act", lambda Tg=Tg, sg=sg: nc.scalar.activation(out=sg[:], in_=Tg[:], func=AF.Silu), reads=[r_Tg], writes=[r_sg])
                        P.op("dve", lambda Ta=Ta, sg=sg, c=c: nc.vector.tensor_tensor(out=z[:, c, :], in0=Ta[:], in1=sg[:], op=ALU.mult),
                             reads=[r_Ta, r_sg], writes=[r_z])
                for cb in range(8):
                    wt, wr = wdn.next()
                    ch = chwd[cnt["d"] % 2]
                    cnt["d"] += 1
                    P.dma("sp", ch, lambda wt=wt, cb=cb: nc.sync.dma_start(
                        out=wt[:], in_=wdnb[:, cb * 256:(cb + 1) * 256].rearrange("(k p) c -> p k c", p=128)), reads=[self.r_wb], writes=[wr])
                    for s_ in range(4):
                        ps, r_ps = dn.next()
                        for kc in range(44):
                            P.op("pe", lambda ps=ps, kc=kc, s_=s_, wt=wt: nc.tensor.matmul(
                                ps[:, 0:256], lhsT=z[:, kc, s_ * 128:(s_ + 1) * 128], rhs=wt[:, kc, :], start=(kc == 0), stop=(kc == 43)),
                                reads=[wr, r_z], writes=[r_ps])
                        tf, r_tf = tmp.next()
                        P.op("dve", lambda ps=ps, tf=tf, cb=cb: nc.vector.tensor_tensor(out=tf[:], in0=ps[:, 0:256], in1=g2bc[:, cb * 256:(cb + 1) * 256], op=ALU.mult),
                             reads=[r_ps, r_cst], writes=[r_tf])
                        P.op("dve", lambda tf=tf, s_=s_, cb=cb, xt=xt: nc.vector.tensor_tensor(
                            out=xt[:, s_, cb * 256:(cb + 1) * 256], in0=xt[:, s_, cb * 256:(cb + 1) * 256], in1=tf[:], op=ALU.add),
                            reads=[r_tf, r_xt], writes=[r_xt])
                ss, r_ss = sspool.next()
                rs, r_rs = rspool.next()
                for s_ in range(4):
                    P.op("act", lambda s_=s_, xt=xt, ss=ss: nc.scalar.activation(out=junk[:], in_=xt[:, s_, :], func=AF.Square, accum_out=ss[:, s_:s_ + 1]),
                         reads=[r_xt], writes=[r_junk, r_ss])
                P.op("dve", lambda ss=ss, rs=rs: nc.vector.tensor_scalar(out=rs[:], in0=ss[:], scalar1=1.0 / D, scalar2=EPS, op0=ALU.mult, op1=ALU.add),
                     reads=[r_ss], writes=[r_rs])
                P.op("act", lambda rs=rs: nc.scalar.sqrt(out=rs[:], in_=rs[:]), reads=[r_rs], writes=[r_rs])
                P.op("dve", lambda rs=rs: nc.vector.reciprocal(out=rs[:], in_=rs[:]), reads=[r_rs], writes=[r_rs])
                for s_ in range(4):
                    P.op("dve", lambda s_=s_, xt=xt, rs=rs: nc.vector.scalar_tensor_tensor(
                        out=xt[:, s_, :], in0=xt[:, s_, :], scalar=rs[:, s_:s_ + 1], in1=fgbc[:], op0=ALU.mult, op1=ALU.mult),
                        reads=[r_xt, r_rs, r_cst], writes=[r_xt])
                    i2 = cnt["o"] % 2
                    cnt["o"] += 1
                    t0 = 512 * w + 128 * s_
                    P.dma("act", cho[i2], lambda s_=s_, xt=xt, t0=t0: nc.scalar.dma_start(out=self.out[t0:t0 + 128, :], in_=xt[:, s_, :]), reads=[r_xt])

            for w in range(4):
                if self.dbg and "wins" in self.dbg and w not in self.dbg["wins"]:
                    continue
                window(w)
            P.barrier()

_STATIC = {}


def static_tables():
    if _STATIC:
        return _STATIC
    sl = np.array(SLOPES, np.float64)
    i = np.arange(128, dtype=np.float64)
    bs = sl[None, :, None] * (i[:, None, None] + 64.0 * (np.arange(NDS)[None, None, :] - DOFF))
    bc = sl[None, :, None] * (16.0 * i[:, None, None] + 31.0 + 64.0 * (np.arange(NDC)[None, None, :] - ROFF))
    masks = np.zeros((128, NMASK, 512), np.float32)
    for n, v in enumerate(MASK_LIST):
        masks[:, n, :v.shape[1]] = np.where(v, 0.0, -30000.0)
    E = np.zeros((128, 64, 128), np.float32)
    for u in range(64):
        for k in range(128):
            E[2 * u + k // 64, u, k] = 1.0
    Ov = np.zeros((128, 9, NBW), np.float32)
    for jj in range(9):
        for ii in range(128):
            for b in range(NBLKW):
                dlt = ii - 128 * (jj + 1) - 4 * b + 1056
                if -1 <= dlt <= 3:
                    Ov[ii, jj, b] = 1.0
    _STATIC.update({"ident": np.eye(128, dtype=np.float32).astype(NPBF),
                    "bias_s": bs.astype(np.float32), "bias_c": bc.astype(np.float32),
                    "masks": masks.astype(NPBF), "esel": E.astype(NPBF), "ovm": Ov.astype(NPBF),
                    "ones_bf": np.ones((128, 128), np.float32).astype(NPBF)})
    return _STATIC


def host_tables(c):
    t_start = c * TOWN
    real = np.arange(TV) - OWN0 + t_start
    vt = (real >= 0).astype(np.float32)
    nv = np.arange(NCV)
    rn = nv - (OWN0 - t_start) // 16
    vc = ((rn >= 0) & (rn <= 1022) & (nv <= 1150)).astype(np.float32)
    selb = np.zeros((5, 4, 128, NBW), np.float32)
    selv = np.zeros((5, 4, 128, NBW), np.float32)
    b = np.arange(NBLKW)
    for ti, (q0, nq) in enumerate(QTILES):
        qend = q0 + nq
        jv = b + qend // 64 - NBLKW
        realb = jv - (OWN0 - t_start) // 64
        for s_ in range(max(1, nq // 128)):
            rows = min(128, nq)
            tq = q0 + 128 * s_ + np.arange(rows)
            cur = tq // 64
            valid = (realb[None, :] >= 0) & (jv[None, :] <= cur[:, None])
            forced = (realb[None, :] == 0) | (jv[None, :] == cur[:, None]) | (jv[None, :] == cur[:, None] - 1)
            selv[ti, s_, :rows, :NBLKW] = valid
            selb[ti, s_, :rows, :NBLKW] = 1.0 + 1e6 * forced
    d = {"vtok": np.ascontiguousarray(vt.reshape(TV // 128, 128).T),
         "vcmp": np.ascontiguousarray(vc.reshape(NCV // 128, 128).T),
         "selb": selb, "selv": selv,
         "hflag": np.full((128, 1), 1.0 if c > 0 else 0.0, np.float32)}
    d.update(static_tables())
    return d


def make_inputs(inputs, c):
    x = np.asarray(inputs["x"], np.float32)[0]
    t_start = c * TOWN
    xv = np.zeros((TV, D), np.float32)
    lo = OWN0 - t_start
    xv[lo:] = x[:t_start + TOWN]
    m = {"xv": xv, "c": np.asarray(inputs["c"], np.float32),
         "w_ada": np.asarray(inputs["w_ada"], np.float32)[0], "b_ada": np.asarray(inputs["b_ada"], np.float32)[0],
         "norm1_g": np.asarray(inputs["norm1_g"], np.float32)[0], "w_in": np.asarray(inputs["w_in"], np.float32)[0]}
    for n in ("cmp_pe", "w_kc1", "w_kc2", "w_vc1", "w_vc2", "w_o_nsa", "conv_dw_w", "conv_dw_b", "conv_ln_g", "conv_ln_b",
              "conv_pw_w", "conv_pw_b", "w_out", "norm2_g", "ffn_w_up", "ffn_dw_w", "ffn_dw_b", "ffn_w_down"):
        m[n] = np.asarray(inputs[n], np.float32)[0]
    m["final_g"] = np.asarray(inputs["final_g"], np.float32)
    m.update(host_tables(c))
    return m


def kernel(**inputs):
    k = K()
    nc = k.build()
    in_maps = [{n: v for n, v in make_inputs(inputs, c).items() if n in k.ins} for c in range(NCORE)]
    res = run_bass_kernel_spmd(nc, in_maps, core_ids=list(range(NCORE)))
    outs = [np.asarray(res.results[c]["out"], np.float32) for c in range(NCORE)]
    return np.concatenate(outs, axis=0)[None]
```

```python
from contextlib import ExitStack
import numpy as np
import ml_dtypes
import concourse.bass as bass
import concourse.mybir as mybir
from concourse.bass_utils import run_bass_kernel_spmd

F32 = mybir.dt.float32
BF16 = mybir.dt.bfloat16
I32 = mybir.dt.int32
AF = mybir.ActivationFunctionType
ALU = mybir.AluOpType
NPBF = ml_dtypes.bfloat16

D = 2048
T = 16384
NCORE = 8
TOWN = T // NCORE
NH, NG, HG, DK = 16, 2, 8, 128
EPS = 1e-6
FF = 5632
WIN = 512
OWN0 = 16384
TV = OWN0 + TOWN
HALO = 64
Q0 = OWN0 - HALO
NQ = TOWN + HALO
W0 = OWN0 - 640
NA = TV - W0
B0 = OWN0 - 128
NB = TV - B0
NCV = TV // 16
C_Q, C_KC, C_VC, C_KS, C_VS, C_KW, C_VW, C_GN, C_GLU, C_GM = 0, 2048, 2304, 2560, 2816, 3072, 3328, 3584, 3632, 7728
INW = 11824
EPOCH = 12000
SLOPES = [2.0 ** (-8.0 * (h + 1) / 16) for h in range(NH)]
NBLKW = 264
NBW = 272
DOFF, NDS = 272, 280
ROFF, NDC = 296, 280
QTILES = [(Q0, 64)] + [(OWN0 + 512 * i, 512) for i in range(4)]
SKIP = 64.0


def exp_width(h, nq):
    w = 64
    while w * 2 <= min(512, nq) and SLOPES[h] * (w * 2) <= 64.0:
        w *= 2
    return min(w, nq)


def n_slc_chunks(h, nq, maxc):
    return int(min(maxc, np.floor((SKIP / SLOPES[h] + nq) / 128) + 1))


def n_cmp_chunks(h, nq):
    return int(min(9, np.floor((SKIP / SLOPES[h] + nq + 15) / 2048) + 1))


def chunk_valid(kind, nq, j):
    i = np.arange(128)[:, None]
    q = np.arange(nq)[None, :]
    if kind == "cmp":
        kp = 16 * i + 31 + nq - 2048 * (j + 1)
        v = kp <= q
    else:
        kp = i + nq - 128 * (j + 1)
        dist = q - kp
        v = dist >= 0
        if kind == "win":
            v = v & (dist < WIN)
    return None if v.all() else v


def mask_index():
    idx = {}
    for nq in (64, 512):
        nwin = 8 if nq == 512 else 5
        for kind, nj in (("slc", 4), ("win", nwin), ("cmp", 1)):
            for j in range(nj):
                v = chunk_valid(kind, nq, j)
                if v is None:
                    continue
                idx[(kind, nq, j)] = v
    uniq, out = [], {}
    for key, v in idx.items():
        for n, u in enumerate(uniq):
            if u.shape == v.shape and (u == v).all():
                out[key] = n
                break
        else:
            uniq.append(v)
            out[key] = len(uniq) - 1
    return out, uniq


MASK_IDX, MASK_LIST = mask_index()
NMASK = len(MASK_LIST)


class Res:
    __slots__ = ("name", "last_w", "readers")

    def __init__(self, name=""):
        self.name = name
        self.last_w = None
        self.readers = []


class Op:
    __slots__ = ("eng", "fn", "deps", "seq", "sig", "is_dma", "chan", "needs_sig", "pos")


class Chan:
    def __init__(self, prog, name):
        self.sem = prog.new_sem("ch_" + name)
        self.count = 0
        self.last_op = None
        self.group = []


class Prog:
    ENGS = ("pe", "act", "dve", "pool", "sp")

    def __init__(self, nc, stack):
        self.nc = nc
        self.stack = stack
        self.ops = {e: [] for e in self.ENGS}
        self.seq = 0
        self.nsem = 0
        self.eng_sems = {e: [] for e in self.ENGS}
        self.chans = []
        self.uid = 0

    def new_sem(self, name):
        self.nsem += 1
        return self.stack.enter_context(self.nc.semaphore(name))

    def chan(self, name):
        c = Chan(self, name)
        self.chans.append(c)
        return c

    def _mk(self, eng, fn, reads, writes):
        op = Op()
        op.eng, op.fn, op.seq = eng, fn, self.seq
        self.seq += 1
        op.is_dma, op.chan, op.needs_sig, op.sig = False, None, False, None
        deps = []
        for r in reads:
            if r.last_w is not None:
                deps.append(r.last_w)
        for w in writes:
            if w.last_w is not None:
                deps.append(w.last_w)
            deps.extend(w.readers)
        for r in reads:
            if not getattr(op, "is_dma", False) and eng in ("pe", "act", "dve"):
                r.readers = [x for x in r.readers if x.eng != eng or x.is_dma]
            r.readers.append(op)
        for w in writes:
            w.last_w = op
            w.readers = []
        seen, dd = set(), []
        for d in deps:
            if id(d) in seen or d is op:
                continue
            seen.add(id(d))
            if eng == "pe" and d.eng == "pe" and not d.is_dma:
                continue
            dd.append(d)
        op.deps = dd
        self.ops[eng].append(op)
        return op

    def op(self, eng, fn, reads=(), writes=()):
        return self._mk(eng, fn, list(reads), list(writes))

    def dma(self, queue, chan, fn, reads=(), writes=(), cont=False):
        op = self._mk(queue, fn, list(reads), list(writes))
        op.is_dma, op.chan = True, chan
        if not cont:
            if chan.last_op is not None and chan.last_op not in op.deps:
                op.deps.append(chan.last_op)
            chan.group = []
        op.deps = [d for d in op.deps if d not in chan.group]
        chan.count += 16
        chan.group.append(op)
        for o in chan.group:
            o.sig = (chan.sem, chan.count)
        op.needs_sig = True
        chan.last_op = op
        return op

    def barrier(self):
        lasts = []
        for e in self.ENGS:
            for op in reversed(self.ops[e]):
                if not op.is_dma:
                    lasts.append(op)
                    break
        for c in self.chans:
            if c.last_op is not None:
                lasts.append(c.last_op)
        nc = self.nc
        eo = {"pe": nc.tensor, "act": nc.scalar, "dve": nc.vector, "pool": nc.gpsimd, "sp": nc.sync}
        for e in self.ENGS:
            op = self._mk(e, (lambda e=e: eo[e].nop()), [], [])
            op.deps = [d for d in lasts if not (d.eng == e and not d.is_dma)]

    def emit(self):
        nc = self.nc
        for e in self.ENGS:
            for op in self.ops[e]:
                for d in op.deps:
                    if not d.is_dma:
                        d.needs_sig = True
        for e in self.ENGS:
            cnt, sem = 0, None
            for op in self.ops[e]:
                if op.is_dma or not op.needs_sig:
                    continue
                if sem is None or cnt >= EPOCH:
                    sem = self.new_sem(f"e_{e}_{len(self.eng_sems[e])}")
                    self.eng_sems[e].append(sem)
                    cnt = 0
                cnt += 1
                op.sig = (sem, cnt)
        with nc.Block() as block:
            for e in self.ENGS:
                ops = self.ops[e]
                if not ops:
                    continue
                deco = {"pe": block.tensor, "act": block.scalar, "dve": block.vector,
                        "pool": block.gpsimd, "sp": block.sync}[e]

                def body(engobj, ops=ops):
                    known = {}
                    for op in ops:
                        for d in op.deps:
                            sem, val = d.sig
                            if known.get(id(sem), 0) >= val:
                                continue
                            engobj.wait_ge(sem, val)
                            known[id(sem)] = val
                        ins = op.fn()
                        if op.needs_sig:
                            ins.then_inc(op.sig[0], 16 if op.is_dma else 1)
                    last = {}
                    for op in ops:
                        if op.is_dma:
                            last[id(op.chan)] = op
                    for op in last.values():
                        sem, val = op.sig
                        if known.get(id(sem), 0) < val:
                            engobj.wait_ge(sem, val)
                            known[id(sem)] = val
                deco(body)


class Pool:
    def __init__(self, P, st, name, n, shape, dt, psum=False):
        self.t, self.r = [], []
        for i in range(n):
            if psum:
                self.t.append(st.enter_context(P.nc.psum_tensor(f"{name}{i}", list(shape), dt)))
            else:
                self.t.append(st.enter_context(P.nc.sbuf_tensor(f"{name}{i}", list(shape), dt)))
            self.r.append(Res(f"{name}{i}"))
        self.i = 0
        self.n = n

    def next(self):
        k = self.i % self.n
        self.i += 1
        return self.t[k], self.r[k]


class K:
    def __init__(self, dbg=None):
        self.dbg = dbg
        nc = self.nc = bass.Bass("TRN2", target_bir_lowering=False)
        self.ins = {}
        self.outs = {}

    def din(self, name, shape, dt=F32):
        t = self.nc.dram_tensor(name, list(shape), dt, kind="ExternalInput").ap()
        self.ins[name] = t
        return t

    def dscr(self, name, shape, dt):
        kind = "ExternalOutput" if (self.dbg and name in self.dbg) else "Internal"
        t = self.nc.dram_tensor(name, list(shape), dt, kind=kind).ap()
        return t

    def build(self, upto=99):
        nc = self.nc
        xv = self.din("xv", [TV, D])
        c_in = self.din("c", [1, D])
        w_ada = self.din("w_ada", [D, 6 * D])
        b_ada = self.din("b_ada", [6 * D])
        norm1_g = self.din("norm1_g", [D])
        w_in = self.din("w_in", [D, INW])
        vtok = self.din("vtok", [128, TV // 128])
        ident_in = self.din("ident", [128, 128], BF16)
        self.i_vcmp = self.din("vcmp", [128, NCV // 128])
        self.i_selb = self.din("selb", [5, 4, 128, NBW])
        self.i_selv = self.din("selv", [5, 4, 128, NBW])
        self.i_hflag = self.din("hflag", [128, 1])
        self.i_bias_s = self.din("bias_s", [128, NH, NDS])
        self.i_bias_c = self.din("bias_c", [128, NH, NDC])
        self.i_masks = self.din("masks", [128, NMASK, 512], BF16)
        self.i_esel = self.din("esel", [128, 64, 128], BF16)
        self.i_ovm = self.din("ovm", [128, 9, NBW], BF16)
        self.i_ones = self.din("ones_bf", [128, 128], BF16)
        self.i_cmp_pe = self.din("cmp_pe", [32, 128])
        self.i_w1 = [self.din("w_kc1", [4096, 256]), self.din("w_vc1", [4096, 256])]
        self.i_w2 = [self.din("w_kc2", [256, 128]), self.din("w_vc2", [256, 128])]
        self.i_wo = self.din("w_o_nsa", [D, D])
        self.i_dww = self.din("conv_dw_w", [31, D])
        self.i_dwb = self.din("conv_dw_b", [D])
        self.i_lng = self.din("conv_ln_g", [D])
        self.i_lnb = self.din("conv_ln_b", [D])
        self.i_wpw = self.din("conv_pw_w", [D, D])
        self.i_pwb = self.din("conv_pw_b", [D])
        self.i_wout = self.din("w_out", [D, D])
        self.i_n2g = self.din("norm2_g", [D])
        self.i_wup = self.din("ffn_w_up", [D, 2 * FF])
        self.i_fdw = self.din("ffn_dw_w", [3, 2 * FF])
        self.i_fdb = self.din("ffn_dw_b", [2 * FF])
        self.i_wdn = self.din("ffn_w_down", [FF, D])
        self.i_fg = self.din("final_g", [D])
        out = self.nc.dram_tensor("out", [TOWN, D], F32, kind="ExternalOutput").ap()
        self.out = out
        ada_d = self.dscr("ada_d", [6 * D], F32)
        kcT_raw = self.dscr("kcT_raw", [NG, 128, TV + 16], BF16)
        vcT_raw = self.dscr("vcT_raw", [NG, 128, TV + 16], BF16)
        kslT = self.dscr("kslT", [NG, 128, TV], BF16)
        vsl = self.dscr("vsl", [TV, NG, 130], BF16)
        QT = self.dscr("QT", [128, NH, NB], BF16)
        kwT = self.dscr("kwT", [NG, 128, NA], BF16)
        vw = self.dscr("vw", [NA, NG, 130], BF16)
        gates = self.dscr("gates", [NB, 48], F32)
        gluT = self.dscr("gluT", [128, 16, NB], BF16)
        mgT = self.dscr("mgT", [128, 32, NB], BF16)
        self.kcT = self.dscr("kcT", [NG, 128, NCV], BF16)
        self.vca = self.dscr("vca", [NCV, NG, 130], BF16)
        self.xmid = self.dscr("xmid", [NQ, D], F32)
        self.accd = self.dscr("accd", [NQ, D], F32)
        self.dumpS = self.dscr("dumpS", [128, 512], F32)
        self.impd = self.dscr("impd", [5, 4, 128, NG, NBW], F32)
        self.wb_d = {n: self.dscr(n, sh, BF16) for n, sh in (("wo_b", [D, D]), ("wpw_b", [D, D]), ("wout_b", [D, D]),
                                                           ("wup_b", [D, 2 * FF]), ("wdn_b", [FF, D]))}
        self.xv, self.kcT_raw, self.vcT_raw, self.kslT, self.vsl = xv, kcT_raw, vcT_raw, kslT, vsl
        self.QT, self.kwT, self.vw, self.gates, self.gluT, self.mgT, self.ada_d = QT, kwT, vw, gates, gluT, mgT, ada_d

        with ExitStack() as st0:
            P = self.P = Prog(nc, st0)
            self.st0 = st0
            sb = lambda name, shape, dt: st0.enter_context(nc.sbuf_tensor(name, list(shape), dt))
            ident = sb("ident_sb", [128, 128], BF16)
            r_const = Res("const")
            ch_c = P.chan("const")
            P.dma("sp", ch_c, lambda: nc.sync.dma_start(out=ident[:], in_=ident_in[:, :]), writes=[r_const])
            vtok_sb = sb("vtok_sb", [128, TV // 128], F32)
            P.dma("sp", ch_c, lambda: nc.sync.dma_start(out=vtok_sb[:], in_=vtok[:, :]), writes=[r_const], cont=True)
            ada = sb("ada_sb", [128, 96], F32)
            r_ada = Res("ada")
            s1 = sb("s1", [128, 16], F32)
            r_s1 = Res("s1")
            self.ident, self.r_const, self.vtok_sb, self.ada, self.r_ada, self.sb0 = ident, r_const, vtok_sb, ada, r_ada, sb
            self.ones = sb("ones_sb", [128, 128], BF16)
            P.dma("sp", ch_c, lambda: nc.sync.dma_start(out=self.ones[:], in_=self.i_ones[:, :]), writes=[r_const], cont=True)
            self.hfl = sb("hfl_sb", [128, 1], F32)
            P.dma("sp", ch_c, lambda: nc.sync.dma_start(out=self.hfl[:], in_=self.i_hflag[:, :]), writes=[r_const], cont=True)
            self.vcmp_sb = sb("vcmp_sb", [128, NCV // 128], F32)
            P.dma("sp", ch_c, lambda: nc.sync.dma_start(out=self.vcmp_sb[:], in_=self.i_vcmp[:, :]), writes=[r_const], cont=True)

            with ExitStack() as st:
                cT = st.enter_context(nc.sbuf_tensor("cT", [128, 16], F32))
                cact = st.enter_context(nc.sbuf_tensor("cact", [128, 16], F32))
                bT = st.enter_context(nc.sbuf_tensor("bT", [128, 96], F32))
                g1T = st.enter_context(nc.sbuf_tensor("g1T", [128, 16], F32))
                r_cT, r_cact, r_bT, r_g1T = Res(), Res(), Res(), Res()
                ch0 = P.chan("p0")
                P.dma("sp", ch0, lambda: nc.sync.dma_start(out=cT[:], in_=c_in[0, :].rearrange("(j p) -> p j", p=128),
                                                         allow_slow_non_contiguous=True), writes=[r_cT])
                P.dma("sp", ch0, lambda: nc.sync.dma_start(out=bT[:], in_=b_ada.rearrange("(f p) -> p f", p=128),
                                                         allow_slow_non_contiguous=True), writes=[r_bT], cont=True)
                P.dma("sp", ch0, lambda: nc.sync.dma_start(out=g1T[:], in_=norm1_g.rearrange("(j p) -> p j", p=128),
                                                         allow_slow_non_contiguous=True), writes=[r_g1T], cont=True)
                P.op("act", lambda: nc.scalar.activation(out=cact[:], in_=cT[:], func=AF.Silu), reads=[r_cT], writes=[r_cact])
                wpool = Pool(P, st, "wada", 2, [128, 16, 512], F32)
                chw = [P.chan("wada0"), P.chan("wada1")]
                aps = st.enter_context(nc.psum_tensor("ada_ps", [128, 96], F32))
                r_aps = Res()
                for blk in range(24):
                    wt, wr = wpool.next()
                    P.dma("sp", chw[blk % 2],
                          lambda wt=wt, blk=blk: nc.sync.dma_start(
                              out=wt[:], in_=w_ada[:, blk * 512:(blk + 1) * 512].rearrange("(k p) c -> p k c", p=128)),
                          writes=[wr])
                    for fl in range(4):
                        f = blk * 4 + fl
                        for kc in range(16):
                            P.op("pe", lambda wt=wt, fl=fl, kc=kc, f=f: nc.tensor.matmul(
                                aps[:, f:f + 1], lhsT=wt[:, kc, fl * 128:(fl + 1) * 128], rhs=cact[:, kc:kc + 1],
                                start=(kc == 0), stop=(kc == 15)), reads=[wr, r_cact], writes=[r_aps])
                P.op("dve", lambda: nc.vector.tensor_tensor(out=ada[:], in0=aps[:], in1=bT[:], op=ALU.add),
                     reads=[r_aps, r_bT], writes=[r_ada])
                P.op("dve", lambda: nc.vector.scalar_tensor_tensor(out=s1[:], in0=ada[:, 16:32], scalar=1.0, in1=g1T[:],
                                                                   op0=ALU.add, op1=ALU.mult),
                     reads=[r_ada, r_g1T], writes=[r_s1])
                r_adad = Res()
                ch_ad = P.chan("adad")
                P.dma("act", ch_ad, lambda: nc.scalar.dma_start(out=ada_d.rearrange("(f p) -> p f", p=128), in_=ada[:],
                                                              allow_slow_non_contiguous=True),
                      reads=[r_ada], writes=[r_adad])
                P.barrier()
            if upto <= 0:
                P.emit()
                return nc

            self.r_wb = Res("wb_scratch")
            if not (self.dbg and "nocast" in self.dbg):
                with ExitStack() as st:
                    c32 = Pool(P, st, "cst32_", 3, [128, 8192], F32)
                    cbf = Pool(P, st, "cstbf_", 3, [128, 8192], BF16)
                    cci = [P.chan(f"cji{i}") for i in range(3)]
                    cco = [P.chan(f"cjo{i}") for i in range(3)]
                    for ji, (sv, dv, o, w) in enumerate(self.cast_jobs()):
                        t32, r32 = c32.next()
                        tbf, rbf = cbf.next()
                        P.dma("sp", cci[ji % 3], lambda t32=t32, sv=sv, o=o, w=w: nc.sync.dma_start(out=t32[:, 0:w], in_=sv[:, o:o + w]), writes=[r32])
                        if ji % 2 == 0:
                            P.op("dve", lambda t32=t32, tbf=tbf, w=w: nc.vector.tensor_copy(out=tbf[:, 0:w], in_=t32[:, 0:w]), reads=[r32], writes=[rbf])
                        else:
                            P.op("act", lambda t32=t32, tbf=tbf, w=w: nc.scalar.copy(out=tbf[:, 0:w], in_=t32[:, 0:w]), reads=[r32], writes=[rbf])
                        P.dma("act", cco[ji % 3], lambda tbf=tbf, dv=dv, o=o, w=w: nc.scalar.dma_start(out=dv[:, o:o + w], in_=tbf[:, 0:w]),
                              reads=[rbf], writes=[self.r_wb])
                    P.barrier()
            with ExitStack() as st:
                wkv32 = Pool(P, st, "wkv32_", 2, [128, 16, 128], F32)
                wkv = st.enter_context(nc.sbuf_tensor("wkv", [128, 16, 1024], BF16))
                r_wkv = Res()
                chw = [P.chan("wkv0"), P.chan("wkv1")]
                for q in range(8):
                    wt, wr = wkv32.next()
                    P.dma("sp", chw[q % 2], lambda wt=wt, q=q: nc.sync.dma_start(
                        out=wt[:], in_=w_in[:, C_KC + q * 128:C_KC + (q + 1) * 128].rearrange("(k p) c -> p k c", p=128)),
                        writes=[wr])
                    P.op("pool", lambda wt=wt, q=q: nc.gpsimd.tensor_copy(out=wkv[:, :, q * 128:(q + 1) * 128], in_=wt[:]),
                         reads=[wr], writes=[r_wkv])
                xpool = Pool(P, st, "xt", 2, [128, 4, D], F32)
                chx = [P.chan("x0"), P.chan("x1")]
                junk = st.enter_context(nc.sbuf_tensor("junk", [128, D], BF16))
                r_junk = Res()
                sspool = Pool(P, st, "ss", 2, [128, 4], F32)
                rspool = Pool(P, st, "rs", 2, [128, 4], F32)
                xnpool = Pool(P, st, "xn", 2, [128, 4, D], BF16)
                hTpool = Pool(P, st, "hT", 2, [128, 16, 512], BF16)
                tpp = Pool(P, st, "tp", 4, [128, 512], BF16, psum=True)
                mmp = Pool(P, st, "mm", 3, [128, 512], F32, psum=True)
                stg = Pool(P, st, "stg", 2, [128, 6, 512], BF16)
                stv = Pool(P, st, "stv", 2, [128, 4, NG, 130], BF16)
                chs = [P.chan("st0"), P.chan("st1")]
                chv = [P.chan("sv0"), P.chan("sv1")]
                r_scr = Res("scr1a")
                for i in range(2):
                    P.op("dve", lambda i=i: nc.vector.memset(stv.t[i][:], 0.0), writes=[stv.r[i]])
                nblk = TV // 512
                if self.dbg and "nblk" in self.dbg:
                    nblk = self.dbg["nblk"]
                pend = []

                def start_block(tb):
                    xn_, r_xn_ = self.norm_prep(P, xv, tb * 512, 4, xpool, chx[tb % 2], junk, r_junk, sspool, rspool, xnpool)
                    hT_, r_hT_ = hTpool.next()
                    return hT_, r_hT_, self.norm_groups(P, xn_, r_xn_, hT_, r_hT_, 4, tpp, s1, r_s1, 0)

                def pump(n):
                    for _ in range(n):
                        if pend:
                            pend.pop(0)()

                nxt = start_block(0)
                for tb in range(nblk):
                    t0 = tb * 512
                    hT, r_hT, grps = nxt
                    pend.extend(grps)
                    pump(16)
                    if tb + 1 < nblk:
                        nxt = start_block(tb + 1)
                        pend.extend(nxt[2])
                        nxt = (nxt[0], nxt[1], [])
                    sg, r_sg = stg.next()
                    for ci in range(6):
                        ps, r_ps = mmp.next()
                        for kc in range(16):
                            P.op("pe", lambda ps=ps, ci=ci, kc=kc, hT=hT: nc.tensor.matmul(
                                ps[:], lhsT=wkv[:, kc, ci * 128:(ci + 1) * 128], rhs=hT[:, kc, :],
                                start=(kc == 0), stop=(kc == 15)), reads=[r_wkv, r_hT], writes=[r_ps])
                        if ci % 2 == 0:
                            P.op("act", lambda ps=ps, sg=sg, ci=ci: nc.scalar.copy(out=sg[:, ci, :], in_=ps[:]),
                                 reads=[r_ps], writes=[r_sg])
                        else:
                            P.op("dve", lambda ps=ps, sg=sg, ci=ci: nc.vector.tensor_copy(out=sg[:, ci, :], in_=ps[:]),
                                 reads=[r_ps], writes=[r_sg])
                        pump(2)
                    dsts = [kcT_raw, vcT_raw, kslT]
                    for k3 in range(3):
                        P.dma("act", chs[tb % 2], lambda sg=sg, k3=k3, t0=t0: nc.scalar.dma_start(
                            out=dsts[k3][:, :, t0:t0 + 512].rearrange("g p t -> p g t"), in_=sg[:, 2 * k3:2 * k3 + 2, :]),
                            reads=[r_sg], writes=[r_scr], cont=(k3 > 0))
                    sv, r_sv = stv.next()
                    for s in range(4):
                        ps, r_ps = mmp.next()
                        for kc in range(16):
                            P.op("pe", lambda ps=ps, s=s, kc=kc, hT=hT: nc.tensor.matmul(
                                ps[:, 0:256], lhsT=hT[:, kc, s * 128:(s + 1) * 128], rhs=wkv[:, kc, 768:1024],
                                start=(kc == 0), stop=(kc == 15)), reads=[r_wkv, r_hT], writes=[r_ps])
                        pump(1)
                        tile = tb * 4 + s
                        P.op("dve", lambda ps=ps, sv=sv, s=s, tile=tile: nc.vector.tensor_scalar(
                            out=sv[:, s, :, 0:128], in0=ps[:, 0:256].rearrange("p (g d) -> p g d", g=NG),
                            scalar1=vtok_sb[:, tile:tile + 1], scalar2=None, op0=ALU.mult),
                            reads=[r_ps, r_const], writes=[r_sv])
                        P.op("dve", lambda sv=sv, s=s, tile=tile: nc.vector.tensor_copy(
                            out=sv[:, s, :, 128:130], in_=vtok_sb[:, tile:tile + 1].unsqueeze(1).to_broadcast([128, NG, 2])),
                            reads=[r_const], writes=[r_sv])
                    P.dma("act", chv[tb % 2], lambda sv=sv, t0=t0: nc.scalar.dma_start(
                        out=vsl[t0:t0 + 512, :, :].rearrange("(s p) g d -> p s g d", p=128), in_=sv[:]),
                        reads=[r_sv], writes=[r_scr])
                P.barrier()
            if upto <= 1:
                P.emit()
                return nc

            with ExitStack() as st:
                hTo = st.enter_context(nc.sbuf_tensor("hTo", [128, 16, NA], BF16))
                r_hTo = Res()
                with ExitStack() as st2:
                    xpool = Pool(P, st2, "xtb", 2, [128, 4, D], F32)
                    chx = [P.chan("xb0"), P.chan("xb1")]
                    junk = st2.enter_context(nc.sbuf_tensor("junkb", [128, D], BF16))
                    r_junk = Res()
                    sspool = Pool(P, st2, "ssb", 2, [128, 4], F32)
                    rspool = Pool(P, st2, "rsb", 2, [128, 4], F32)
                    xnpool = Pool(P, st2, "xnb", 2, [128, 4, D], BF16)
                    tpp = Pool(P, st2, "tpb", 4, [128, 512], BF16, psum=True)
                    for tb in range(6):
                        t0 = W0 + tb * 512
                        nsub = 4 if tb < 5 else 1

                        class _HP:
                            def next(self_inner):
                                return hTo[:, :, tb * 512:tb * 512 + nsub * 128], r_hTo
                        self.norm_T(P, st2, xv, t0, nsub, xpool, chx[tb % 2], junk, r_junk, sspool, rspool, xnpool,
                                    _HP(), tpp, ident, r_const, s1, r_s1, ada, r_ada, 0)
                    P.barrier()
                w32 = Pool(P, st, "w32_", 2, [128, 16, 512], F32)
                wbf = Pool(P, st, "wbf_", 2, [128, 16, 512], BF16)
                chw = [P.chan("w1b0"), P.chan("w1b1")]
                mmp = Pool(P, st, "mmb", 4, [128, 512], F32, psum=True)
                stA = Pool(P, st, "stA", 2, [128, NA], BF16)
                chst = [P.chan("stA0"), P.chan("stA1"), P.chan("stA2")]
                sgp = Pool(P, st, "sgp", 1, [128, NB], BF16)
                r_scr = Res("scr1b")
                self.wblk = 0

                def load_w(colranges):
                    wt, wr = w32.next()
                    wb, wbr = wbf.next()
                    ch = chw[self.wblk % 2]
                    self.wblk += 1
                    o = 0
                    for i, (c0, n) in enumerate(colranges):
                        P.dma("sp", ch, lambda wt=wt, c0=c0, n=n, o=o: nc.sync.dma_start(
                            out=wt[:, :, o:o + n], in_=w_in[:, c0:c0 + n].rearrange("(k p) c -> p k c", p=128)),
                            writes=[wr], cont=(i > 0))
                        o += n
                    P.op("pool", lambda wt=wt, wb=wb, o=o: nc.gpsimd.tensor_copy(out=wb[:, :, 0:o], in_=wt[:, :, 0:o]),
                         reads=[wr], writes=[wbr])
                    return wb, wbr

                def blocks(lo, hi):
                    b = []
                    t = lo
                    while t < hi:
                        n = min(512, hi - t)
                        b.append((t, n))
                        t += n
                    return b

                def fm_chunk(wb, wbr, woff, lo, hi, evac):
                    for (t, n) in blocks(lo, hi):
                        ps, r_ps = mmp.next()
                        for kc in range(16):
                            P.op("pe", lambda ps=ps, kc=kc, t=t, n=n: nc.tensor.matmul(
                                ps[:, 0:n], lhsT=wb[:, kc, woff:woff + 128], rhs=hTo[:, kc, t:t + n],
                                start=(kc == 0), stop=(kc == 15)), reads=[wbr, r_hTo], writes=[r_ps])
                        evac(ps, r_ps, t, n)

                ecount = [0]

                def copy_evac(dst, r_dst, off, func=None, scale=1.0):
                    def ev(ps, r_ps, t, n):
                        ecount[0] += 1
                        if func is None and scale == 1.0 and ecount[0] % 2 == 0:
                            P.op("dve", lambda: nc.vector.tensor_copy(out=dst[:, t - off:t - off + n], in_=ps[:, 0:n]),
                                 reads=[r_ps], writes=[r_dst])
                        else:
                            P.op("act", lambda: nc.scalar.activation(out=dst[:, t - off:t - off + n], in_=ps[:, 0:n],
                                                                     func=(func or AF.Copy), scale=scale),
                                 reads=[r_ps], writes=[r_dst])
                    return ev

                stn = [0]

                def store(dst_ap, sg, r_sg, n):
                    ch = chst[stn[0] % 3]
                    stn[0] += 1
                    P.dma("act", ch, lambda: nc.scalar.dma_start(out=dst_ap, in_=sg[:, 0:n]), reads=[r_sg], writes=[r_scr])

                OB = B0 - W0
                for qb in range(4):
                    wb, wbr = load_w([(C_Q + qb * 512, 512)])
                    for hl in range(4):
                        h = qb * 4 + hl
                        sg, r_sg = stA.next()
                        fm_chunk(wb, wbr, hl * 128, OB, NA, copy_evac(sg, r_sg, OB, scale=float(DK) ** -0.5))
                        store(QT[:, h, :], sg, r_sg, NB)
                wb, wbr = load_w([(C_KW, 256), (C_VW, 256)])
                for g in range(NG):
                    sg, r_sg = stA.next()
                    fm_chunk(wb, wbr, g * 128, 0, NA, copy_evac(sg, r_sg, 0))
                    store(kwT[g, :, :], sg, r_sg, NA)
                stv = Pool(P, st, "stvb", 2, [128, NG, 130], BF16)
                chv = [P.chan("svb0"), P.chan("svb1")]
                for i in range(2):
                    P.op("dve", lambda i=i: nc.vector.memset(stv.t[i][:], 0.0), writes=[stv.r[i]])
                for s in range(NA // 128):
                    ps, r_ps = mmp.next()
                    for kc in range(16):
                        P.op("pe", lambda ps=ps, s=s, kc=kc, wb=wb: nc.tensor.matmul(
                            ps[:, 0:256], lhsT=hTo[:, kc, s * 128:(s + 1) * 128], rhs=wb[:, kc, 256:512],
                            start=(kc == 0), stop=(kc == 15)), reads=[wbr, r_hTo], writes=[r_ps])
                    sv, r_sv = stv.next()
                    tile = W0 // 128 + s
                    P.op("dve", lambda ps=ps, sv=sv, tile=tile: nc.vector.tensor_scalar(
                        out=sv[:, :, 0:128], in0=ps[:, 0:256].rearrange("p (g d) -> p g d", g=NG),
                        scalar1=vtok_sb[:, tile:tile + 1], scalar2=None, op0=ALU.mult),
                        reads=[r_ps, r_const], writes=[r_sv])
                    P.op("pool", lambda sv=sv, tile=tile: nc.gpsimd.tensor_copy(
                        out=sv[:, :, 128:130], in_=vtok_sb[:, tile:tile + 1].unsqueeze(1).to_broadcast([128, NG, 2])),
                        reads=[r_const], writes=[r_sv])
                    P.dma("act", chv[s % 2], lambda sv=sv, s=s: nc.scalar.dma_start(
                        out=vw[s * 128:(s + 1) * 128, :, :], in_=sv[:]), reads=[r_sv], writes=[r_scr])
                wb, wbr = load_w([(C_GN, 48)])
                gst = Pool(P, st, "gst", 2, [128, 48], F32)
                chg = [P.chan("gs0"), P.chan("gs1")]
                for s in range(NB // 128):
                    ps, r_ps = mmp.next()
                    for kc in range(16):
                        P.op("pe", lambda ps=ps, s=s, kc=kc, wb=wb: nc.tensor.matmul(
                            ps[:, 0:48], lhsT=hTo[:, kc, OB + s * 128:OB + (s + 1) * 128], rhs=wb[:, kc, 0:48],
                            start=(kc == 0), stop=(kc == 15)), reads=[wbr, r_hTo], writes=[r_ps])
                    gs, r_gs = gst.next()
                    P.op("act", lambda ps=ps, gs=gs: nc.scalar.activation(out=gs[:], in_=ps[:, 0:48], func=AF.Sigmoid),
                         reads=[r_ps], writes=[r_gs])
                    P.dma("act", chg[s % 2], lambda gs=gs, s=s: nc.scalar.dma_start(
                        out=gates[s * 128:(s + 1) * 128, :], in_=gs[:]), reads=[r_gs], writes=[r_scr])
                for cb in range(8):
                    wb, wbr = load_w([(C_GLU + cb * 256, 256), (C_GLU + D + cb * 256, 256)])
                    for cl in range(2):
                        ch_ = cb * 2 + cl
                        sgm, r_sgm = sgp.next()
                        fm_chunk(wb, wbr, 256 + cl * 128, OB, NA, copy_evac(sgm, r_sgm, OB, func=AF.Sigmoid))
                        sg, r_sg = stA.next()

                        def ev(ps, r_ps, t, n, sg=sg, r_sg=r_sg, sgm=sgm, r_sgm=r_sgm):
                            P.op("dve", lambda: nc.vector.tensor_tensor(out=sg[:, t - OB:t - OB + n], in0=ps[:, 0:n],
                                                                        in1=sgm[:, t - OB:t - OB + n], op=ALU.mult),
                                 reads=[r_ps, r_sgm], writes=[r_sg])
                        fm_chunk(wb, wbr, cl * 128, OB, NA, ev)
                        P.op("dve", lambda sg=sg: nc.vector.tensor_scalar(out=sg[:, 0:OWN0 - B0], in0=sg[:, 0:OWN0 - B0], scalar1=self.hfl[:, 0:1],
                                                                        scalar2=None, op0=ALU.mult), reads=[r_sg, r_const], writes=[r_sg])
                        store(gluT[:, ch_, :], sg, r_sg, NB)
                for mb in range(8):
                    wb, wbr = load_w([(C_GM + mb * 512, 512)])
                    for cl in range(4):
                        sg, r_sg = stA.next()
                        fm_chunk(wb, wbr, cl * 128, OB, NA, copy_evac(sg, r_sg, OB, func=AF.Sigmoid))
                        store(mgT[:, mb * 4 + cl, :], sg, r_sg, NB)
                P.barrier()
            self.r_scr_all = Res("scr_all")
            if upto >= 3:
                self.phase_compress()
            if upto >= 4:
                self.phase_attn(upto)
            if upto >= 5:
                self.phase_mix()
            if upto >= 6:
                self.phase_ffn()
            P.emit()
        return nc

    def norm_prep(self, P, src, t0, nsub, xpool, chx, junk, r_junk, sspool, rspool, xnpool):
        nc = self.nc
        xt, r_xt = xpool.next()
        P.dma("sp", chx, lambda: nc.sync.dma_start(
            out=xt[:, 0:nsub, :], in_=src[t0:t0 + nsub * 128, :].rearrange("(s p) d -> p s d", p=128)), writes=[r_xt])
        ss, r_ss = sspool.next()
        rs, r_rs = rspool.next()
        for s in range(nsub):
            P.op("act", lambda s=s: nc.scalar.activation(out=junk[:], in_=xt[:, s, :], func=AF.Square, accum_out=ss[:, s:s + 1]),
                 reads=[r_xt], writes=[r_junk, r_ss])
        P.op("dve", lambda: nc.vector.tensor_scalar(out=rs[:, 0:nsub], in0=ss[:, 0:nsub], scalar1=1.0 / D, scalar2=EPS,
                                                    op0=ALU.mult, op1=ALU.add), reads=[r_ss], writes=[r_rs])
        P.op("act", lambda: nc.scalar.sqrt(out=rs[:, 0:nsub], in_=rs[:, 0:nsub]), reads=[r_rs], writes=[r_rs])
        P.op("dve", lambda: nc.vector.reciprocal(out=rs[:, 0:nsub], in_=rs[:, 0:nsub]), reads=[r_rs], writes=[r_rs])
        xn, r_xn = xnpool.next()
        for s in range(nsub):
            P.op("dve", lambda s=s: nc.vector.tensor_scalar(out=xn[:, s, :], in0=xt[:, s, :], scalar1=rs[:, s:s + 1],
                                                            scalar2=None, op0=ALU.mult), reads=[r_xt, r_rs], writes=[r_xn])
        return xn, r_xn

    def norm_groups(self, P, xn, r_xn, hT, r_hT, nsub, tpp, sc, r_sc, sh_col):
        nc = self.nc
        ident, r_const, ada, r_ada = self.ident, self.r_const, self.ada, self.r_ada

        def grp(j):
            tp, r_tp = tpp.next()
            for s in range(nsub):
                P.op("pe", lambda s=s: nc.tensor.transpose(out=tp[:, s * 128:(s + 1) * 128], in_=xn[:, s, j * 128:(j + 1) * 128], identity=ident[:]),
                     reads=[r_xn, r_const], writes=[r_tp])
            if j % 2 == 0:
                P.op("act", lambda: nc.scalar.activation(out=hT[:, j, 0:nsub * 128], in_=tp[:, 0:nsub * 128], func=AF.Identity,
                                                         scale=sc[:, j:j + 1], bias=ada[:, sh_col + j:sh_col + j + 1]),
                     reads=[r_tp, r_sc, r_ada], writes=[r_hT])
            else:
                P.op("dve", lambda: nc.vector.tensor_scalar(out=hT[:, j, 0:nsub * 128], in0=tp[:, 0:nsub * 128], scalar1=sc[:, j:j + 1],
                                                            scalar2=ada[:, sh_col + j:sh_col + j + 1], op0=ALU.mult, op1=ALU.add),
                     reads=[r_tp, r_sc, r_ada], writes=[r_hT])
        return [(lambda j=j: grp(j)) for j in range(16)]

    def norm_T(self, P, st, src, t0, nsub, xpool, chx, junk, r_junk, sspool, rspool, xnpool, hTpool, tpp,
               ident, r_const, sc, r_sc, ada, r_ada, sh_col, xt_in=None):
        nc = self.nc
        if xt_in is None:
            xt, r_xt = xpool.next()
            P.dma("sp", chx, lambda: nc.sync.dma_start(
                out=xt[:, 0:nsub, :], in_=src[t0:t0 + nsub * 128, :].rearrange("(s p) d -> p s d", p=128)), writes=[r_xt])
        else:
            xt, r_xt = xt_in
        ss, r_ss = sspool.next()
        rs, r_rs = rspool.next()
        for s in range(nsub):
            P.op("act", lambda s=s: nc.scalar.activation(out=junk[:], in_=xt[:, s, :], func=AF.Square,
                                                         accum_out=ss[:, s:s + 1]),
                 reads=[r_xt], writes=[r_junk, r_ss])
        P.op("dve", lambda: nc.vector.tensor_scalar(out=rs[:, 0:nsub], in0=ss[:, 0:nsub], scalar1=1.0 / D, scalar2=EPS,
                                                    op0=ALU.mult, op1=ALU.add), reads=[r_ss], writes=[r_rs])
        P.op("act", lambda: nc.scalar.sqrt(out=rs[:, 0:nsub], in_=rs[:, 0:nsub]), reads=[r_rs], writes=[r_rs])
        P.op("dve", lambda: nc.vector.reciprocal(out=rs[:, 0:nsub], in_=rs[:, 0:nsub]), reads=[r_rs], writes=[r_rs])
        xn, r_xn = xnpool.next()
        for s in range(nsub):
            P.op("dve", lambda s=s: nc.vector.tensor_scalar(out=xn[:, s, :], in0=xt[:, s, :], scalar1=rs[:, s:s + 1],
                                                            scalar2=None, op0=ALU.mult),
                 reads=[r_xt, r_rs], writes=[r_xn])
        hT, r_hT = hTpool.next()
        for j in range(16):
            tp, r_tp = tpp.next()
            for s in range(nsub):
                P.op("pe", lambda tp=tp, s=s, j=j: nc.tensor.transpose(
                    out=tp[:, s * 128:(s + 1) * 128], in_=xn[:, s, j * 128:(j + 1) * 128], identity=ident[:]),
                    reads=[r_xn, r_const], writes=[r_tp])
            if j % 2 == 0:
                P.op("act", lambda tp=tp, j=j: nc.scalar.activation(
                    out=hT[:, j, 0:nsub * 128], in_=tp[:, 0:nsub * 128], func=AF.Identity,
                    scale=sc[:, j:j + 1], bias=ada[:, sh_col + j:sh_col + j + 1]),
                    reads=[r_tp, r_sc, r_ada], writes=[r_hT])
            else:
                P.op("dve", lambda tp=tp, j=j: nc.vector.tensor_scalar(
                    out=hT[:, j, 0:nsub * 128], in0=tp[:, 0:nsub * 128], scalar1=sc[:, j:j + 1],
                    scalar2=ada[:, sh_col + j:sh_col + j + 1], op0=ALU.mult, op1=ALU.add),
                    reads=[r_tp, r_sc, r_ada], writes=[r_hT])
        self.last_rs = (rs, r_rs)
        self.last_xt = (xt, r_xt)
        return hT, r_hT


    def phase_compress(self):
        nc, P = self.nc, self.P
        with ExitStack() as st:
            sbt = lambda name, shape, dt: st.enter_context(nc.sbuf_tensor(name, list(shape), dt))
            raw = sbt("c_raw", [128, TV + 16], BF16)
            R = sbt("c_R", [128, 16, NCV + 1], BF16)
            w1f = sbt("c_w1f", [128, 32, 256], F32)
            w1b = sbt("c_w1b", [128, 32, 256], BF16)
            w2f = sbt("c_w2f", [128, 2, 128], F32)
            w2b = sbt("c_w2b", [128, 2, 128], BF16)
            pef = sbt("c_pef", [128, 32], F32)
            peb = sbt("c_peb", [128, 32], BF16)
            bia = sbt("c_bia", [128, 2], F32)
            hid = sbt("c_hid", [128, 2, NCV], BF16)
            kst = sbt("c_kst", [128, NCV], BF16)
            zpad = sbt("c_zpad", [128, NG, 16], BF16)
            r_raw, r_R, r_w1f, r_w1b, r_w2f, r_w2b, r_pe, r_bia, r_hid, r_kst, r_z = [Res() for _ in range(11)]
            vst = Pool(P, st, "c_vst", 2, [128, NG, 130], BF16)
            mmp = Pool(P, st, "c_mm", 3, [128, 512], F32, psum=True)
            bps = st.enter_context(nc.psum_tensor("c_bps", [128, 2], F32))
            r_bps = Res()
            ch = {n: P.chan("c_" + n) for n in ("raw", "w1", "w2", "pe", "k", "v0", "v1", "z")}
            r_out = self.r_scr_all
            P.op("dve", lambda: nc.vector.memset(zpad[:], 0.0), writes=[r_z])
            for i, rt in enumerate((self.kcT_raw, self.vcT_raw)):
                P.dma("act", ch["z"], lambda rt=rt: nc.scalar.dma_start(
                    out=rt[:, :, TV:TV + 16].rearrange("g p t -> p g t"), in_=zpad[:]), reads=[r_z], writes=[r_out], cont=(i > 0))
            P.dma("sp", ch["pe"], lambda: nc.sync.dma_start(out=pef[:], in_=self.i_cmp_pe.rearrange("l d -> d l"),
                                                           allow_slow_non_contiguous=True), writes=[r_pe])
            P.op("dve", lambda: nc.vector.tensor_copy(out=peb[:], in_=pef[:]), reads=[r_pe], writes=[r_pe])
            for i in range(2):
                P.op("dve", lambda i=i: nc.vector.memset(vst.t[i][:], 0.0), writes=[vst.r[i]])
            for kv in range(2):
                P.dma("sp", ch["w1"], lambda kv=kv: nc.sync.dma_start(
                    out=w1f[:], in_=self.i_w1[kv].rearrange("(l d) c -> d l c", d=128)), writes=[r_w1f])
                P.op("pool", lambda: nc.gpsimd.tensor_copy(out=w1b[:], in_=w1f[:]), reads=[r_w1f], writes=[r_w1b])
                P.dma("sp", ch["w2"], lambda kv=kv: nc.sync.dma_start(
                    out=w2f[:], in_=self.i_w2[kv].rearrange("(c p) d -> p c d", p=128)), writes=[r_w2f])
                P.op("dve", lambda: nc.vector.tensor_copy(out=w2b[:], in_=w2f[:]), reads=[r_w2f], writes=[r_w2b])
                for hc in range(2):
                    for l in range(32):
                        P.op("pe", lambda hc=hc, l=l: nc.tensor.matmul(
                            bps[:, hc:hc + 1], lhsT=w1b[:, l, hc * 128:(hc + 1) * 128], rhs=peb[:, l:l + 1],
                            start=(l == 0), stop=(l == 31)), reads=[r_w1b, r_pe], writes=[r_bps])
                P.op("dve", lambda: nc.vector.tensor_copy(out=bia[:], in_=bps[:]), reads=[r_bps], writes=[r_bia])
                src = (self.kcT_raw, self.vcT_raw)[kv]
                for g in range(NG):
                    P.dma("sp", ch["raw"], lambda g=g, src=src: nc.sync.dma_start(out=raw[:], in_=src[g, :, :]),
                          reads=[r_out], writes=[r_raw])
                    P.op("dve", lambda: nc.vector.tensor_copy(
                        out=R[:], in_=raw[:].rearrange("p (m l) -> p l m", l=16)), reads=[r_raw], writes=[r_R])
                    for hc in range(2):
                        for (n0, nn) in ((0, 512), (512, 512), (1024, 128)):
                            ps, r_ps = mmp.next()
                            for l in range(32):
                                P.op("pe", lambda ps=ps, hc=hc, l=l, n0=n0, nn=nn: nc.tensor.matmul(
                                    ps[:, 0:nn], lhsT=w1b[:, l, hc * 128:(hc + 1) * 128],
                                    rhs=R[:, l % 16, (l // 16) + n0:(l // 16) + n0 + nn],
                                    start=(l == 0), stop=(l == 31)), reads=[r_w1b, r_R], writes=[r_ps])
                            P.op("act", lambda ps=ps, hc=hc, n0=n0, nn=nn: nc.scalar.activation(
                                out=hid[:, hc, n0:n0 + nn], in_=ps[:, 0:nn], func=AF.Silu, bias=bia[:, hc:hc + 1]),
                                reads=[r_ps, r_bia], writes=[r_hid])
                    if kv == 0:
                        for (n0, nn) in ((0, 512), (512, 512), (1024, 128)):
                            ps, r_ps = mmp.next()
                            for hc in range(2):
                                P.op("pe", lambda ps=ps, hc=hc, n0=n0, nn=nn: nc.tensor.matmul(
                                    ps[:, 0:nn], lhsT=w2b[:, hc, :], rhs=hid[:, hc, n0:n0 + nn],
                                    start=(hc == 0), stop=(hc == 1)), reads=[r_w2b, r_hid], writes=[r_ps])
                            P.op("dve", lambda ps=ps, n0=n0, nn=nn: nc.vector.tensor_copy(out=kst[:, n0:n0 + nn], in_=ps[:, 0:nn]),
                                 reads=[r_ps], writes=[r_kst])
                        P.dma("act", ch["k"], lambda g=g: nc.scalar.dma_start(out=self.kcT[g, :, :], in_=kst[:]),
                              reads=[r_kst], writes=[r_out])
                    else:
                        for tl in range(NCV // 128):
                            ps, r_ps = mmp.next()
                            for hc in range(2):
                                P.op("pe", lambda ps=ps, hc=hc, tl=tl: nc.tensor.matmul(
                                    ps[:, 0:128], lhsT=hid[:, hc, tl * 128:(tl + 1) * 128], rhs=w2b[:, hc, :],
                                    start=(hc == 0), stop=(hc == 1)), reads=[r_w2b, r_hid], writes=[r_ps])
                            sv, r_sv = vst.next()
                            P.op("dve", lambda ps=ps, sv=sv, tl=tl, g=g: nc.vector.tensor_scalar(
                                out=sv[:, g, 0:128], in0=ps[:, 0:128], scalar1=self.vcmp_sb[:, tl:tl + 1], scalar2=None,
                                op0=ALU.mult), reads=[r_ps, self.r_const], writes=[r_sv])
                            P.op("pool", lambda sv=sv, tl=tl, g=g: nc.gpsimd.tensor_copy(
                                out=sv[:, g, 128:130], in_=self.vcmp_sb[:, tl:tl + 1].to_broadcast([128, 2])), reads=[self.r_const], writes=[r_sv])
                            P.dma("act", ch["v%d" % (tl % 2)], lambda sv=sv, tl=tl, g=g: nc.scalar.dma_start(
                                out=self.vca[tl * 128:(tl + 1) * 128, g, :], in_=sv[:, g, :]), reads=[r_sv], writes=[r_out])
            P.barrier()


    def phase_attn(self, upto):
        nc, P = self.nc, self.P
        with ExitStack() as st:
            sbt = lambda name, shape, dt: st.enter_context(nc.sbuf_tensor(name, list(shape), dt))
            bias_s = sbt("a_bs", [128, NH, NDS], F32)
            bias_c = sbt("a_bc", [128, NH, NDC], F32)
            masks = sbt("a_mk", [128, NMASK, 512], BF16)
            esel = sbt("a_es", [128, 64, 128], BF16)
            r_tab = Res("tables")
            cht = P.chan("a_tab")
            for i, (dst, src) in enumerate(((bias_s, self.i_bias_s), (bias_c, self.i_bias_c), (masks, self.i_masks), (esel, self.i_esel))):
                P.dma("sp", cht, lambda dst=dst, src=src: nc.sync.dma_start(out=dst[:], in_=src[:, :, :]), writes=[r_tab], cont=(i > 0))
            crhs = [[sbt(f"a_cr{g}_{jj}", [128, 130 + NBW], BF16) for jj in range(9)] for g in range(NG)]
            r_crhs = [[Res() for jj in range(9)] for g in range(NG)]
            ch_cr = [P.chan("a_cr0"), P.chan("a_cr1")]
            ovm = sbt("a_ovm", [128, 9, NBW], BF16)
            P.dma("sp", cht, lambda: nc.sync.dma_start(out=ovm[:], in_=self.i_ovm[:, :, :]), writes=[r_tab], cont=True)
            QTt = sbt("a_qt", [128, NH, 512], BF16)
            gat = sbt("a_gat", [128, 4, 48], F32)
            selb = sbt("a_selb", [128, 4, NBW], F32)
            selv = sbt("a_selv", [128, 4, NBW], F32)
            acc = sbt("a_acc", [128, 4, D], F32)
            imp = sbt("a_imp", [128, 4, NG, NBW], F32)
            mneg = sbt("a_mneg", [128, NG, 3, 512], BF16)
            r_qt, r_gat, r_sel, r_acc, r_imp, r_mneg = [Res() for _ in range(6)]
            ch_q = P.chan("a_q")
            kpool = Pool(P, st, "a_k", 3, [128, 2048], BF16)
            vpool = Pool(P, st, "a_v", 3, [128, 16, 130], BF16)
            chk = [P.chan(f"a_k{i}") for i in range(3)]
            chv = [P.chan(f"a_v{i}") for i in range(3)]
            ptp = Pool(P, st, "a_pt", 5, [128, 512], BF16)
            sps = Pool(P, st, "a_S", 3, [128, 512], F32, psum=True)
            ops_ = Pool(P, st, "a_o", 4, [128, 512], F32, psum=True)
            tps = Pool(P, st, "a_tp", 1, [128, 512], BF16, psum=True)
            small = Pool(P, st, "a_sm", 4, [128, 4], F32)
            osb = Pool(P, st, "a_osb", 8, [128, 130 + NBW], F32)
            sc1 = sbt("a_sc1", [128, 384], F32)
            sc2 = sbt("a_sc2", [128, 384], F32)
            m8 = sbt("a_m8", [128, 16], F32)
            mbf = sbt("a_mbf", [128, 8, 384], BF16)
            r_sc1, r_sc2, r_m8, r_mbf = Res(), Res(), Res(), Res()
            P.op("dve", lambda: nc.vector.memset(mbf[:], 0.0), writes=[r_mbf])
            P.op("dve", lambda: nc.vector.memset(sc1[:], 0.0), writes=[r_sc1])
            ch_dbg = P.chan("a_dbg")
            r_out = self.r_scr_all
            kcount = [0]

            def do_tile(ti, q0, nq):
                self._cr_loaded = [0, 0]
                qend = q0 + nq
                nsub = max(1, nq // 128)
                rows = min(128, nq)
                nend = qend // 16
                P.dma("sp", ch_q, lambda q0=q0, nq=nq: nc.sync.dma_start(out=QTt[:, :, 0:nq], in_=self.QT[:, :, q0 - B0:q0 - B0 + nq]),
                      reads=[r_out], writes=[r_qt])
                P.dma("sp", ch_q, lambda q0=q0, nq=nq, rows=rows, nsub=nsub: nc.sync.dma_start(
                    out=gat[0:rows, 0:nsub, :], in_=self.gates[q0 - B0:q0 - B0 + nq, :].rearrange("(s p) c -> p s c", p=rows)),
                    reads=[r_out], writes=[r_gat], cont=True)
                P.dma("sp", ch_q, lambda ti=ti: nc.sync.dma_start(out=selb[:], in_=self.i_selb[ti].rearrange("s p b -> p s b")),
                      writes=[r_sel], cont=True)
                P.dma("sp", ch_q, lambda ti=ti: nc.sync.dma_start(out=selv[:], in_=self.i_selv[ti].rearrange("s p b -> p s b")),
                      writes=[r_sel], cont=True)

                def run_branch(bi, h):
                    g = h // HG
                    W = exp_width(h, nq)
                    if bi == 0:
                        nch = min(n_cmp_chunks(h, nq), nend // 128)
                    elif bi == 1:
                        nch = min(n_slc_chunks(h, nq, 132), qend // 128)
                    else:
                        nch = min(n_slc_chunks(h, nq, 8 if nq == 512 else 5), 8 if nq == 512 else 5)
                    ncol = 130 + NBW if bi == 0 else 130
                    oacc = [ops_.next() for _ in range(nsub)]
                    kt = vt = None
                    stA = {}

                    def stageB(j, pt, r_pt, rhsV, r_rhsV):
                        for s_ in range(nsub):
                            o, r_o = oacc[s_]
                            P.op("pe", lambda o=o, s_=s_, pt=pt, rhsV=rhsV, j=j: nc.tensor.matmul(
                                o[0:rows, 0:ncol], lhsT=pt[:, s_ * 128:s_ * 128 + rows], rhs=rhsV,
                                start=(j == 0), stop=(j == nch - 1)), reads=[r_pt, r_rhsV], writes=[r_o])

                    for j in range(nch):
                        if bi == 0:
                            n0 = nend - 128 * (j + 1)
                            kt, r_kt = kpool.next()
                            kc_ = kcount[0] % 3
                            kcount[0] += 1
                            P.dma("sp", chk[kc_], lambda kt=kt, n0=n0: nc.sync.dma_start(out=kt[:, 0:128], in_=self.kcT[g, :, n0:n0 + 128]),
                                  reads=[r_out], writes=[r_kt])
                            if h % HG == 0 or j >= self._cr_loaded[g]:
                                P.dma("sp", ch_cr[g], lambda n0=n0, j=j: nc.sync.dma_start(out=crhs[g][j][:, 0:130], in_=self.vca[n0:n0 + 128, g, :]),
                                      reads=[r_out], writes=[r_crhs[g][j]])
                                P.op("dve", lambda j=j: nc.vector.tensor_scalar(out=crhs[g][j][:, 130:130 + NBW], in0=ovm[:, j, :],
                                                                               scalar1=crhs[g][j][:, 128:129], scalar2=None, op0=ALU.mult),
                                     reads=[r_tab, r_crhs[g][j]], writes=[r_crhs[g][j]])
                                self._cr_loaded[g] = max(self._cr_loaded[g], j + 1) if h % HG else j + 1
                            klhs, rhsV, r_rhsV = kt[:, 0:128], crhs[g][j][:, 0:ncol], r_crhs[g][j]
                            mk = MASK_IDX.get(("cmp", nq, j))
                            bcol = lambda r, j=j: bias_c[:, h, (nq - 2048 * (j + 1) - r * W) // 64 + ROFF:(nq - 2048 * (j + 1) - r * W) // 64 + ROFF + 1]
                        else:
                            if j % 16 == 0:
                                nsup = min(16, nch - j)
                                lo = qend - 128 * (j + nsup)
                                kt, r_kt = kpool.next()
                                vt, r_vt = vpool.next()
                                kc_ = kcount[0] % 3
                                kcount[0] += 1
                                if bi == 1:
                                    ksrc, vsrc, off = self.kslT, self.vsl, 0
                                else:
                                    ksrc, vsrc, off = self.kwT, self.vw, W0
                                P.dma("sp", chk[kc_], lambda kt=kt, lo=lo, nsup=nsup, ksrc=ksrc, off=off: nc.sync.dma_start(
                                    out=kt[:, 0:128 * nsup], in_=ksrc[g, :, lo - off:lo - off + 128 * nsup]), reads=[r_out], writes=[r_kt])
                                P.dma("sp", chv[kc_], lambda vt=vt, lo=lo, nsup=nsup, vsrc=vsrc, off=off: nc.sync.dma_start(
                                    out=vt[:, 0:nsup, :], in_=vsrc[lo - off:lo - off + 128 * nsup, g, :].rearrange("(s p) d -> p s d", p=128)),
                                    reads=[r_out], writes=[r_vt])
                                sup_n = nsup
                            sl = sup_n - 1 - (j % 16)
                            klhs, rhsV, r_rhsV = kt[:, sl * 128:(sl + 1) * 128], vt[:, sl, :], r_vt
                            mk = MASK_IDX.get(("slc" if bi == 1 else "win", nq, j))
                            bcol = lambda r, j=j: bias_s[:, h, (nq - 128 * (j + 1) - r * W) // 64 + DOFF:(nq - 128 * (j + 1) - r * W) // 64 + DOFF + 1]
                        S, r_S = sps.next()
                        nmm = 1 + (mk is not None) + (bi == 1)
                        cnt = [0]

                        def mm(lhsT, rhs, reads):
                            first, last = cnt[0] == 0, cnt[0] == nmm - 1
                            cnt[0] += 1
                            P.op("pe", lambda S=S: nc.tensor.matmul(S[:, 0:nq], lhsT=lhsT, rhs=rhs, start=first, stop=last),
                                 reads=reads, writes=[r_S])
                        mm(klhs, QTt[:, h, 0:nq], [r_kt, r_qt])
                        if mk is not None:
                            mm(self.ident[:], masks[:, mk, 0:nq], [self.r_const, r_tab])
                        if bi == 1:
                            b0 = NBLKW - 2 * (j + 1)
                            mm(esel[:, (b0 % 128) // 2, :], mneg[:, g, b0 // 128, 0:nq], [r_tab, r_mneg])
                        if self.dbg and "dumpS" in self.dbg and bi == 0 and h == 0 and j == 0:
                            dS = sbt("dbg_S", [128, 512], F32)
                            r_dS = Res()
                            P.op("dve", lambda: nc.vector.tensor_copy(out=dS[:, 0:nq], in_=S[:, 0:nq]), reads=[r_S], writes=[r_dS])
                            P.dma("act", ch_dbg, lambda: nc.scalar.dma_start(out=self.dumpS[:, 0:nq], in_=dS[:, 0:nq]), reads=[r_dS], writes=[r_out])
                            raise StopIteration
                        pt, r_pt = ptp.next()
                        for r in range(nq // W):
                            P.op("act", lambda r=r, bcol=bcol, S=S, pt=pt: nc.scalar.activation(
                                out=pt[:, r * W:(r + 1) * W], in_=S[:, r * W:(r + 1) * W], func=AF.Exp, bias=bcol(r)),
                                reads=[r_S, r_tab], writes=[r_pt])
                        stA[j] = (pt, r_pt, rhsV, r_rhsV)
                        if j > 1:
                            stageB(j - 2, *stA.pop(j - 2))
                    for jj_ in sorted(stA):
                        stageB(jj_, *stA[jj_])
                    stA.clear()
                    evac = []
                    for s_ in range(nsub):
                        o, r_o = oacc[s_]
                        ob_, r_ob = osb.next()
                        P.op("act", lambda o=o, ob_=ob_: nc.scalar.copy(out=ob_[0:rows, 0:ncol], in_=o[0:rows, 0:ncol]), reads=[r_o], writes=[r_ob])
                        evac.append((ob_, r_ob))
                    for s_ in range(nsub):
                        o, r_o = evac[s_]
                        sm, r_sm = small.next()
                        P.op("dve", lambda o=o, sm=sm: nc.vector.tensor_scalar(out=sm[0:rows, 0:1], in0=o[0:rows, 128:129], scalar1=1e-30,
                                                                             scalar2=None, op0=ALU.max), reads=[r_o], writes=[r_sm])
                        P.op("dve", lambda sm=sm: nc.vector.reciprocal(out=sm[0:rows, 1:2], in_=sm[0:rows, 0:1]), reads=[r_sm], writes=[r_sm])
                        P.op("dve", lambda sm=sm, s_=s_: nc.vector.tensor_tensor(
                            out=sm[0:rows, 2:3], in0=sm[0:rows, 1:2], in1=gat[0:rows, s_, bi * 16 + h:bi * 16 + h + 1], op=ALU.mult),
                            reads=[r_sm, r_gat], writes=[r_sm])
                        dst = acc[0:rows, s_, h * 128:(h + 1) * 128]
                        if bi == 0:
                            P.op("dve", lambda o=o, sm=sm, dst=dst: nc.vector.tensor_scalar(
                                out=dst, in0=o[0:rows, 0:128], scalar1=sm[0:rows, 2:3], scalar2=None, op0=ALU.mult),
                                reads=[r_o, r_sm], writes=[r_acc])
                            idst = imp[0:rows, s_, g, :]
                            if h % HG == 0:
                                P.op("dve", lambda o=o, sm=sm, idst=idst: nc.vector.tensor_scalar(
                                    out=idst, in0=o[0:rows, 130:130 + NBW], scalar1=sm[0:rows, 1:2], scalar2=None, op0=ALU.mult),
                                    reads=[r_o, r_sm], writes=[r_imp])
                            else:
                                P.op("dve", lambda o=o, sm=sm, idst=idst: nc.vector.scalar_tensor_tensor(
                                    out=idst, in0=o[0:rows, 130:130 + NBW], scalar=sm[0:rows, 1:2], in1=idst, op0=ALU.mult, op1=ALU.add),
                                    reads=[r_o, r_sm, r_imp], writes=[r_imp])
                        else:
                            P.op("dve", lambda o=o, sm=sm, dst=dst: nc.vector.scalar_tensor_tensor(
                                out=dst, in0=o[0:rows, 0:128], scalar=sm[0:rows, 2:3], in1=dst, op0=ALU.mult, op1=ALU.add),
                                reads=[r_o, r_sm, r_acc], writes=[r_acc])

                try:
                    for h in range(NH):
                        run_branch(0, h)
                except StopIteration:
                    return "stop"
                if self.dbg and "impd" in self.dbg:
                    P.dma("act", ch_dbg, lambda ti=ti: nc.scalar.dma_start(out=self.impd[ti].rearrange("s p g b -> p s g b"), in_=imp[:]),
                          reads=[r_imp], writes=[r_out])
                for g in range(NG):
                    for s_ in range(nsub):
                        P.op("dve", lambda s_=s_, g=g: nc.vector.tensor_tensor(out=sc1[0:rows, 0:NBW], in0=imp[0:rows, s_, g, :],
                                                                               in1=selb[0:rows, s_, :], op=ALU.add),
                             reads=[r_imp, r_sel], writes=[r_sc1])
                        P.op("dve", lambda s_=s_: nc.vector.tensor_tensor(out=sc1[0:rows, 0:NBW], in0=sc1[0:rows, 0:NBW],
                                                                          in1=selv[0:rows, s_, :], op=ALU.mult),
                             reads=[r_sc1, r_sel], writes=[r_sc1])
                        P.op("dve", lambda: nc.vector.max(out=m8[0:rows, 0:8], in_=sc1[0:rows, :]), reads=[r_sc1], writes=[r_m8])
                        P.op("dve", lambda: nc.vector.match_replace(out=sc2[0:rows, :], in_to_replace=m8[0:rows, 0:8],
                                                                    in_values=sc1[0:rows, :], imm_value=-1e30),
                             reads=[r_sc1, r_m8], writes=[r_sc2])
                        P.op("dve", lambda: nc.vector.max(out=m8[0:rows, 8:16], in_=sc2[0:rows, :]), reads=[r_sc2], writes=[r_m8])
                        P.op("dve", lambda g=g, s_=s_: nc.vector.tensor_scalar(out=mbf[0:rows, g * 4 + s_, 0:NBW], in0=sc1[0:rows, 0:NBW], scalar1=m8[0:rows, 15:16],
                                                                            scalar2=None, op0=ALU.is_ge), reads=[r_sc1, r_m8], writes=[r_mbf])
                if not (self.dbg and "branches" in self.dbg and 2 not in self.dbg["branches"]):
                    for h in range(NH):
                        run_branch(2, h)
                for g in range(NG):
                    for bg in range(3):
                        tp, r_tp = tps.next()
                        for s_ in range(nsub):
                            P.op("pe", lambda tp=tp, bg=bg, s_=s_, g=g: nc.tensor.transpose(
                                out=tp[:, s_ * 128:s_ * 128 + rows], in_=mbf[0:rows, g * 4 + s_, bg * 128:(bg + 1) * 128], identity=self.ident[0:rows, 0:rows]),
                                reads=[r_mbf, self.r_const], writes=[r_tp])
                        P.op("act", lambda tp=tp, bg=bg, g=g: nc.scalar.activation(out=mneg[:, g, bg, 0:nq], in_=tp[:, 0:nq], func=AF.Identity,
                                                                                  scale=30000.0, bias=-30000.0),
                             reads=[r_tp], writes=[r_mneg])
                for bi in (1,):
                    if self.dbg and "branches" in self.dbg and bi not in self.dbg["branches"]:
                        continue
                    for h in range(NH):
                        run_branch(bi, h)
                if True:
                    P.dma("act", ch_dbg, lambda q0=q0, nq=nq, rows=rows, nsub=nsub: nc.scalar.dma_start(
                        out=self.accd[q0 - Q0:q0 - Q0 + nq, :].rearrange("(s p) d -> p s d", p=rows), in_=acc[0:rows, 0:nsub, :]),
                        reads=[r_acc], writes=[r_out])

            for ti, (q0, nq) in enumerate(QTILES):
                if self.dbg and "tiles" in self.dbg and ti not in self.dbg["tiles"]:
                    continue
                if do_tile(ti, q0, nq) == "stop":
                    return
            P.barrier()


    def cast_jobs(self):
        jobs = []
        for name, src in (("wo_b", self.i_wo), ("wpw_b", self.i_wpw), ("wout_b", self.i_wout), ("wup_b", self.i_wup), ("wdn_b", self.i_wdn)):
            dst = self.wb_d[name]
            R, C = src.shape
            sv = src.rearrange("(p a) c -> p (a c)", p=128)
            dv = dst.rearrange("(p a) c -> p (a c)", p=128)
            F = R * C // 128
            for o in range(0, F, 8192):
                jobs.append((sv, dv, o, min(8192, F - o)))
        return jobs

    def cast_setup(self, st):
        nc, P = self.nc, self.P
        self.cj = self.cast_jobs()
        self.cji = 0
        self.cpend = None
        self.c32 = Pool(P, st, "cst32_", 2, [128, 2048], F32)
        self.cbf = Pool(P, st, "cstbf_", 2, [128, 2048], BF16)
        self.cch_i = [P.chan("cji0"), P.chan("cji1")]
        self.cch_o = [P.chan("cjo0"), P.chan("cjo1")]
        self.r_wb = Res("wb_scratch")

    def cast_step(self, n):
        nc, P = self.nc, self.P
        for _ in range(n):
            if self.cji >= len(self.cj):
                break
            sv, dv, o, w = self.cj[self.cji]
            i = self.cji % 2
            self.cji += 1
            t32, r32 = self.c32.next()
            tbf, rbf = self.cbf.next()
            P.dma("sp", self.cch_i[i], lambda t32=t32, sv=sv, o=o, w=w: nc.sync.dma_start(out=t32[:, 0:w], in_=sv[:, o:o + w]), writes=[r32])
            P.op("pool", lambda t32=t32, tbf=tbf, w=w: nc.gpsimd.tensor_copy(out=tbf[:, 0:w], in_=t32[:, 0:w]), reads=[r32], writes=[rbf])
            if self.cpend is not None:
                self.cpend()
            self.cpend = (lambda i=i, tbf=tbf, dv=dv, o=o, w=w, rbf=rbf: P.dma(
                "sp", self.cch_o[i], lambda: nc.sync.dma_start(out=dv[:, o:o + w], in_=tbf[:, 0:w]), reads=[rbf], writes=[self.r_wb]))
        if self.cji >= len(self.cj) and self.cpend is not None:
            self.cpend()
            self.cpend = None

    def phase_mix(self):
        nc, P = self.nc, self.P
        with ExitStack() as st:
            sbt = lambda name, shape, dt: st.enter_context(nc.sbuf_tensor(name, list(shape), dt))
            g1bc = sbt("m_g1bc", [128, D], F32)
            dww = sbt("m_dww", [128, 16, 31], F32)
            cols = sbt("m_cols", [128, 4, 16], F32)
            r_cst = Res()
            chc = P.chan("m_c")
            P.dma("sp", chc, lambda: nc.sync.dma_start(out=g1bc[:], in_=self.ada_d[2 * D:3 * D].partition_broadcast(128)), writes=[r_cst])
            for c_ in range(16):
                P.dma("sp", chc, lambda c_=c_: nc.sync.dma_start(out=dww[:, c_, :], in_=self.i_dww[:, c_ * 128:(c_ + 1) * 128].rearrange("k p -> p k"),
                                                              allow_slow_non_contiguous=True), writes=[r_cst], cont=True)
            for i, src in enumerate((self.i_dwb, self.i_lng, self.i_lnb, self.i_pwb)):
                P.dma("sp", chc, lambda i=i, src=src: nc.sync.dma_start(out=cols[:, i, :], in_=src.rearrange("(c p) -> p c", p=128),
                                                                      allow_slow_non_contiguous=True), writes=[r_cst], cont=True)
            oT = sbt("m_oT", [128, 16, 512], BF16)
            glu = sbt("m_glu", [128, 16, 544], BF16)
            ybf = sbt("m_ybf", [128, 16, 512], BF16)
            uc = sbt("m_uc", [128, 16, 512], BF16)
            mer = sbt("m_mer", [128, 16, 512], BF16)
            r_oT, r_glu, r_ybf, r_uc, r_mer = [Res() for _ in range(5)]
            accp = Pool(P, st, "m_acc", 2, [128, D], F32)
            accb = Pool(P, st, "m_accb", 2, [128, D], BF16)
            wblk = Pool(P, st, "m_w", 2, [128, 16, 512], BF16)
            dgp = Pool(P, st, "m_dg", 2, [128, 31, 128], BF16)
            ysq = Pool(P, st, "m_ysq", 2, [128, 512], BF16)
            mgp = Pool(P, st, "m_mg", 4, [128, 512], BF16)
            tmpf = Pool(P, st, "m_tmp", 3, [128, 512], F32)
            xp = Pool(P, st, "m_x", 3, [128, 512], F32)
            stat = sbt("m_stat", [128, 3, 512], F32)
            r_stat = Res()
            chl = [P.chan(f"m_l{i}") for i in range(4)]
            chw = [P.chan("m_w0"), P.chan("m_w1")]
            chs = [P.chan(f"m_s{i}") for i in range(3)]
            mm = Pool(P, st, "m_mm", 3, [128, 512], F32, psum=True)
            sps = Pool(P, st, "m_sp", 2, [128, 512], F32, psum=True)
            tps = Pool(P, st, "m_tp", 2, [128, 512], BF16, psum=True)
            r_in, r_out = self.r_scr_all, Res("xmid")
            cnt = {"l": 0, "w": 0, "s": 0}

            def load_wblk(wsrc, cb):
                wt, wr = wblk.next()
                ch = chw[cnt["w"] % 2]
                cnt["w"] += 1
                P.dma("sp", ch, lambda: nc.sync.dma_start(out=wt[:], in_=wsrc[:, cb * 512:(cb + 1) * 512].rearrange("(k p) c -> p k c", p=128)),
                      reads=[self.r_wb], writes=[wr])
                return wt, wr

            def mix_tile(ti, q0, nq):
                nsub, rows = max(1, nq // 128), min(128, nq)
                for s_ in range(nsub):
                    at, r_at = accp.next()
                    ab, r_ab = accb.next()
                    ch = chl[cnt["l"] % 4]
                    cnt["l"] += 1
                    P.dma("sp", ch, lambda at=at, s_=s_: nc.sync.dma_start(out=at[0:rows, :], in_=self.accd[q0 - Q0 + s_ * 128:q0 - Q0 + s_ * 128 + rows, :]),
                          reads=[r_in], writes=[r_at])
                    P.op("act", lambda at=at, ab=ab: nc.scalar.copy(out=ab[0:rows, :], in_=at[0:rows, :]), reads=[r_at], writes=[r_ab])
                    for hq in range(4):
                        tp, r_tp = tps.next()
                        for hl in range(4):
                            h = hq * 4 + hl
                            P.op("pe", lambda tp=tp, hl=hl, h=h, ab=ab: nc.tensor.transpose(
                                out=tp[:, hl * 128:hl * 128 + rows], in_=ab[0:rows, h * 128:(h + 1) * 128], identity=self.ident[0:rows, 0:rows]),
                                reads=[r_ab, self.r_const], writes=[r_tp])
                        P.op("dve", lambda tp=tp, hq=hq, s_=s_: nc.vector.tensor_copy(
                            out=oT[:, hq * 4:hq * 4 + 4, s_ * 128:s_ * 128 + rows],
                            in_=tp[:, :].rearrange("p (h q) -> p h q", h=4)[:, :, 0:rows]), reads=[r_tp], writes=[r_oT])
                for cb in range(4):
                    wt, wr = load_wblk(self.wb_d["wo_b"], cb)
                    for cl in range(4):
                        c = cb * 4 + cl
                        ps, r_ps = mm.next()
                        for kc in range(16):
                            P.op("pe", lambda ps=ps, kc=kc, cl=cl, wt=wt: nc.tensor.matmul(
                                ps[:, 0:nq], lhsT=wt[:, kc, cl * 128:(cl + 1) * 128], rhs=oT[:, kc, 0:nq], start=(kc == 0), stop=(kc == 15)),
                                reads=[wr, r_oT], writes=[r_ps])
                        mg, r_mg = mgp.next()
                        ch = chl[cnt["l"] % 4]
                        cnt["l"] += 1
                        P.dma("sp", ch, lambda mg=mg, c=c: nc.sync.dma_start(out=mg[:, 0:nq], in_=self.mgT[:, c, q0 - B0:q0 - B0 + nq]),
                              reads=[r_in], writes=[r_mg])
                        P.op("dve", lambda ps=ps, mg=mg, c=c: nc.vector.tensor_tensor(out=mer[:, c, 0:nq], in0=ps[:, 0:nq], in1=mg[:, 0:nq], op=ALU.mult),
                             reads=[r_ps, r_mg], writes=[r_mer])
                P.dma("sp", chl[cnt["l"] % 4], lambda: nc.sync.dma_start(out=glu[:, :, 0:nq + 32], in_=self.gluT[:, :, q0 - 32 - B0:q0 - B0 + nq]),
                      reads=[r_in], writes=[r_glu])
                cnt["l"] += 1
                s_sum, r_ssum = sps.next()
                s_sq, r_ssq = sps.next()
                for chn in range(16):
                    dg, r_dg = dgp.next()
                    P.op("dve", lambda dg=dg, chn=chn: nc.vector.tensor_tensor(
                        out=dg[:], in0=self.ident[:].unsqueeze(1).to_broadcast([128, 31, 128]),
                        in1=dww[:, chn, :].unsqueeze(2).to_broadcast([128, 31, 128]), op=ALU.mult),
                        reads=[self.r_const, r_cst], writes=[r_dg])
                    ps, r_ps = mm.next()
                    for k in range(31):
                        P.op("pe", lambda ps=ps, dg=dg, k=k, chn=chn: nc.tensor.matmul(
                            ps[:, 0:nq], lhsT=dg[:, k, :], rhs=glu[:, chn, k + 2:k + 2 + nq], start=(k == 0), stop=(k == 30)),
                            reads=[r_dg, r_glu], writes=[r_ps])
                    P.op("act", lambda ps=ps, chn=chn: nc.scalar.activation(out=ybf[:, chn, 0:nq], in_=ps[:, 0:nq], func=AF.Identity,
                                                                          bias=cols[:, 0, chn:chn + 1]), reads=[r_ps, r_cst], writes=[r_ybf])
                    yq, r_yq = ysq.next()
                    P.op("act", lambda ps=ps, chn=chn, yq=yq: nc.scalar.activation(out=yq[:, 0:nq], in_=ps[:, 0:nq], func=AF.Square,
                                                                                 bias=cols[:, 0, chn:chn + 1]), reads=[r_ps, r_cst], writes=[r_yq])
                    P.op("pe", lambda chn=chn: nc.tensor.matmul(s_sum[:, 0:nq], lhsT=self.ones[:], rhs=ybf[:, chn, 0:nq],
                                                                start=(chn == 0), stop=(chn == 15)), reads=[r_ybf, self.r_const], writes=[r_ssum])
                    P.op("pe", lambda chn=chn, yq=yq: nc.tensor.matmul(s_sq[:, 0:nq], lhsT=self.ones[:], rhs=yq[:, 0:nq],
                                                                       start=(chn == 0), stop=(chn == 15)), reads=[r_yq, self.r_const], writes=[r_ssq])
                mean, rstd, msq = stat[:, 0, 0:nq], stat[:, 1, 0:nq], stat[:, 2, 0:nq]
                P.op("dve", lambda: nc.vector.tensor_scalar(out=mean, in0=s_sum[:, 0:nq], scalar1=1.0 / D, scalar2=None, op0=ALU.mult),
                     reads=[r_ssum], writes=[r_stat])
                P.op("dve", lambda: nc.vector.tensor_tensor(out=msq, in0=mean, in1=mean, op=ALU.mult), reads=[r_stat], writes=[r_stat])
                P.op("dve", lambda: nc.vector.scalar_tensor_tensor(out=rstd, in0=s_sq[:, 0:nq], scalar=1.0 / D, in1=msq, op0=ALU.mult, op1=ALU.subtract),
                     reads=[r_ssq, r_stat], writes=[r_stat])
                P.op("dve", lambda: nc.vector.tensor_scalar(out=rstd, in0=rstd, scalar1=EPS, scalar2=None, op0=ALU.add), reads=[r_stat], writes=[r_stat])
                P.op("act", lambda: nc.scalar.sqrt(out=rstd, in_=rstd), reads=[r_stat], writes=[r_stat])
                P.op("dve", lambda: nc.vector.reciprocal(out=rstd, in_=rstd), reads=[r_stat], writes=[r_stat])
                for chn in range(16):
                    tf, r_tf = tmpf.next()
                    P.op("dve", lambda tf=tf, chn=chn: nc.vector.tensor_tensor(out=tf[:, 0:nq], in0=ybf[:, chn, 0:nq], in1=mean, op=ALU.subtract),
                         reads=[r_ybf, r_stat], writes=[r_tf])
                    P.op("dve", lambda tf=tf: nc.vector.tensor_tensor(out=tf[:, 0:nq], in0=tf[:, 0:nq], in1=rstd, op=ALU.mult),
                         reads=[r_tf, r_stat], writes=[r_tf])
                    P.op("act", lambda tf=tf, chn=chn: nc.scalar.activation(out=uc[:, chn, 0:nq], in_=tf[:, 0:nq], func=AF.Silu,
                                                                          scale=cols[:, 1, chn:chn + 1], bias=cols[:, 2, chn:chn + 1]),
                         reads=[r_tf, r_cst], writes=[r_uc])
                for cb in range(4):
                    wt, wr = load_wblk(self.wb_d["wpw_b"], cb)
                    for cl in range(4):
                        c = cb * 4 + cl
                        ps, r_ps = mm.next()
                        for kc in range(16):
                            P.op("pe", lambda ps=ps, kc=kc, cl=cl, wt=wt: nc.tensor.matmul(
                                ps[:, 0:nq], lhsT=wt[:, kc, cl * 128:(cl + 1) * 128], rhs=uc[:, kc, 0:nq], start=(kc == 0), stop=(kc == 15)),
                                reads=[wr, r_uc], writes=[r_ps])
                        mg, r_mg = mgp.next()
                        ch = chl[cnt["l"] % 4]
                        cnt["l"] += 1
                        P.dma("sp", ch, lambda mg=mg, c=c: nc.sync.dma_start(out=mg[:, 0:nq], in_=self.mgT[:, 16 + c, q0 - B0:q0 - B0 + nq]),
                              reads=[r_in], writes=[r_mg])
                        tf, r_tf = tmpf.next()
                        P.op("dve", lambda ps=ps, mg=mg, c=c, tf=tf: nc.vector.scalar_tensor_tensor(
                            out=tf[:, 0:nq], in0=ps[:, 0:nq], scalar=cols[:, 3, c:c + 1], in1=mg[:, 0:nq], op0=ALU.add, op1=ALU.mult),
                            reads=[r_ps, r_mg, r_cst], writes=[r_tf])
                        P.op("dve", lambda c=c, tf=tf: nc.vector.tensor_tensor(out=mer[:, c, 0:nq], in0=mer[:, c, 0:nq], in1=tf[:, 0:nq], op=ALU.add),
                             reads=[r_tf, r_mer], writes=[r_mer])
                for cb in range(4):
                    wt, wr = load_wblk(self.wb_d["wout_b"], cb)
                    for s_ in range(nsub):
                        ps, r_ps = mm.next()
                        for kc in range(16):
                            P.op("pe", lambda ps=ps, kc=kc, s_=s_, wt=wt: nc.tensor.matmul(
                                ps[0:rows, :], lhsT=mer[:, kc, s_ * 128:s_ * 128 + rows], rhs=wt[:, kc, :], start=(kc == 0), stop=(kc == 15)),
                                reads=[wr, r_mer], writes=[r_ps])
                        xt, r_xt = xp.next()
                        i3 = cnt["s"] % 3
                        cnt["s"] += 1
                        r0 = q0 + s_ * 128
                        P.dma("sp", chs[i3], lambda xt=xt, r0=r0, cb=cb: nc.sync.dma_start(out=xt[0:rows, :], in_=self.xv[r0:r0 + rows, cb * 512:(cb + 1) * 512]),
                              writes=[r_xt])
                        tf, r_tf = tmpf.next()
                        P.op("dve", lambda ps=ps, tf=tf, cb=cb: nc.vector.tensor_tensor(out=tf[0:rows, :], in0=ps[0:rows, :], in1=g1bc[0:rows, cb * 512:(cb + 1) * 512],
                                                                                     op=ALU.mult), reads=[r_ps, r_cst], writes=[r_tf])
                        P.op("dve", lambda xt=xt, tf=tf: nc.vector.tensor_tensor(out=xt[0:rows, :], in0=xt[0:rows, :], in1=tf[0:rows, :], op=ALU.add),
                             reads=[r_tf, r_xt], writes=[r_xt])
                        P.dma("act", chs[i3], lambda xt=xt, r0=r0, cb=cb: nc.scalar.dma_start(
                            out=self.xmid[r0 - Q0:r0 - Q0 + rows, cb * 512:(cb + 1) * 512], in_=xt[0:rows, :]), reads=[r_xt], writes=[r_out])

            for ti, (q0, nq) in enumerate(QTILES):
                if self.dbg and "tiles" in self.dbg and ti not in self.dbg["tiles"]:
                    continue
                mix_tile(ti, q0, nq)
            P.barrier()


    def phase_ffn(self):
        nc, P = self.nc, self.P
        with ExitStack() as st:
            sbt = lambda name, shape, dt: st.enter_context(nc.sbuf_tensor(name, list(shape), dt))
            g2bc = sbt("f_g2bc", [128, D], F32)
            fgbc = sbt("f_fgbc", [128, D], F32)
            n2g = sbt("f_n2g", [128, 16], F32)
            s2 = sbt("f_s2", [128, 16], F32)
            fdw = sbt("f_fdw", [128, 3, 88], F32)
            fdb = sbt("f_fdb", [128, 88], F32)
            hfl = sbt("f_hfl", [128, 1], F32)
            r_cst, r_s2 = Res(), Res()
            chc = P.chan("f_c")
            P.dma("sp", chc, lambda: nc.sync.dma_start(out=g2bc[:], in_=self.ada_d[5 * D:6 * D].partition_broadcast(128)), writes=[r_cst])
            P.dma("sp", chc, lambda: nc.sync.dma_start(out=fgbc[:], in_=self.i_fg.partition_broadcast(128)), writes=[r_cst], cont=True)
            P.dma("sp", chc, lambda: nc.sync.dma_start(out=n2g[:], in_=self.i_n2g.rearrange("(c p) -> p c", p=128), allow_slow_non_contiguous=True),
                  writes=[r_cst], cont=True)
            for k in range(3):
                P.dma("sp", chc, lambda k=k: nc.sync.dma_start(out=fdw[:, k, :], in_=self.i_fdw[k, :].rearrange("(c p) -> p c", p=128),
                                                            allow_slow_non_contiguous=True), writes=[r_cst], cont=True)
            P.dma("sp", chc, lambda: nc.sync.dma_start(out=fdb[:], in_=self.i_fdb.rearrange("(c p) -> p c", p=128), allow_slow_non_contiguous=True),
                  writes=[r_cst], cont=True)
            P.dma("sp", chc, lambda: nc.sync.dma_start(out=hfl[:], in_=self.i_hflag[:, :]), writes=[r_cst], cont=True)
            P.op("dve", lambda: nc.vector.scalar_tensor_tensor(out=s2[:], in0=self.ada[:, 64:80], scalar=1.0, in1=n2g[:], op0=ALU.add, op1=ALU.mult),
                 reads=[self.r_ada, r_cst], writes=[r_s2])
            xpool = Pool(P, st, "f_x", 1, [128, 4, D], F32)
            chx = P.chan("f_x")
            junk = sbt("f_junk", [128, D], BF16)
            r_junk = Res()
            sspool = Pool(P, st, "f_ss", 2, [128, 4], F32)
            rspool = Pool(P, st, "f_rs", 2, [128, 4], F32)
            xnpool = Pool(P, st, "f_xn", 1, [128, 4, D], BF16)
            h2T = sbt("f_h2T", [128, 16, 514], BF16)
            r_h2T = Res()
            tpp = Pool(P, st, "f_tp", 2, [128, 512], BF16, psum=True)
            up = Pool(P, st, "f_up", 2, [128, 1024], F32, psum=True)
            dn = Pool(P, st, "f_dn", 2, [128, 512], F32, psum=True)
            wup = Pool(P, st, "f_wu", 2, [128, 16, 256], BF16)
            wdn = Pool(P, st, "f_wd", 2, [128, 44, 256], BF16)
            chwu = [P.chan("f_wu0"), P.chan("f_wu1")]
            chwd = [P.chan("f_wd0"), P.chan("f_wd1")]
            z = sbt("f_z", [128, 44, 512], BF16)
            r_z = Res()
            hh, r_hh = z[:, 0:16, 0:128], r_z
            Tp = Pool(P, st, "f_T", 3, [128, 512], F32)
            sgp = Pool(P, st, "f_sg", 1, [128, 512], BF16)
            tmp = Pool(P, st, "f_tmp", 1, [128, 256], F32)
            cho = [P.chan("f_o0"), P.chan("f_o1")]
            r_in = Res()
            wupb, wdnb = self.wb_d["wup_b"], self.wb_d["wdn_b"]
            cnt = {"u": 0, "d": 0, "o": 0}

            class HP:
                def __init__(s_, ap, res):
                    s_.ap, s_.res = ap, res

                def next(s_):
                    return s_.ap, s_.res

            self.norm_T(P, st, self.xmid, 0, 1, xpool, chx, junk, r_junk, sspool, rspool, xnpool, HP(hh, r_hh), tpp,
                        self.ident, self.r_const, s2, r_s2, self.ada, self.r_ada, 48)
            P.op("dve", lambda: nc.vector.tensor_scalar(out=h2T[:, :, 0:2], in0=hh[:, :, 62:64], scalar1=hfl[:, 0:1], scalar2=None, op0=ALU.mult),
                 reads=[r_hh, r_cst], writes=[r_h2T])

            def window(w):
                row0 = HALO + 512 * w
                if w > 0:
                    P.op("pool", lambda: nc.gpsimd.tensor_copy(out=h2T[:, :, 0:2], in_=h2T[:, :, 512:514]), reads=[r_h2T], writes=[r_h2T])
                self.norm_T(P, st, self.xmid, row0, 4, xpool, chx, junk, r_junk, sspool, rspool, xnpool, HP(h2T[:, :, 2:514], r_h2T), tpp,
                            self.ident, self.r_const, s2, r_s2, self.ada, self.r_ada, 48)
                xt, r_xt = self.last_xt
                for pb in range(44):
                    wt, wr = wup.next()
                    ch = chwu[cnt["u"] % 2]
                    cnt["u"] += 1
                    P.dma("sp", ch, lambda wt=wt, pb=pb: nc.sync.dma_start(
                        out=wt[:, :, 0:128], in_=wupb[:, pb * 128:(pb + 1) * 128].rearrange("(k p) c -> p k c", p=128)), reads=[self.r_wb], writes=[wr])
                    P.dma("sp", ch, lambda wt=wt, pb=pb: nc.sync.dma_start(
                        out=wt[:, :, 128:256], in_=wupb[:, FF + pb * 128:FF + (pb + 1) * 128].rearrange("(k p) c -> p k c", p=128)),
                        reads=[self.r_wb], writes=[wr], cont=True)
                    for cl in range(1):
                        c = pb
                        Ts = []
                        for half in range(2):
                            cc = c + 44 * half
                            woff = half * 128
                            ps, r_ps = up.next()
                            for kc in range(16):
                                P.op("pe", lambda ps=ps, kc=kc, woff=woff, wt=wt: nc.tensor.matmul(
                                    ps[:, 512:1024], lhsT=wt[:, kc, woff:woff + 128], rhs=h2T[:, kc, 2:514], start=(kc == 0), stop=(kc == 15)),
                                    reads=[wr, r_h2T], writes=[r_ps])
                            for kc in range(16):
                                P.op("pe", lambda ps=ps, kc=kc, woff=woff, wt=wt: nc.tensor.matmul(
                                    ps[:, 510:512], lhsT=wt[:, kc, woff:woff + 128], rhs=h2T[:, kc, 0:2], start=(kc == 0), stop=(kc == 15)),
                                    reads=[wr, r_h2T], writes=[r_ps])
                            T_, r_T = Tp.next()
                            P.op("act", lambda ps=ps, T_=T_, cc=cc: nc.scalar.activation(out=T_[:], in_=ps[:, 512:1024], func=AF.Identity,
                                                                                        scale=fdw[:, 2, cc:cc + 1], bias=fdb[:, cc:cc + 1]),
                                 reads=[r_ps, r_cst], writes=[r_T])
                            P.op("dve", lambda ps=ps, T_=T_, cc=cc: nc.vector.scalar_tensor_tensor(
                                out=T_[:], in0=ps[:, 511:1023], scalar=fdw[:, 1, cc:cc + 1], in1=T_[:], op0=ALU.mult, op1=ALU.add),
                                reads=[r_ps, r_T, r_cst], writes=[r_T])
                            P.op("dve", lambda ps=ps, T_=T_, cc=cc: nc.vector.scalar_tensor_tensor(
                                out=T_[:], in0=ps[:, 510:1022], scalar=fdw[:, 0, cc:cc + 1], in1=T_[:], op0=ALU.mult, op1=ALU.add),
                                reads=[r_ps, r_T, r_cst], writes=[r_T])
                            Ts.append((T_, r_T))
                        (Ta, r_Ta), (Tg, r_Tg) = Ts
                        sg, r_sg = sgp.next()
                        P.op("act", lambda Tg=Tg, sg=sg: nc.scalar.activation(out=sg[:], in_=Tg[:], func=AF.Silu), reads=[r_Tg], writes=[r_sg])
                        P.op("dve", lambda Ta=Ta, sg=sg, c=c: nc.vector.tensor_tensor(out=z[:, c, :], in0=Ta[:], in1=sg[:], op=ALU.mult),
                             reads=[r_Ta, r_sg], writes=[r_z])
                for cb in range(8):
                    wt, wr = wdn.next()
                    ch = chwd[cnt["d"] % 2]
                    cnt["d"] += 1
                    P.dma("sp", ch, lambda wt=wt, cb=cb: nc.sync.dma_start(
                        out=wt[:], in_=wdnb[:, cb * 256:(cb + 1) * 256].rearrange("(k p) c -> p k c", p=128)), reads=[self.r_wb], writes=[wr])
                    for s_ in range(4):
                        ps, r_ps = dn.next()
                        for kc in range(44):
                            P.op("pe", lambda ps=ps, kc=kc, s_=s_, wt=wt: nc.tensor.matmul(
                                ps[:, 0:256], lhsT=z[:, kc, s_ * 128:(s_ + 1) * 128], rhs=wt[:, kc, :], start=(kc == 0), stop=(kc == 43)),
                                reads=[wr, r_z], writes=[r_ps])
                        tf, r_tf = tmp.next()
                        P.op("dve", lambda ps=ps, tf=tf, cb=cb: nc.vector.tensor_tensor(out=tf[:], in0=ps[:, 0:256], in1=g2bc[:, cb * 256:(cb + 1) * 256], op=ALU.mult),
                             reads=[r_ps, r_cst], writes=[r_tf])
                        P.op("dve", lambda tf=tf, s_=s_, cb=cb, xt=xt: nc.vector.tensor_tensor(
                            out=xt[:, s_, cb * 256:(cb + 1) * 256], in0=xt[:, s_, cb * 256:(cb + 1) * 256], in1=tf[:], op=ALU.add),
                            reads=[r_tf, r_xt], writes=[r_xt])
                ss, r_ss = sspool.next()
                rs, r_rs = rspool.next()
                for s_ in range(4):
                    P.op("act", lambda s_=s_, xt=xt, ss=ss: nc.scalar.activation(out=junk[:], in_=xt[:, s_, :], func=AF.Square, accum_out=ss[:, s_:s_ + 1]),
                         reads=[r_xt], writes=[r_junk, r_ss])
                P.op("dve", lambda ss=ss, rs=rs: nc.vector.tensor_scalar(out=rs[:], in0=ss[:], scalar1=1.0 / D, scalar2=EPS, op0=ALU.mult, op1=ALU.add),
                     reads=[r_ss], writes=[r_rs])
                P.op("act", lambda rs=rs: nc.scalar.sqrt(out=rs[:], in_=rs[:]), reads=[r_rs], writes=[r_rs])
                P.op("dve", lambda rs=rs: nc.vector.reciprocal(out=rs[:], in_=rs[:]), reads=[r_rs], writes=[r_rs])
                for s_ in range(4):
                    P.op("dve", lambda s_=s_, xt=xt, rs=rs: nc.vector.scalar_tensor_tensor(
                        out=xt[:, s_, :], in0=xt[:, s_, :], scalar=rs[:, s_:s_ + 1], in1=fgbc[:], op0=ALU.mult, op1=ALU.mult),
                        reads=[r_xt, r_rs, r_cst], writes=[r_xt])
                    i2 = cnt["o"] % 2
                    cnt["o"] += 1
                    t0 = 512 * w + 128 * s_
                    P.dma("act", cho[i2], lambda s_=s_, xt=xt, t0=t0: nc.scalar.dma_start(out=self.out[t0:t0 + 128, :], in_=xt[:, s_, :]), reads=[r_xt])

            for w in range(4):
                if self.dbg and "wins" in self.dbg and w not in self.dbg["wins"]:
                    continue
                window(w)
            P.barrier()

_STATIC = {}


def static_tables():
    if _STATIC:
        return _STATIC
    sl = np.array(SLOPES, np.float64)
    i = np.arange(128, dtype=np.float64)
    bs = sl[None, :, None] * (i[:, None, None] + 64.0 * (np.arange(NDS)[None, None, :] - DOFF))
    bc = sl[None, :, None] * (16.0 * i[:, None, None] + 31.0 + 64.0 * (np.arange(NDC)[None, None, :] - ROFF))
    masks = np.zeros((128, NMASK, 512), np.float32)
    for n, v in enumerate(MASK_LIST):
        masks[:, n, :v.shape[1]] = np.where(v, 0.0, -30000.0)
    E = np.zeros((128, 64, 128), np.float32)
    for u in range(64):
        for k in range(128):
            E[2 * u + k // 64, u, k] = 1.0
    Ov = np.zeros((128, 9, NBW), np.float32)
    for jj in range(9):
        for ii in range(128):
            for b in range(NBLKW):
                dlt = ii - 128 * (jj + 1) - 4 * b + 1056
                if -1 <= dlt <= 3:
                    Ov[ii, jj, b] = 1.0
    _STATIC.update({"ident": np.eye(128, dtype=np.float32).astype(NPBF),
                    "bias_s": bs.astype(np.float32), "bias_c": bc.astype(np.float32),
                    "masks": masks.astype(NPBF), "esel": E.astype(NPBF), "ovm": Ov.astype(NPBF),
                    "ones_bf": np.ones((128, 128), np.float32).astype(NPBF)})
    return _STATIC


def host_tables(c):
    t_start = c * TOWN
    real = np.arange(TV) - OWN0 + t_start
    vt = (real >= 0).astype(np.float32)
    nv = np.arange(NCV)
    rn = nv - (OWN0 - t_start) // 16
    vc = ((rn >= 0) & (rn <= 1022) & (nv <= 1150)).astype(np.float32)
    selb = np.zeros((5, 4, 128, NBW), np.float32)
    selv = np.zeros((5, 4, 128, NBW), np.float32)
    b = np.arange(NBLKW)
    for ti, (q0, nq) in enumerate(QTILES):
        qend = q0 + nq
        jv = b + qend // 64 - NBLKW
        realb = jv - (OWN0 - t_start) // 64
        for s_ in range(max(1, nq // 128)):
            rows = min(128, nq)
            tq = q0 + 128 * s_ + np.arange(rows)
            cur = tq // 64
            valid = (realb[None, :] >= 0) & (jv[None, :] <= cur[:, None])
            forced = (realb[None, :] == 0) | (jv[None, :] == cur[:, None]) | (jv[None, :] == cur[:, None] - 1)
            selv[ti, s_, :rows, :NBLKW] = valid
            selb[ti, s_, :rows, :NBLKW] = 1.0 + 1e6 * forced
    d = {"vtok": np.ascontiguousarray(vt.reshape(TV // 128, 128).T),
         "vcmp": np.ascontiguousarray(vc.reshape(NCV // 128, 128).T),
         "selb": selb, "selv": selv,
         "hflag": np.full((128, 1), 1.0 if c > 0 else 0.0, np.float32)}
    d.update(static_tables())
    return d


def make_inputs(inputs, c):
    x = np.asarray(inputs["x"], np.float32)[0]
    t_start = c * TOWN
    xv = np.zeros((TV, D), np.float32)
    lo = OWN0 - t_start
    xv[lo:] = x[:t_start + TOWN]
    m = {"xv": xv, "c": np.asarray(inputs["c"], np.float32),
         "w_ada": np.asarray(inputs["w_ada"], np.float32)[0], "b_ada": np.asarray(inputs["b_ada"], np.float32)[0],
         "norm1_g": np.asarray(inputs["norm1_g"], np.float32)[0], "w_in": np.asarray(inputs["w_in"], np.float32)[0]}
    for n in ("cmp_pe", "w_kc1", "w_kc2", "w_vc1", "w_vc2", "w_o_nsa", "conv_dw_w", "conv_dw_b", "conv_ln_g", "conv_ln_b",
              "conv_pw_w", "conv_pw_b", "w_out", "norm2_g", "ffn_w_up", "ffn_dw_w", "ffn_dw_b", "ffn_w_down"):
        m[n] = np.asarray(inputs[n], np.float32)[0]
    m["final_g"] = np.asarray(inputs["final_g"], np.float32)
    m.update(host_tables(c))
    return m


def kernel(**inputs):
    k = K()
    nc = k.build()
    in_maps = [{n: v for n, v in make_inputs(inputs, c).items() if n in k.ins} for c in range(NCORE)]
    res = run_bass_kernel_spmd(nc, in_maps, core_ids=list(range(NCORE)))
    outs = [np.asarray(res.results[c]["out"], np.float32) for c in range(NCORE)]
    return np.concatenate(outs, axis=0)[None]
```
